# Optimizing a Trainium2 kernel written in Bass

```python
import math
import jax, jax.numpy as jnp
from jax import lax
import numpy as np

D_MODEL = 1024
BATCH = 8
SEQ = 2048
DEPTH = 1

FOX_HEADS = 8
FOX_HEAD_DIM = 64
FOX_WIDTH = FOX_HEADS * FOX_HEAD_DIM
MLA_HEADS = 8
MLA_NOPE_DIM = 64
MLA_ROPE_DIM = 32
MLA_V_DIM = 64
MLA_Q_LORA = 768
MLA_KV_LORA = 256
MLA_WIDTH = MLA_HEADS * MLA_V_DIM
ROPE_THETA = 10000.0
D_FF = ((8 * D_MODEL + 3 * 256 - 1) // (3 * 256)) * 256
Q_BLOCK = 128
NORM_EPS = 1e-6

IN_WIDTHS = (
    FOX_WIDTH,
    FOX_WIDTH,
    FOX_WIDTH,
    FOX_HEADS,
    MLA_Q_LORA,
    MLA_KV_LORA,
    MLA_ROPE_DIM,
    D_MODEL,
    D_MODEL,
)
D_IN = sum(IN_WIDTHS)

kernel_name = "hybrid_fox_mla_sandwich_adaln_block"


def rmsnorm(x, g):
    xf = x.astype(jnp.float32)
    y = xf * lax.rsqrt(jnp.mean(xf * xf, axis=-1, keepdims=True) + NORM_EPS)
    return (y * g.astype(jnp.float32)).astype(x.dtype)


def rope(x, cos, sin):
    x1, x2 = jnp.split(x, 2, axis=-1)
    return jnp.concatenate([x1 * cos - x2 * sin, x2 * cos + x1 * sin], axis=-1).astype(x.dtype)


def blocked_causal_attention(q, k, v, scale, log_decay=None):
    B, H, S, dk = q.shape
    nb = S // Q_BLOCK
    q_blocks = q.reshape(B, H, nb, Q_BLOCK, dk).transpose(2, 0, 1, 3, 4)
    d_blocks = None if log_decay is None else log_decay.reshape(B, H, nb, Q_BLOCK).transpose(2, 0, 1, 3)
    key_pos = jnp.arange(S)

    def one_block(args):
        blk, q_blk, d_blk = args
        s = jnp.einsum("bhqd,bhkd->bhqk", q_blk, k, preferred_element_type=jnp.float32) * scale
        if d_blk is not None:
            s = s + (d_blk[..., :, None] - log_decay[:, :, None, :])
        query_pos = blk * Q_BLOCK + jnp.arange(Q_BLOCK)
        s = jnp.where(key_pos[None, :] <= query_pos[:, None], s, -jnp.inf)
        p = jax.nn.softmax(s, axis=-1)
        return jnp.einsum("bhqk,bhkd->bhqd", p.astype(v.dtype), v)

    out = lax.map(one_block, (jnp.arange(nb), q_blocks, d_blocks))
    return out.transpose(1, 2, 0, 3, 4).reshape(B, H, S, v.shape[-1])


def setup_inputs(seed: int = 0) -> dict:
    key = jax.random.key(seed)
    ks = jax.random.split(key, 24)
    f32 = jnp.float32

    def normal(k, shape, fan_in):
        return jax.random.normal(k, shape, f32) * (fan_in ** -0.5)

    def gain(k, shape):
        return 1.0 + 0.05 * jax.random.normal(k, shape, f32)

    x = jax.random.normal(ks[0], (BATCH, SEQ, D_MODEL), f32)
    c = jax.random.normal(ks[1], (BATCH, D_MODEL), f32)
    offsets = jax.random.randint(ks[2], (BATCH, 1), 0, 1024, dtype=jnp.int32)
    positions = (offsets + jnp.arange(SEQ, dtype=jnp.int32)[None, :]).astype(jnp.int32)

    return {
        "x": x,
        "c": c,
        "positions": positions,
        "w_ada": normal(ks[3], (DEPTH, D_MODEL, 6 * D_MODEL), D_MODEL),
        "b_ada": 0.02 * jax.random.normal(ks[4], (DEPTH, 6 * D_MODEL), f32),
        "g_pre_mix": gain(ks[5], (DEPTH, D_MODEL)),
        "g_post_mix": gain(ks[6], (DEPTH, D_MODEL)),
        "g_pre_ffn": gain(ks[7], (DEPTH, D_MODEL)),
        "g_post_ffn": gain(ks[8], (DEPTH, D_MODEL)),
        "w_in": normal(ks[9], (DEPTH, D_MODEL, D_IN), D_MODEL),
        "b_forget": 3.0 + 0.5 * jax.random.normal(ks[10], (DEPTH, FOX_HEADS), f32),
        "g_q_lora": gain(ks[11], (DEPTH, MLA_Q_LORA)),
        "w_uq": normal(ks[12], (DEPTH, MLA_Q_LORA, MLA_HEADS * (MLA_NOPE_DIM + MLA_ROPE_DIM)), MLA_Q_LORA),
        "g_kv_lora": gain(ks[13], (DEPTH, MLA_KV_LORA)),
        "w_ukv": normal(ks[14], (DEPTH, MLA_KV_LORA, MLA_HEADS * (MLA_NOPE_DIM + MLA_V_DIM)), MLA_KV_LORA),
        "w_proj_fox": normal(ks[15], (DEPTH, FOX_WIDTH, D_MODEL), FOX_WIDTH),
        "w_proj_mla": normal(ks[16], (DEPTH, MLA_WIDTH, D_MODEL), MLA_WIDTH),
        "w_out": normal(ks[17], (DEPTH, D_MODEL, D_MODEL), D_MODEL),
        "w_ffn_in": normal(ks[18], (DEPTH, D_MODEL, 2 * D_FF), D_MODEL),
        "w_ffn_out": normal(ks[19], (DEPTH, D_FF, D_MODEL), D_FF),
    }


def reference(x, c, positions, w_ada, b_ada, g_pre_mix, g_post_mix, g_pre_ffn, g_post_ffn,
              w_in, b_forget, g_q_lora, w_uq, g_kv_lora, w_ukv, w_proj_fox, w_proj_mla,
              w_out, w_ffn_in, w_ffn_out):
    B, S, D = x.shape
    inv_freq = 1.0 / (ROPE_THETA ** (jnp.arange(0, MLA_ROPE_DIM, 2, dtype=jnp.float32) / MLA_ROPE_DIM))
    angles = positions.astype(jnp.float32)[..., None] * inv_freq
    cos, sin = jnp.cos(angles), jnp.sin(angles)
    split_points = [int(v) for v in np.cumsum(IN_WIDTHS)[:-1]]
    silu_c = jax.nn.silu(c)

    for l in range(DEPTH):
        mod = (silu_c @ w_ada[l] + b_ada[l])[:, None, :]
        shift_mix, scale_mix, gate_mix, shift_ffn, scale_ffn, gate_ffn = jnp.split(mod, 6, axis=-1)

        h = rmsnorm(x, g_pre_mix[l]) * (1.0 + scale_mix) + shift_mix
        proj = h @ w_in[l]
        (fq, fk, fv, f_logit, cq, ckv, k_rope_in, gate_fox, gate_mla) = jnp.split(proj, split_points, axis=-1)

        q_a = fq.reshape(B, S, FOX_HEADS, FOX_HEAD_DIM).transpose(0, 2, 1, 3)
        k_a = fk.reshape(B, S, FOX_HEADS, FOX_HEAD_DIM).transpose(0, 2, 1, 3)
        v_a = fv.reshape(B, S, FOX_HEADS, FOX_HEAD_DIM).transpose(0, 2, 1, 3)
        log_f = jax.nn.log_sigmoid((f_logit + b_forget[l]).astype(jnp.float32))
        cum_log_f = jnp.cumsum(log_f, axis=1).transpose(0, 2, 1)
        o_a = blocked_causal_attention(q_a, k_a, v_a, 1.0 / math.sqrt(FOX_HEAD_DIM), cum_log_f)
        o_a = o_a.transpose(0, 2, 1, 3).reshape(B, S, FOX_WIDTH)

        q_b = (rmsnorm(cq, g_q_lora[l]) @ w_uq[l]).reshape(B, S, MLA_HEADS, MLA_NOPE_DIM + MLA_ROPE_DIM)
        q_nope, q_pe = jnp.split(q_b, [MLA_NOPE_DIM], axis=-1)
        q_pe = rope(q_pe, cos[:, :, None, :], sin[:, :, None, :])
        q_b = jnp.concatenate([q_nope, q_pe], axis=-1).transpose(0, 2, 1, 3)
        kv_b = (rmsnorm(ckv, g_kv_lora[l]) @ w_ukv[l]).reshape(B, S, MLA_HEADS, MLA_NOPE_DIM + MLA_V_DIM)
        k_nope, v_b = jnp.split(kv_b, [MLA_NOPE_DIM], axis=-1)
        k_pe = rope(k_rope_in, cos, sin)
        k_pe = jnp.broadcast_to(k_pe[:, :, None, :], (B, S, MLA_HEADS, MLA_ROPE_DIM))
        k_b = jnp.concatenate([k_nope, k_pe], axis=-1).transpose(0, 2, 1, 3)
        v_b = v_b.transpose(0, 2, 1, 3)
        o_b = blocked_causal_attention(q_b, k_b, v_b, 1.0 / math.sqrt(MLA_NOPE_DIM + MLA_ROPE_DIM))
        o_b = o_b.transpose(0, 2, 1, 3).reshape(B, S, MLA_WIDTH)

        merged = (jax.nn.sigmoid(gate_fox) * (o_a @ w_proj_fox[l])
                  + jax.nn.sigmoid(gate_mla) * (o_b @ w_proj_mla[l]))
        y = merged @ w_out[l]
        x = x + gate_mix * rmsnorm(y, g_post_mix[l])

        h = rmsnorm(x, g_pre_ffn[l]) * (1.0 + scale_ffn) + shift_ffn
        g, u = jnp.split(h @ w_ffn_in[l], 2, axis=-1)
        y = (jax.nn.silu(g) * u) @ w_ffn_out[l]
        x = x + gate_ffn * rmsnorm(y, g_post_ffn[l])

    return x
```

```python
import math
import numpy as np
import ml_dtypes
import concourse.bass as bass
import concourse.mybir as mybir
from concourse.bass_utils import run_bass_kernel_spmd

F32 = mybir.dt.float32
BF16 = mybir.dt.bfloat16
I32 = mybir.dt.int32
U8 = mybir.dt.uint8
AF = mybir.ActivationFunctionType
ALU = mybir.AluOpType

S = 2048
D = 1024
NT = 16
NB = 4
KC = 8
DFF = 2816
NFC = 22
D_IN = 4648
EPS = 1e-6
C_Q, C_K, C_V, C_F, C_CQ, C_CKV, C_KR, C_GF, C_GM = 0, 512, 1024, 1536, 1544, 2312, 2568, 2600, 3624
NEG = -30000.0


class Sched:
    def __init__(self, nc, sems, dma_sems):
        self.nc = nc
        self.eng = {"pe": nc.tensor, "act": nc.scalar, "dve": nc.vector, "pool": nc.gpsimd, "sp": nc.sync}
        self.sem = sems
        self.cnt = {e: 0 for e in sems}
        self.seen = {e: {} for e in self.eng}
        self.last_w = {}
        self.readers = {}
        self.dma_sems = dma_sems
        self.dma_n = {q: 0 for q in dma_sems}
        self.inherit = {}
        self.n_wait = 0

    def _collect(self, reads, writes):
        raw, other = [], []
        for k in reads:
            w = self.last_w.get(k)
            if w is not None:
                raw.append(w)
        for k in writes:
            w = self.last_w.get(k)
            if w is not None:
                other.append(w)
            other.extend(self.readers.get(k, ()))
            inh = self.inherit.get(k[0])
            if inh:
                other.extend(inh)
        return raw, other

    def _emit_waits(self, e, raw, other):
        for tok in raw:
            if tok[2] == e and e == "pe":
                continue
            self._wait(e, tok)
        for tok in other:
            if tok[2] == e:
                continue
            self._wait(e, tok)

    def _wait(self, e, tok):
        sem, val, _ = tok
        sid = id(sem)
        if self.seen[e].get(sid, 0) >= val:
            return
        self.eng[e].wait_ge(sem, val)
        self.seen[e][sid] = val
        self.n_wait += 1

    def _register(self, tok, reads, writes):
        for k in reads:
            self.readers.setdefault(k, []).append(tok)
        for k in writes:
            self.last_w[k] = tok
            self.readers[k] = []

    def op(self, e, fns, reads=(), writes=()):
        if callable(fns):
            fns = [fns]
        raw, other = self._collect(reads, writes)
        self._emit_waits(e, raw, other)
        eng = self.eng[e]
        ins = None
        for f in fns:
            ins = f(eng)
        self.cnt[e] += 1
        ins.then_inc(self.sem[e], 1)
        tok = (self.sem[e], self.cnt[e], e)
        self._register(tok, reads, writes)
        return tok

    def dma(self, q, out, in_, reads=(), writes=()):
        raw, other = self._collect(reads, writes)
        self._emit_waits(q, raw, other)
        pool = self.dma_sems[q]
        n = self.dma_n[q]
        self.dma_n[q] += 1
        sem = pool[n % len(pool)]
        rnd = n // len(pool)
        if rnd > 0:
            self._wait(q, (sem, 16 * rnd, "dma"))
        self.eng[q].dma_start(out=out, in_=in_).then_inc(sem, 16)
        tok = (sem, 16 * (rnd + 1), "dma")
        self._register(tok, reads, writes)
        return tok

    def release(self, name):
        toks = []
        for k in list(self.last_w.keys()):
            if k[0] == name:
                toks.append(self.last_w.pop(k))
        for k in list(self.readers.keys()):
            if k[0] == name:
                toks.extend(self.readers.pop(k))
        best = {}
        for t in toks:
            sid = id(t[0])
            if sid not in best or best[sid][1] < t[1]:
                best[sid] = t
        return list(best.values())


class Arena:
    def __init__(self, sched, arena_ap, nbytes):
        self.s = sched
        self.arena = arena_ap
        self.free = [(0, nbytes)]
        self.live = {}
        self.dead = []
        self.peak = 0

    def alloc(self, name, shape, dtype, parts=(0, 128)):
        esz = 4 if dtype in (F32, I32) else 2
        n = 1
        for d in shape:
            n *= d
        size = (n * esz + 63) // 64 * 64
        for i, (off, sz) in enumerate(self.free):
            if sz >= size:
                self.free[i] = (off + size, sz - size)
                break
        else:
            raise RuntimeError(f"arena OOM for {name} ({size} B); live={ {k: v[1] for k, v in self.live.items()} }")
        self.live[name] = (off, size)
        self.peak = max(self.peak, off + size)
        toks = []
        keep = []
        for (o, s_, tk) in self.dead:
            if o < off + size and off < o + s_:
                toks.extend(tk)
            keep.append((o, s_, tk))
        self.s.inherit[name] = toks
        ap = self.arena[parts[0]:parts[1], off // 2:(off + size) // 2]
        if esz == 4:
            ap = ap.bitcast(dtype)
        elif dtype != BF16:
            ap = ap.bitcast(dtype)
        ap = ap[:, 0:n]
        if len(shape) == 2:
            ap = ap.rearrange("p (a b) -> p a b", a=shape[0], b=shape[1])
        elif len(shape) == 3:
            ap = ap.rearrange("p (a b c) -> p a b c", a=shape[0], b=shape[1], c=shape[2])
        return ap

    def release(self, name):
        off, size = self.live.pop(name)
        toks = self.s.release(name)
        self.dead.append((off, size, toks))
        self.free.append((off, size))
        self.free.sort()
        merged = []
        for o, s_ in self.free:
            if merged and merged[-1][0] + merged[-1][1] == o:
                merged[-1] = (merged[-1][0], merged[-1][1] + s_)
            else:
                merged.append((o, s_))
        self.free = merged


class Builder:
    def __init__(self, debug=None):
        self.debug = debug or []
        nc = bass.Bass("TRN2", target_bir_lowering=False)
        self.nc = nc
        self.dbg_out = {}
        d = lambda n, sh, dt, kind="ExternalInput": nc.dram_tensor(n, list(sh), dt, kind=kind).ap()
        self.x = d("x", [S, D], F32)
        self.cT = d("cT", [128, KC], F32)
        self.pos = d("pos", [128, S], I32)
        self.w_ada = d("w_ada", [D, 6 * D], F32)
        self.badaT = d("badaT", [128, 48], F32)
        self.bada_rep = d("bada_rep", [128, 2 * D], F32)
        self.gpost_rep = d("gpost_rep", [128, 2 * D], F32)
        self.gpreT = d("gpreT", [128, 16], F32)
        self.w_in = d("w_in", [D, D_IN], F32)
        self.nbf = d("nbf", [8, 1], F32)
        self.gqT = d("gqT", [128, 6], F32)
        self.gkvT = d("gkvT", [128, 2], F32)
        self.w_uq = d("w_uq", [768, 768], F32)
        self.w_ukv = d("w_ukv", [256, 1024], F32)
        self.w_pf = d("w_pf", [512, D], F32)
        self.w_pm = d("w_pm", [512, D], F32)
        self.w_out = d("w_out", [D, D], F32)
        self.w_f1 = d("w_f1", [D, 2 * DFF], F32)
        self.w_f2 = d("w_f2", [DFF, D], F32)
        self.c_ident = d("c_ident", [128, 128], BF16)
        self.c_identf = d("c_identf", [128, 128], F32)
        self.c_mask = d("c_mask", [128, 128], BF16)
        self.c_rope = d("c_rope", [128, 4], F32)
        self.out = d("out", [S, D], F32, kind="ExternalOutput")

    def dbg(self, name, ap, shape, dtype):
        if name in self.debug:
            o = self.nc.dram_tensor("dbg_" + name, list(shape), dtype, kind="ExternalOutput").ap()
            self.dbg_out[name] = o
            return o
        return None

    def build(self, upto=99):
        import contextlib
        nc = self.nc
        with contextlib.ExitStack() as es:
            es.enter_context(nc.allow_low_precision("bf16 matmul operands by design; fp32 accumulation"))
            es.enter_context(nc.allow_non_contiguous_dma("small strided weight / constant loads"))
            ARENA = 207 * 1024
            arena_t = es.enter_context(nc.sbuf_tensor("arena", [128, ARENA // 2], BF16))
            self.banks = [es.enter_context(nc.psum_tensor(f"ps{b}", [128, 512], F32)) for b in range(8)]
            sems = {e: es.enter_context(nc.semaphore("s_" + e)) for e in ("pe", "act", "dve", "pool")}
            dma_sems = {q: [es.enter_context(nc.semaphore(f"d{q}{i}")) for i in range(12)] for q in ("sp", "pool")}
            self.s = Sched(nc, sems, dma_sems)
            self.A = Arena(self.s, arena_t, ARENA)
            self._body(upto)
            self._finish()
        return nc

    def bank(self, b):
        return self.banks[b][:, :]

    def bankbf(self, b):
        return self.banks[b][:, :].bitcast(BF16)

    def _finish(self):
        s = self.s
        for q in ("sp", "pool"):
            pool = s.dma_sems[q]
            n = s.dma_n[q]
            for i, sem in enumerate(pool):
                cnt = (n - i + len(pool) - 1) // len(pool) if n > i else 0
                if cnt > 0:
                    s._wait("sp", (sem, 16 * cnt, "dma"))

    def dump(self, name, ap, parts, ncols, dtype, reads):
        o = self.dbg(name, ap, [parts, ncols], dtype)
        if o is not None:
            self.s.dma("sp", o, ap, reads=reads)

    def _body(self, upto):
        nc, s, A = self.nc, self.s, self.A
        bank, bankbf = self.bank, self.bankbf

        def load_const(name, src, shape, dtype, parts=(0, 128), q="pool"):
            t = A.alloc(name, shape, dtype, parts)
            s.dma(q, t, src, writes=[(name,)])
            return t

        ident = load_const("ident", self.c_ident, (128,), BF16)
        identf = load_const("identf", self.c_identf, (128,), F32)
        mask = load_const("mask", self.c_mask, (128,), BF16)
        ropec = load_const("ropec", self.c_rope, (4,), F32)
        cT = load_const("cT", self.cT, (KC,), F32)
        badaT = load_const("badaT", self.badaT, (48,), F32)
        gpreT = load_const("gpreT", self.gpreT, (16,), F32)
        gqT = load_const("gqT", self.gqT, (6,), F32)
        gkvT = load_const("gkvT", self.gkvT, (2,), F32)
        nbf = load_const("nbf", self.nbf, (1,), F32, parts=(0, 8))
        self.ident, self.mask, self.identf, self.nbf_t = ident, mask, identf, nbf
        self.gqT_t, self.gkvT_t = gqT, gkvT
        eps_c = A.alloc("eps_c", (1,), F32)
        s.op("dve", lambda e: e.memset(eps_c, EPS), writes=[("eps_c",)])
        self.eps_c = eps_c

        siluf = A.alloc("siluf", (KC,), F32)
        silub = A.alloc("silub", (KC,), BF16)
        ones_b = A.alloc("ones_b", (128,), BF16)
        modT = A.alloc("modT", (48,), F32)
        amix = A.alloc("amix", (KC,), F32)
        affn = A.alloc("affn", (KC,), F32)
        s.op("act", lambda e: e.activation(out=siluf, in_=cT, func=AF.Silu), reads=[("cT",)], writes=[("siluf",)])
        s.op("dve", lambda e: e.tensor_copy(out=silub, in_=siluf), reads=[("siluf",)], writes=[("silub",)])
        s.op("dve", lambda e: e.memset(ones_b, 1.0), writes=[("ones_b",)])
        wada = [A.alloc("wada0", (KC, D), BF16), A.alloc("wada1", (KC, D), BF16)]
        w_ada_v = self.w_ada.rearrange("(k p) n -> p k n", p=128)
        MODB = 7
        for v in (0, 1, 3, 4):
            i = {0: 0, 1: 1, 3: 0, 4: 1}[v]
            wn = "wada%d" % i
            s.dma("pool", wada[i], w_ada_v[:, :, v * D:(v + 1) * D], writes=[(wn,)])
            for c in range(KC):
                col = v * 8 + c
                s.op("pe", [(lambda e, k=k: e.matmul(bank(MODB)[:, col:col + 1], wada[i][:, k, c * 128:(c + 1) * 128],
                                                    silub[:, k:k + 1], start=(k == 0), stop=(k == KC - 1)))
                            for k in range(KC)],
                     reads=[(wn,), ("silub",)], writes=[("ps", MODB)])
        s.op("dve", lambda e: e.tensor_tensor(out=modT, in0=bank(MODB)[:, 0:48], in1=badaT, op=ALU.add),
             reads=[("badaT",)], writes=[("ps", MODB), ("modT",)])
        s.op("dve", lambda e: e.scalar_tensor_tensor(out=amix, in0=modT[:, 8:16], scalar=1.0, in1=gpreT[:, 0:8],
                                                     op0=ALU.add, op1=ALU.mult),
             reads=[("modT",), ("gpreT",)], writes=[("amix",)])
        s.op("dve", lambda e: e.scalar_tensor_tensor(out=affn, in0=modT[:, 32:40], scalar=1.0, in1=gpreT[:, 8:16],
                                                     op0=ALU.add, op1=ALU.mult),
             reads=[("modT",), ("gpreT",)], writes=[("affn",)])
        self.modT, self.amix, self.affn, self.ones_b, self.siluf = modT, amix, affn, ones_b, siluf
        self.dump("modT", modT, 128, 48, F32, [("modT",)])
        A.release("wada0"); A.release("wada1")
        if upto <= 0:
            return

        R = (64, 96)
        posi = A.alloc("posi", (S,), I32, parts=R)
        posf = A.alloc("posf", (S,), F32, parts=R)
        CC = A.alloc("CC", (S,), BF16, parts=R)
        SS = A.alloc("SS", (S,), BF16, parts=R)
        s.dma("pool", posi, self.pos[64:96, :], writes=[("posi",)])
        s.op("dve", lambda e: e.tensor_copy(out=posf, in_=posi), reads=[("posi",)], writes=[("posf",)])
        ang = posi.bitcast(F32)
        kf = A.alloc("kf", (S,), F32, parts=R)
        ki = A.alloc("ki", (S,), I32, parts=R)
        for (tab, name, c0) in ((CC, "CC", 0), (SS, "SS", 2)):
            s.op("dve", lambda e: e.tensor_scalar(out=ang, in0=posf, scalar1=ropec[64:96, c0:c0 + 1],
                                                  scalar2=ropec[64:96, c0 + 1:c0 + 2], op0=ALU.mult, op1=ALU.add),
                 reads=[("posf",), ("ropec",)], writes=[("posi",)])
            s.op("dve", lambda e: e.tensor_scalar(out=kf, in0=ang, scalar1=1.0 / (2.0 * math.pi), scalar2=None,
                                                  op0=ALU.mult),
                 reads=[("posi",)], writes=[("kf",)])
            s.op("dve", lambda e: e.tensor_copy(out=ki, in_=kf), reads=[("kf",)], writes=[("ki",)])
            s.op("dve", lambda e: e.tensor_copy(out=kf, in_=ki), reads=[("ki",)], writes=[("kf",)])
            s.op("dve", lambda e: e.scalar_tensor_tensor(out=ang, in0=kf, scalar=-2.0 * math.pi, in1=ang,
                                                         op0=ALU.mult, op1=ALU.add),
                 reads=[("kf",), ("posi",)], writes=[("posi",)])
            s.op("dve", lambda e: e.tensor_scalar(out=kf, in0=ang, scalar1=math.pi, scalar2=-2.0 * math.pi,
                                                  op0=ALU.is_gt, op1=ALU.mult),
                 reads=[("posi",)], writes=[("kf",)])
            s.op("dve", lambda e: e.tensor_tensor(out=ang, in0=ang, in1=kf, op=ALU.add),
                 reads=[("posi",), ("kf",)], writes=[("posi",)])
            s.op("dve", lambda e: e.tensor_scalar(out=ang, in0=ang, scalar1=-math.pi, scalar2=math.pi,
                                                  op0=ALU.max, op1=ALU.min),
                 reads=[("posi",)], writes=[("posi",)])
            s.op("act", lambda e: e.activation(out=tab, in_=ang, func=AF.Sin),
                 reads=[("posi",)], writes=[(name,)])
        A.release("kf"); A.release("ki")
        self.CC, self.SS = CC, SS
        self.dump("CC", CC, 32, S, BF16, [("CC",)])
        self.dump("SS", SS, 32, S, BF16, [("SS",)])
        A.release("posi"); A.release("posf")

        hT = A.alloc("hT", (KC, S), BF16)
        self.hT = hT
        ss = A.alloc("ss", (NT,), F32)
        rstd = A.alloc("rstd", (NT,), F32)
        self.ss, self.rstd = ss, rstd
        self.prenorm(self.x, amix, modT[:, 0:8], hT, "hT", ss, rstd, "amix", list(range(NB)), 0)
        self.dump("hT0", hT[:, 0, :], 128, S, BF16, [("hT", 0, T) for T in range(NB)])
        self.dump("hT7", hT[:, 7, :], 128, S, BF16, [("hT", 7, T) for T in range(NB)])
        if upto <= 1:
            return
        self._body2(upto)

    def prenorm(self, src_rows, a_sc, shift_sc, dst, dname, ss, rstd, aname, Ts, tok0, src_sbuf=None):
        s, A = self.s, self.A
        xts = [A.alloc("xt%d" % i, (D,), F32) for i in range(4)] if src_sbuf is None else None
        xb = [A.alloc("xb%d" % i, (D,), BF16) for i in range(4)]
        junk = A.alloc("junk", (D,), BF16)
        for T in Ts:
            for i in range(4):
                it = 4 * T + i
                if src_sbuf is None:
                    xt, xk = xts[i], ("xt%d" % i,)
                    s.dma("sp", xt, src_rows[it * 128:(it + 1) * 128, :], writes=[xk])
                else:
                    xt, xk = src_sbuf(it)
                s.op("act", lambda e: e.activation(out=junk, in_=xt, func=AF.Square, accum_out=ss[:, it:it + 1]),
                     reads=[xk], writes=[("junk",), ("ss", it)])
            sl4 = slice(4 * T, 4 * T + 4)
            s.op("act", lambda e: e.activation(out=rstd[:, sl4], in_=ss[:, sl4], func=AF.Ln, bias=self.eps_c, scale=1.0 / D),
                 reads=[("ss", 4 * T + i) for i in range(4)] + [("eps_c",)], writes=[("rstd", T)])
            s.op("act", lambda e: e.activation(out=rstd[:, sl4], in_=rstd[:, sl4], func=AF.Exp, scale=-0.5),
                 reads=[("rstd", T)], writes=[("rstd", T)])
            for i in range(4):
                it = 4 * T + i
                xt, xk = (xts[i], ("xt%d" % i,)) if src_sbuf is None else src_sbuf(it)
                s.op("dve", lambda e: e.tensor_scalar(out=xb[i], in0=xt, scalar1=rstd[:, it:it + 1], scalar2=None,
                                                      op0=ALU.mult),
                     reads=[xk, ("rstd", T)], writes=[("xb%d" % i,)])
            for c in range(KC):
                b = 6 + (c % 2)
                psb = self.bankbf(b)
                s.op("pe", [(lambda e, i=i: e.transpose(out=psb[:, i * 128:(i + 1) * 128],
                                                        in_=xb[i][:, c * 128:(c + 1) * 128], identity=self.ident))
                            for i in range(4)],
                     reads=[("xb%d" % i,) for i in range(4)] + [("ident",)], writes=[("ps", b)])
                dsl = dst[:, c, (T - tok0) * 512:(T - tok0 + 1) * 512]
                if c % 2 == 0:
                    s.op("act", lambda e: e.activation(out=dsl, in_=psb[:, 0:512], func=AF.Identity,
                                                       bias=shift_sc[:, c:c + 1], scale=a_sc[:, c:c + 1]),
                         reads=[(aname,), ("modT",)], writes=[("ps", b), (dname, c, T if dname == "hT" else 0)])
                else:
                    s.op("dve", lambda e: e.tensor_scalar(out=dsl, in0=psb[:, 0:512], scalar1=a_sc[:, c:c + 1],
                                                          scalar2=shift_sc[:, c:c + 1], op0=ALU.mult, op1=ALU.add),
                         reads=[(aname,), ("modT",)], writes=[("ps", b), (dname, c, T if dname == "hT" else 0)])
        if xts is not None:
            for i in range(4):
                A.release("xt%d" % i)
        for i in range(4):
            A.release("xb%d" % i)
        A.release("junk")

    def _body2(self, upto):
        nc, s, A = self.nc, self.s, self.A
        bank, bankbf = self.bank, self.bankbf
        hT, CC, SS = self.hT, self.CC, self.SS
        w_in_v = self.w_in.rearrange("(k p) n -> p k n", p=128)
        hkeys = lambda T: [("hT", k, T) for k in range(KC)]
        TS = lambda T: slice(T * 512, (T + 1) * 512)
        mm = lambda out, l, r, st, sp: (lambda e: e.matmul(out, l, r, start=st, stop=sp))

        oT = A.alloc("oT", (4, S), BF16)
        qk = {n: A.alloc(n, (S,), BF16, parts=(0, 96)) for n in ("qA", "qB", "kA", "kB")}
        vaug = A.alloc("vaug", (NT, 4, 192), BF16)
        self.pT = [A.alloc("pT%d" % i, (512,), BF16) for i in range(4)]
        self.rec = [A.alloc("rec%d" % i, (512,), F32) for i in range(2)]
        self.qk, self.vaug, self.oT = qk, vaug, oT
        self.att_cnt = 0

        s.op("dve", lambda e: e.memset(vaug.rearrange("p a b c -> p (a b c)"), 1.0),
             writes=[("vaug", it, e_) for it in range(NT) for e_ in range(3)])
        wv = A.alloc("wv", (KC, 512), BF16)
        s.dma("pool", wv, w_in_v[:, :, C_V:C_V + 512], writes=[("wv",)])

        def v_proj(lhs_of, nk, wt, wname, rkeys):
            for it in range(NT):
                b = 6 + it % 2
                s.op("pe", [mm(bank(b), lhs_of(k, it), wt[:, k, :], k == 0, k == nk - 1) for k in range(nk)],
                     reads=rkeys(it // 4) + [(wname,)], writes=[("ps", b)])
                pv = bank(b).rearrange("p (c e d) -> p c e d", c=4, e=2, d=64)
                s.op("act", lambda e: e.activation(out=vaug[:, it, :, 0:64], in_=pv[:, :, 0, :], func=AF.Copy),
                     writes=[("ps", b), ("vaug", it, 0)])
                s.op("dve", lambda e: e.tensor_copy(out=vaug[:, it, :, 128:192], in_=pv[:, :, 1, :]),
                     writes=[("ps", b), ("vaug", it, 1)])

        v_proj(lambda k, it: hT[:, k, it * 128:(it + 1) * 128], KC, wv, "wv", hkeys)
        A.release("wv")

        wmisc = A.alloc("wmisc", (KC, 128), BF16)
        s.op("dve", lambda e: e.memset(wmisc.rearrange("p a b -> p (a b)"), 0.0), writes=[("wmisc", i) for i in range(4)])
        s.dma("pool", wmisc[:, :, 0:8], w_in_v[:, :, C_F:C_F + 8], writes=[("wmisc", 0)])
        s.dma("pool", wmisc[:, :, 64:96], w_in_v[:, :, C_KR:C_KR + 32], writes=[("wmisc", 1)])
        s.dma("pool", wmisc[:, :, 96:112], w_in_v[:, :, C_KR + 16:C_KR + 32], writes=[("wmisc", 2)])
        s.dma("pool", wmisc[:, :, 112:128], w_in_v[:, :, C_KR:C_KR + 16], writes=[("wmisc", 3)])
        P8 = (0, 8)
        nbneg = A.alloc("nbneg", (1,), F32, parts=P8)
        eT = A.alloc("eT", (512,), F32, parts=P8)
        nlf = A.alloc("nlf", (512,), F32, parts=P8)
        onesf = A.alloc("onesf", (512,), F32, parts=P8)
        G = A.alloc("G", (S,), F32, parts=P8)
        negGb = A.alloc("negGb", (S,), BF16, parts=P8)
        Gtok = A.alloc("Gtok", (128,), F32)
        R = (64, 96)
        kpe = A.alloc("kpe", (S,), BF16, parts=R)
        t1 = A.alloc("t1", (512,), F32, parts=R)
        t2 = A.alloc("t2", (512,), F32, parts=R)
        self.t1, self.t2, self.Gtok, self.kpe_t = t1, t2, Gtok, kpe
        s.op("dve", lambda e: e.tensor_scalar(out=nbneg, in0=self.nbf_t, scalar1=-1.0, scalar2=None, op0=ALU.mult),
             reads=[("nbf",)], writes=[("nbneg",)])
        s.op("dve", lambda e: e.memset(onesf, 1.0), writes=[("onesf",)])
        for T in range(NB):
            b = 6 + T % 2
            s.op("pe", [mm(bank(b), wmisc[:, k, :], hT[:, k, TS(T)], k == 0, k == KC - 1) for k in range(KC)],
                 reads=hkeys(T) + [("wmisc", i) for i in range(4)], writes=[("ps", b)])
            s.op("act", lambda e: e.activation(out=eT, in_=bank(b)[0:8, :], func=AF.Exp, bias=nbneg, scale=-1.0),
                 reads=[("nbneg",)], writes=[("ps", b), ("eT",)])
            s.op("act", lambda e: e.activation(out=nlf, in_=eT, func=AF.Ln, bias=1.0), reads=[("eT",)], writes=[("nlf",)])
            init = 0.0 if T == 0 else G[:, T * 512 - 1:T * 512]
            s.op("dve", lambda e: e.tensor_tensor_scan(out=G[:, TS(T)], data0=onesf, data1=nlf, initial=init,
                                                       op0=ALU.mult, op1=ALU.add),
                 reads=[("nlf",), ("onesf",)] + ([("G", T - 1)] if T else []), writes=[("G", T)])
            s.op("dve", lambda e: e.tensor_tensor(out=t1, in0=bank(b)[64:96, :], in1=CC[:, TS(T)], op=ALU.mult),
                 reads=[("CC",)], writes=[("ps", b), ("t1",)])
            s.op("dve", lambda e: e.tensor_tensor(out=t2, in0=bank(b)[96:128, :], in1=SS[:, TS(T)], op=ALU.mult),
                 reads=[("SS",)], writes=[("ps", b), ("t2",)])
            s.op("dve", lambda e: e.tensor_tensor(out=kpe[:, TS(T)], in0=t1, in1=t2, op=ALU.add),
                 reads=[("t1",), ("t2",)], writes=[("kpe", T)])
        s.op("dve", lambda e: e.tensor_scalar(out=negGb, in0=G, scalar1=-1.0, scalar2=None, op0=ALU.mult),
             reads=[("G", T) for T in range(NB)], writes=[("negGb",)])
        GB = 5
        s.op("pe", [(lambda e, it=it: e.transpose(out=bank(GB)[:, it * 8:(it + 1) * 8], in_=G[:, it * 128:(it + 1) * 128],
                                                  identity=self.identf[0:8, 0:8])) for it in range(NT)],
             reads=[("G", T) for T in range(NB)] + [("identf",)], writes=[("ps", GB)])
        s.op("dve", lambda e: e.tensor_copy(out=Gtok, in_=bank(GB)[:, 0:128]), writes=[("ps", GB), ("Gtok",)])
        self.dump("G", G, 8, S, F32, [("G", T) for T in range(NB)])
        self.dump("Gtok", Gtok, 128, 128, F32, [("Gtok",)])
        self.dump("kpe", kpe, 32, S, BF16, [("kpe", T) for T in range(NB)])
        A.release("wmisc"); A.release("eT"); A.release("nlf"); A.release("onesf")
        if upto <= 2:
            return

        wf = [A.alloc("wf0", (KC, 256), BF16), A.alloc("wf1", (KC, 256), BF16)]
        for n in ("kA", "kB"):
            s.op("dve", lambda e: e.memset(qk[n][64:65, :], 1.0), writes=[(n, "g")])
        for c in range(4):
            w, wn = wf[c % 2], "wf%d" % (c % 2)
            s.dma("pool", w[:, :, 0:128], w_in_v[:, :, C_Q + c * 128:C_Q + (c + 1) * 128], writes=[(wn, "q")])
            s.dma("pool", w[:, :, 128:256], w_in_v[:, :, C_K + c * 128:C_K + (c + 1) * 128], writes=[(wn, "k")])
            s.dma("pool", qk["qA"][64:65, :], negGb[2 * c:2 * c + 1, :], reads=[("negGb",)], writes=[("qA", "g")])
            s.dma("pool", qk["qB"][64:65, :], negGb[2 * c + 1:2 * c + 2, :], reads=[("negGb",)], writes=[("qB", "g")])
            for T in range(NB):
                s.op("pe", [mm(bank(6), w[:, k, 0:128], hT[:, k, TS(T)], k == 0, k == KC - 1) for k in range(KC)],
                     reads=hkeys(T) + [(wn, "q")], writes=[("ps", 6)])
                s.op("act", lambda e: e.activation(out=qk["qA"][0:64, TS(T)], in_=bank(6)[0:64, :], func=AF.Copy, scale=0.125),
                     writes=[("ps", 6), ("qA", T)])
                s.op("act", lambda e: e.activation(out=qk["qB"][0:64, TS(T)], in_=bank(6)[64:128, :], func=AF.Copy, scale=0.125),
                     writes=[("ps", 6), ("qB", T)])
                s.op("pe", [mm(bank(7), w[:, k, 128:256], hT[:, k, TS(T)], k == 0, k == KC - 1) for k in range(KC)],
                     reads=hkeys(T) + [(wn, "k")], writes=[("ps", 7)])
                s.op("dve", lambda e: e.tensor_copy(out=qk["kA"][0:64, TS(T)], in_=bank(7)[0:64, :]),
                     writes=[("ps", 7), ("kA", T)])
                s.op("dve", lambda e: e.tensor_copy(out=qk["kB"][0:64, TS(T)], in_=bank(7)[64:128, :]),
                     writes=[("ps", 7), ("kB", T)])
            if c == 0:
                self.dump("qA", qk["qA"][0:65, :], 65, S, BF16, [("qA", T) for T in range(NB)] + [("qA", "g")])
                self.dump("kA", qk["kA"][0:65, :], 65, S, BF16, [("kA", T) for T in range(NB)] + [("kA", "g")])
            self.attention(c, True)
        for c in range(4):
            self.dump("oaT%d" % c, oT[:, c, :], 128, S, BF16, [("oT", c, T, hh) for T in range(NB) for hh in range(2)])
        A.release("wf0"); A.release("wf1"); A.release("G"); A.release("negGb"); A.release("nbneg")
        if upto <= 3:
            return
        self._body3(upto)

    def attention(self, c, fox):
        s = self.s
        bank = self.bank
        qk, vaug, pT, rec = self.qk, self.vaug, self.pT, self.rec
        oT, on = (self.oT, "oT") if fox else (self.oT2, "oT2")
        KD = 65 if fox else 96
        scale = 1.0 if fox else 1.0 / math.sqrt(96.0)
        for T in range(NB):
            for hh in (0, 1):
                h = 2 * c + hh
                qn, kn = ("qA", "kA") if hh == 0 else ("qB", "kB")
                q, k = qk[qn], qk[kn]
                acc = 4 + hh
                nj = 4 * T + 4
                vsl = slice(0, 128) if hh == 0 else slice(64, 192)
                info = {}

                def emit_S(j):
                    off = 0 if j < 4 * T else (j - 4 * T) * 128
                    w = 512 - off
                    n = self.att_cnt
                    self.att_cnt += 1
                    b = n % 4
                    pt, ptn = pT[n % 4], "pT%d" % (n % 4)
                    diag = j >= 4 * T
                    fns = [lambda e: e.matmul(bank(b)[:, 0:w], k[0:KD, j * 128:(j + 1) * 128],
                                              q[0:KD, T * 512 + off:(T + 1) * 512], start=True, stop=not diag)]
                    if diag:
                        fns.append(lambda e: e.matmul(bank(b)[:, 0:128], self.ident, self.mask, start=False, stop=True))
                    s.op("pe", fns, reads=[(qn, T), (qn, "g"), (kn, j // 4), (kn, "g"), ("ident",), ("mask",)],
                         writes=[("ps", b)])
                    if fox:
                        s.op("act", lambda e: e.activation(out=pt[:, 0:w], in_=bank(b)[:, 0:w], func=AF.Exp,
                                                           bias=self.Gtok[:, j * 8 + h:j * 8 + h + 1], scale=1.0),
                             reads=[("Gtok",)], writes=[("ps", b), (ptn,)])
                    else:
                        s.op("act", lambda e: e.activation(out=pt[:, 0:w], in_=bank(b)[:, 0:w], func=AF.Exp, scale=scale),
                             writes=[("ps", b), (ptn,)])
                    info[j] = (off, w, pt, ptn)

                def emit_PV(j):
                    off, w, pt, ptn = info[j]
                    s.op("pe", lambda e: e.matmul(bank(acc)[:, off:512], vaug[:, j, c, vsl], pt[:, 0:w],
                                                  start=(j == 0), stop=(j == nj - 1)),
                         reads=[(ptn,), ("vaug", j, 0), ("vaug", j, 1), ("vaug", j, 2)], writes=[("ps", acc)])

                emit_S(0)
                if nj > 1:
                    emit_S(1)
                for j in range(nj):
                    emit_PV(j)
                    if j + 2 < nj:
                        emit_S(j + 2)
                r = rec[hh]
                if hh == 0:
                    osl, dsl = slice(0, 64), slice(64, 128)
                else:
                    osl, dsl = slice(64, 128), slice(0, 64)
                s.op("dve", lambda e: e.reciprocal(out=r[osl, :], in_=bank(acc)[dsl, :]),
                     writes=[("ps", acc), ("rec%d" % hh,)])
                s.op("dve", lambda e: e.tensor_tensor(out=oT[osl, c, T * 512:(T + 1) * 512], in0=bank(acc)[osl, :],
                                                      in1=r[osl, :], op=ALU.mult),
                     reads=[("rec%d" % hh,)], writes=[("ps", acc), (on, c, T, hh)])

    def gate_merge(self, col0, w_proj, first, oT, on):
        s, A = self.s, self.A
        bank = self.bank
        hT, merged = self.hT, self.merged
        mm = lambda out, l, r, st, sp: (lambda e: e.matmul(out, l, r, start=st, stop=sp))
        w_in_v = self.w_in.rearrange("(k p) n -> p k n", p=128)
        wg = A.alloc("wg", (KC, D), BF16)
        wp = A.alloc("wp", (4, D), BF16)
        sg = [A.alloc("sg%d" % i, (512,), BF16) for i in range(2)]
        tmp = [A.alloc("gtmp%d" % i, (512,), BF16) for i in range(2)]
        s.dma("pool", wg, w_in_v[:, :, col0:col0 + D], writes=[("wg",)])
        s.dma("pool", wp, w_proj.rearrange("(k p) n -> p k n", p=128), writes=[("wp",)])
        n = 0
        for T in range(NB):
            Tsl = slice(T * 512, (T + 1) * 512)
            for m in range(KC):
                i = n % 2
                n += 1
                bg, bp = 0 + i, 2 + i
                s.op("pe", [mm(bank(bg), wg[:, k, m * 128:(m + 1) * 128], hT[:, k, Tsl], k == 0, k == KC - 1) for k in range(KC)],
                     reads=[("hT", k, T) for k in range(KC)] + [("wg",)], writes=[("ps", bg)])
                s.op("act", lambda e: e.activation(out=sg[i], in_=bank(bg), func=AF.Sigmoid),
                     writes=[("ps", bg), ("sg%d" % i,)])
                s.op("pe", [mm(bank(bp), wp[:, cc, m * 128:(m + 1) * 128], oT[:, cc, Tsl], cc == 0, cc == 3) for cc in range(4)],
                     reads=[(on, cc, T, hh) for cc in range(4) for hh in range(2)] + [("wp",)], writes=[("ps", bp)])
                if first:
                    s.op("dve", lambda e: e.tensor_tensor(out=merged[:, m, Tsl], in0=bank(bp), in1=sg[i], op=ALU.mult),
                         reads=[("sg%d" % i,)], writes=[("ps", bp), ("merged", m, T)])
                else:
                    s.op("dve", lambda e: e.tensor_tensor(out=tmp[i], in0=bank(bp), in1=sg[i], op=ALU.mult),
                         reads=[("sg%d" % i,)], writes=[("ps", bp), ("gtmp%d" % i,)])
                    s.op("dve", lambda e: e.tensor_tensor(out=merged[:, m, Tsl], in0=merged[:, m, Tsl], in1=tmp[i], op=ALU.add),
                         reads=[("gtmp%d" % i,), ("merged", m, T)], writes=[("merged", m, T)])
        for nme in ("wg", "wp", "sg0", "sg1", "gtmp0", "gtmp1"):
            A.release(nme)

    def _body3(self, upto):
        nc, s, A = self.nc, self.s, self.A
        bank = self.bank
        hT, CC, SS, qk, vaug = self.hT, self.CC, self.SS, self.qk, self.vaug
        t1, t2 = self.t1, self.t2
        w_in_v = self.w_in.rearrange("(k p) n -> p k n", p=128)
        hkeys = lambda T: [("hT", k, T) for k in range(KC)]
        TS = lambda T: slice(T * 512, (T + 1) * 512)
        mm = lambda out, l, r, st, sp: (lambda e: e.matmul(out, l, r, start=st, stop=sp))

        oT2 = A.alloc("oT2", (4, S), BF16)
        self.oT2 = oT2

        cqn = A.alloc("cqn", (6, S), BF16)
        ckvn = A.alloc("ckvn", (2, S), BF16)
        sqb = [A.alloc("sqb%d" % i, (512,), BF16) for i in range(2)]
        rstdb = A.alloc("rstdb", (512,), F32)
        wcq = A.alloc("wcq", (KC, 768), BF16)
        wckv = A.alloc("wckv", (KC, 256), BF16)
        s.dma("pool", wcq, w_in_v[:, :, C_CQ:C_CQ + 768], writes=[("wcq",)])
        s.dma("pool", wckv, w_in_v[:, :, C_CKV:C_CKV + 256], writes=[("wckv",)])
        for (wt, wn, nm, dst, dn, gT, gname, nfeat) in ((wcq, "wcq", 6, cqn, "cqn", self.gqT_t, "gqT", 768.0),
                                                        (wckv, "wckv", 2, ckvn, "ckvn", self.gkvT_t, "gkvT", 256.0)):
            for T in range(NB):
                for m in range(nm):
                    b = 6 + m % 2
                    i = m % 2
                    s.op("pe", [mm(bank(b), wt[:, k, m * 128:(m + 1) * 128], hT[:, k, TS(T)], k == 0, k == KC - 1) for k in range(KC)],
                         reads=hkeys(T) + [(wn,)], writes=[("ps", b)])
                    s.op("act", lambda e: e.activation(out=sqb[i], in_=bank(b), func=AF.Square),
                         writes=[("ps", b), ("sqb%d" % i,)])
                    s.op("dve", lambda e: e.tensor_scalar(out=dst[:, m, TS(T)], in0=bank(b), scalar1=gT[:, m:m + 1],
                                                          scalar2=None, op0=ALU.mult),
                         reads=[(gname,)], writes=[("ps", b), (dn, m, T)])
                    s.op("pe", mm(bank(5), self.ones_b, sqb[i], m == 0, m == nm - 1),
                         reads=[("sqb%d" % i,), ("ones_b",)], writes=[("ps", 5)])
                s.op("act", lambda e: e.activation(out=rstdb, in_=bank(5), func=AF.Ln, bias=self.eps_c, scale=1.0 / nfeat),
                     reads=[("eps_c",)], writes=[("ps", 5), ("rstdb",)])
                s.op("act", lambda e: e.activation(out=rstdb, in_=rstdb, func=AF.Exp, scale=-0.5),
                     reads=[("rstdb",)], writes=[("rstdb",)])
                for m in range(nm):
                    s.op("dve", lambda e: e.tensor_tensor(out=dst[:, m, TS(T)], in0=dst[:, m, TS(T)], in1=rstdb, op=ALU.mult),
                         reads=[(dn, m, T), ("rstdb",)], writes=[(dn, m, T)])
        self.dump("cqn0", cqn[:, 0, :], 128, S, BF16, [("cqn", 0, T) for T in range(NB)])
        self.dump("ckvn1", ckvn[:, 1, :], 128, S, BF16, [("ckvn", 1, T) for T in range(NB)])
        A.release("wcq"); A.release("wckv"); A.release("sqb0"); A.release("sqb1"); A.release("rstdb")

        wuq = A.alloc("wuq", (6, 8, 128), BF16)
        w_uq_v = self.w_uq.rearrange("(k p) (h e) -> p k h e", p=128, e=96)
        for kc in range(6):
            s.dma("pool", wuq[:, kc, :, 0:96], w_uq_v[:, kc, :, :], writes=[("wuq", 0, kc)])
            s.dma("pool", wuq[:, kc, :, 96:112], w_uq_v[:, kc, :, 80:96], writes=[("wuq", 1, kc)])
            s.dma("pool", wuq[:, kc, :, 112:128], w_uq_v[:, kc, :, 64:80], writes=[("wuq", 2, kc)])
        wkn = A.alloc("wkn", (2, 512), BF16)
        wvm = A.alloc("wvm", (2, 512), BF16)
        w_ukv_v = self.w_ukv.rearrange("(k p) (h e) -> p k h e", p=128, e=128)
        for kc in range(2):
            s.dma("pool", wkn[:, kc, :].rearrange("p (h e) -> p h e", e=64), w_ukv_v[:, kc, :, 0:64], writes=[("wkn", kc)])
            s.dma("pool", wvm[:, kc, :].rearrange("p (h e) -> p h e", e=64), w_ukv_v[:, kc, :, 64:128], writes=[("wvm", kc)])

        for it in range(NT):
            b = 6 + it % 2
            T = it // 4
            s.op("pe", [mm(bank(b), ckvn[:, k, it * 128:(it + 1) * 128], wvm[:, k, :], k == 0, k == 1) for k in range(2)],
                 reads=[("ckvn", k, T) for k in range(2)] + [("wvm", 0), ("wvm", 1)], writes=[("ps", b)])
            pv = bank(b).rearrange("p (c e d) -> p c e d", c=4, e=2, d=64)
            s.op("act", lambda e: e.activation(out=vaug[:, it, :, 0:64], in_=pv[:, :, 0, :], func=AF.Copy),
                 writes=[("ps", b), ("vaug", it, 0)])
            s.op("dve", lambda e: e.tensor_copy(out=vaug[:, it, :, 128:192], in_=pv[:, :, 1, :]),
                 writes=[("ps", b), ("vaug", it, 1)])

        wuq_keys = [("wuq", i, kc) for i in range(3) for kc in range(6)]
        for c in range(4):
            for kn in ("kA", "kB"):
                s.op("dve", lambda e: e.tensor_copy(out=qk[kn][64:96, :], in_=self.kpe_t),
                     reads=[("kpe", T) for T in range(NB)], writes=[(kn, "g")])
            for T in range(NB):
                s.op("pe", [mm(bank(7), wkn[:, k, c * 128:(c + 1) * 128], ckvn[:, k, TS(T)], k == 0, k == 1) for k in range(2)],
                     reads=[("ckvn", k, T) for k in range(2)] + [("wkn", 0), ("wkn", 1)], writes=[("ps", 7)])
                s.op("dve", lambda e: e.tensor_copy(out=qk["kA"][0:64, TS(T)], in_=bank(7)[0:64, :]),
                     writes=[("ps", 7), ("kA", T)])
                s.op("dve", lambda e: e.tensor_copy(out=qk["kB"][0:64, TS(T)], in_=bank(7)[64:128, :]),
                     writes=[("ps", 7), ("kB", T)])
                for hh in (0, 1):
                    h = 2 * c + hh
                    qn = "qA" if hh == 0 else "qB"
                    q = qk[qn]
                    s.op("pe", [mm(bank(6), wuq[:, k, h, :], cqn[:, k, TS(T)], k == 0, k == 5) for k in range(6)],
                         reads=[("cqn", k, T) for k in range(6)] + wuq_keys, writes=[("ps", 6)])
                    s.op("act", lambda e: e.activation(out=q[0:64, TS(T)], in_=bank(6)[0:64, :], func=AF.Copy),
                         writes=[("ps", 6), (qn, T)])
                    s.op("dve", lambda e: e.tensor_tensor(out=t1, in0=bank(6)[64:96, :], in1=CC[:, TS(T)], op=ALU.mult),
                         reads=[("CC",)], writes=[("ps", 6), ("t1",)])
                    s.op("dve", lambda e: e.tensor_tensor(out=t2, in0=bank(6)[96:128, :], in1=SS[:, TS(T)], op=ALU.mult),
                         reads=[("SS",)], writes=[("ps", 6), ("t2",)])
                    s.op("dve", lambda e: e.tensor_tensor(out=q[64:96, TS(T)], in0=t1, in1=t2, op=ALU.add),
                         reads=[("t1",), ("t2",)], writes=[(qn, T)])
            if c == 0:
                self.dump("qm0", qk["qA"][0:96, :], 96, S, BF16, [("qA", T) for T in range(NB)])
                self.dump("km0", qk["kA"][0:96, :], 96, S, BF16, [("kA", T) for T in range(NB)] + [("kA", "g")])
            self.attention(c, False)
        for c in range(4):
            self.dump("obT%d" % c, oT2[:, c, :], 128, S, BF16, [("oT2", c, T, hh) for T in range(NB) for hh in range(2)])
        for nme in ("wuq", "wkn", "wvm", "cqn", "ckvn", "qA", "qB", "kA", "kB", "vaug", "pT0", "pT1", "pT2", "pT3",
                    "rec0", "rec1", "kpe", "t1", "t2", "CC", "SS", "Gtok"):
            A.release(nme)
        if upto <= 4:
            return

        merged = A.alloc("merged", (KC, S), BF16)
        self.merged = merged
        self.gate_merge(C_GF, self.w_pf, True, self.oT, "oT")
        self.gate_merge(C_GM, self.w_pm, False, self.oT2, "oT2")
        for m in (0, 7):
            self.dump("mg%d" % m, merged[:, m, :], 128, S, BF16, [("merged", m, T) for T in range(NB)])
        A.release("hT"); A.release("oT"); A.release("oT2")
        if upto <= 5:
            return
        self._body4(upto)

    def build_gvec(self):
        s, A = self.s, self.A
        bank = self.bank
        gvec = A.alloc("gvec", (2, D), F32)
        silurep = A.alloc("silurep", (KC, 128), BF16)
        bada_rep = A.alloc("bada_rep", (2 * D,), F32)
        gpost_rep = A.alloc("gpost_rep", (2 * D,), F32)
        s.dma("pool", bada_rep, self.bada_rep, writes=[("bada_rep",)])
        s.dma("pool", gpost_rep, self.gpost_rep, writes=[("gpost_rep",)])
        for k in range(KC):
            s.op("dve", lambda e: e.tensor_scalar(out=silurep[:, k, :], in0=self.ones_b, scalar1=self.siluf[:, k:k + 1],
                                                  scalar2=None, op0=ALU.mult),
                 reads=[("ones_b",), ("siluf",)], writes=[("silurep", k)])
        wada = A.alloc("wadag", (KC, D), BF16)
        w_ada_v = self.w_ada.rearrange("(k p) n -> p k n", p=128)
        for g, v in enumerate((2, 5)):
            s.dma("pool", wada, w_ada_v[:, :, v * D:(v + 1) * D], writes=[("wadag",)])
            for half in range(2):
                b = 5 + half
                s.op("pe", [(lambda e, k=k: e.matmul(bank(b), silurep[:, k, :], wada[:, k, half * 512:(half + 1) * 512],
                                                    start=(k == 0), stop=(k == KC - 1))) for k in range(KC)],
                     reads=[("wadag",)] + [("silurep", k) for k in range(KC)], writes=[("ps", b)])
                sl = slice(g * D + half * 512, g * D + (half + 1) * 512)
                hs = slice(half * 512, (half + 1) * 512)
                s.op("dve", lambda e: e.tensor_tensor(out=gvec[:, g, hs], in0=bank(b), in1=bada_rep[:, sl], op=ALU.add),
                     reads=[("bada_rep",)], writes=[("ps", b), ("gvec", g, half)])
                s.op("dve", lambda e: e.tensor_tensor(out=gvec[:, g, hs], in0=gvec[:, g, hs], in1=gpost_rep[:, sl], op=ALU.mult),
                     reads=[("gpost_rep",), ("gvec", g, half)], writes=[("gvec", g, half)])
        self.gvec = gvec
        self.dump("gvec", gvec.rearrange("p a b -> p (a b)"), 128, 2 * D, F32, [("gvec", g, h) for g in range(2) for h in range(2)])
        for n in ("silurep", "bada_rep", "gpost_rep", "wadag"):
            A.release(n)

    def _body4(self, upto):
        nc, s, A = self.nc, self.s, self.A
        bank = self.bank
        self.build_gvec()
        merged, gvec = self.merged, self.gvec
        mm = lambda out, l, r, st, sp: (lambda e: e.matmul(out, l, r, start=st, stop=sp))
        wout = A.alloc("wout", (KC, D), BF16)
        s.dma("pool", wout, self.w_out.rearrange("(k p) n -> p k n", p=128), writes=[("wout",)])
        w2 = A.alloc("w2", (NFC, D), BF16)
        w2v = self.w_f2.rearrange("(j p) n -> p j n", p=128)
        w1v = self.w_f1.rearrange("(k p) n -> p k n", p=128)
        x2 = A.alloc("x2", (4, D), F32)
        h2T = A.alloc("h2T", (KC, 512), BF16)
        actT = A.alloc("actT", (NFC, 512), BF16)
        w1b = [A.alloc("w1b%d" % i, (KC, 512), BF16) for i in range(3)]
        sgl = [A.alloc("sgl%d" % i, (512,), BF16) for i in range(2)]
        xin = [A.alloc("xin%d" % i, (D,), F32) for i in range(2)]
        ot = [A.alloc("ot%d" % i, (D,), F32) for i in range(2)]
        tmpf = [A.alloc("tmpf%d" % i, (512,), F32) for i in range(2)]
        junkp = A.alloc("junkp", (512,), BF16)
        ssq = A.alloc("ssq", (4 * NT,), F32)
        ssy = A.alloc("ssy", (2 * NT,), F32)
        rsy = A.alloc("rsy", (2 * NT,), F32)
        ss, rstd = self.ss, self.rstd
        w2_loaded = [False]

        def postnorm(bks, col, g, resid, rkeys, out_ap, okeys):
            for half, b in enumerate(bks):
                s.op("act", lambda e: e.activation(out=junkp, in_=bank(b), func=AF.Square,
                                                   accum_out=ssq[:, 2 * col + half:2 * col + half + 1]),
                     writes=[("ps", b), ("junkp",), ("ssq", col, half)])
            s.op("dve", lambda e: e.tensor_tensor(out=ssy[:, col:col + 1], in0=ssq[:, 2 * col:2 * col + 1],
                                                  in1=ssq[:, 2 * col + 1:2 * col + 2], op=ALU.add),
                 reads=[("ssq", col, 0), ("ssq", col, 1)], writes=[("ssy", col)])
            s.op("act", lambda e: e.activation(out=rsy[:, col:col + 1], in_=ssy[:, col:col + 1], func=AF.Ln,
                                               bias=self.eps_c, scale=1.0 / D),
                 reads=[("ssy", col), ("eps_c",)], writes=[("rsy", col)])
            s.op("act", lambda e: e.activation(out=rsy[:, col:col + 1], in_=rsy[:, col:col + 1], func=AF.Exp, scale=-0.5),
                 reads=[("rsy", col)], writes=[("rsy", col)])
            for half, b in enumerate(bks):
                hs = slice(half * 512, (half + 1) * 512)
                s.op("dve", lambda e: e.scalar_tensor_tensor(out=tmpf[half], in0=bank(b), scalar=rsy[:, col:col + 1],
                                                             in1=gvec[:, g, hs], op0=ALU.mult, op1=ALU.mult),
                     reads=[("rsy", col), ("gvec", g, half)], writes=[("ps", b), ("tmpf%d" % half,)])
                s.op("dve", lambda e: e.tensor_tensor(out=out_ap[:, hs], in0=tmpf[half], in1=resid[:, hs], op=ALU.add),
                     reads=[("tmpf%d" % half,)] + rkeys, writes=okeys)

        nw1 = 0
        import os
        for T in [int(t_) for t_ in os.environ.get("TLIST", "0,1,2,3").split(",")]:
            for i in range(4):
                it = 4 * T + i
                xi, xn = xin[it % 2], "xin%d" % (it % 2)
                s.dma("sp", xi, self.x[it * 128:(it + 1) * 128, :], writes=[(xn,)])
                bks = (0, 1) if it % 2 == 0 else (2, 3)
                for half, b in enumerate(bks):
                    s.op("pe", [mm(bank(b), merged[:, k, it * 128:(it + 1) * 128], wout[:, k, half * 512:(half + 1) * 512],
                                   k == 0, k == KC - 1) for k in range(KC)],
                         reads=[("merged", k, T) for k in range(KC)] + [("wout",)], writes=[("ps", b)])
                postnorm(bks, it, 0, xi, [(xn,)], x2[:, i, :], [("x2", i)])
            if T == 0:
                self.dump("x2", x2[:, 0, :], 128, D, F32, [("x2", 0)])
            if upto <= 6:
                return
            if not w2_loaded[0]:
                for gi, g0 in enumerate(range(0, NFC, 6)):
                    g1 = min(NFC, g0 + 6)
                    s.dma("pool", w2[:, g0:g1, :], w2v[:, g0:g1, :], writes=[("w2", gi)])
                w2_loaded[0] = True
            self.prenorm(None, self.affn, self.modT[:, 24:32], h2T, "h2T", ss, rstd, "affn", [T], T,
                         src_sbuf=lambda it: (x2[:, it % 4, :], ("x2", it % 4)))
            for j in range(NFC):
                if j % 2 == 0:
                    wb, wbn = w1b[nw1 % 3], "w1b%d" % (nw1 % 3)
                    nw1 += 1
                    s.dma("pool", wb[:, :, 0:256], w1v[:, :, j * 128:(j + 2) * 128], writes=[(wbn, "g")])
                    s.dma("pool", wb[:, :, 256:512], w1v[:, :, DFF + j * 128:DFF + (j + 2) * 128], writes=[(wbn, "u")])
                jo = (j % 2) * 128
                gb, ub = 4 + 2 * (j % 2), 5 + 2 * (j % 2)
                hk = [("h2T", k, 0) for k in range(KC)]
                s.op("pe", [mm(bank(gb), wb[:, k, jo:jo + 128], h2T[:, k, :], k == 0, k == KC - 1) for k in range(KC)],
                     reads=hk + [(wbn, "g")], writes=[("ps", gb)])
                s.op("pe", [mm(bank(ub), wb[:, k, 256 + jo:256 + jo + 128], h2T[:, k, :], k == 0, k == KC - 1) for k in range(KC)],
                     reads=hk + [(wbn, "u")], writes=[("ps", ub)])
                s.op("act", lambda e: e.activation(out=sgl[j % 2], in_=bank(gb), func=AF.Silu),
                     writes=[("ps", gb), ("sgl%d" % (j % 2),)])
                s.op("dve", lambda e: e.tensor_tensor(out=actT[:, j, :], in0=bank(ub), in1=sgl[j % 2], op=ALU.mult),
                     reads=[("sgl%d" % (j % 2),)], writes=[("ps", ub), ("actT", j)])
            if upto <= 7:
                return
            for i in range(4):
                it = 4 * T + i
                bks = (0, 1) if it % 2 == 0 else (2, 3)
                for half, b in enumerate(bks):
                    s.op("pe", [mm(bank(b), actT[:, j, i * 128:(i + 1) * 128], w2[:, j, half * 512:(half + 1) * 512],
                                   j == 0, j == NFC - 1) for j in range(NFC)],
                         reads=[("actT", j) for j in range(NFC)] + [("w2", gi) for gi in range(4)], writes=[("ps", b)])
                o_t, on = ot[it % 2], "ot%d" % (it % 2)
                postnorm(bks, NT + it, 1, x2[:, i, :], [("x2", i)], o_t, [(on,)])
                s.dma("sp", self.out[it * 128:(it + 1) * 128, :], o_t, reads=[(on,)])
            if upto <= 8:
                return


def _consts():
    ident = np.eye(128, dtype=np.float32)
    sk = np.arange(128)[:, None]
    tq = np.arange(128)[None, :]
    mask = np.where(sk > tq, NEG, 0.0).astype(np.float32)
    inv_freq = 1.0 / (10000.0 ** (np.arange(0, 32, 2, dtype=np.float32) / 32.0))
    rope = np.zeros((128, 4), np.float32)
    for p in range(64, 96):
        i = (p - 64) % 16
        rope[p, 0] = inv_freq[i]
        rope[p, 1] = math.pi / 2
        rope[p, 2] = inv_freq[i]
        rope[p, 3] = (math.pi if p < 80 else 0.0)
    return dict(c_ident=ident.astype(ml_dtypes.bfloat16), c_identf=ident, c_mask=mask.astype(ml_dtypes.bfloat16),
                c_rope=rope)


def make_in_maps(x, c, positions, w_ada, b_ada, g_pre_mix, g_post_mix, g_pre_ffn, g_post_ffn,
                 w_in, b_forget, g_q_lora, w_uq, g_kv_lora, w_ukv, w_proj_fox, w_proj_mla,
                 w_out, w_ffn_in, w_ffn_out, cores=range(8)):
    f = lambda a: np.ascontiguousarray(np.asarray(a, dtype=np.float32))
    colT = lambda v, n: f(np.asarray(v, np.float32).reshape(n, 128).T)
    rep = lambda v: f(np.broadcast_to(np.asarray(v, np.float32)[None, :], (128, v.shape[0])))
    b_ada0 = np.asarray(b_ada[0], np.float32)
    shared = dict(
        w_ada=f(w_ada[0]), badaT=colT(b_ada0, 48),
        bada_rep=rep(np.concatenate([b_ada0[2 * D:3 * D], b_ada0[5 * D:6 * D]])),
        gpost_rep=rep(np.concatenate([np.asarray(g_post_mix[0], np.float32), np.asarray(g_post_ffn[0], np.float32)])),
        gpreT=f(np.concatenate([colT(g_pre_mix[0], 8), colT(g_pre_ffn[0], 8)], axis=1)),
        w_in=f(w_in[0]), nbf=f(np.asarray(b_forget[0], np.float32).reshape(8, 1)),
        gqT=colT(g_q_lora[0], 6), gkvT=colT(g_kv_lora[0], 2),
        w_uq=f(w_uq[0]), w_ukv=f(w_ukv[0]), w_pf=f(w_proj_fox[0]), w_pm=f(w_proj_mla[0]),
        w_out=f(w_out[0]), w_f1=f(w_ffn_in[0]), w_f2=f(w_ffn_out[0]),
    )
    shared.update(_consts())
    maps = []
    for b in cores:
        m = dict(shared)
        m["x"] = f(x[b])
        m["cT"] = colT(c[b], 8)
        m["pos"] = np.ascontiguousarray(np.broadcast_to(np.asarray(positions[b], np.int32)[None, :], (128, S)))
        maps.append(m)
    return maps


_NC_CACHE = {}


def kernel(**inputs):
    if "nc" not in _NC_CACHE:
        _NC_CACHE["nc"] = Builder().build()
    nc = _NC_CACHE["nc"]
    maps = make_in_maps(**inputs)
    res = run_bass_kernel_spmd(nc, maps, core_ids=list(range(8)))
    out = np.stack([np.asarray(r["out"], dtype=np.float32) for r in res.results], axis=0)
    return out
```

```python
import math
import numpy as np
import ml_dtypes
import concourse.bass as bass
import concourse.mybir as mybir
from concourse.bass_utils import run_bass_kernel_spmd

F32 = mybir.dt.float32
BF16 = mybir.dt.bfloat16
I32 = mybir.dt.int32
U8 = mybir.dt.uint8
AF = mybir.ActivationFunctionType
ALU = mybir.AluOpType

S = 2048
D = 1024
NT = 16
NB = 4
KC = 8
DFF = 2816
NFC = 22
D_IN = 4648
EPS = 1e-6
C_Q, C_K, C_V, C_F, C_CQ, C_CKV, C_KR, C_GF, C_GM = 0, 512, 1024, 1536, 1544, 2312, 2568, 2600, 3624
NEG = -30000.0


class Sched:
    def __init__(self, nc, sems, dma_sems):
        self.nc = nc
        self.eng = {"pe": nc.tensor, "act": nc.scalar, "dve": nc.vector, "pool": nc.gpsimd, "sp": nc.sync}
        self.sem = sems
        self.cnt = {e: 0 for e in sems}
        self.seen = {e: {} for e in self.eng}
        self.last_w = {}
        self.readers = {}
        self.dma_sems = dma_sems
        self.dma_n = {q: 0 for q in dma_sems}
        self.inherit = {}
        self.n_wait = 0

    def _collect(self, reads, writes):
        raw, other = [], []
        for k in reads:
            w = self.last_w.get(k)
            if w is not None:
                raw.append(w)
        for k in writes:
            w = self.last_w.get(k)
            if w is not None:
                other.append(w)
            other.extend(self.readers.get(k, ()))
            inh = self.inherit.get(k[0])
            if inh:
                other.extend(inh)
        return raw, other

    def _emit_waits(self, e, raw, other):
        for tok in raw:
            if tok[2] == e and e == "pe":
                continue
            self._wait(e, tok)
        for tok in other:
            if tok[2] == e:
                continue
            self._wait(e, tok)

    def _wait(self, e, tok):
        sem, val, _ = tok
        sid = id(sem)
        if self.seen[e].get(sid, 0) >= val:
            return
        self.eng[e].wait_ge(sem, val)
        self.seen[e][sid] = val
        self.n_wait += 1

    def _register(self, tok, reads, writes):
        for k in reads:
            self.readers.setdefault(k, []).append(tok)
        for k in writes:
            self.last_w[k] = tok
            self.readers[k] = []

    def op(self, e, fns, reads=(), writes=()):
        if callable(fns):
            fns = [fns]
        raw, other = self._collect(reads, writes)
        self._emit_waits(e, raw, other)
        eng = self.eng[e]
        ins = None
        for f in fns:
            ins = f(eng)
        self.cnt[e] += 1
        ins.then_inc(self.sem[e], 1)
        tok = (self.sem[e], self.cnt[e], e)
        self._register(tok, reads, writes)
        return tok

    def dma(self, q, out, in_, reads=(), writes=()):
        raw, other = self._collect(reads, writes)
        self._emit_waits(q, raw, other)
        pool = self.dma_sems[q]
        n = self.dma_n[q]
        self.dma_n[q] += 1
        sem = pool[n % len(pool)]
        rnd = n // len(pool)
        if rnd > 0:
            self._wait(q, (sem, 16 * rnd, "dma"))
        self.eng[q].dma_start(out=out, in_=in_).then_inc(sem, 16)
        tok = (sem, 16 * (rnd + 1), "dma")
        self._register(tok, reads, writes)
        return tok

    def release(self, name):
        toks = []
        for k in list(self.last_w.keys()):
            if k[0] == name:
                toks.append(self.last_w.pop(k))
        for k in list(self.readers.keys()):
            if k[0] == name:
                toks.extend(self.readers.pop(k))
        best = {}
        for t in toks:
            sid = id(t[0])
            if sid not in best or best[sid][1] < t[1]:
                best[sid] = t
        return list(best.values())


class Arena:
    def __init__(self, sched, arena_ap, nbytes):
        self.s = sched
        self.arena = arena_ap
        self.free = [(0, nbytes)]
        self.live = {}
        self.dead = []
        self.peak = 0

    def alloc(self, name, shape, dtype, parts=(0, 128)):
        esz = 4 if dtype in (F32, I32) else 2
        n = 1
        for d in shape:
            n *= d
        size = (n * esz + 63) // 64 * 64
        for i, (off, sz) in enumerate(self.free):
            if sz >= size:
                self.free[i] = (off + size, sz - size)
                break
        else:
            raise RuntimeError(f"arena OOM for {name} ({size} B); live={ {k: v[1] for k, v in self.live.items()} }")
        self.live[name] = (off, size)
        self.peak = max(self.peak, off + size)
        toks = []
        keep = []
        for (o, s_, tk) in self.dead:
            if o < off + size and off < o + s_:
                toks.extend(tk)
            keep.append((o, s_, tk))
        self.s.inherit[name] = toks
        ap = self.arena[parts[0]:parts[1], off // 2:(off + size) // 2]
        if esz == 4:
            ap = ap.bitcast(dtype)
        elif dtype != BF16:
            ap = ap.bitcast(dtype)
        ap = ap[:, 0:n]
        if len(shape) == 2:
            ap = ap.rearrange("p (a b) -> p a b", a=shape[0], b=shape[1])
        elif len(shape) == 3:
            ap = ap.rearrange("p (a b c) -> p a b c", a=shape[0], b=shape[1], c=shape[2])
        return ap

    def release(self, name):
        off, size = self.live.pop(name)
        toks = self.s.release(name)
        self.dead.append((off, size, toks))
        self.free.append((off, size))
        self.free.sort()
        merged = []
        for o, s_ in self.free:
            if merged and merged[-1][0] + merged[-1][1] == o:
                merged[-1] = (merged[-1][0], merged[-1][1] + s_)
            else:
                merged.append((o, s_))
        self.free = merged


class Builder:
    def __init__(self, debug=None):
        self.debug = debug or []
        nc = bass.Bass("TRN2", target_bir_lowering=False)
        self.nc = nc
        self.dbg_out = {}
        d = lambda n, sh, dt, kind="ExternalInput": nc.dram_tensor(n, list(sh), dt, kind=kind).ap()
        self.x = d("x", [S, D], F32)
        self.cT = d("cT", [128, KC], F32)
        self.pos = d("pos", [128, S], I32)
        self.w_ada = d("w_ada", [D, 6 * D], F32)
        self.badaT = d("badaT", [128, 48], F32)
        self.bada_rep = d("bada_rep", [128, 2 * D], F32)
        self.gpost_rep = d("gpost_rep", [128, 2 * D], F32)
        self.gpreT = d("gpreT", [128, 16], F32)
        self.w_in = d("w_in", [D, D_IN], F32)
        self.nbf = d("nbf", [8, 1], F32)
        self.gqT = d("gqT", [128, 6], F32)
        self.gkvT = d("gkvT", [128, 2], F32)
        self.w_uq = d("w_uq", [768, 768], F32)
        self.w_ukv = d("w_ukv", [256, 1024], F32)
        self.w_pf = d("w_pf", [512, D], F32)
        self.w_pm = d("w_pm", [512, D], F32)
        self.w_out = d("w_out", [D, D], F32)
        self.w_f1 = d("w_f1", [D, 2 * DFF], F32)
        self.w_f2 = d("w_f2", [DFF, D], F32)
        self.c_ident = d("c_ident", [128, 128], BF16)
        self.c_identf = d("c_identf", [128, 128], F32)
        self.c_mask = d("c_mask", [128, 128], BF16)
        self.c_rope = d("c_rope", [128, 4], F32)
        self.out = d("out", [S, D], F32, kind="ExternalOutput")

    def dbg(self, name, ap, shape, dtype):
        if name in self.debug:
            o = self.nc.dram_tensor("dbg_" + name, list(shape), dtype, kind="ExternalOutput").ap()
            self.dbg_out[name] = o
            return o
        return None

    def build(self, upto=99):
        import contextlib
        nc = self.nc
        with contextlib.ExitStack() as es:
            es.enter_context(nc.allow_low_precision("bf16 matmul operands by design; fp32 accumulation"))
            es.enter_context(nc.allow_non_contiguous_dma("small strided weight / constant loads"))
            ARENA = 207 * 1024
            arena_t = es.enter_context(nc.sbuf_tensor("arena", [128, ARENA // 2], BF16))
            self.banks = [es.enter_context(nc.psum_tensor(f"ps{b}", [128, 512], F32)) for b in range(8)]
            sems = {e: es.enter_context(nc.semaphore("s_" + e)) for e in ("pe", "act", "dve", "pool")}
            dma_sems = {q: [es.enter_context(nc.semaphore(f"d{q}{i}")) for i in range(12)] for q in ("sp", "pool")}
            self.s = Sched(nc, sems, dma_sems)
            self.A = Arena(self.s, arena_t, ARENA)
            self._body(upto)
            self._finish()
        return nc

    def bank(self, b):
        return self.banks[b][:, :]

    def bankbf(self, b):
        return self.banks[b][:, :].bitcast(BF16)

    def _finish(self):
        s = self.s
        for q in ("sp", "pool"):
            pool = s.dma_sems[q]
            n = s.dma_n[q]
            for i, sem in enumerate(pool):
                cnt = (n - i + len(pool) - 1) // len(pool) if n > i else 0
                if cnt > 0:
                    s._wait("sp", (sem, 16 * cnt, "dma"))

    def dump(self, name, ap, parts, ncols, dtype, reads):
        o = self.dbg(name, ap, [parts, ncols], dtype)
        if o is not None:
            self.s.dma("sp", o, ap, reads=reads)

    def _body(self, upto):
        nc, s, A = self.nc, self.s, self.A
        bank, bankbf = self.bank, self.bankbf

        def load_const(name, src, shape, dtype, parts=(0, 128), q="pool"):
            t = A.alloc(name, shape, dtype, parts)
            s.dma(q, t, src, writes=[(name,)])
            return t

        ident = load_const("ident", self.c_ident, (128,), BF16)
        identf = load_const("identf", self.c_identf, (128,), F32)
        mask = load_const("mask", self.c_mask, (128,), BF16)
        ropec = load_const("ropec", self.c_rope, (4,), F32)
        cT = load_const("cT", self.cT, (KC,), F32)
        badaT = load_const("badaT", self.badaT, (48,), F32)
        gpreT = load_const("gpreT", self.gpreT, (16,), F32)
        gqT = load_const("gqT", self.gqT, (6,), F32)
        gkvT = load_const("gkvT", self.gkvT, (2,), F32)
        nbf = load_const("nbf", self.nbf, (1,), F32, parts=(0, 8))
        self.ident, self.mask, self.identf, self.nbf_t = ident, mask, identf, nbf
        self.gqT_t, self.gkvT_t = gqT, gkvT
        eps_c = A.alloc("eps_c", (1,), F32)
        s.op("dve", lambda e: e.memset(eps_c, EPS), writes=[("eps_c",)])
        self.eps_c = eps_c

        siluf = A.alloc("siluf", (KC,), F32)
        silub = A.alloc("silub", (KC,), BF16)
        ones_b = A.alloc("ones_b", (128,), BF16)
        modT = A.alloc("modT", (48,), F32)
        amix = A.alloc("amix", (KC,), F32)
        affn = A.alloc("affn", (KC,), F32)
        s.op("act", lambda e: e.activation(out=siluf, in_=cT, func=AF.Silu), reads=[("cT",)], writes=[("siluf",)])
        s.op("dve", lambda e: e.tensor_copy(out=silub, in_=siluf), reads=[("siluf",)], writes=[("silub",)])
        s.op("dve", lambda e: e.memset(ones_b, 1.0), writes=[("ones_b",)])
        wada = [A.alloc("wada0", (KC, D), BF16), A.alloc("wada1", (KC, D), BF16)]
        w_ada_v = self.w_ada.rearrange("(k p) n -> p k n", p=128)
        MODB = 7
        for v in (0, 1):
            i = v
            wn = "wada%d" % i
            s.dma("pool", wada[i], w_ada_v[:, :, v * D:(v + 1) * D], writes=[(wn,)])
            for c in range(KC):
                col = v * 8 + c
                s.op("pe", [(lambda e, k=k: e.matmul(bank(MODB)[:, col:col + 1], wada[i][:, k, c * 128:(c + 1) * 128],
                                                    silub[:, k:k + 1], start=(k == 0), stop=(k == KC - 1)))
                            for k in range(KC)],
                     reads=[(wn,), ("silub",)], writes=[("ps", MODB)])
        s.op("dve", lambda e: e.tensor_tensor(out=modT[:, 0:16], in0=bank(MODB)[:, 0:16], in1=badaT[:, 0:16], op=ALU.add),
             reads=[("badaT",)], writes=[("ps", MODB), ("modT",)])
        s.op("dve", lambda e: e.scalar_tensor_tensor(out=amix, in0=modT[:, 8:16], scalar=1.0, in1=gpreT[:, 0:8],
                                                     op0=ALU.add, op1=ALU.mult),
             reads=[("modT",), ("gpreT",)], writes=[("amix",)])
        self.silub, self.badaT_t, self.gpreT_t = silub, badaT, gpreT
        self.modT, self.amix, self.affn, self.ones_b, self.siluf = modT, amix, affn, ones_b, siluf
        self.dump("modT", modT, 128, 48, F32, [("modT",)])
        A.release("wada0"); A.release("wada1")
        if upto <= 0:
            return

        R = (64, 96)
        posi = A.alloc("posi", (S,), I32, parts=R)
        posf = A.alloc("posf", (S,), F32, parts=R)
        CC = A.alloc("CC", (S,), BF16, parts=R)
        SS = A.alloc("SS", (S,), BF16, parts=R)
        s.dma("pool", posi, self.pos[64:96, :], writes=[("posi",)])
        s.op("dve", lambda e: e.tensor_copy(out=posf, in_=posi), reads=[("posi",)], writes=[("posf",)])
        ang = posi.bitcast(F32)
        kf = A.alloc("kf", (S,), F32, parts=R)
        ki = A.alloc("ki", (S,), I32, parts=R)
        for (tab, name, c0) in ((CC, "CC", 0), (SS, "SS", 2)):
            s.op("dve", lambda e: e.tensor_scalar(out=ang, in0=posf, scalar1=ropec[64:96, c0:c0 + 1],
                                                  scalar2=ropec[64:96, c0 + 1:c0 + 2], op0=ALU.mult, op1=ALU.add),
                 reads=[("posf",), ("ropec",)], writes=[("posi",)])
            s.op("dve", lambda e: e.tensor_scalar(out=kf, in0=ang, scalar1=1.0 / (2.0 * math.pi), scalar2=None,
                                                  op0=ALU.mult),
                 reads=[("posi",)], writes=[("kf",)])
            s.op("dve", lambda e: e.tensor_copy(out=ki, in_=kf), reads=[("kf",)], writes=[("ki",)])
            s.op("dve", lambda e: e.tensor_copy(out=kf, in_=ki), reads=[("ki",)], writes=[("kf",)])
            s.op("dve", lambda e: e.scalar_tensor_tensor(out=ang, in0=kf, scalar=-2.0 * math.pi, in1=ang,
                                                         op0=ALU.mult, op1=ALU.add),
                 reads=[("kf",), ("posi",)], writes=[("posi",)])
            s.op("dve", lambda e: e.tensor_scalar(out=kf, in0=ang, scalar1=math.pi, scalar2=-2.0 * math.pi,
                                                  op0=ALU.is_gt, op1=ALU.mult),
                 reads=[("posi",)], writes=[("kf",)])
            s.op("dve", lambda e: e.tensor_tensor(out=ang, in0=ang, in1=kf, op=ALU.add),
                 reads=[("posi",), ("kf",)], writes=[("posi",)])
            s.op("dve", lambda e: e.tensor_scalar(out=ang, in0=ang, scalar1=-math.pi, scalar2=math.pi,
                                                  op0=ALU.max, op1=ALU.min),
                 reads=[("posi",)], writes=[("posi",)])
            s.op("act", lambda e: e.activation(out=tab, in_=ang, func=AF.Sin),
                 reads=[("posi",)], writes=[(name,)])
        A.release("kf"); A.release("ki")
        self.CC, self.SS = CC, SS
        self.dump("CC", CC, 32, S, BF16, [("CC",)])
        self.dump("SS", SS, 32, S, BF16, [("SS",)])
        A.release("posi"); A.release("posf")

        hT = A.alloc("hT", (KC, S), BF16)
        self.hT = hT
        ss = A.alloc("ss", (NT,), F32)
        rstd = A.alloc("rstd", (NT,), F32)
        self.ss, self.rstd = ss, rstd
        self.prenorm(self.x, amix, modT[:, 0:8], hT, "hT", ss, rstd, "amix", list(range(NB)), 0)
        self.dump("hT0", hT[:, 0, :], 128, S, BF16, [("hT", 0, T) for T in range(NB)])
        self.dump("hT7", hT[:, 7, :], 128, S, BF16, [("hT", 7, T) for T in range(NB)])
        if upto <= 1:
            return
        self._body2(upto)

    def prenorm(self, src_rows, a_sc, shift_sc, dst, dname, ss, rstd, aname, Ts, tok0, src_sbuf=None):
        s, A = self.s, self.A
        xts = [A.alloc("xt%d" % i, (D,), F32) for i in range(4)] if src_sbuf is None else None
        xb = [A.alloc("xb%d" % i, (D,), BF16) for i in range(4)]
        junk = A.alloc("junk", (D,), BF16)
        for T in Ts:
            for i in range(4):
                it = 4 * T + i
                if src_sbuf is None:
                    xt, xk = xts[i], ("xt%d" % i,)
                    s.dma("sp", xt, src_rows[it * 128:(it + 1) * 128, :], writes=[xk])
                else:
                    xt, xk = src_sbuf(it)
                s.op("act", lambda e: e.activation(out=junk, in_=xt, func=AF.Square, accum_out=ss[:, it:it + 1]),
                     reads=[xk], writes=[("junk",), ("ss", it)])
            sl4 = slice(4 * T, 4 * T + 4)
            s.op("act", lambda e: e.activation(out=rstd[:, sl4], in_=ss[:, sl4], func=AF.Ln, bias=self.eps_c, scale=1.0 / D),
                 reads=[("ss", 4 * T + i) for i in range(4)] + [("eps_c",)], writes=[("rstd", T)])
            s.op("act", lambda e: e.activation(out=rstd[:, sl4], in_=rstd[:, sl4], func=AF.Exp, scale=-0.5),
                 reads=[("rstd", T)], writes=[("rstd", T)])
            for i in range(4):
                it = 4 * T + i
                xt, xk = (xts[i], ("xt%d" % i,)) if src_sbuf is None else src_sbuf(it)
                s.op("dve", lambda e: e.tensor_scalar(out=xb[i], in0=xt, scalar1=rstd[:, it:it + 1], scalar2=None,
                                                      op0=ALU.mult),
                     reads=[xk, ("rstd", T)], writes=[("xb%d" % i,)])
            for c in range(KC):
                b = 6 + (c % 2)
                psb = self.bankbf(b)
                s.op("pe", [(lambda e, i=i: e.transpose(out=psb[:, i * 128:(i + 1) * 128],
                                                        in_=xb[i][:, c * 128:(c + 1) * 128], identity=self.ident))
                            for i in range(4)],
                     reads=[("xb%d" % i,) for i in range(4)] + [("ident",)], writes=[("ps", b)])
                dsl = dst[:, c, (T - tok0) * 512:(T - tok0 + 1) * 512]
                if c % 2 == 0:
                    s.op("act", lambda e: e.activation(out=dsl, in_=psb[:, 0:512], func=AF.Identity,
                                                       bias=shift_sc[:, c:c + 1], scale=a_sc[:, c:c + 1]),
                         reads=[(aname,), ("modT",), ("modT2",)], writes=[("ps", b), (dname, c, T if dname == "hT" else 0)])
                else:
                    s.op("dve", lambda e: e.tensor_scalar(out=dsl, in0=psb[:, 0:512], scalar1=a_sc[:, c:c + 1],
                                                          scalar2=shift_sc[:, c:c + 1], op0=ALU.mult, op1=ALU.add),
                         reads=[(aname,), ("modT",), ("modT2",)], writes=[("ps", b), (dname, c, T if dname == "hT" else 0)])
        if xts is not None:
            for i in range(4):
                A.release("xt%d" % i)
        for i in range(4):
            A.release("xb%d" % i)
        A.release("junk")

    def _body2(self, upto):
        nc, s, A = self.nc, self.s, self.A
        bank, bankbf = self.bank, self.bankbf
        hT, CC, SS = self.hT, self.CC, self.SS
        w_in_v = self.w_in.rearrange("(k p) n -> p k n", p=128)
        hkeys = lambda T: [("hT", k, T) for k in range(KC)]
        TS = lambda T: slice(T * 512, (T + 1) * 512)
        mm = lambda out, l, r, st, sp: (lambda e: e.matmul(out, l, r, start=st, stop=sp))

        oT = A.alloc("oT", (4, S), BF16)
        qk = {n: A.alloc(n, (S,), BF16, parts=(0, 96)) for n in ("qA", "qB", "kA", "kB")}
        vaug = A.alloc("vaug", (NT, 4, 192), BF16)
        self.pT = [A.alloc("pT%d" % i, (512,), BF16) for i in range(8)]
        self.rec = [A.alloc("rec%d" % i, (512,), F32) for i in range(2)]
        self.qk, self.vaug, self.oT = qk, vaug, oT
        self.att_cnt = 0

        s.op("dve", lambda e: e.memset(vaug.rearrange("p a b c -> p (a b c)"), 1.0),
             writes=[("vaug", it, e_) for it in range(NT) for e_ in range(3)])
        wv = A.alloc("wv", (KC, 512), BF16)
        s.dma("pool", wv, w_in_v[:, :, C_V:C_V + 512], writes=[("wv",)])

        def v_proj(lhs_of, nk, wt, wname, rkeys):
            for it in range(NT):
                b = 6 + it % 2
                s.op("pe", [mm(bank(b), lhs_of(k, it), wt[:, k, :], k == 0, k == nk - 1) for k in range(nk)],
                     reads=rkeys(it // 4) + [(wname,)], writes=[("ps", b)])
                pv = bank(b).rearrange("p (c e d) -> p c e d", c=4, e=2, d=64)
                s.op("act", lambda e: e.activation(out=vaug[:, it, :, 0:64], in_=pv[:, :, 0, :], func=AF.Copy),
                     writes=[("ps", b), ("vaug", it, 0)])
                s.op("dve", lambda e: e.tensor_copy(out=vaug[:, it, :, 128:192], in_=pv[:, :, 1, :]),
                     writes=[("ps", b), ("vaug", it, 1)])

        v_proj(lambda k, it: hT[:, k, it * 128:(it + 1) * 128], KC, wv, "wv", hkeys)
        A.release("wv")

        wmisc = A.alloc("wmisc", (KC, 128), BF16)
        s.op("dve", lambda e: e.memset(wmisc.rearrange("p a b -> p (a b)"), 0.0), writes=[("wmisc", i) for i in range(4)])
        s.dma("pool", wmisc[:, :, 0:8], w_in_v[:, :, C_F:C_F + 8], writes=[("wmisc", 0)])
        s.dma("pool", wmisc[:, :, 64:96], w_in_v[:, :, C_KR:C_KR + 32], writes=[("wmisc", 1)])
        s.dma("pool", wmisc[:, :, 96:112], w_in_v[:, :, C_KR + 16:C_KR + 32], writes=[("wmisc", 2)])
        s.dma("pool", wmisc[:, :, 112:128], w_in_v[:, :, C_KR:C_KR + 16], writes=[("wmisc", 3)])
        P8 = (0, 8)
        nbneg = A.alloc("nbneg", (1,), F32, parts=P8)
        eT = A.alloc("eT", (512,), F32, parts=P8)
        nlf = A.alloc("nlf", (512,), F32, parts=P8)
        onesf = A.alloc("onesf", (512,), F32, parts=P8)
        G = A.alloc("G", (S,), F32, parts=P8)
        negGb = A.alloc("negGb", (S,), BF16, parts=P8)
        Gtok = A.alloc("Gtok", (128,), F32)
        R = (64, 96)
        kpe = A.alloc("kpe", (S,), BF16, parts=R)
        t1 = A.alloc("t1", (512,), F32, parts=R)
        t2 = A.alloc("t2", (512,), F32, parts=R)
        self.t1, self.t2, self.Gtok, self.kpe_t = t1, t2, Gtok, kpe
        s.op("dve", lambda e: e.tensor_scalar(out=nbneg, in0=self.nbf_t, scalar1=-1.0, scalar2=None, op0=ALU.mult),
             reads=[("nbf",)], writes=[("nbneg",)])
        s.op("dve", lambda e: e.memset(onesf, 1.0), writes=[("onesf",)])
        for T in range(NB):
            b = 6 + T % 2
            s.op("pe", [mm(bank(b), wmisc[:, k, :], hT[:, k, TS(T)], k == 0, k == KC - 1) for k in range(KC)],
                 reads=hkeys(T) + [("wmisc", i) for i in range(4)], writes=[("ps", b)])
            s.op("act", lambda e: e.activation(out=eT, in_=bank(b)[0:8, :], func=AF.Exp, bias=nbneg, scale=-1.0),
                 reads=[("nbneg",)], writes=[("ps", b), ("eT",)])
            s.op("act", lambda e: e.activation(out=nlf, in_=eT, func=AF.Ln, bias=1.0), reads=[("eT",)], writes=[("nlf",)])
            init = 0.0 if T == 0 else G[:, T * 512 - 1:T * 512]
            s.op("dve", lambda e: e.tensor_tensor_scan(out=G[:, TS(T)], data0=onesf, data1=nlf, initial=init,
                                                       op0=ALU.mult, op1=ALU.add),
                 reads=[("nlf",), ("onesf",)] + ([("G", T - 1)] if T else []), writes=[("G", T)])
            s.op("dve", lambda e: e.tensor_tensor(out=t1, in0=bank(b)[64:96, :], in1=CC[:, TS(T)], op=ALU.mult),
                 reads=[("CC",)], writes=[("ps", b), ("t1",)])
            s.op("dve", lambda e: e.tensor_tensor(out=t2, in0=bank(b)[96:128, :], in1=SS[:, TS(T)], op=ALU.mult),
                 reads=[("SS",)], writes=[("ps", b), ("t2",)])
            s.op("dve", lambda e: e.tensor_tensor(out=kpe[:, TS(T)], in0=t1, in1=t2, op=ALU.add),
                 reads=[("t1",), ("t2",)], writes=[("kpe", T)])
        s.op("dve", lambda e: e.tensor_scalar(out=negGb, in0=G, scalar1=-1.0, scalar2=None, op0=ALU.mult),
             reads=[("G", T) for T in range(NB)], writes=[("negGb",)])
        GB = 5
        s.op("pe", [(lambda e, it=it: e.transpose(out=bank(GB)[:, it * 8:(it + 1) * 8], in_=G[:, it * 128:(it + 1) * 128],
                                                  identity=self.identf[0:8, 0:8])) for it in range(NT)],
             reads=[("G", T) for T in range(NB)] + [("identf",)], writes=[("ps", GB)])
        s.op("dve", lambda e: e.tensor_copy(out=Gtok, in_=bank(GB)[:, 0:128]), writes=[("ps", GB), ("Gtok",)])
        self.dump("G", G, 8, S, F32, [("G", T) for T in range(NB)])
        self.dump("Gtok", Gtok, 128, 128, F32, [("Gtok",)])
        self.dump("kpe", kpe, 32, S, BF16, [("kpe", T) for T in range(NB)])
        A.release("wmisc"); A.release("eT"); A.release("nlf"); A.release("onesf")
        if upto <= 2:
            return

        wf = [A.alloc("wf0", (KC, 256), BF16), A.alloc("wf1", (KC, 256), BF16)]
        for n in ("kA", "kB"):
            s.op("dve", lambda e: e.memset(qk[n][64:65, :], 1.0), writes=[(n, "g")])
        for c in range(4):
            w, wn = wf[c % 2], "wf%d" % (c % 2)
            s.dma("pool", w[:, :, 0:128], w_in_v[:, :, C_Q + c * 128:C_Q + (c + 1) * 128], writes=[(wn, "q")])
            s.dma("pool", w[:, :, 128:256], w_in_v[:, :, C_K + c * 128:C_K + (c + 1) * 128], writes=[(wn, "k")])
            s.dma("pool", qk["qA"][64:65, :], negGb[2 * c:2 * c + 1, :], reads=[("negGb",)], writes=[("qA", "g")])
            s.dma("pool", qk["qB"][64:65, :], negGb[2 * c + 1:2 * c + 2, :], reads=[("negGb",)], writes=[("qB", "g")])
            for T in range(NB):
                s.op("pe", [mm(bank(6), w[:, k, 0:128], hT[:, k, TS(T)], k == 0, k == KC - 1) for k in range(KC)],
                     reads=hkeys(T) + [(wn, "q")], writes=[("ps", 6)])
                s.op("dve", lambda e: e.tensor_scalar(out=qk["qA"][0:64, TS(T)], in0=bank(6)[0:64, :], scalar1=0.125, scalar2=None, op0=ALU.mult),
                     writes=[("ps", 6), ("qA", T)])
                s.op("dve", lambda e: e.tensor_scalar(out=qk["qB"][0:64, TS(T)], in0=bank(6)[64:128, :], scalar1=0.125, scalar2=None, op0=ALU.mult),
                     writes=[("ps", 6), ("qB", T)])
                s.op("pe", [mm(bank(7), w[:, k, 128:256], hT[:, k, TS(T)], k == 0, k == KC - 1) for k in range(KC)],
                     reads=hkeys(T) + [(wn, "k")], writes=[("ps", 7)])
                s.op("dve", lambda e: e.tensor_copy(out=qk["kA"][0:64, TS(T)], in_=bank(7)[0:64, :]),
                     writes=[("ps", 7), ("kA", T)])
                s.op("dve", lambda e: e.tensor_copy(out=qk["kB"][0:64, TS(T)], in_=bank(7)[64:128, :]),
                     writes=[("ps", 7), ("kB", T)])
            if c == 0:
                self.dump("qA", qk["qA"][0:65, :], 65, S, BF16, [("qA", T) for T in range(NB)] + [("qA", "g")])
                self.dump("kA", qk["kA"][0:65, :], 65, S, BF16, [("kA", T) for T in range(NB)] + [("kA", "g")])
            self.attention(c, True)
        for c in range(4):
            self.dump("oaT%d" % c, oT[:, c, :], 128, S, BF16, [("oT", c, T, hh) for T in range(NB) for hh in range(2)])
        A.release("wf0"); A.release("wf1"); A.release("G"); A.release("negGb"); A.release("nbneg")
        if upto <= 3:
            return
        self._body3(upto)

    def attention(self, c, fox):
        s = self.s
        bank = self.bank
        qk, vaug, pT, rec = self.qk, self.vaug, self.pT, self.rec
        oT, on = (self.oT, "oT") if fox else (self.oT2, "oT2")
        KD = 65 if fox else 96
        scale = 1.0 if fox else 1.0 / math.sqrt(96.0)
        SB = (0, 1, 2, 3, 6, 7)
        LA = 2
        NPT = len(pT)
        for T in range(NB):
            nj = 4 * T + 4
            info = {}

            def emit_S(hh, j):
                h = 2 * c + hh
                qn, kn = ("qA", "kA") if hh == 0 else ("qB", "kB")
                q, k = qk[qn], qk[kn]
                off = 0 if j < 4 * T else (j - 4 * T) * 128
                w = 512 - off
                n = self.att_cnt
                self.att_cnt += 1
                b = SB[n % len(SB)]
                pt, ptn = pT[n % NPT], "pT%d" % (n % NPT)
                diag = j >= 4 * T
                fns = [lambda e: e.matmul(bank(b)[:, 0:w], k[0:KD, j * 128:(j + 1) * 128],
                                          q[0:KD, T * 512 + off:(T + 1) * 512], start=True, stop=not diag)]
                if diag:
                    fns.append(lambda e: e.matmul(bank(b)[:, 0:128], self.ident, self.mask, start=False, stop=True))
                s.op("pe", fns, reads=[(qn, T), (qn, "g"), (kn, j // 4), (kn, "g"), ("ident",), ("mask",)],
                     writes=[("ps", b)])
                if fox:
                    s.op("act", lambda e: e.activation(out=pt[:, 0:w], in_=bank(b)[:, 0:w], func=AF.Exp,
                                                       bias=self.Gtok[:, j * 8 + h:j * 8 + h + 1], scale=1.0),
                         reads=[("Gtok",)], writes=[("ps", b), (ptn,)])
                else:
                    s.op("act", lambda e: e.activation(out=pt[:, 0:w], in_=bank(b)[:, 0:w], func=AF.Exp, scale=scale),
                         writes=[("ps", b), (ptn,)])
                info[(hh, j)] = (off, w, pt, ptn)

            def emit_PV(hh, j):
                off, w, pt, ptn = info[(hh, j)]
                acc = 4 + hh
                vsl = slice(0, 128) if hh == 0 else slice(64, 192)
                s.op("pe", lambda e: e.matmul(bank(acc)[:, off:512], vaug[:, j, c, vsl], pt[:, 0:w],
                                              start=(j == 0), stop=(j == nj - 1)),
                     reads=[(ptn,), ("vaug", j, 0), ("vaug", j, 1), ("vaug", j, 2)], writes=[("ps", acc)])

            for step in range(nj + LA):
                for hh in (0, 1):
                    if step < nj:
                        emit_S(hh, step)
                    if step - LA >= 0:
                        emit_PV(hh, step - LA)
            for hh in (0, 1):
                acc = 4 + hh
                r = rec[hh]
                if hh == 0:
                    osl, dsl = slice(0, 64), slice(64, 128)
                else:
                    osl, dsl = slice(64, 128), slice(0, 64)
                s.op("act", lambda e: e.activation(out=r[osl, :], in_=bank(acc)[dsl, :], func=AF.Ln),
                     writes=[("ps", acc), ("rec%d" % hh,)])
                s.op("act", lambda e: e.activation(out=r[osl, :], in_=r[osl, :], func=AF.Exp, scale=-1.0),
                     reads=[("rec%d" % hh,)], writes=[("rec%d" % hh,)])
                s.op("dve", lambda e: e.tensor_tensor(out=oT[osl, c, T * 512:(T + 1) * 512], in0=bank(acc)[osl, :],
                                                      in1=r[osl, :], op=ALU.mult),
                     reads=[("rec%d" % hh,)], writes=[("ps", acc), (on, c, T, hh)])

    def gate_merge(self, col0, w_proj, first, oT, on):
        s, A = self.s, self.A
        bank = self.bank
        hT, merged = self.hT, self.merged
        mm = lambda out, l, r, st, sp: (lambda e: e.matmul(out, l, r, start=st, stop=sp))
        w_in_v = self.w_in.rearrange("(k p) n -> p k n", p=128)
        wg = A.alloc("wg", (KC, D), BF16)
        wp = A.alloc("wp", (4, D), BF16)
        sg = [A.alloc("sg%d" % i, (512,), BF16) for i in range(2)]
        tmp = [A.alloc("gtmp%d" % i, (512,), BF16) for i in range(2)]
        s.dma("pool", wg, w_in_v[:, :, col0:col0 + D], writes=[("wg",)])
        s.dma("pool", wp, w_proj.rearrange("(k p) n -> p k n", p=128), writes=[("wp",)])
        n = 0
        for T in range(NB):
            Tsl = slice(T * 512, (T + 1) * 512)
            for m in range(KC):
                i = n % 2
                n += 1
                bg, bp = 0 + i, 2 + i
                s.op("pe", [mm(bank(bg), wg[:, k, m * 128:(m + 1) * 128], hT[:, k, Tsl], k == 0, k == KC - 1) for k in range(KC)],
                     reads=[("hT", k, T) for k in range(KC)] + [("wg",)], writes=[("ps", bg)])
                s.op("act", lambda e: e.activation(out=sg[i], in_=bank(bg), func=AF.Sigmoid),
                     writes=[("ps", bg), ("sg%d" % i,)])
                s.op("pe", [mm(bank(bp), wp[:, cc, m * 128:(m + 1) * 128], oT[:, cc, Tsl], cc == 0, cc == 3) for cc in range(4)],
                     reads=[(on, cc, T, hh) for cc in range(4) for hh in range(2)] + [("wp",)], writes=[("ps", bp)])
                if first:
                    s.op("dve", lambda e: e.tensor_tensor(out=merged[:, m, Tsl], in0=bank(bp), in1=sg[i], op=ALU.mult),
                         reads=[("sg%d" % i,)], writes=[("ps", bp), ("merged", m, T)])
                else:
                    s.op("dve", lambda e: e.tensor_tensor(out=tmp[i], in0=bank(bp), in1=sg[i], op=ALU.mult),
                         reads=[("sg%d" % i,)], writes=[("ps", bp), ("gtmp%d" % i,)])
                    s.op("dve", lambda e: e.tensor_tensor(out=merged[:, m, Tsl], in0=merged[:, m, Tsl], in1=tmp[i], op=ALU.add),
                         reads=[("gtmp%d" % i,), ("merged", m, T)], writes=[("merged", m, T)])
        for nme in ("wg", "wp", "sg0", "sg1", "gtmp0", "gtmp1"):
            A.release(nme)

    def _body3(self, upto):
        nc, s, A = self.nc, self.s, self.A
        bank = self.bank
        hT, CC, SS, qk, vaug = self.hT, self.CC, self.SS, self.qk, self.vaug
        t1, t2 = self.t1, self.t2
        w_in_v = self.w_in.rearrange("(k p) n -> p k n", p=128)
        hkeys = lambda T: [("hT", k, T) for k in range(KC)]
        TS = lambda T: slice(T * 512, (T + 1) * 512)
        mm = lambda out, l, r, st, sp: (lambda e: e.matmul(out, l, r, start=st, stop=sp))

        oT2 = A.alloc("oT2", (4, S), BF16)
        self.oT2 = oT2

        cqn = A.alloc("cqn", (6, S), BF16)
        ckvn = A.alloc("ckvn", (2, S), BF16)
        sqb = [A.alloc("sqb%d" % i, (512,), BF16) for i in range(2)]
        rstdb = A.alloc("rstdb", (512,), F32)
        wcq = A.alloc("wcq", (KC, 768), BF16)
        wckv = A.alloc("wckv", (KC, 256), BF16)
        s.dma("pool", wcq, w_in_v[:, :, C_CQ:C_CQ + 768], writes=[("wcq",)])
        s.dma("pool", wckv, w_in_v[:, :, C_CKV:C_CKV + 256], writes=[("wckv",)])
        for (wt, wn, nm, dst, dn, gT, gname, nfeat) in ((wcq, "wcq", 6, cqn, "cqn", self.gqT_t, "gqT", 768.0),
                                                        (wckv, "wckv", 2, ckvn, "ckvn", self.gkvT_t, "gkvT", 256.0)):
            for T in range(NB):
                for m in range(nm):
                    b = 6 + m % 2
                    i = m % 2
                    s.op("pe", [mm(bank(b), wt[:, k, m * 128:(m + 1) * 128], hT[:, k, TS(T)], k == 0, k == KC - 1) for k in range(KC)],
                         reads=hkeys(T) + [(wn,)], writes=[("ps", b)])
                    s.op("act", lambda e: e.activation(out=sqb[i], in_=bank(b), func=AF.Square),
                         writes=[("ps", b), ("sqb%d" % i,)])
                    s.op("dve", lambda e: e.tensor_scalar(out=dst[:, m, TS(T)], in0=bank(b), scalar1=gT[:, m:m + 1],
                                                          scalar2=None, op0=ALU.mult),
                         reads=[(gname,)], writes=[("ps", b), (dn, m, T)])
                    s.op("pe", mm(bank(5), self.ones_b, sqb[i], m == 0, m == nm - 1),
                         reads=[("sqb%d" % i,), ("ones_b",)], writes=[("ps", 5)])
                s.op("act", lambda e: e.activation(out=rstdb, in_=bank(5), func=AF.Ln, bias=self.eps_c, scale=1.0 / nfeat),
                     reads=[("eps_c",)], writes=[("ps", 5), ("rstdb",)])
                s.op("act", lambda e: e.activation(out=rstdb, in_=rstdb, func=AF.Exp, scale=-0.5),
                     reads=[("rstdb",)], writes=[("rstdb",)])
                for m in range(nm):
                    s.op("dve", lambda e: e.tensor_tensor(out=dst[:, m, TS(T)], in0=dst[:, m, TS(T)], in1=rstdb, op=ALU.mult),
                         reads=[(dn, m, T), ("rstdb",)], writes=[(dn, m, T)])
        self.dump("cqn0", cqn[:, 0, :], 128, S, BF16, [("cqn", 0, T) for T in range(NB)])
        self.dump("ckvn1", ckvn[:, 1, :], 128, S, BF16, [("ckvn", 1, T) for T in range(NB)])
        A.release("wcq"); A.release("wckv"); A.release("sqb0"); A.release("sqb1"); A.release("rstdb")

        wuq = A.alloc("wuq", (6, 8, 128), BF16)
        w_uq_v = self.w_uq.rearrange("(k p) (h e) -> p k h e", p=128, e=96)
        for kc in range(6):
            s.dma("pool", wuq[:, kc, :, 0:96], w_uq_v[:, kc, :, :], writes=[("wuq", 0, kc)])
            s.dma("pool", wuq[:, kc, :, 96:112], w_uq_v[:, kc, :, 80:96], writes=[("wuq", 1, kc)])
            s.dma("pool", wuq[:, kc, :, 112:128], w_uq_v[:, kc, :, 64:80], writes=[("wuq", 2, kc)])
        wkn = A.alloc("wkn", (2, 512), BF16)
        wvm = A.alloc("wvm", (2, 512), BF16)
        w_ukv_v = self.w_ukv.rearrange("(k p) (h e) -> p k h e", p=128, e=128)
        for kc in range(2):
            s.dma("pool", wkn[:, kc, :].rearrange("p (h e) -> p h e", e=64), w_ukv_v[:, kc, :, 0:64], writes=[("wkn", kc)])
            s.dma("pool", wvm[:, kc, :].rearrange("p (h e) -> p h e", e=64), w_ukv_v[:, kc, :, 64:128], writes=[("wvm", kc)])

        for it in range(NT):
            b = 6 + it % 2
            T = it // 4
            s.op("pe", [mm(bank(b), ckvn[:, k, it * 128:(it + 1) * 128], wvm[:, k, :], k == 0, k == 1) for k in range(2)],
                 reads=[("ckvn", k, T) for k in range(2)] + [("wvm", 0), ("wvm", 1)], writes=[("ps", b)])
            pv = bank(b).rearrange("p (c e d) -> p c e d", c=4, e=2, d=64)
            s.op("dve", lambda e: e.tensor_copy(out=vaug[:, it, :, 0:64], in_=pv[:, :, 0, :]),
                 writes=[("ps", b), ("vaug", it, 0)])
            s.op("dve", lambda e: e.tensor_copy(out=vaug[:, it, :, 128:192], in_=pv[:, :, 1, :]),
                 writes=[("ps", b), ("vaug", it, 1)])

        wuq_keys = [("wuq", i, kc) for i in range(3) for kc in range(6)]
        for c in range(4):
            for kn in ("kA", "kB"):
                s.op("dve", lambda e: e.tensor_copy(out=qk[kn][64:96, :], in_=self.kpe_t),
                     reads=[("kpe", T) for T in range(NB)], writes=[(kn, "g")])
            for T in range(NB):
                s.op("pe", [mm(bank(7), wkn[:, k, c * 128:(c + 1) * 128], ckvn[:, k, TS(T)], k == 0, k == 1) for k in range(2)],
                     reads=[("ckvn", k, T) for k in range(2)] + [("wkn", 0), ("wkn", 1)], writes=[("ps", 7)])
                s.op("dve", lambda e: e.tensor_copy(out=qk["kA"][0:64, TS(T)], in_=bank(7)[0:64, :]),
                     writes=[("ps", 7), ("kA", T)])
                s.op("dve", lambda e: e.tensor_copy(out=qk["kB"][0:64, TS(T)], in_=bank(7)[64:128, :]),
                     writes=[("ps", 7), ("kB", T)])
                for hh in (0, 1):
                    h = 2 * c + hh
                    qn = "qA" if hh == 0 else "qB"
                    q = qk[qn]
                    s.op("pe", [mm(bank(6), wuq[:, k, h, :], cqn[:, k, TS(T)], k == 0, k == 5) for k in range(6)],
                         reads=[("cqn", k, T) for k in range(6)] + wuq_keys, writes=[("ps", 6)])
                    s.op("dve", lambda e: e.tensor_copy(out=q[0:64, TS(T)], in_=bank(6)[0:64, :]),
                         writes=[("ps", 6), (qn, T)])
                    s.op("dve", lambda e: e.tensor_tensor(out=t1, in0=bank(6)[64:96, :], in1=CC[:, TS(T)], op=ALU.mult),
                         reads=[("CC",)], writes=[("ps", 6), ("t1",)])
                    s.op("dve", lambda e: e.tensor_tensor(out=t2, in0=bank(6)[96:128, :], in1=SS[:, TS(T)], op=ALU.mult),
                         reads=[("SS",)], writes=[("ps", 6), ("t2",)])
                    s.op("dve", lambda e: e.tensor_tensor(out=q[64:96, TS(T)], in0=t1, in1=t2, op=ALU.add),
                         reads=[("t1",), ("t2",)], writes=[(qn, T)])
            if c == 0:
                self.dump("qm0", qk["qA"][0:96, :], 96, S, BF16, [("qA", T) for T in range(NB)])
                self.dump("km0", qk["kA"][0:96, :], 96, S, BF16, [("kA", T) for T in range(NB)] + [("kA", "g")])
            self.attention(c, False)
        for c in range(4):
            self.dump("obT%d" % c, oT2[:, c, :], 128, S, BF16, [("oT2", c, T, hh) for T in range(NB) for hh in range(2)])
        for nme in ("wuq", "wkn", "wvm", "cqn", "ckvn", "qA", "qB", "kA", "kB", "vaug", "pT0", "pT1", "pT2", "pT3", "pT4", "pT5", "pT6", "pT7",
                    "rec0", "rec1", "kpe", "t1", "t2", "CC", "SS", "Gtok"):
            A.release(nme)
        if upto <= 4:
            return

        self.build_gvec()
        merged = A.alloc("merged", (KC, S), BF16)
        self.merged = merged
        self.gate_merge(C_GF, self.w_pf, True, self.oT, "oT")
        self.gate_merge(C_GM, self.w_pm, False, self.oT2, "oT2")
        for m in (0, 7):
            self.dump("mg%d" % m, merged[:, m, :], 128, S, BF16, [("merged", m, T) for T in range(NB)])
        A.release("hT"); A.release("oT"); A.release("oT2")
        if upto <= 5:
            return
        self._body4(upto)

    def build_gvec(self):
        s, A = self.s, self.A
        bank = self.bank
        gvec = A.alloc("gvec", (2, D), F32)
        silurep = A.alloc("silurep", (KC, 128), BF16)
        bada_rep = A.alloc("bada_rep", (2 * D,), F32)
        gpost_rep = A.alloc("gpost_rep", (2 * D,), F32)
        s.dma("pool", bada_rep, self.bada_rep, writes=[("bada_rep",)])
        s.dma("pool", gpost_rep, self.gpost_rep, writes=[("gpost_rep",)])
        for k in range(KC):
            s.op("dve", lambda e: e.tensor_scalar(out=silurep[:, k, :], in0=self.ones_b, scalar1=self.siluf[:, k:k + 1],
                                                  scalar2=None, op0=ALU.mult),
                 reads=[("ones_b",), ("siluf",)], writes=[("silurep", k)])
        wada = A.alloc("wadag", (KC, D), BF16)
        w_ada_v = self.w_ada.rearrange("(k p) n -> p k n", p=128)
        modT, affn = self.modT, self.affn
        MODB = 7
        for v in (3, 4):
            s.dma("pool", wada, w_ada_v[:, :, v * D:(v + 1) * D], writes=[("wadag",)])
            for c in range(KC):
                col = v * 8 + c
                s.op("pe", [(lambda e, k=k: e.matmul(bank(MODB)[:, col:col + 1], wada[:, k, c * 128:(c + 1) * 128],
                                                    self.silub[:, k:k + 1], start=(k == 0), stop=(k == KC - 1)))
                            for k in range(KC)],
                     reads=[("wadag",), ("silub",)], writes=[("ps", MODB)])
        s.op("dve", lambda e: e.tensor_tensor(out=modT[:, 24:40], in0=bank(MODB)[:, 24:40], in1=self.badaT_t[:, 24:40], op=ALU.add),
             reads=[("badaT",)], writes=[("ps", MODB), ("modT2",)])
        s.op("dve", lambda e: e.scalar_tensor_tensor(out=affn, in0=modT[:, 32:40], scalar=1.0, in1=self.gpreT_t[:, 8:16],
                                                     op0=ALU.add, op1=ALU.mult),
             reads=[("modT2",), ("gpreT",)], writes=[("affn",)])
        for g, v in enumerate((2, 5)):
            s.dma("pool", wada, w_ada_v[:, :, v * D:(v + 1) * D], writes=[("wadag",)])
            for half in range(2):
                b = 5 + half
                s.op("pe", [(lambda e, k=k: e.matmul(bank(b), silurep[:, k, :], wada[:, k, half * 512:(half + 1) * 512],
                                                    start=(k == 0), stop=(k == KC - 1))) for k in range(KC)],
                     reads=[("wadag",)] + [("silurep", k) for k in range(KC)], writes=[("ps", b)])
                sl = slice(g * D + half * 512, g * D + (half + 1) * 512)
                hs = slice(half * 512, (half + 1) * 512)
                s.op("dve", lambda e: e.tensor_tensor(out=gvec[:, g, hs], in0=bank(b), in1=bada_rep[:, sl], op=ALU.add),
                     reads=[("bada_rep",)], writes=[("ps", b), ("gvec", g, half)])
                s.op("dve", lambda e: e.tensor_tensor(out=gvec[:, g, hs], in0=gvec[:, g, hs], in1=gpost_rep[:, sl], op=ALU.mult),
                     reads=[("gpost_rep",), ("gvec", g, half)], writes=[("gvec", g, half)])
        self.gvec = gvec
        self.dump("gvec", gvec.rearrange("p a b -> p (a b)"), 128, 2 * D, F32, [("gvec", g, h) for g in range(2) for h in range(2)])
        for n in ("silurep", "bada_rep", "gpost_rep", "wadag"):
            A.release(n)

    def _body4(self, upto):
        nc, s, A = self.nc, self.s, self.A
        bank = self.bank
        merged, gvec = self.merged, self.gvec
        mm = lambda out, l, r, st, sp: (lambda e: e.matmul(out, l, r, start=st, stop=sp))
        wout = A.alloc("wout", (KC, D), BF16)
        s.dma("pool", wout, self.w_out.rearrange("(k p) n -> p k n", p=128), writes=[("wout",)])
        w2 = A.alloc("w2", (NFC, D), BF16)
        w2v = self.w_f2.rearrange("(j p) n -> p j n", p=128)
        w1v = self.w_f1.rearrange("(k p) n -> p k n", p=128)
        x2 = A.alloc("x2", (4, D), F32)
        h2T = A.alloc("h2T", (KC, 512), BF16)
        actT = A.alloc("actT", (NFC, 512), BF16)
        w1b = [A.alloc("w1b%d" % i, (KC, 512), BF16) for i in range(3)]
        sgl = [A.alloc("sgl%d" % i, (512,), BF16) for i in range(2)]
        xin = [A.alloc("xin%d" % i, (D,), F32) for i in range(2)]
        ot = [A.alloc("ot%d" % i, (D,), F32) for i in range(2)]
        tmpf = [A.alloc("tmpf%d" % i, (512,), F32) for i in range(2)]
        junkp = A.alloc("junkp", (512,), BF16)
        ssq = A.alloc("ssq", (4 * NT,), F32)
        ssy = A.alloc("ssy", (2 * NT,), F32)
        rsy = A.alloc("rsy", (2 * NT,), F32)
        ss, rstd = self.ss, self.rstd
        w2_loaded = [False]

        def postnorm(bks, col, g, resid, rkeys, out_ap, okeys):
            for half, b in enumerate(bks):
                s.op("act", lambda e: e.activation(out=junkp, in_=bank(b), func=AF.Square,
                                                   accum_out=ssq[:, 2 * col + half:2 * col + half + 1]),
                     writes=[("ps", b), ("junkp",), ("ssq", col, half)])
            s.op("dve", lambda e: e.tensor_tensor(out=ssy[:, col:col + 1], in0=ssq[:, 2 * col:2 * col + 1],
                                                  in1=ssq[:, 2 * col + 1:2 * col + 2], op=ALU.add),
                 reads=[("ssq", col, 0), ("ssq", col, 1)], writes=[("ssy", col)])
            s.op("act", lambda e: e.activation(out=rsy[:, col:col + 1], in_=ssy[:, col:col + 1], func=AF.Ln,
                                               bias=self.eps_c, scale=1.0 / D),
                 reads=[("ssy", col), ("eps_c",)], writes=[("rsy", col)])
            s.op("act", lambda e: e.activation(out=rsy[:, col:col + 1], in_=rsy[:, col:col + 1], func=AF.Exp, scale=-0.5),
                 reads=[("rsy", col)], writes=[("rsy", col)])
            for half, b in enumerate(bks):
                hs = slice(half * 512, (half + 1) * 512)
                s.op("dve", lambda e: e.scalar_tensor_tensor(out=tmpf[half], in0=bank(b), scalar=rsy[:, col:col + 1],
                                                             in1=gvec[:, g, hs], op0=ALU.mult, op1=ALU.mult),
                     reads=[("rsy", col), ("gvec", g, half)], writes=[("ps", b), ("tmpf%d" % half,)])
                s.op("dve", lambda e: e.tensor_tensor(out=out_ap[:, hs], in0=tmpf[half], in1=resid[:, hs], op=ALU.add),
                     reads=[("tmpf%d" % half,)] + rkeys, writes=okeys)

        nw1 = 0
        import os
        for T in [int(t_) for t_ in os.environ.get("TLIST", "0,1,2,3").split(",")]:
            for i in range(4):
                it = 4 * T + i
                xi, xn = xin[it % 2], "xin%d" % (it % 2)
                s.dma("sp", xi, self.x[it * 128:(it + 1) * 128, :], writes=[(xn,)])
                bks = (0, 1) if it % 2 == 0 else (2, 3)
                for half, b in enumerate(bks):
                    s.op("pe", [mm(bank(b), merged[:, k, it * 128:(it + 1) * 128], wout[:, k, half * 512:(half + 1) * 512],
                                   k == 0, k == KC - 1) for k in range(KC)],
                         reads=[("merged", k, T) for k in range(KC)] + [("wout",)], writes=[("ps", b)])
                postnorm(bks, it, 0, xi, [(xn,)], x2[:, i, :], [("x2", i)])
            if T == 0:
                self.dump("x2", x2[:, 0, :], 128, D, F32, [("x2", 0)])
            if upto <= 6:
                return
            if not w2_loaded[0]:
                for gi, g0 in enumerate(range(0, NFC, 6)):
                    g1 = min(NFC, g0 + 6)
                    s.dma("pool", w2[:, g0:g1, :], w2v[:, g0:g1, :], writes=[("w2", gi)])
                w2_loaded[0] = True
            self.prenorm(None, self.affn, self.modT[:, 24:32], h2T, "h2T", ss, rstd, "affn", [T], T,
                         src_sbuf=lambda it: (x2[:, it % 4, :], ("x2", it % 4)))
            for j in range(NFC):
                if j % 2 == 0:
                    wb, wbn = w1b[nw1 % 3], "w1b%d" % (nw1 % 3)
                    nw1 += 1
                    s.dma("pool", wb[:, :, 0:256], w1v[:, :, j * 128:(j + 2) * 128], writes=[(wbn, "g")])
                    s.dma("pool", wb[:, :, 256:512], w1v[:, :, DFF + j * 128:DFF + (j + 2) * 128], writes=[(wbn, "u")])
                jo = (j % 2) * 128
                gb, ub = 4 + 2 * (j % 2), 5 + 2 * (j % 2)
                hk = [("h2T", k, 0) for k in range(KC)]
                s.op("pe", [mm(bank(gb), wb[:, k, jo:jo + 128], h2T[:, k, :], k == 0, k == KC - 1) for k in range(KC)],
                     reads=hk + [(wbn, "g")], writes=[("ps", gb)])
                s.op("pe", [mm(bank(ub), wb[:, k, 256 + jo:256 + jo + 128], h2T[:, k, :], k == 0, k == KC - 1) for k in range(KC)],
                     reads=hk + [(wbn, "u")], writes=[("ps", ub)])
                s.op("act", lambda e: e.activation(out=sgl[j % 2], in_=bank(gb), func=AF.Silu),
                     writes=[("ps", gb), ("sgl%d" % (j % 2),)])
                s.op("dve", lambda e: e.tensor_tensor(out=actT[:, j, :], in0=bank(ub), in1=sgl[j % 2], op=ALU.mult),
                     reads=[("sgl%d" % (j % 2),)], writes=[("ps", ub), ("actT", j)])
            if upto <= 7:
                return
            for i in range(4):
                it = 4 * T + i
                bks = (0, 1) if it % 2 == 0 else (2, 3)
                for half, b in enumerate(bks):
                    s.op("pe", [mm(bank(b), actT[:, j, i * 128:(i + 1) * 128], w2[:, j, half * 512:(half + 1) * 512],
                                   j == 0, j == NFC - 1) for j in range(NFC)],
                         reads=[("actT", j) for j in range(NFC)] + [("w2", gi) for gi in range(4)], writes=[("ps", b)])
                o_t, on = ot[it % 2], "ot%d" % (it % 2)
                postnorm(bks, NT + it, 1, x2[:, i, :], [("x2", i)], o_t, [(on,)])
                s.dma("sp", self.out[it * 128:(it + 1) * 128, :], o_t, reads=[(on,)])
            if upto <= 8:
                return


def _consts():
    ident = np.eye(128, dtype=np.float32)
    sk = np.arange(128)[:, None]
    tq = np.arange(128)[None, :]
    mask = np.where(sk > tq, NEG, 0.0).astype(np.float32)
    inv_freq = 1.0 / (10000.0 ** (np.arange(0, 32, 2, dtype=np.float32) / 32.0))
    rope = np.zeros((128, 4), np.float32)
    for p in range(64, 96):
        i = (p - 64) % 16
        rope[p, 0] = inv_freq[i]
        rope[p, 1] = math.pi / 2
        rope[p, 2] = inv_freq[i]
        rope[p, 3] = (math.pi if p < 80 else 0.0)
    return dict(c_ident=ident.astype(ml_dtypes.bfloat16), c_identf=ident, c_mask=mask.astype(ml_dtypes.bfloat16),
                c_rope=rope)


def make_in_maps(x, c, positions, w_ada, b_ada, g_pre_mix, g_post_mix, g_pre_ffn, g_post_ffn,
                 w_in, b_forget, g_q_lora, w_uq, g_kv_lora, w_ukv, w_proj_fox, w_proj_mla,
                 w_out, w_ffn_in, w_ffn_out, cores=range(8)):
    f = lambda a: np.ascontiguousarray(np.asarray(a, dtype=np.float32))
    colT = lambda v, n: f(np.asarray(v, np.float32).reshape(n, 128).T)
    rep = lambda v: f(np.broadcast_to(np.asarray(v, np.float32)[None, :], (128, v.shape[0])))
    b_ada0 = np.asarray(b_ada[0], np.float32)
    shared = dict(
        w_ada=f(w_ada[0]), badaT=colT(b_ada0, 48),
        bada_rep=rep(np.concatenate([b_ada0[2 * D:3 * D], b_ada0[5 * D:6 * D]])),
        gpost_rep=rep(np.concatenate([np.asarray(g_post_mix[0], np.float32), np.asarray(g_post_ffn[0], np.float32)])),
        gpreT=f(np.concatenate([colT(g_pre_mix[0], 8), colT(g_pre_ffn[0], 8)], axis=1)),
        w_in=f(w_in[0]), nbf=f(np.asarray(b_forget[0], np.float32).reshape(8, 1)),
        gqT=colT(g_q_lora[0], 6), gkvT=colT(g_kv_lora[0], 2),
        w_uq=f(w_uq[0]), w_ukv=f(w_ukv[0]), w_pf=f(w_proj_fox[0]), w_pm=f(w_proj_mla[0]),
        w_out=f(w_out[0]), w_f1=f(w_ffn_in[0]), w_f2=f(w_ffn_out[0]),
    )
    shared.update(_consts())
    maps = []
    for b in cores:
        m = dict(shared)
        m["x"] = f(x[b])
        m["cT"] = colT(c[b], 8)
        m["pos"] = np.ascontiguousarray(np.broadcast_to(np.asarray(positions[b], np.int32)[None, :], (128, S)))
        maps.append(m)
    return maps


_NC_CACHE = {}


def kernel(**inputs):
    if "nc" not in _NC_CACHE:
        _NC_CACHE["nc"] = Builder().build()
    nc = _NC_CACHE["nc"]
    maps = make_in_maps(**inputs)
    res = run_bass_kernel_spmd(nc, maps, core_ids=list(range(8)))
    out = np.stack([np.asarray(r["out"], dtype=np.float32) for r in res.results], axis=0)
    return out
```

```python
import math
import numpy as np
import ml_dtypes
import concourse.bass as bass
import concourse.mybir as mybir
from concourse.bass_utils import run_bass_kernel_spmd

F32 = mybir.dt.float32
BF16 = mybir.dt.bfloat16
I32 = mybir.dt.int32
U8 = mybir.dt.uint8
AF = mybir.ActivationFunctionType
ALU = mybir.AluOpType

S = 2048
D = 1024
NT = 16
NB = 4
KC = 8
DFF = 2816
NFC = 22
D_IN = 4648
EPS = 1e-6
C_Q, C_K, C_V, C_F, C_CQ, C_CKV, C_KR, C_GF, C_GM = 0, 512, 1024, 1536, 1544, 2312, 2568, 2600, 3624
NEG = -30000.0


class Sched:
    def __init__(self, nc, sems, dma_sems):
        self.nc = nc
        self.eng = {"pe": nc.tensor, "act": nc.scalar, "dve": nc.vector, "pool": nc.gpsimd, "sp": nc.sync}
        self.sem = sems
        self.cnt = {e: 0 for e in sems}
        self.seen = {e: {} for e in self.eng}
        self.last_w = {}
        self.readers = {}
        self.dma_sems = dma_sems
        self.dma_n = {q: 0 for q in dma_sems}
        self.inherit = {}
        self.n_wait = 0

    def _collect(self, reads, writes):
        raw, other = [], []
        for k in reads:
            w = self.last_w.get(k)
            if w is not None:
                raw.append(w)
        for k in writes:
            w = self.last_w.get(k)
            if w is not None:
                other.append(w)
            other.extend(self.readers.get(k, ()))
            inh = self.inherit.get(k[0])
            if inh:
                other.extend(inh)
        return raw, other

    def _emit_waits(self, e, raw, other):
        for tok in raw:
            if tok[2] == e and e == "pe":
                continue
            self._wait(e, tok)
        for tok in other:
            if tok[2] == e:
                continue
            self._wait(e, tok)

    def _wait(self, e, tok):
        sem, val, _ = tok
        sid = id(sem)
        if self.seen[e].get(sid, 0) >= val:
            return
        self.eng[e].wait_ge(sem, val)
        self.seen[e][sid] = val
        self.n_wait += 1

    def _register(self, tok, reads, writes):
        for k in reads:
            self.readers.setdefault(k, []).append(tok)
        for k in writes:
            self.last_w[k] = tok
            self.readers[k] = []

    def op(self, e, fns, reads=(), writes=()):
        if callable(fns):
            fns = [fns]
        raw, other = self._collect(reads, writes)
        self._emit_waits(e, raw, other)
        eng = self.eng[e]
        ins = None
        for f in fns:
            ins = f(eng)
        self.cnt[e] += 1
        ins.then_inc(self.sem[e], 1)
        tok = (self.sem[e], self.cnt[e], e)
        self._register(tok, reads, writes)
        return tok

    def dma(self, q, out, in_, reads=(), writes=()):
        raw, other = self._collect(reads, writes)
        self._emit_waits(q, raw, other)
        pool = self.dma_sems[q]
        n = self.dma_n[q]
        self.dma_n[q] += 1
        sem = pool[n % len(pool)]
        rnd = n // len(pool)
        if rnd > 0:
            self._wait(q, (sem, 16 * rnd, "dma"))
        self.eng[q].dma_start(out=out, in_=in_).then_inc(sem, 16)
        tok = (sem, 16 * (rnd + 1), "dma")
        self._register(tok, reads, writes)
        return tok

    def release(self, name):
        toks = []
        for k in list(self.last_w.keys()):
            if k[0] == name:
                toks.append(self.last_w.pop(k))
        for k in list(self.readers.keys()):
            if k[0] == name:
                toks.extend(self.readers.pop(k))
        best = {}
        for t in toks:
            sid = id(t[0])
            if sid not in best or best[sid][1] < t[1]:
                best[sid] = t
        return list(best.values())


class Arena:
    def __init__(self, sched, arena_ap, nbytes):
        self.s = sched
        self.arena = arena_ap
        self.free = [(0, nbytes)]
        self.live = {}
        self.dead = []
        self.peak = 0

    def alloc(self, name, shape, dtype, parts=(0, 128)):
        esz = 4 if dtype in (F32, I32) else 2
        n = 1
        for d in shape:
            n *= d
        size = (n * esz + 63) // 64 * 64
        for i, (off, sz) in enumerate(self.free):
            if sz >= size:
                self.free[i] = (off + size, sz - size)
                break
        else:
            raise RuntimeError(f"arena OOM for {name} ({size} B); live={ {k: v[1] for k, v in self.live.items()} }")
        self.live[name] = (off, size)
        self.peak = max(self.peak, off + size)
        toks = []
        keep = []
        for (o, s_, tk) in self.dead:
            if o < off + size and off < o + s_:
                toks.extend(tk)
            keep.append((o, s_, tk))
        self.s.inherit[name] = toks
        ap = self.arena[parts[0]:parts[1], off // 2:(off + size) // 2]
        if esz == 4:
            ap = ap.bitcast(dtype)
        elif dtype != BF16:
            ap = ap.bitcast(dtype)
        ap = ap[:, 0:n]
        if len(shape) == 2:
            ap = ap.rearrange("p (a b) -> p a b", a=shape[0], b=shape[1])
        elif len(shape) == 3:
            ap = ap.rearrange("p (a b c) -> p a b c", a=shape[0], b=shape[1], c=shape[2])
        return ap

    def release(self, name):
        off, size = self.live.pop(name)
        toks = self.s.release(name)
        self.dead.append((off, size, toks))
        self.free.append((off, size))
        self.free.sort()
        merged = []
        for o, s_ in self.free:
            if merged and merged[-1][0] + merged[-1][1] == o:
                merged[-1] = (merged[-1][0], merged[-1][1] + s_)
            else:
                merged.append((o, s_))
        self.free = merged


class Builder:
    def __init__(self, debug=None):
        self.debug = debug or []
        nc = bass.Bass("TRN2", target_bir_lowering=False)
        self.nc = nc
        self.dbg_out = {}
        d = lambda n, sh, dt, kind="ExternalInput": nc.dram_tensor(n, list(sh), dt, kind=kind).ap()
        self.x = d("x", [S, D], F32)
        self.cT = d("cT", [128, KC], F32)
        self.pos = d("pos", [128, S], I32)
        self.w_ada = d("w_ada", [D, 6 * D], F32)
        self.badaT = d("badaT", [128, 48], F32)
        self.bada_rep = d("bada_rep", [128, 2 * D], F32)
        self.gpost_rep = d("gpost_rep", [128, 2 * D], F32)
        self.gpreT = d("gpreT", [128, 16], F32)
        self.w_in = d("w_in", [D, D_IN], F32)
        self.nbf = d("nbf", [8, 1], F32)
        self.gqT = d("gqT", [128, 6], F32)
        self.gkvT = d("gkvT", [128, 2], F32)
        self.w_uq = d("w_uq", [768, 768], F32)
        self.w_ukv = d("w_ukv", [256, 1024], F32)
        self.w_pf = d("w_pf", [512, D], F32)
        self.w_pm = d("w_pm", [512, D], F32)
        self.w_out = d("w_out", [D, D], F32)
        self.w_f1 = d("w_f1", [D, 2 * DFF], F32)
        self.w_f2 = d("w_f2", [DFF, D], F32)
        self.c_ident = d("c_ident", [128, 128], BF16)
        self.c_identf = d("c_identf", [128, 128], F32)
        self.c_mask = d("c_mask", [128, 128], BF16)
        self.c_rope = d("c_rope", [128, 4], F32)
        self.out = d("out", [S, D], F32, kind="ExternalOutput")

    def dbg(self, name, ap, shape, dtype):
        if name in self.debug:
            o = self.nc.dram_tensor("dbg_" + name, list(shape), dtype, kind="ExternalOutput").ap()
            self.dbg_out[name] = o
            return o
        return None

    def build(self, upto=99):
        import contextlib
        nc = self.nc
        with contextlib.ExitStack() as es:
            es.enter_context(nc.allow_low_precision("bf16 matmul operands by design; fp32 accumulation"))
            es.enter_context(nc.allow_non_contiguous_dma("small strided weight / constant loads"))
            ARENA = 207 * 1024
            arena_t = es.enter_context(nc.sbuf_tensor("arena", [128, ARENA // 2], BF16))
            self.banks = [es.enter_context(nc.psum_tensor(f"ps{b}", [128, 512], F32)) for b in range(8)]
            sems = {e: es.enter_context(nc.semaphore("s_" + e)) for e in ("pe", "act", "dve", "pool")}
            dma_sems = {q: [es.enter_context(nc.semaphore(f"d{q}{i}")) for i in range(12)] for q in ("sp", "pool")}
            self.s = Sched(nc, sems, dma_sems)
            self.A = Arena(self.s, arena_t, ARENA)
            self._body(upto)
            self._finish()
        return nc

    def bank(self, b):
        return self.banks[b][:, :]

    def bankbf(self, b):
        return self.banks[b][:, :].bitcast(BF16)

    def _finish(self):
        s = self.s
        for q in ("sp", "pool"):
            pool = s.dma_sems[q]
            n = s.dma_n[q]
            for i, sem in enumerate(pool):
                cnt = (n - i + len(pool) - 1) // len(pool) if n > i else 0
                if cnt > 0:
                    s._wait("sp", (sem, 16 * cnt, "dma"))

    def dump(self, name, ap, parts, ncols, dtype, reads):
        o = self.dbg(name, ap, [parts, ncols], dtype)
        if o is not None:
            self.s.dma("sp", o, ap, reads=reads)

    def _body(self, upto):
        nc, s, A = self.nc, self.s, self.A
        bank, bankbf = self.bank, self.bankbf

        def load_const(name, src, shape, dtype, parts=(0, 128), q="pool"):
            t = A.alloc(name, shape, dtype, parts)
            s.dma(q, t, src, writes=[(name,)])
            return t

        ident = load_const("ident", self.c_ident, (128,), BF16)
        identf = load_const("identf", self.c_identf, (128,), F32)
        mask = load_const("mask", self.c_mask, (128,), BF16)
        ropec = load_const("ropec", self.c_rope, (4,), F32)
        cT = load_const("cT", self.cT, (KC,), F32)
        badaT = load_const("badaT", self.badaT, (48,), F32)
        gpreT = load_const("gpreT", self.gpreT, (16,), F32)
        gqT = load_const("gqT", self.gqT, (6,), F32)
        gkvT = load_const("gkvT", self.gkvT, (2,), F32)
        nbf = load_const("nbf", self.nbf, (1,), F32, parts=(0, 8))
        self.ident, self.mask, self.identf, self.nbf_t = ident, mask, identf, nbf
        self.gqT_t, self.gkvT_t = gqT, gkvT
        eps_c = A.alloc("eps_c", (1,), F32)
        s.op("dve", lambda e: e.memset(eps_c, EPS), writes=[("eps_c",)])
        self.eps_c = eps_c

        siluf = A.alloc("siluf", (KC,), F32)
        silub = A.alloc("silub", (KC,), BF16)
        ones_b = A.alloc("ones_b", (128,), BF16)
        modT = A.alloc("modT", (48,), F32)
        amix = A.alloc("amix", (KC,), F32)
        affn = A.alloc("affn", (KC,), F32)
        s.op("act", lambda e: e.activation(out=siluf, in_=cT, func=AF.Silu), reads=[("cT",)], writes=[("siluf",)])
        s.op("dve", lambda e: e.tensor_copy(out=silub, in_=siluf), reads=[("siluf",)], writes=[("silub",)])
        s.op("dve", lambda e: e.memset(ones_b, 1.0), writes=[("ones_b",)])
        wada = [A.alloc("wada0", (KC, D), BF16), A.alloc("wada1", (KC, D), BF16)]
        w_ada_v = self.w_ada.rearrange("(k p) n -> p k n", p=128)
        MODB = 7
        for v in (0, 1):
            i = v
            wn = "wada%d" % i
            s.dma("pool", wada[i], w_ada_v[:, :, v * D:(v + 1) * D], writes=[(wn,)])
            for c in range(KC):
                col = v * 8 + c
                s.op("pe", [(lambda e, k=k: e.matmul(bank(MODB)[:, col:col + 1], wada[i][:, k, c * 128:(c + 1) * 128],
                                                    silub[:, k:k + 1], start=(k == 0), stop=(k == KC - 1)))
                            for k in range(KC)],
                     reads=[(wn,), ("silub",)], writes=[("ps", MODB)])
        s.op("dve", lambda e: e.tensor_tensor(out=modT[:, 0:16], in0=bank(MODB)[:, 0:16], in1=badaT[:, 0:16], op=ALU.add),
             reads=[("badaT",)], writes=[("ps", MODB), ("modT",)])
        s.op("dve", lambda e: e.scalar_tensor_tensor(out=amix, in0=modT[:, 8:16], scalar=1.0, in1=gpreT[:, 0:8],
                                                     op0=ALU.add, op1=ALU.mult),
             reads=[("modT",), ("gpreT",)], writes=[("amix",)])
        self.silub, self.badaT_t, self.gpreT_t = silub, badaT, gpreT
        self.modT, self.amix, self.affn, self.ones_b, self.siluf = modT, amix, affn, ones_b, siluf
        self.dump("modT", modT, 128, 48, F32, [("modT",)])
        A.release("wada0"); A.release("wada1")
        if upto <= 0:
            return

        hT = A.alloc("hT", (KC, S), BF16)
        self.hT = hT
        ss = A.alloc("ss", (NT,), F32)
        rstd = A.alloc("rstd", (NT,), F32)
        self.ss, self.rstd = ss, rstd
        self.prenorm(self.x, amix, modT[:, 0:8], hT, "hT", ss, rstd, "amix", list(range(NB)), 0)
        self.dump("hT0", hT[:, 0, :], 128, S, BF16, [("hT", 0, T) for T in range(NB)])
        self.dump("hT7", hT[:, 7, :], 128, S, BF16, [("hT", 7, T) for T in range(NB)])
        R = (64, 96)
        posi = A.alloc("posi", (S,), I32, parts=R)
        posf = A.alloc("posf", (S,), F32, parts=R)
        CC = A.alloc("CC", (S,), BF16, parts=R)
        SS = A.alloc("SS", (S,), BF16, parts=R)
        s.dma("pool", posi, self.pos[64:96, :], writes=[("posi",)])
        s.op("dve", lambda e: e.tensor_copy(out=posf, in_=posi), reads=[("posi",)], writes=[("posf",)])
        ang = posi.bitcast(F32)
        kf = A.alloc("kf", (S,), F32, parts=R)
        ki = A.alloc("ki", (S,), I32, parts=R)
        for (tab, name, c0) in ((CC, "CC", 0), (SS, "SS", 2)):
            s.op("dve", lambda e: e.tensor_scalar(out=ang, in0=posf, scalar1=ropec[64:96, c0:c0 + 1],
                                                  scalar2=ropec[64:96, c0 + 1:c0 + 2], op0=ALU.mult, op1=ALU.add),
                 reads=[("posf",), ("ropec",)], writes=[("posi",)])
            s.op("dve", lambda e: e.tensor_scalar(out=kf, in0=ang, scalar1=1.0 / (2.0 * math.pi), scalar2=None,
                                                  op0=ALU.mult),
                 reads=[("posi",)], writes=[("kf",)])
            s.op("dve", lambda e: e.tensor_copy(out=ki, in_=kf), reads=[("kf",)], writes=[("ki",)])
            s.op("dve", lambda e: e.tensor_copy(out=kf, in_=ki), reads=[("ki",)], writes=[("kf",)])
            s.op("dve", lambda e: e.scalar_tensor_tensor(out=ang, in0=kf, scalar=-2.0 * math.pi, in1=ang,
                                                         op0=ALU.mult, op1=ALU.add),
                 reads=[("kf",), ("posi",)], writes=[("posi",)])
            s.op("dve", lambda e: e.tensor_scalar(out=kf, in0=ang, scalar1=math.pi, scalar2=-2.0 * math.pi,
                                                  op0=ALU.is_gt, op1=ALU.mult),
                 reads=[("posi",)], writes=[("kf",)])
            s.op("dve", lambda e: e.tensor_tensor(out=ang, in0=ang, in1=kf, op=ALU.add),
                 reads=[("posi",), ("kf",)], writes=[("posi",)])
            s.op("dve", lambda e: e.tensor_scalar(out=ang, in0=ang, scalar1=-math.pi, scalar2=math.pi,
                                                  op0=ALU.max, op1=ALU.min),
                 reads=[("posi",)], writes=[("posi",)])
            s.op("act", lambda e: e.activation(out=tab, in_=ang, func=AF.Sin),
                 reads=[("posi",)], writes=[(name,)])
        A.release("kf"); A.release("ki")
        self.CC, self.SS = CC, SS
        self.dump("CC", CC, 32, S, BF16, [("CC",)])
        self.dump("SS", SS, 32, S, BF16, [("SS",)])
        A.release("posi"); A.release("posf")

        if upto <= 1:
            return
        self._body2(upto)

    def prenorm(self, src_rows, a_sc, shift_sc, dst, dname, ss, rstd, aname, Ts, tok0, src_sbuf=None):
        s, A = self.s, self.A
        nxt = 8 if src_sbuf is None else 4
        xts = [A.alloc("xt%d" % i, (D,), F32) for i in range(nxt)] if src_sbuf is None else None
        tbanks = (0, 1, 2, 3, 6, 7) if src_sbuf is None else (6, 7)
        ntb = 0
        xb = [A.alloc("xb%d" % i, (D,), BF16) for i in range(4)]
        junk = A.alloc("junk", (D,), BF16)
        for T in Ts:
            for i in range(4):
                it = 4 * T + i
                if src_sbuf is None:
                    xt, xk = xts[it % nxt], ("xt%d" % (it % nxt),)
                    s.dma("sp", xt, src_rows[it * 128:(it + 1) * 128, :], writes=[xk])
                else:
                    xt, xk = src_sbuf(it)
                s.op("act", lambda e: e.activation(out=junk, in_=xt, func=AF.Square, accum_out=ss[:, it:it + 1]),
                     reads=[xk], writes=[("junk",), ("ss", it)])
            sl4 = slice(4 * T, 4 * T + 4)
            s.op("act", lambda e: e.activation(out=rstd[:, sl4], in_=ss[:, sl4], func=AF.Ln, bias=self.eps_c, scale=1.0 / D),
                 reads=[("ss", 4 * T + i) for i in range(4)] + [("eps_c",)], writes=[("rstd", T)])
            s.op("act", lambda e: e.activation(out=rstd[:, sl4], in_=rstd[:, sl4], func=AF.Exp, scale=-0.5),
                 reads=[("rstd", T)], writes=[("rstd", T)])
            for i in range(4):
                it = 4 * T + i
                xt, xk = (xts[it % nxt], ("xt%d" % (it % nxt),)) if src_sbuf is None else src_sbuf(it)
                s.op("dve", lambda e: e.tensor_scalar(out=xb[i], in0=xt, scalar1=rstd[:, it:it + 1], scalar2=None,
                                                      op0=ALU.mult),
                     reads=[xk, ("rstd", T)], writes=[("xb%d" % i,)])
            for c in range(KC):
                b = tbanks[ntb % len(tbanks)]
                ntb += 1
                psb = self.bankbf(b)
                s.op("pe", [(lambda e, i=i: e.transpose(out=psb[:, i * 128:(i + 1) * 128],
                                                        in_=xb[i][:, c * 128:(c + 1) * 128], identity=self.ident))
                            for i in range(4)],
                     reads=[("xb%d" % i,) for i in range(4)] + [("ident",)], writes=[("ps", b)])
                dsl = dst[:, c, (T - tok0) * 512:(T - tok0 + 1) * 512]
                if c % 2 == 0:
                    s.op("act", lambda e: e.activation(out=dsl, in_=psb[:, 0:512], func=AF.Identity,
                                                       bias=shift_sc[:, c:c + 1], scale=a_sc[:, c:c + 1]),
                         reads=[(aname,), ("modT",), ("modT2",)], writes=[("ps", b), (dname, c, T if dname == "hT" else 0)])
                else:
                    s.op("dve", lambda e: e.tensor_scalar(out=dsl, in0=psb[:, 0:512], scalar1=a_sc[:, c:c + 1],
                                                          scalar2=shift_sc[:, c:c + 1], op0=ALU.mult, op1=ALU.add),
                         reads=[(aname,), ("modT",), ("modT2",)], writes=[("ps", b), (dname, c, T if dname == "hT" else 0)])
        if xts is not None:
            for i in range(nxt):
                A.release("xt%d" % i)
        for i in range(4):
            A.release("xb%d" % i)
        A.release("junk")

    def _body2(self, upto):
        nc, s, A = self.nc, self.s, self.A
        bank, bankbf = self.bank, self.bankbf
        hT, CC, SS = self.hT, self.CC, self.SS
        w_in_v = self.w_in.rearrange("(k p) n -> p k n", p=128)
        hkeys = lambda T: [("hT", k, T) for k in range(KC)]
        TS = lambda T: slice(T * 512, (T + 1) * 512)
        mm = lambda out, l, r, st, sp: (lambda e: e.matmul(out, l, r, start=st, stop=sp))

        oT = A.alloc("oT", (4, S), BF16)
        qk = [{n: A.alloc(n + str(st), (S,), BF16, parts=(0, 96)) for n in ("qA", "qB", "kA", "kB")} for st in range(2)]
        vaug = A.alloc("vaug", (NT, 4, 192), BF16)
        self.pT = [A.alloc("pT%d" % i, (512,), BF16) for i in range(8)]
        self.rec = [A.alloc("rec%d" % i, (512,), F32) for i in range(2)]
        self.qk, self.vaug, self.oT = qk, vaug, oT
        self.att_cnt = 0

        s.op("dve", lambda e: e.memset(vaug.rearrange("p a b c -> p (a b c)"), 1.0),
             writes=[("vaug", it, e_) for it in range(NT) for e_ in range(3)])
        wv = A.alloc("wv", (KC, 512), BF16)
        s.dma("pool", wv, w_in_v[:, :, C_V:C_V + 512], writes=[("wv",)])

        def v_proj(lhs_of, nk, wt, wname, rkeys):
            for it in range(NT):
                b = 6 + it % 2
                s.op("pe", [mm(bank(b), lhs_of(k, it), wt[:, k, :], k == 0, k == nk - 1) for k in range(nk)],
                     reads=rkeys(it // 4) + [(wname,)], writes=[("ps", b)])
                pv = bank(b).rearrange("p (c e d) -> p c e d", c=4, e=2, d=64)
                s.op("act", lambda e: e.activation(out=vaug[:, it, :, 0:64], in_=pv[:, :, 0, :], func=AF.Copy),
                     writes=[("ps", b), ("vaug", it, 0)])
                s.op("dve", lambda e: e.tensor_copy(out=vaug[:, it, :, 128:192], in_=pv[:, :, 1, :]),
                     writes=[("ps", b), ("vaug", it, 1)])

        v_proj(lambda k, it: hT[:, k, it * 128:(it + 1) * 128], KC, wv, "wv", hkeys)
        A.release("wv")

        wmisc = A.alloc("wmisc", (KC, 128), BF16)
        s.op("dve", lambda e: e.memset(wmisc.rearrange("p a b -> p (a b)"), 0.0), writes=[("wmisc", i) for i in range(4)])
        s.dma("pool", wmisc[:, :, 0:8], w_in_v[:, :, C_F:C_F + 8], writes=[("wmisc", 0)])
        s.dma("pool", wmisc[:, :, 64:96], w_in_v[:, :, C_KR:C_KR + 32], writes=[("wmisc", 1)])
        s.dma("pool", wmisc[:, :, 96:112], w_in_v[:, :, C_KR + 16:C_KR + 32], writes=[("wmisc", 2)])
        s.dma("pool", wmisc[:, :, 112:128], w_in_v[:, :, C_KR:C_KR + 16], writes=[("wmisc", 3)])
        P8 = (0, 8)
        nbneg = A.alloc("nbneg", (1,), F32, parts=P8)
        eT = A.alloc("eT", (512,), F32, parts=P8)
        nlf = A.alloc("nlf", (512,), F32, parts=P8)
        onesf = A.alloc("onesf", (512,), F32, parts=P8)
        G = A.alloc("G", (S,), F32, parts=P8)
        negGb = A.alloc("negGb", (S,), BF16, parts=P8)
        Gtok = A.alloc("Gtok", (128,), F32)
        R = (64, 96)
        kpe = A.alloc("kpe", (S,), BF16, parts=R)
        t1 = A.alloc("t1", (512,), F32, parts=R)
        t2 = A.alloc("t2", (512,), F32, parts=R)
        self.t1, self.t2, self.Gtok, self.kpe_t = t1, t2, Gtok, kpe
        s.op("dve", lambda e: e.tensor_scalar(out=nbneg, in0=self.nbf_t, scalar1=-1.0, scalar2=None, op0=ALU.mult),
             reads=[("nbf",)], writes=[("nbneg",)])
        s.op("dve", lambda e: e.memset(onesf, 1.0), writes=[("onesf",)])
        for T in range(NB):
            b = 6 + T % 2
            s.op("pe", [mm(bank(b), wmisc[:, k, :], hT[:, k, TS(T)], k == 0, k == KC - 1) for k in range(KC)],
                 reads=hkeys(T) + [("wmisc", i) for i in range(4)], writes=[("ps", b)])
            s.op("act", lambda e: e.activation(out=eT, in_=bank(b)[0:8, :], func=AF.Exp, bias=nbneg, scale=-1.0),
                 reads=[("nbneg",)], writes=[("ps", b), ("eT",)])
            s.op("act", lambda e: e.activation(out=nlf, in_=eT, func=AF.Ln, bias=1.0), reads=[("eT",)], writes=[("nlf",)])
            init = 0.0 if T == 0 else G[:, T * 512 - 1:T * 512]
            s.op("dve", lambda e: e.tensor_tensor_scan(out=G[:, TS(T)], data0=onesf, data1=nlf, initial=init,
                                                       op0=ALU.mult, op1=ALU.add),
                 reads=[("nlf",), ("onesf",)] + ([("G", T - 1)] if T else []), writes=[("G", T)])
            s.op("dve", lambda e: e.tensor_tensor(out=t1, in0=bank(b)[64:96, :], in1=CC[:, TS(T)], op=ALU.mult),
                 reads=[("CC",)], writes=[("ps", b), ("t1",)])
            s.op("dve", lambda e: e.tensor_tensor(out=t2, in0=bank(b)[96:128, :], in1=SS[:, TS(T)], op=ALU.mult),
                 reads=[("SS",)], writes=[("ps", b), ("t2",)])
            s.op("dve", lambda e: e.tensor_tensor(out=kpe[:, TS(T)], in0=t1, in1=t2, op=ALU.add),
                 reads=[("t1",), ("t2",)], writes=[("kpe", T)])
        s.op("dve", lambda e: e.tensor_scalar(out=negGb, in0=G, scalar1=-1.0, scalar2=None, op0=ALU.mult),
             reads=[("G", T) for T in range(NB)], writes=[("negGb",)])
        GB = 5
        s.op("pe", [(lambda e, it=it: e.transpose(out=bank(GB)[:, it * 8:(it + 1) * 8], in_=G[:, it * 128:(it + 1) * 128],
                                                  identity=self.identf[0:8, 0:8])) for it in range(NT)],
             reads=[("G", T) for T in range(NB)] + [("identf",)], writes=[("ps", GB)])
        s.op("dve", lambda e: e.tensor_copy(out=Gtok, in_=bank(GB)[:, 0:128]), writes=[("ps", GB), ("Gtok",)])
        self.dump("G", G, 8, S, F32, [("G", T) for T in range(NB)])
        self.dump("Gtok", Gtok, 128, 128, F32, [("Gtok",)])
        self.dump("kpe", kpe, 32, S, BF16, [("kpe", T) for T in range(NB)])
        A.release("wmisc"); A.release("eT"); A.release("nlf"); A.release("onesf")
        if upto <= 2:
            return

        wf = [A.alloc("wf0", (KC, 256), BF16), A.alloc("wf1", (KC, 256), BF16)]
        for st in range(2):
            for n in ("kA", "kB"):
                s.op("dve", lambda e: e.memset(qk[st][n][64:65, :], 1.0), writes=[(n + str(st), "g")])

        def fox_proj(c):
            st = c % 2
            Q = qk[st]
            w, wn = wf[c % 2], "wf%d" % (c % 2)
            s.dma("pool", w[:, :, 0:128], w_in_v[:, :, C_Q + c * 128:C_Q + (c + 1) * 128], writes=[(wn, "q")])
            s.dma("pool", w[:, :, 128:256], w_in_v[:, :, C_K + c * 128:C_K + (c + 1) * 128], writes=[(wn, "k")])
            s.dma("pool", Q["qA"][64:65, :], negGb[2 * c:2 * c + 1, :], reads=[("negGb",)], writes=[("qA%d" % st, "g")])
            s.dma("pool", Q["qB"][64:65, :], negGb[2 * c + 1:2 * c + 2, :], reads=[("negGb",)], writes=[("qB%d" % st, "g")])
            yield
            for T in range(NB):
                s.op("pe", [mm(bank(6), w[:, k, 0:128], hT[:, k, TS(T)], k == 0, k == KC - 1) for k in range(KC)],
                     reads=hkeys(T) + [(wn, "q")], writes=[("ps", 6)])
                s.op("dve", lambda e: e.tensor_scalar(out=Q["qA"][0:64, TS(T)], in0=bank(6)[0:64, :], scalar1=0.125, scalar2=None, op0=ALU.mult),
                     writes=[("ps", 6), ("qA%d" % st, T)])
                s.op("dve", lambda e: e.tensor_scalar(out=Q["qB"][0:64, TS(T)], in0=bank(6)[64:128, :], scalar1=0.125, scalar2=None, op0=ALU.mult),
                     writes=[("ps", 6), ("qB%d" % st, T)])
                yield
                s.op("pe", [mm(bank(7), w[:, k, 128:256], hT[:, k, TS(T)], k == 0, k == KC - 1) for k in range(KC)],
                     reads=hkeys(T) + [(wn, "k")], writes=[("ps", 7)])
                s.op("dve", lambda e: e.tensor_copy(out=Q["kA"][0:64, TS(T)], in_=bank(7)[0:64, :]),
                     writes=[("ps", 7), ("kA%d" % st, T)])
                s.op("dve", lambda e: e.tensor_copy(out=Q["kB"][0:64, TS(T)], in_=bank(7)[64:128, :]),
                     writes=[("ps", 7), ("kB%d" % st, T)])
                yield

        for _ in fox_proj(0):
            pass
        self.dump("qA", qk[0]["qA"][0:65, :], 65, S, BF16, [("qA0", T) for T in range(NB)] + [("qA0", "g")])
        self.dump("kA", qk[0]["kA"][0:65, :], 65, S, BF16, [("kA0", T) for T in range(NB)] + [("kA0", "g")])
        for c in range(4):
            bg = fox_proj(c + 1) if c < 3 else None
            self.attention(c, True, c % 2, bg)
        for c in range(4):
            self.dump("oaT%d" % c, oT[:, c, :], 128, S, BF16, [("oT", c, T, hh) for T in range(NB) for hh in range(2)])
        A.release("wf0"); A.release("wf1"); A.release("G"); A.release("negGb"); A.release("nbneg")
        if upto <= 3:
            return
        self._body3(upto)

    def attention(self, c, fox, st=0, bg=None):
        s = self.s
        bank = self.bank
        qk, vaug, pT, rec = self.qk[st], self.vaug, self.pT, self.rec
        oT, on = (self.oT, "oT") if fox else (self.oT2, "oT2")
        nstep = 0
        KD = 65 if fox else 96
        scale = 1.0 if fox else 1.0 / math.sqrt(96.0)
        SB = (0, 1, 2, 3)
        LA = 2
        NPT = len(pT)
        for T in range(NB):
            nj = 4 * T + 4
            info = {}

            def emit_S(hh, j):
                h = 2 * c + hh
                qn, kn = ("qA", "kA") if hh == 0 else ("qB", "kB")
                q, k = qk[qn], qk[kn]
                qn, kn = qn + str(st), kn + str(st)
                off = 0 if j < 4 * T else (j - 4 * T) * 128
                w = 512 - off
                n = self.att_cnt
                self.att_cnt += 1
                b = SB[n % len(SB)]
                pt, ptn = pT[n % NPT], "pT%d" % (n % NPT)
                diag = j >= 4 * T
                fns = [lambda e: e.matmul(bank(b)[:, 0:w], k[0:KD, j * 128:(j + 1) * 128],
                                          q[0:KD, T * 512 + off:(T + 1) * 512], start=True, stop=not diag)]
                if diag:
                    fns.append(lambda e: e.matmul(bank(b)[:, 0:128], self.ident, self.mask, start=False, stop=True))
                s.op("pe", fns, reads=[(qn, T), (qn, "g"), (kn, j // 4), (kn, "g"), ("ident",), ("mask",)],
                     writes=[("ps", b)])
                if fox:
                    s.op("act", lambda e: e.activation(out=pt[:, 0:w], in_=bank(b)[:, 0:w], func=AF.Exp,
                                                       bias=self.Gtok[:, j * 8 + h:j * 8 + h + 1], scale=1.0),
                         reads=[("Gtok",)], writes=[("ps", b), (ptn,)])
                else:
                    s.op("act", lambda e: e.activation(out=pt[:, 0:w], in_=bank(b)[:, 0:w], func=AF.Exp, scale=scale),
                         writes=[("ps", b), (ptn,)])
                info[(hh, j)] = (off, w, pt, ptn)

            def emit_PV(hh, j):
                off, w, pt, ptn = info[(hh, j)]
                acc = 4 + hh
                vsl = slice(0, 128) if hh == 0 else slice(64, 192)
                s.op("pe", lambda e: e.matmul(bank(acc)[:, off:512], vaug[:, j, c, vsl], pt[:, 0:w],
                                              start=(j == 0), stop=(j == nj - 1)),
                     reads=[(ptn,), ("vaug", j, 0), ("vaug", j, 1), ("vaug", j, 2)], writes=[("ps", acc)])

            for step in range(nj + LA):
                for hh in (0, 1):
                    if step < nj:
                        emit_S(hh, step)
                    if step - LA >= 0:
                        emit_PV(hh, step - LA)
                nstep += 1
                if bg is not None and nstep % 4 == 0:
                    next(bg, None)
            for hh in (0, 1):
                acc = 4 + hh
                r = rec[hh]
                if hh == 0:
                    osl, dsl = slice(0, 64), slice(64, 128)
                else:
                    osl, dsl = slice(64, 128), slice(0, 64)
                s.op("act", lambda e: e.activation(out=r[osl, :], in_=bank(acc)[dsl, :], func=AF.Ln),
                     writes=[("ps", acc), ("rec%d" % hh,)])
                s.op("act", lambda e: e.activation(out=r[osl, :], in_=r[osl, :], func=AF.Exp, scale=-1.0),
                     reads=[("rec%d" % hh,)], writes=[("rec%d" % hh,)])
                s.op("dve", lambda e: e.tensor_tensor(out=oT[osl, c, T * 512:(T + 1) * 512], in0=bank(acc)[osl, :],
                                                      in1=r[osl, :], op=ALU.mult),
                     reads=[("rec%d" % hh,)], writes=[("ps", acc), (on, c, T, hh)])
        if bg is not None:
            for _ in bg:
                pass

    def gate_merge(self, col0, w_proj, first, oT, on):
        s, A = self.s, self.A
        bank = self.bank
        hT, merged = self.hT, self.merged
        mm = lambda out, l, r, st, sp: (lambda e: e.matmul(out, l, r, start=st, stop=sp))
        w_in_v = self.w_in.rearrange("(k p) n -> p k n", p=128)
        wg = A.alloc("wg", (KC, D), BF16)
        wp = A.alloc("wp", (4, D), BF16)
        sg = [A.alloc("sg%d" % i, (512,), BF16) for i in range(2)]
        tmp = [A.alloc("gtmp%d" % i, (512,), BF16) for i in range(2)]
        s.dma("pool", wg, w_in_v[:, :, col0:col0 + D], writes=[("wg",)])
        s.dma("pool", wp, w_proj.rearrange("(k p) n -> p k n", p=128), writes=[("wp",)])
        n = 0
        for T in range(NB):
            Tsl = slice(T * 512, (T + 1) * 512)
            for m in range(KC):
                i = n % 2
                n += 1
                bg, bp = 0 + i, 2 + i
                s.op("pe", [mm(bank(bg), wg[:, k, m * 128:(m + 1) * 128], hT[:, k, Tsl], k == 0, k == KC - 1) for k in range(KC)],
                     reads=[("hT", k, T) for k in range(KC)] + [("wg",)], writes=[("ps", bg)])
                s.op("act", lambda e: e.activation(out=sg[i], in_=bank(bg), func=AF.Sigmoid),
                     writes=[("ps", bg), ("sg%d" % i,)])
                s.op("pe", [mm(bank(bp), wp[:, cc, m * 128:(m + 1) * 128], oT[:, cc, Tsl], cc == 0, cc == 3) for cc in range(4)],
                     reads=[(on, cc, T, hh) for cc in range(4) for hh in range(2)] + [("wp",)], writes=[("ps", bp)])
                if first:
                    s.op("dve", lambda e: e.tensor_tensor(out=merged[:, m, Tsl], in0=bank(bp), in1=sg[i], op=ALU.mult),
                         reads=[("sg%d" % i,)], writes=[("ps", bp), ("merged", m, T)])
                else:
                    s.op("dve", lambda e: e.tensor_tensor(out=tmp[i], in0=bank(bp), in1=sg[i], op=ALU.mult),
                         reads=[("sg%d" % i,)], writes=[("ps", bp), ("gtmp%d" % i,)])
                    s.op("dve", lambda e: e.tensor_tensor(out=merged[:, m, Tsl], in0=merged[:, m, Tsl], in1=tmp[i], op=ALU.add),
                         reads=[("gtmp%d" % i,), ("merged", m, T)], writes=[("merged", m, T)])
        for nme in ("wg", "wp", "sg0", "sg1", "gtmp0", "gtmp1"):
            A.release(nme)

    def _body3(self, upto):
        nc, s, A = self.nc, self.s, self.A
        bank = self.bank
        hT, CC, SS, qk, vaug = self.hT, self.CC, self.SS, self.qk, self.vaug
        t1, t2 = self.t1, self.t2
        w_in_v = self.w_in.rearrange("(k p) n -> p k n", p=128)
        hkeys = lambda T: [("hT", k, T) for k in range(KC)]
        TS = lambda T: slice(T * 512, (T + 1) * 512)
        mm = lambda out, l, r, st, sp: (lambda e: e.matmul(out, l, r, start=st, stop=sp))

        oT2 = A.alloc("oT2", (4, S), BF16)
        self.oT2 = oT2

        cqn = A.alloc("cqn", (6, S), BF16)
        ckvn = A.alloc("ckvn", (2, S), BF16)
        sqb = [A.alloc("sqb%d" % i, (512,), BF16) for i in range(2)]
        rstdb = A.alloc("rstdb", (512,), F32)
        wcq = A.alloc("wcq", (KC, 768), BF16)
        wckv = A.alloc("wckv", (KC, 256), BF16)
        s.dma("pool", wcq, w_in_v[:, :, C_CQ:C_CQ + 768], writes=[("wcq",)])
        s.dma("pool", wckv, w_in_v[:, :, C_CKV:C_CKV + 256], writes=[("wckv",)])
        for (wt, wn, nm, dst, dn, gT, gname, nfeat) in ((wcq, "wcq", 6, cqn, "cqn", self.gqT_t, "gqT", 768.0),
                                                        (wckv, "wckv", 2, ckvn, "ckvn", self.gkvT_t, "gkvT", 256.0)):
            for T in range(NB):
                for m in range(nm):
                    b = 6 + m % 2
                    i = m % 2
                    s.op("pe", [mm(bank(b), wt[:, k, m * 128:(m + 1) * 128], hT[:, k, TS(T)], k == 0, k == KC - 1) for k in range(KC)],
                         reads=hkeys(T) + [(wn,)], writes=[("ps", b)])
                    s.op("act", lambda e: e.activation(out=sqb[i], in_=bank(b), func=AF.Square),
                         writes=[("ps", b), ("sqb%d" % i,)])
                    s.op("dve", lambda e: e.tensor_scalar(out=dst[:, m, TS(T)], in0=bank(b), scalar1=gT[:, m:m + 1],
                                                          scalar2=None, op0=ALU.mult),
                         reads=[(gname,)], writes=[("ps", b), (dn, m, T)])
                    s.op("pe", mm(bank(5), self.ones_b, sqb[i], m == 0, m == nm - 1),
                         reads=[("sqb%d" % i,), ("ones_b",)], writes=[("ps", 5)])
                s.op("act", lambda e: e.activation(out=rstdb, in_=bank(5), func=AF.Ln, bias=self.eps_c, scale=1.0 / nfeat),
                     reads=[("eps_c",)], writes=[("ps", 5), ("rstdb",)])
                s.op("act", lambda e: e.activation(out=rstdb, in_=rstdb, func=AF.Exp, scale=-0.5),
                     reads=[("rstdb",)], writes=[("rstdb",)])
                for m in range(nm):
                    s.op("dve", lambda e: e.tensor_tensor(out=dst[:, m, TS(T)], in0=dst[:, m, TS(T)], in1=rstdb, op=ALU.mult),
                         reads=[(dn, m, T), ("rstdb",)], writes=[(dn, m, T)])
        self.dump("cqn0", cqn[:, 0, :], 128, S, BF16, [("cqn", 0, T) for T in range(NB)])
        self.dump("ckvn1", ckvn[:, 1, :], 128, S, BF16, [("ckvn", 1, T) for T in range(NB)])
        A.release("wcq"); A.release("wckv"); A.release("sqb0"); A.release("sqb1"); A.release("rstdb")

        wuq = A.alloc("wuq", (6, 8, 128), BF16)
        w_uq_v = self.w_uq.rearrange("(k p) (h e) -> p k h e", p=128, e=96)
        for kc in range(6):
            s.dma("pool", wuq[:, kc, :, 0:96], w_uq_v[:, kc, :, :], writes=[("wuq", 0, kc)])
            s.dma("pool", wuq[:, kc, :, 96:112], w_uq_v[:, kc, :, 80:96], writes=[("wuq", 1, kc)])
            s.dma("pool", wuq[:, kc, :, 112:128], w_uq_v[:, kc, :, 64:80], writes=[("wuq", 2, kc)])
        wkn = A.alloc("wkn", (2, 512), BF16)
        wvm = A.alloc("wvm", (2, 512), BF16)
        w_ukv_v = self.w_ukv.rearrange("(k p) (h e) -> p k h e", p=128, e=128)
        for kc in range(2):
            s.dma("pool", wkn[:, kc, :].rearrange("p (h e) -> p h e", e=64), w_ukv_v[:, kc, :, 0:64], writes=[("wkn", kc)])
            s.dma("pool", wvm[:, kc, :].rearrange("p (h e) -> p h e", e=64), w_ukv_v[:, kc, :, 64:128], writes=[("wvm", kc)])

        for it in range(NT):
            b = 6 + it % 2
            T = it // 4
            s.op("pe", [mm(bank(b), ckvn[:, k, it * 128:(it + 1) * 128], wvm[:, k, :], k == 0, k == 1) for k in range(2)],
                 reads=[("ckvn", k, T) for k in range(2)] + [("wvm", 0), ("wvm", 1)], writes=[("ps", b)])
            pv = bank(b).rearrange("p (c e d) -> p c e d", c=4, e=2, d=64)
            s.op("act", lambda e: e.activation(out=vaug[:, it, :, 0:64], in_=pv[:, :, 0, :], func=AF.Copy),
                 writes=[("ps", b), ("vaug", it, 0)])
            s.op("dve", lambda e: e.tensor_copy(out=vaug[:, it, :, 128:192], in_=pv[:, :, 1, :]),
                 writes=[("ps", b), ("vaug", it, 1)])

        wuq_keys = [("wuq", i, kc) for i in range(3) for kc in range(6)]

        def mla_proj(c):
            st = c % 2
            Q = qk[st]
            for kn in ("kA", "kB"):
                s.op("dve", lambda e: e.tensor_copy(out=Q[kn][64:96, :], in_=self.kpe_t),
                     reads=[("kpe", T) for T in range(NB)], writes=[(kn + str(st), "g")])
            yield
            nb = 0
            for T in range(NB):
                s.op("pe", [mm(bank(7), wkn[:, k, c * 128:(c + 1) * 128], ckvn[:, k, TS(T)], k == 0, k == 1) for k in range(2)],
                     reads=[("ckvn", k, T) for k in range(2)] + [("wkn", 0), ("wkn", 1)], writes=[("ps", 7)])
                s.op("dve", lambda e: e.tensor_copy(out=Q["kA"][0:64, TS(T)], in_=bank(7)[0:64, :]),
                     writes=[("ps", 7), ("kA%d" % st, T)])
                s.op("dve", lambda e: e.tensor_copy(out=Q["kB"][0:64, TS(T)], in_=bank(7)[64:128, :]),
                     writes=[("ps", 7), ("kB%d" % st, T)])
                yield
                for hh in (0, 1):
                    h = 2 * c + hh
                    qn = ("qA" if hh == 0 else "qB")
                    q = Q[qn]
                    qn = qn + str(st)
                    s.op("pe", [mm(bank(6), wuq[:, k, h, :], cqn[:, k, TS(T)], k == 0, k == 5) for k in range(6)],
                         reads=[("cqn", k, T) for k in range(6)] + wuq_keys, writes=[("ps", 6)])
                    s.op("dve", lambda e: e.tensor_copy(out=q[0:64, TS(T)], in_=bank(6)[0:64, :]),
                         writes=[("ps", 6), (qn, T)])
                    s.op("dve", lambda e: e.tensor_tensor(out=t1, in0=bank(6)[64:96, :], in1=CC[:, TS(T)], op=ALU.mult),
                         reads=[("CC",)], writes=[("ps", 6), ("t1",)])
                    s.op("dve", lambda e: e.tensor_tensor(out=t2, in0=bank(6)[96:128, :], in1=SS[:, TS(T)], op=ALU.mult),
                         reads=[("SS",)], writes=[("ps", 6), ("t2",)])
                    s.op("dve", lambda e: e.tensor_tensor(out=q[64:96, TS(T)], in0=t1, in1=t2, op=ALU.add),
                         reads=[("t1",), ("t2",)], writes=[(qn, T)])
                    yield

        for _ in mla_proj(0):
            pass
        self.dump("qm0", qk[0]["qA"][0:96, :], 96, S, BF16, [("qA0", T) for T in range(NB)])
        self.dump("km0", qk[0]["kA"][0:96, :], 96, S, BF16, [("kA0", T) for T in range(NB)] + [("kA0", "g")])
        for c in range(4):
            bg = mla_proj(c + 1) if c < 3 else None
            if c == 3:
                for nme in ("wuq", "wkn", "wvm", "cqn", "ckvn", "kpe", "t1", "t2", "CC", "SS"):
                    A.release(nme)
                self.gvec_prefetch()
            self.attention(c, False, c % 2, bg)
        for c in range(4):
            self.dump("obT%d" % c, oT2[:, c, :], 128, S, BF16, [("oT2", c, T, hh) for T in range(NB) for hh in range(2)])
        for nme in ("qA0", "qB0", "kA0", "kB0", "qA1", "qB1", "kA1", "kB1", "vaug", "pT0", "pT1", "pT2", "pT3", "pT4", "pT5", "pT6", "pT7",
                    "rec0", "rec1", "Gtok"):
            A.release(nme)
        if upto <= 4:
            return

        self.gvec_compute()
        merged = A.alloc("merged", (KC, S), BF16)
        self.merged = merged
        self.gate_merge(C_GF, self.w_pf, True, self.oT, "oT")
        self.gate_merge(C_GM, self.w_pm, False, self.oT2, "oT2")
        for m in (0, 7):
            self.dump("mg%d" % m, merged[:, m, :], 128, S, BF16, [("merged", m, T) for T in range(NB)])
        A.release("hT"); A.release("oT"); A.release("oT2")
        if upto <= 5:
            return
        self._body4(upto)

    def gvec_prefetch(self):
        s, A = self.s, self.A
        self.gvec = A.alloc("gvec", (2, D), F32)
        self.silurep = A.alloc("silurep", (KC, 128), BF16)
        self.bada_rep_t = A.alloc("bada_rep", (2 * D,), F32)
        self.gpost_rep_t = A.alloc("gpost_rep", (2 * D,), F32)
        s.dma("pool", self.bada_rep_t, self.bada_rep, writes=[("bada_rep",)])
        s.dma("pool", self.gpost_rep_t, self.gpost_rep, writes=[("gpost_rep",)])
        w_ada_v = self.w_ada.rearrange("(k p) n -> p k n", p=128)
        self.wadag = {}
        for v in (3, 4):
            for hf in range(2):
                nm = "wadag%d%d" % (v, hf)
                t = A.alloc(nm, (KC, 512), BF16)
                s.dma("pool", t, w_ada_v[:, :, v * D + hf * 512:v * D + (hf + 1) * 512], writes=[(nm,)])
                self.wadag[(v, hf)] = (t, nm)

    def gvec_compute(self):
        s, A = self.s, self.A
        bank = self.bank
        gvec, silurep, bada_rep, gpost_rep = self.gvec, self.silurep, self.bada_rep_t, self.gpost_rep_t
        for k in range(KC):
            s.op("dve", lambda e: e.tensor_scalar(out=silurep[:, k, :], in0=self.ones_b, scalar1=self.siluf[:, k:k + 1],
                                                  scalar2=None, op0=ALU.mult),
                 reads=[("ones_b",), ("siluf",)], writes=[("silurep", k)])
        modT, affn = self.modT, self.affn
        MODB = 7
        for v in (3, 4):
            for c in range(KC):
                col = v * 8 + c
                wt, wn = self.wadag[(v, c // 4)]
                cc = c % 4
                s.op("pe", [(lambda e, k=k: e.matmul(bank(MODB)[:, col:col + 1], wt[:, k, cc * 128:(cc + 1) * 128],
                                                    self.silub[:, k:k + 1], start=(k == 0), stop=(k == KC - 1)))
                            for k in range(KC)],
                     reads=[(wn,), ("silub",)], writes=[("ps", MODB)])
        s.op("dve", lambda e: e.tensor_tensor(out=modT[:, 24:40], in0=bank(MODB)[:, 24:40], in1=self.badaT_t[:, 24:40], op=ALU.add),
             reads=[("badaT",)], writes=[("ps", MODB), ("modT2",)])
        s.op("dve", lambda e: e.scalar_tensor_tensor(out=affn, in0=modT[:, 32:40], scalar=1.0, in1=self.gpreT_t[:, 8:16],
                                                     op0=ALU.add, op1=ALU.mult),
             reads=[("modT2",), ("gpreT",)], writes=[("affn",)])
        w_ada_v = self.w_ada.rearrange("(k p) n -> p k n", p=128)
        for (v, vs) in ((2, 3), (5, 4)):
            for hf in range(2):
                t, nm = self.wadag[(vs, hf)]
                s.dma("pool", t, w_ada_v[:, :, v * D + hf * 512:v * D + (hf + 1) * 512], writes=[(nm,)])
                self.wadag[(v, hf)] = (t, nm)
        for g, v in enumerate((2, 5)):
            for half in range(2):
                wt, wn = self.wadag[(v, half)]
                b = 5 + half
                s.op("pe", [(lambda e, k=k: e.matmul(bank(b), silurep[:, k, :], wt[:, k, :],
                                                    start=(k == 0), stop=(k == KC - 1))) for k in range(KC)],
                     reads=[(wn,)] + [("silurep", k) for k in range(KC)], writes=[("ps", b)])
                sl = slice(g * D + half * 512, g * D + (half + 1) * 512)
                hs = slice(half * 512, (half + 1) * 512)
                s.op("dve", lambda e: e.tensor_tensor(out=gvec[:, g, hs], in0=bank(b), in1=bada_rep[:, sl], op=ALU.add),
                     reads=[("bada_rep",)], writes=[("ps", b), ("gvec", g, half)])
                s.op("dve", lambda e: e.tensor_tensor(out=gvec[:, g, hs], in0=gvec[:, g, hs], in1=gpost_rep[:, sl], op=ALU.mult),
                     reads=[("gpost_rep",), ("gvec", g, half)], writes=[("gvec", g, half)])
        self.dump("gvec", gvec.rearrange("p a b -> p (a b)"), 128, 2 * D, F32, [("gvec", g, h) for g in range(2) for h in range(2)])
        for n in ["silurep", "bada_rep", "gpost_rep"] + sorted(set(nm for (_, nm) in self.wadag.values())):
            A.release(n)

    def _body4(self, upto):
        nc, s, A = self.nc, self.s, self.A
        bank = self.bank
        merged, gvec = self.merged, self.gvec
        mm = lambda out, l, r, st, sp: (lambda e: e.matmul(out, l, r, start=st, stop=sp))
        wout = A.alloc("wout", (KC, D), BF16)
        s.dma("pool", wout, self.w_out.rearrange("(k p) n -> p k n", p=128), writes=[("wout",)])
        w2 = A.alloc("w2", (NFC, D), BF16)
        w2v = self.w_f2.rearrange("(j p) n -> p j n", p=128)
        w1v = self.w_f1.rearrange("(k p) n -> p k n", p=128)
        x2 = A.alloc("x2", (4, D), F32)
        h2T = A.alloc("h2T", (KC, 512), BF16)
        actT = A.alloc("actT", (NFC, 512), BF16)
        w1b = [A.alloc("w1b%d" % i, (KC, 512), BF16) for i in range(3)]
        sgl = [A.alloc("sgl%d" % i, (512,), BF16) for i in range(2)]
        xin = [A.alloc("xin%d" % i, (D,), F32) for i in range(2)]
        ot = [A.alloc("ot%d" % i, (D,), F32) for i in range(2)]
        tmpf = [A.alloc("tmpf%d" % i, (512,), F32) for i in range(2)]
        junkp = A.alloc("junkp", (512,), BF16)
        ssq = A.alloc("ssq", (4 * NT,), F32)
        ssy = A.alloc("ssy", (2 * NT,), F32)
        rsy = A.alloc("rsy", (2 * NT,), F32)
        ss, rstd = self.ss, self.rstd
        w2_loaded = [False]

        def postnorm(bks, col, g, resid, rkeys, out_ap, okeys):
            for half, b in enumerate(bks):
                s.op("act", lambda e: e.activation(out=junkp, in_=bank(b), func=AF.Square,
                                                   accum_out=ssq[:, 2 * col + half:2 * col + half + 1]),
                     writes=[("ps", b), ("junkp",), ("ssq", col, half)])
            s.op("dve", lambda e: e.tensor_tensor(out=ssy[:, col:col + 1], in0=ssq[:, 2 * col:2 * col + 1],
                                                  in1=ssq[:, 2 * col + 1:2 * col + 2], op=ALU.add),
                 reads=[("ssq", col, 0), ("ssq", col, 1)], writes=[("ssy", col)])
            s.op("act", lambda e: e.activation(out=rsy[:, col:col + 1], in_=ssy[:, col:col + 1], func=AF.Ln,
                                               bias=self.eps_c, scale=1.0 / D),
                 reads=[("ssy", col), ("eps_c",)], writes=[("rsy", col)])
            s.op("act", lambda e: e.activation(out=rsy[:, col:col + 1], in_=rsy[:, col:col + 1], func=AF.Exp, scale=-0.5),
                 reads=[("rsy", col)], writes=[("rsy", col)])
            for half, b in enumerate(bks):
                hs = slice(half * 512, (half + 1) * 512)
                s.op("dve", lambda e: e.scalar_tensor_tensor(out=tmpf[half], in0=bank(b), scalar=rsy[:, col:col + 1],
                                                             in1=gvec[:, g, hs], op0=ALU.mult, op1=ALU.mult),
                     reads=[("rsy", col), ("gvec", g, half)], writes=[("ps", b), ("tmpf%d" % half,)])
                s.op("dve", lambda e: e.tensor_tensor(out=out_ap[:, hs], in0=tmpf[half], in1=resid[:, hs], op=ALU.add),
                     reads=[("tmpf%d" % half,)] + rkeys, writes=okeys)

        nw1 = 0
        import os
        for T in [int(t_) for t_ in os.environ.get("TLIST", "0,1,2,3").split(",")]:
            for i in range(4):
                it = 4 * T + i
                xi, xn = xin[it % 2], "xin%d" % (it % 2)
                s.dma("sp", xi, self.x[it * 128:(it + 1) * 128, :], writes=[(xn,)])
                bks = (0, 1) if it % 2 == 0 else (2, 3)
                for half, b in enumerate(bks):
                    s.op("pe", [mm(bank(b), merged[:, k, it * 128:(it + 1) * 128], wout[:, k, half * 512:(half + 1) * 512],
                                   k == 0, k == KC - 1) for k in range(KC)],
                         reads=[("merged", k, T) for k in range(KC)] + [("wout",)], writes=[("ps", b)])
                postnorm(bks, it, 0, xi, [(xn,)], x2[:, i, :], [("x2", i)])
            if T == 0:
                self.dump("x2", x2[:, 0, :], 128, D, F32, [("x2", 0)])
            if upto <= 6:
                return
            if not w2_loaded[0]:
                for gi, g0 in enumerate(range(0, NFC, 6)):
                    g1 = min(NFC, g0 + 6)
                    s.dma("pool", w2[:, g0:g1, :], w2v[:, g0:g1, :], writes=[("w2", gi)])
                w2_loaded[0] = True
            self.prenorm(None, self.affn, self.modT[:, 24:32], h2T, "h2T", ss, rstd, "affn", [T], T,
                         src_sbuf=lambda it: (x2[:, it % 4, :], ("x2", it % 4)))
            for j in range(NFC):
                if j % 2 == 0:
                    wb, wbn = w1b[nw1 % 3], "w1b%d" % (nw1 % 3)
                    nw1 += 1
                    s.dma("pool", wb[:, :, 0:256], w1v[:, :, j * 128:(j + 2) * 128], writes=[(wbn, "g")])
                    s.dma("pool", wb[:, :, 256:512], w1v[:, :, DFF + j * 128:DFF + (j + 2) * 128], writes=[(wbn, "u")])
                jo = (j % 2) * 128
                gb, ub = 4 + 2 * (j % 2), 5 + 2 * (j % 2)
                hk = [("h2T", k, 0) for k in range(KC)]
                s.op("pe", [mm(bank(gb), wb[:, k, jo:jo + 128], h2T[:, k, :], k == 0, k == KC - 1) for k in range(KC)],
                     reads=hk + [(wbn, "g")], writes=[("ps", gb)])
                s.op("pe", [mm(bank(ub), wb[:, k, 256 + jo:256 + jo + 128], h2T[:, k, :], k == 0, k == KC - 1) for k in range(KC)],
                     reads=hk + [(wbn, "u")], writes=[("ps", ub)])
                s.op("act", lambda e: e.activation(out=sgl[j % 2], in_=bank(gb), func=AF.Silu),
                     writes=[("ps", gb), ("sgl%d" % (j % 2),)])
                s.op("dve", lambda e: e.tensor_tensor(out=actT[:, j, :], in0=bank(ub), in1=sgl[j % 2], op=ALU.mult),
                     reads=[("sgl%d" % (j % 2),)], writes=[("ps", ub), ("actT", j)])
            if upto <= 7:
                return
            for i in range(4):
                it = 4 * T + i
                bks = (0, 1) if it % 2 == 0 else (2, 3)
                for half, b in enumerate(bks):
                    s.op("pe", [mm(bank(b), actT[:, j, i * 128:(i + 1) * 128], w2[:, j, half * 512:(half + 1) * 512],
                                   j == 0, j == NFC - 1) for j in range(NFC)],
                         reads=[("actT", j) for j in range(NFC)] + [("w2", gi) for gi in range(4)], writes=[("ps", b)])
                o_t, on = ot[it % 2], "ot%d" % (it % 2)
                postnorm(bks, NT + it, 1, x2[:, i, :], [("x2", i)], o_t, [(on,)])
                s.dma("sp", self.out[it * 128:(it + 1) * 128, :], o_t, reads=[(on,)])
            if upto <= 8:
                return


def _consts():
    ident = np.eye(128, dtype=np.float32)
    sk = np.arange(128)[:, None]
    tq = np.arange(128)[None, :]
    mask = np.where(sk > tq, NEG, 0.0).astype(np.float32)
    inv_freq = 1.0 / (10000.0 ** (np.arange(0, 32, 2, dtype=np.float32) / 32.0))
    rope = np.zeros((128, 4), np.float32)
    for p in range(64, 96):
        i = (p - 64) % 16
        rope[p, 0] = inv_freq[i]
        rope[p, 1] = math.pi / 2
        rope[p, 2] = inv_freq[i]
        rope[p, 3] = (math.pi if p < 80 else 0.0)
    return dict(c_ident=ident.astype(ml_dtypes.bfloat16), c_identf=ident, c_mask=mask.astype(ml_dtypes.bfloat16),
                c_rope=rope)


def make_in_maps(x, c, positions, w_ada, b_ada, g_pre_mix, g_post_mix, g_pre_ffn, g_post_ffn,
                 w_in, b_forget, g_q_lora, w_uq, g_kv_lora, w_ukv, w_proj_fox, w_proj_mla,
                 w_out, w_ffn_in, w_ffn_out, cores=range(8)):
    f = lambda a: np.ascontiguousarray(np.asarray(a, dtype=np.float32))
    colT = lambda v, n: f(np.asarray(v, np.float32).reshape(n, 128).T)
    rep = lambda v: f(np.broadcast_to(np.asarray(v, np.float32)[None, :], (128, v.shape[0])))
    b_ada0 = np.asarray(b_ada[0], np.float32)
    shared = dict(
        w_ada=f(w_ada[0]), badaT=colT(b_ada0, 48),
        bada_rep=rep(np.concatenate([b_ada0[2 * D:3 * D], b_ada0[5 * D:6 * D]])),
        gpost_rep=rep(np.concatenate([np.asarray(g_post_mix[0], np.float32), np.asarray(g_post_ffn[0], np.float32)])),
        gpreT=f(np.concatenate([colT(g_pre_mix[0], 8), colT(g_pre_ffn[0], 8)], axis=1)),
        w_in=f(w_in[0]), nbf=f(np.asarray(b_forget[0], np.float32).reshape(8, 1)),
        gqT=colT(g_q_lora[0], 6), gkvT=colT(g_kv_lora[0], 2),
        w_uq=f(w_uq[0]), w_ukv=f(w_ukv[0]), w_pf=f(w_proj_fox[0]), w_pm=f(w_proj_mla[0]),
        w_out=f(w_out[0]), w_f1=f(w_ffn_in[0]), w_f2=f(w_ffn_out[0]),
    )
    shared.update(_consts())
    maps = []
    for b in cores:
        m = dict(shared)
        m["x"] = f(x[b])
        m["cT"] = colT(c[b], 8)
        m["pos"] = np.ascontiguousarray(np.broadcast_to(np.asarray(positions[b], np.int32)[None, :], (128, S)))
        maps.append(m)
    return maps


_NC_CACHE = {}


def kernel(**inputs):
    if "nc" not in _NC_CACHE:
        _NC_CACHE["nc"] = Builder().build()
    nc = _NC_CACHE["nc"]
    maps = make_in_maps(**inputs)
    res = run_bass_kernel_spmd(nc, maps, core_ids=list(range(8)))
    out = np.stack([np.asarray(r["out"], dtype=np.float32) for r in res.results], axis=0)
    return out
```

```python
import math
import numpy as np
import ml_dtypes
import concourse.bass as bass
import concourse.mybir as mybir
from concourse.bass_utils import run_bass_kernel_spmd

F32 = mybir.dt.float32
BF16 = mybir.dt.bfloat16
I32 = mybir.dt.int32
U8 = mybir.dt.uint8
AF = mybir.ActivationFunctionType
ALU = mybir.AluOpType

S = 2048
D = 1024
NT = 16
NB = 4
KC = 8
DFF = 2816
NFC = 22
D_IN = 4648
EPS = 1e-6
C_Q, C_K, C_V, C_F, C_CQ, C_CKV, C_KR, C_GF, C_GM = 0, 512, 1024, 1536, 1544, 2312, 2568, 2600, 3624
NEG = -30000.0


class Sched:
    def __init__(self, nc, sems, dma_sems):
        self.nc = nc
        self.eng = {"pe": nc.tensor, "act": nc.scalar, "dve": nc.vector, "pool": nc.gpsimd, "sp": nc.sync}
        self.sem = sems
        self.cnt = {e: 0 for e in sems}
        self.seen = {e: {} for e in self.eng}
        self.last_w = {}
        self.readers = {}
        self.dma_sems = dma_sems
        self.dma_n = {q: 0 for q in dma_sems}
        self.inherit = {}
        self.n_wait = 0

    def _collect(self, reads, writes):
        raw, other = [], []
        for k in reads:
            w = self.last_w.get(k)
            if w is not None:
                raw.append(w)
        for k in writes:
            ps = (k[0] == "ps")
            w = self.last_w.get(k)
            if w is not None:
                other.append((w, ps))
            for t in self.readers.get(k, ()):
                other.append((t, ps))
            inh = self.inherit.get(k[0])
            if inh:
                for t in inh:
                    other.append((t, False))
        return raw, other

    def _emit_waits(self, e, raw, other):
        for tok in raw:
            if tok[2] == e and e == "pe":
                continue
            self._wait(e, tok)
        for tok, ps in other:
            if tok[2] == e:
                continue
            self._wait(e, tok)

    def _wait(self, e, tok):
        sem, val, _ = tok
        sid = id(sem)
        if self.seen[e].get(sid, 0) >= val:
            return
        self.eng[e].wait_ge(sem, val)
        self.seen[e][sid] = val
        self.n_wait += 1

    def _register(self, tok, reads, writes):
        for k in reads:
            self.readers.setdefault(k, []).append(tok)
        for k in writes:
            self.last_w[k] = tok
            self.readers[k] = []

    def op(self, e, fns, reads=(), writes=()):
        if callable(fns):
            fns = [fns]
        raw, other = self._collect(reads, writes)
        self._emit_waits(e, raw, other)
        eng = self.eng[e]
        ins = None
        for f in fns:
            ins = f(eng)
        self.cnt[e] += 1
        ins.then_inc(self.sem[e], 1)
        tok = (self.sem[e], self.cnt[e], e)
        self._register(tok, reads, writes)
        return tok

    def dma(self, q, out, in_, reads=(), writes=()):
        raw, other = self._collect(reads, writes)
        self._emit_waits(q, raw, other)
        pool = self.dma_sems[q]
        n = self.dma_n[q]
        self.dma_n[q] += 1
        sem = pool[n % len(pool)]
        rnd = n // len(pool)
        if rnd > 0:
            self._wait(q, (sem, 16 * rnd, "dma"))
        self.eng[q].dma_start(out=out, in_=in_).then_inc(sem, 16)
        tok = (sem, 16 * (rnd + 1), "dma")
        self._register(tok, reads, writes)
        return tok

    def release(self, name):
        toks = []
        for k in list(self.last_w.keys()):
            if k[0] == name:
                toks.append(self.last_w.pop(k))
        for k in list(self.readers.keys()):
            if k[0] == name:
                toks.extend(self.readers.pop(k))
        best = {}
        for t in toks:
            sid = id(t[0])
            if sid not in best or best[sid][1] < t[1]:
                best[sid] = t
        return list(best.values())


class Arena:
    def __init__(self, sched, arena_ap, nbytes):
        self.s = sched
        self.arena = arena_ap
        self.free = [(0, nbytes)]
        self.live = {}
        self.dead = []
        self.peak = 0

    def alloc(self, name, shape, dtype, parts=(0, 128)):
        esz = 4 if dtype in (F32, I32) else 2
        n = 1
        for d in shape:
            n *= d
        size = (n * esz + 63) // 64 * 64
        for i, (off, sz) in enumerate(self.free):
            if sz >= size:
                self.free[i] = (off + size, sz - size)
                break
        else:
            raise RuntimeError(f"arena OOM for {name} ({size} B); free={self.free} live={ {k: v[1] for k, v in self.live.items()} }")
        self.live[name] = (off, size)
        self.peak = max(self.peak, off + size)
        toks = []
        keep = []
        for (o, s_, tk) in self.dead:
            if o < off + size and off < o + s_:
                toks.extend(tk)
            keep.append((o, s_, tk))
        self.s.inherit[name] = toks
        ap = self.arena[parts[0]:parts[1], off // 2:(off + size) // 2]
        if esz == 4:
            ap = ap.bitcast(dtype)
        elif dtype != BF16:
            ap = ap.bitcast(dtype)
        ap = ap[:, 0:n]
        if len(shape) == 2:
            ap = ap.rearrange("p (a b) -> p a b", a=shape[0], b=shape[1])
        elif len(shape) == 3:
            ap = ap.rearrange("p (a b c) -> p a b c", a=shape[0], b=shape[1], c=shape[2])
        return ap

    def release(self, name):
        off, size = self.live.pop(name)
        toks = self.s.release(name)
        self.dead.append((off, size, toks))
        self.free.append((off, size))
        self.free.sort()
        merged = []
        for o, s_ in self.free:
            if merged and merged[-1][0] + merged[-1][1] == o:
                merged[-1] = (merged[-1][0], merged[-1][1] + s_)
            else:
                merged.append((o, s_))
        self.free = merged


class Builder:
    def __init__(self, debug=None):
        self.debug = debug or []
        nc = bass.Bass("TRN2", target_bir_lowering=False)
        self.nc = nc
        self.dbg_out = {}
        d = lambda n, sh, dt, kind="ExternalInput": nc.dram_tensor(n, list(sh), dt, kind=kind).ap()
        self.x = d("x", [S, D], F32)
        self.cT = d("cT", [128, KC], F32)
        self.pos = d("pos", [128, 512], I32)
        self.w_ada = d("w_ada", [D, 6 * D], F32)
        self.badaT = d("badaT", [128, 48], F32)
        self.bada_rep = d("bada_rep", [128, 2 * D], F32)
        self.gpost_rep = d("gpost_rep", [128, 2 * D], F32)
        self.gpreT = d("gpreT", [128, 16], F32)
        self.w_in = d("w_in", [D, D_IN], F32)
        self.nbf = d("nbf", [8, 1], F32)
        self.gqT = d("gqT", [128, 6], F32)
        self.gkvT = d("gkvT", [128, 2], F32)
        self.w_uq = d("w_uq", [768, 768], F32)
        self.w_ukv = d("w_ukv", [256, 1024], F32)
        self.w_pf = d("w_pf", [512, D], F32)
        self.w_pm = d("w_pm", [512, D], F32)
        self.w_out = d("w_out", [D, D], F32)
        self.w_f1 = d("w_f1", [D, 2 * DFF], F32)
        self.w_f2 = d("w_f2", [DFF, D], F32)
        self.c_ident = d("c_ident", [128, 128], BF16)
        self.c_identf = d("c_identf", [128, 128], F32)
        self.c_mask = d("c_mask", [128, 128], BF16)
        self.c_rope = d("c_rope", [128, 4], F32)
        self.out = d("out", [S, D], F32, kind="ExternalOutput")

    def dbg(self, name, ap, shape, dtype):
        if name in self.debug:
            o = self.nc.dram_tensor("dbg_" + name, list(shape), dtype, kind="ExternalOutput").ap()
            self.dbg_out[name] = o
            return o
        return None

    def build(self, upto=99):
        import contextlib
        nc = self.nc
        with contextlib.ExitStack() as es:
            es.enter_context(nc.allow_low_precision("bf16 matmul operands by design; fp32 accumulation"))
            es.enter_context(nc.allow_non_contiguous_dma("small strided weight / constant loads"))
            ARENA = 207 * 1024
            arena_t = es.enter_context(nc.sbuf_tensor("arena", [128, ARENA // 2], BF16))
            self.banks = [es.enter_context(nc.psum_tensor(f"ps{b}", [128, 512], F32)) for b in range(8)]
            sems = {e: es.enter_context(nc.semaphore("s_" + e)) for e in ("pe", "act", "dve", "pool")}
            dma_sems = {q: [es.enter_context(nc.semaphore(f"d{q}{i}")) for i in range(12)] for q in ("sp", "pool")}
            self.s = Sched(nc, sems, dma_sems)
            self.A = Arena(self.s, arena_t, ARENA)
            self._body(upto)
            self._finish()
        return nc

    def bank(self, b):
        return self.banks[b][:, :]

    def bankbf(self, b):
        return self.banks[b][:, :].bitcast(BF16)

    def _finish(self):
        s = self.s
        for q in ("sp", "pool"):
            pool = s.dma_sems[q]
            n = s.dma_n[q]
            for i, sem in enumerate(pool):
                cnt = (n - i + len(pool) - 1) // len(pool) if n > i else 0
                if cnt > 0:
                    s._wait("sp", (sem, 16 * cnt, "dma"))

    def dump(self, name, ap, parts, ncols, dtype, reads):
        o = self.dbg(name, ap, [parts, ncols], dtype)
        if o is not None:
            self.s.dma("sp", o, ap, reads=reads)

    def _body(self, upto):
        nc, s, A = self.nc, self.s, self.A
        bank, bankbf = self.bank, self.bankbf

        def load_const(name, src, shape, dtype, parts=(0, 128), q="pool"):
            t = A.alloc(name, shape, dtype, parts)
            s.dma(q, t, src, writes=[(name,)])
            return t

        ident = load_const("ident", self.c_ident, (128,), BF16)
        identf = load_const("identf", self.c_identf, (128,), F32)
        mask = load_const("mask", self.c_mask, (128,), BF16)
        ropec = load_const("ropec", self.c_rope, (4,), F32)
        cT = load_const("cT", self.cT, (KC,), F32)
        badaT = load_const("badaT", self.badaT, (48,), F32)
        gpreT = load_const("gpreT", self.gpreT, (16,), F32)
        gqT = load_const("gqT", self.gqT, (6,), F32)
        gkvT = load_const("gkvT", self.gkvT, (2,), F32)
        nbf = load_const("nbf", self.nbf, (1,), F32, parts=(0, 8))
        self.ident, self.mask, self.identf, self.nbf_t = ident, mask, identf, nbf
        self.gqT_t, self.gkvT_t = gqT, gkvT
        eps_c = A.alloc("eps_c", (1,), F32)
        s.op("dve", lambda e: e.memset(eps_c, EPS), writes=[("eps_c",)])
        self.eps_c = eps_c

        siluf = A.alloc("siluf", (KC,), F32)
        silub = A.alloc("silub", (KC,), BF16)
        ones_b = A.alloc("ones_b", (128,), BF16)
        modT = A.alloc("modT", (48,), F32)
        amix = A.alloc("amix", (KC,), F32)
        affn = A.alloc("affn", (KC,), F32)
        s.op("act", lambda e: e.activation(out=siluf, in_=cT, func=AF.Silu), reads=[("cT",)], writes=[("siluf",)])
        s.op("dve", lambda e: e.tensor_copy(out=silub, in_=siluf), reads=[("siluf",)], writes=[("silub",)])
        s.op("dve", lambda e: e.memset(ones_b, 1.0), writes=[("ones_b",)])
        wada = [A.alloc("wada0", (KC, D), BF16), A.alloc("wada1", (KC, D), BF16)]
        w_ada_v = self.w_ada.rearrange("(k p) n -> p k n", p=128)
        MODB = 7
        for v in (0, 1):
            s.dma("pool", wada[v], w_ada_v[:, :, v * D:(v + 1) * D], writes=[("wada%d" % v,)])

        def s0_compute():
            for v in (0, 1):
                i = v
                wn = "wada%d" % i
                for c in range(KC):
                    col = v * 8 + c
                    s.op("pe", [(lambda e, k=k: e.matmul(bank(MODB)[:, col:col + 1], wada[i][:, k, c * 128:(c + 1) * 128],
                                                        silub[:, k:k + 1], start=(k == 0), stop=(k == KC - 1)))
                                for k in range(KC)],
                         reads=[(wn,), ("silub",)], writes=[("ps", MODB)])
            s.op("dve", lambda e: e.tensor_tensor(out=modT[:, 0:16], in0=bank(MODB)[:, 0:16], in1=badaT[:, 0:16], op=ALU.add),
                 reads=[("badaT",)], writes=[("ps", MODB), ("modT",)])
            s.op("dve", lambda e: e.scalar_tensor_tensor(out=amix, in0=modT[:, 8:16], scalar=1.0, in1=gpreT[:, 0:8],
                                                         op0=ALU.add, op1=ALU.mult),
                 reads=[("modT",), ("gpreT",)], writes=[("amix",)])
            A.release("wada0"); A.release("wada1")
        self.silub, self.badaT_t, self.gpreT_t = silub, badaT, gpreT
        self.modT, self.amix, self.affn, self.ones_b, self.siluf = modT, amix, affn, ones_b, siluf
        if upto <= 0:
            return

        w_in_v0 = self.w_in.rearrange("(k p) n -> p k n", p=128)
        self.wv_t = A.alloc("wv", (KC, 512), BF16)
        s.dma("pool", self.wv_t, w_in_v0[:, :, C_V:C_V + 512], writes=[("wv",)])
        self.wmisc_t = A.alloc("wmisc", (KC, 128), BF16)
        s.op("dve", lambda e: e.memset(self.wmisc_t.rearrange("p a b -> p (a b)"), 0.0), writes=[("wmisc", i) for i in range(4)])
        s.dma("pool", self.wmisc_t[:, :, 0:8], w_in_v0[:, :, C_F:C_F + 8], writes=[("wmisc", 0)])
        s.dma("pool", self.wmisc_t[:, :, 64:96], w_in_v0[:, :, C_KR:C_KR + 32], writes=[("wmisc", 1)])
        s.dma("pool", self.wmisc_t[:, :, 96:112], w_in_v0[:, :, C_KR + 16:C_KR + 32], writes=[("wmisc", 2)])
        s.dma("pool", self.wmisc_t[:, :, 112:128], w_in_v0[:, :, C_KR:C_KR + 16], writes=[("wmisc", 3)])
        hT = A.alloc("hT", (KC, S), BF16)
        self.hT = hT
        ss = A.alloc("ss", (NT,), F32)
        rstd = A.alloc("rstd", (NT,), F32)
        self.ss, self.rstd = ss, rstd
        self.prenorm(self.x, None, None, hT, "hT", ss, rstd, "amix", list(range(NB)), 0)
        s0_compute()
        for c in range(KC):
            hk = [("hT", c, T) for T in range(NB)]
            if c % 2 == 0:
                s.op("act", lambda e: e.activation(out=hT[:, c, :], in_=hT[:, c, :], func=AF.Identity,
                                                   bias=modT[:, c:c + 1], scale=amix[:, c:c + 1]),
                     reads=hk + [("amix",), ("modT",)], writes=hk)
            else:
                s.op("dve", lambda e: e.tensor_scalar(out=hT[:, c, :], in0=hT[:, c, :], scalar1=amix[:, c:c + 1],
                                                      scalar2=modT[:, c:c + 1], op0=ALU.mult, op1=ALU.add),
                     reads=hk + [("amix",), ("modT",)], writes=hk)
        self.dump("hT0", hT[:, 0, :], 128, S, BF16, [("hT", 0, T) for T in range(NB)])
        self.dump("hT7", hT[:, 7, :], 128, S, BF16, [("hT", 7, T) for T in range(NB)])
        R = (64, 96)
        CC = A.alloc("CC", (S,), BF16, parts=R)
        SS = A.alloc("SS", (S,), BF16, parts=R)
        posi = A.alloc("posi", (512,), I32)
        posf = A.alloc("posf", (512,), F32)
        kf = A.alloc("kf", (512,), F32)
        ki = A.alloc("ki", (512,), I32)
        tq = A.alloc("tabq", (512,), BF16)
        s.dma("pool", posi, self.pos[:, 0:512], writes=[("posi",)])
        s.op("dve", lambda e: e.tensor_copy(out=posf, in_=posi), reads=[("posi",)], writes=[("posf",)])
        ang = posi.bitcast(F32)
        for (tab, name, c0) in ((CC, "CC", 0), (SS, "SS", 2)):
            s.op("dve", lambda e: e.tensor_scalar(out=ang, in0=posf, scalar1=ropec[:, c0:c0 + 1],
                                                  scalar2=ropec[:, c0 + 1:c0 + 2], op0=ALU.mult, op1=ALU.add),
                 reads=[("posf",), ("ropec",)], writes=[("posi",)])
            s.op("dve", lambda e: e.tensor_scalar(out=kf, in0=ang, scalar1=1.0 / (2.0 * math.pi), scalar2=None,
                                                  op0=ALU.mult),
                 reads=[("posi",)], writes=[("kf",)])
            s.op("dve", lambda e: e.tensor_copy(out=ki, in_=kf), reads=[("kf",)], writes=[("ki",)])
            s.op("dve", lambda e: e.tensor_copy(out=kf, in_=ki), reads=[("ki",)], writes=[("kf",)])
            s.op("dve", lambda e: e.scalar_tensor_tensor(out=ang, in0=kf, scalar=-2.0 * math.pi, in1=ang,
                                                         op0=ALU.mult, op1=ALU.add),
                 reads=[("kf",), ("posi",)], writes=[("posi",)])
            s.op("dve", lambda e: e.tensor_scalar(out=kf, in0=ang, scalar1=math.pi, scalar2=-2.0 * math.pi,
                                                  op0=ALU.is_gt, op1=ALU.mult),
                 reads=[("posi",)], writes=[("kf",)])
            s.op("dve", lambda e: e.tensor_tensor(out=ang, in0=ang, in1=kf, op=ALU.add),
                 reads=[("posi",), ("kf",)], writes=[("posi",)])
            s.op("dve", lambda e: e.tensor_scalar(out=ang, in0=ang, scalar1=-math.pi, scalar2=math.pi,
                                                  op0=ALU.max, op1=ALU.min),
                 reads=[("posi",)], writes=[("posi",)])
            s.op("act", lambda e: e.activation(out=tq, in_=ang, func=AF.Sin),
                 reads=[("posi",)], writes=[("tabq",)])
            for q in range(4):
                s.dma("pool", tab[:, q * 512:(q + 1) * 512], tq[q * 32:(q + 1) * 32, :], reads=[("tabq",)], writes=[(name, q)])
        self.CC, self.SS = CC, SS
        self.dump("CC", CC, 32, S, BF16, [("CC", q) for q in range(4)])
        self.dump("SS", SS, 32, S, BF16, [("SS", q) for q in range(4)])
        for nme in ("posi", "posf", "kf", "ki", "tabq"):
            A.release(nme)
        if upto <= 1:
            return
        self._body2(upto)

    def prenorm(self, src_rows, a_sc, shift_sc, dst, dname, ss, rstd, aname, Ts, tok0, src_sbuf=None):
        s, A = self.s, self.A
        nxt = 8 if src_sbuf is None else 4
        xts = [A.alloc("xt%d" % i, (D,), F32) for i in range(nxt)] if src_sbuf is None else None
        tbanks = (0, 1, 2, 3, 6, 7) if src_sbuf is None else (6, 7)
        ntb = 0
        xb = [A.alloc("xb%d" % i, (D,), BF16) for i in range(4)]
        junk = A.alloc("junk", (D,), BF16)
        for T in Ts:
            for i in range(4):
                it = 4 * T + i
                if src_sbuf is None:
                    xt, xk = xts[it % nxt], ("xt%d" % (it % nxt),)
                    s.dma("sp", xt, src_rows[it * 128:(it + 1) * 128, :], writes=[xk])
                else:
                    xt, xk = src_sbuf(it)
                s.op("act", lambda e: e.activation(out=junk, in_=xt, func=AF.Square, accum_out=ss[:, it:it + 1]),
                     reads=[xk], writes=[("junk",), ("ss", it)])
            sl4 = slice(4 * T, 4 * T + 4)
            s.op("act", lambda e: e.activation(out=rstd[:, sl4], in_=ss[:, sl4], func=AF.Ln, bias=self.eps_c, scale=1.0 / D),
                 reads=[("ss", 4 * T + i) for i in range(4)] + [("eps_c",)], writes=[("rstd", T)])
            s.op("act", lambda e: e.activation(out=rstd[:, sl4], in_=rstd[:, sl4], func=AF.Exp, scale=-0.5),
                 reads=[("rstd", T)], writes=[("rstd", T)])
            for i in range(4):
                it = 4 * T + i
                xt, xk = (xts[it % nxt], ("xt%d" % (it % nxt),)) if src_sbuf is None else src_sbuf(it)
                s.op("dve", lambda e: e.tensor_scalar(out=xb[i], in0=xt, scalar1=rstd[:, it:it + 1], scalar2=None,
                                                      op0=ALU.mult),
                     reads=[xk, ("rstd", T)], writes=[("xb%d" % i,)])
            for c in range(KC):
                b = tbanks[ntb % len(tbanks)]
                ntb += 1
                psb = self.bankbf(b)
                s.op("pe", [(lambda e, i=i: e.transpose(out=psb[:, i * 128:(i + 1) * 128],
                                                        in_=xb[i][:, c * 128:(c + 1) * 128], identity=self.ident))
                            for i in range(4)],
                     reads=[("xb%d" % i,) for i in range(4)] + [("ident",)], writes=[("ps", b)])
                dsl = dst[:, c, (T - tok0) * 512:(T - tok0 + 1) * 512]
                if a_sc is None:
                    if c % 2 == 0:
                        s.op("act", lambda e: e.activation(out=dsl, in_=psb[:, 0:512], func=AF.Copy),
                             writes=[("ps", b), (dname, c, T)])
                    else:
                        s.op("dve", lambda e: e.tensor_copy(out=dsl, in_=psb[:, 0:512]),
                             writes=[("ps", b), (dname, c, T)])
                elif c % 2 == 0:
                    s.op("act", lambda e: e.activation(out=dsl, in_=psb[:, 0:512], func=AF.Identity,
                                                       bias=shift_sc[:, c:c + 1], scale=a_sc[:, c:c + 1]),
                         reads=[(aname,), ("modT",), ("modT2",)], writes=[("ps", b), (dname, c, T if dname == "hT" else 0)])
                else:
                    s.op("dve", lambda e: e.tensor_scalar(out=dsl, in0=psb[:, 0:512], scalar1=a_sc[:, c:c + 1],
                                                          scalar2=shift_sc[:, c:c + 1], op0=ALU.mult, op1=ALU.add),
                         reads=[(aname,), ("modT",), ("modT2",)], writes=[("ps", b), (dname, c, T if dname == "hT" else 0)])
        if xts is not None:
            for i in range(nxt):
                A.release("xt%d" % i)
        for i in range(4):
            A.release("xb%d" % i)
        A.release("junk")

    def _body2(self, upto):
        nc, s, A = self.nc, self.s, self.A
        bank, bankbf = self.bank, self.bankbf
        hT, CC, SS = self.hT, self.CC, self.SS
        w_in_v = self.w_in.rearrange("(k p) n -> p k n", p=128)
        hkeys = lambda T: [("hT", k, T) for k in range(KC)]
        TS = lambda T: slice(T * 512, (T + 1) * 512)
        mm = lambda out, l, r, st, sp: (lambda e: e.matmul(out, l, r, start=st, stop=sp))

        oT = A.alloc("oT", (4, S), BF16)
        qk = [{n: A.alloc(n + str(st), (S,), BF16, parts=(0, 96)) for n in ("qA", "qB", "kA", "kB")} for st in range(2)]
        vaug = A.alloc("vaug", (NT, 4, 192), BF16)
        self.pT = [A.alloc("pT%d" % i, (512,), BF16) for i in range(6)]
        rec_t = A.alloc("rec0", (512,), F32)
        self.rec = [rec_t, rec_t]
        self.qk, self.vaug, self.oT = qk, vaug, oT
        self.att_cnt = 0

        s.op("dve", lambda e: e.memset(vaug.rearrange("p a b c -> p (a b c)"), 1.0),
             writes=[("vaug", it, e_) for it in range(NT) for e_ in range(3)])
        wv = self.wv_t

        def v_proj(lhs_of, nk, wt, wname, rkeys):
            for it in range(NT):
                b = 6 + it % 2
                s.op("pe", [mm(bank(b), lhs_of(k, it), wt[:, k, :], k == 0, k == nk - 1) for k in range(nk)],
                     reads=rkeys(it // 4) + [(wname,)], writes=[("ps", b)])
                pv = bank(b).rearrange("p (c e d) -> p c e d", c=4, e=2, d=64)
                s.op("act", lambda e: e.activation(out=vaug[:, it, :, 0:64], in_=pv[:, :, 0, :], func=AF.Copy),
                     writes=[("ps", b), ("vaug", it, 0)])
                s.op("dve", lambda e: e.tensor_copy(out=vaug[:, it, :, 128:192], in_=pv[:, :, 1, :]),
                     writes=[("ps", b), ("vaug", it, 1)])

        v_proj(lambda k, it: hT[:, k, it * 128:(it + 1) * 128], KC, wv, "wv", hkeys)
        A.release("wv")

        wmisc = self.wmisc_t
        P8 = (0, 8)
        nbneg = A.alloc("nbneg", (1,), F32, parts=P8)
        eT = A.alloc("eT", (512,), F32, parts=P8)
        nlf = A.alloc("nlf", (512,), F32, parts=P8)
        onesf = A.alloc("onesf", (512,), F32, parts=P8)
        G = A.alloc("G", (S,), F32, parts=P8)
        negGb = A.alloc("negGb", (S,), BF16, parts=P8)
        Gtok = A.alloc("Gtok", (128,), F32)
        R = (64, 96)
        kpe = A.alloc("kpe", (S,), BF16, parts=R)
        t1 = A.alloc("t1", (512,), BF16, parts=R)
        t2 = A.alloc("t2", (512,), BF16, parts=R)
        self.t1, self.t2, self.Gtok, self.kpe_t = t1, t2, Gtok, kpe
        s.op("dve", lambda e: e.tensor_scalar(out=nbneg, in0=self.nbf_t, scalar1=-1.0, scalar2=None, op0=ALU.mult),
             reads=[("nbf",)], writes=[("nbneg",)])
        s.op("dve", lambda e: e.memset(onesf, 1.0), writes=[("onesf",)])
        for T in range(NB):
            b = 6 + T % 2
            s.op("pe", [mm(bank(b), wmisc[:, k, :], hT[:, k, TS(T)], k == 0, k == KC - 1) for k in range(KC)],
                 reads=hkeys(T) + [("wmisc", i) for i in range(4)], writes=[("ps", b)])
            s.op("act", lambda e: e.activation(out=eT, in_=bank(b)[0:8, :], func=AF.Exp, bias=nbneg, scale=-1.0),
                 reads=[("nbneg",)], writes=[("ps", b), ("eT",)])
            s.op("act", lambda e: e.activation(out=nlf, in_=eT, func=AF.Ln, bias=1.0), reads=[("eT",)], writes=[("nlf",)])
            init = 0.0 if T == 0 else G[:, T * 512 - 1:T * 512]
            s.op("dve", lambda e: e.tensor_tensor_scan(out=G[:, TS(T)], data0=onesf, data1=nlf, initial=init,
                                                       op0=ALU.mult, op1=ALU.add),
                 reads=[("nlf",), ("onesf",)] + ([("G", T - 1)] if T else []), writes=[("G", T)])
            s.op("dve", lambda e: e.tensor_tensor(out=t1, in0=bank(b)[64:96, :], in1=CC[:, TS(T)], op=ALU.mult),
                 reads=[("CC", q_) for q_ in range(4)], writes=[("ps", b), ("t1",)])
            s.op("dve", lambda e: e.tensor_tensor(out=t2, in0=bank(b)[96:128, :], in1=SS[:, TS(T)], op=ALU.mult),
                 reads=[("SS", q_) for q_ in range(4)], writes=[("ps", b), ("t2",)])
            s.op("dve", lambda e: e.tensor_tensor(out=kpe[:, TS(T)], in0=t1, in1=t2, op=ALU.add),
                 reads=[("t1",), ("t2",)], writes=[("kpe", T)])
        s.op("dve", lambda e: e.tensor_scalar(out=negGb, in0=G, scalar1=-1.0, scalar2=None, op0=ALU.mult),
             reads=[("G", T) for T in range(NB)], writes=[("negGb",)])
        GB = 5
        s.op("pe", [(lambda e, it=it: e.transpose(out=bank(GB)[:, it * 8:(it + 1) * 8], in_=G[:, it * 128:(it + 1) * 128],
                                                  identity=self.identf[0:8, 0:8])) for it in range(NT)],
             reads=[("G", T) for T in range(NB)] + [("identf",)], writes=[("ps", GB)])
        s.op("dve", lambda e: e.tensor_copy(out=Gtok, in_=bank(GB)[:, 0:128]), writes=[("ps", GB), ("Gtok",)])
        self.dump("G", G, 8, S, F32, [("G", T) for T in range(NB)])
        self.dump("Gtok", Gtok, 128, 128, F32, [("Gtok",)])
        self.dump("kpe", kpe, 32, S, BF16, [("kpe", T) for T in range(NB)])
        A.release("wmisc"); A.release("eT"); A.release("nlf"); A.release("onesf")
        if upto <= 2:
            return

        wf = [A.alloc("wf0", (KC, 256), BF16), A.alloc("wf1", (KC, 256), BF16)]
        for st in range(2):
            for n in ("kA", "kB"):
                s.op("dve", lambda e: e.memset(qk[st][n][64:65, :], 1.0), writes=[(n + str(st), "g")])

        self.sb_cnt = 0

        def sbank():
            self.sb_cnt += 1
            return 6 + self.sb_cnt % 2

        self.sbank = sbank

        def fox_proj(c):
            st = c % 2
            Q = qk[st]
            w, wn = wf[c % 2], "wf%d" % (c % 2)
            s.dma("pool", w[:, :, 0:128], w_in_v[:, :, C_Q + c * 128:C_Q + (c + 1) * 128], writes=[(wn, "q")])
            s.dma("pool", w[:, :, 128:256], w_in_v[:, :, C_K + c * 128:C_K + (c + 1) * 128], writes=[(wn, "k")])
            s.dma("pool", Q["qA"][64:65, :], negGb[2 * c:2 * c + 1, :], reads=[("negGb",)], writes=[("qA%d" % st, "g")])
            s.dma("pool", Q["qB"][64:65, :], negGb[2 * c + 1:2 * c + 2, :], reads=[("negGb",)], writes=[("qB%d" % st, "g")])
            yield
            for T in range(NB):
                b6 = sbank()
                s.op("pe", [mm(bank(b6), w[:, k, 0:128], hT[:, k, TS(T)], k == 0, k == KC - 1) for k in range(KC)],
                     reads=hkeys(T) + [(wn, "q")], writes=[("ps", b6)])
                s.op("dve", lambda e: e.tensor_scalar(out=Q["qA"][0:64, TS(T)], in0=bank(b6)[0:64, :], scalar1=0.125, scalar2=None, op0=ALU.mult),
                     writes=[("ps", b6), ("qA%d" % st, T)])
                s.op("dve", lambda e: e.tensor_scalar(out=Q["qB"][0:64, TS(T)], in0=bank(b6)[64:128, :], scalar1=0.125, scalar2=None, op0=ALU.mult),
                     writes=[("ps", b6), ("qB%d" % st, T)])
                yield
                b7 = sbank()
                s.op("pe", [mm(bank(b7), w[:, k, 128:256], hT[:, k, TS(T)], k == 0, k == KC - 1) for k in range(KC)],
                     reads=hkeys(T) + [(wn, "k")], writes=[("ps", b7)])
                s.op("dve", lambda e: e.tensor_copy(out=Q["kA"][0:64, TS(T)], in_=bank(b7)[0:64, :]),
                     writes=[("ps", b7), ("kA%d" % st, T)])
                s.op("dve", lambda e: e.tensor_copy(out=Q["kB"][0:64, TS(T)], in_=bank(b7)[64:128, :]),
                     writes=[("ps", b7), ("kB%d" % st, T)])
                yield

        for _ in fox_proj(0):
            pass
        self.dump("qA", qk[0]["qA"][0:65, :], 65, S, BF16, [("qA0", T) for T in range(NB)] + [("qA0", "g")])
        self.dump("kA", qk[0]["kA"][0:65, :], 65, S, BF16, [("kA0", T) for T in range(NB)] + [("kA0", "g")])
        for c in range(4):
            bg = fox_proj(c + 1) if c < 3 else None
            self.attention(c, True, c % 2, bg)
        for c in range(4):
            self.dump("oaT%d" % c, oT[:, c, :], 128, S, BF16, [("oT", c, T, hh) for T in range(NB) for hh in range(2)])
        A.release("wf0"); A.release("wf1"); A.release("G"); A.release("negGb"); A.release("nbneg")
        if upto <= 3:
            return
        self._body3(upto)

    def attention(self, c, fox, st=0, bg=None):
        s = self.s
        bank = self.bank
        qk, vaug, pT, rec = self.qk[st], self.vaug, self.pT, self.rec
        oT, on = (self.oT, "oT") if fox else (self.oT2, "oT2")
        nstep = 0
        KD = 65 if fox else 96
        scale = 1.0 if fox else 1.0 / math.sqrt(96.0)
        SB = (0, 1, 2, 3)
        LA = 2
        NPT = len(pT)
        for T in range(NB):
            nj = 4 * T + 4
            info = {}
            accb = 4

            def emit_S(hh, j):
                h = 2 * c + hh
                qn, kn = ("qA", "kA") if hh == 0 else ("qB", "kB")
                q, k = qk[qn], qk[kn]
                qn, kn = qn + str(st), kn + str(st)
                off = 0 if j < 4 * T else (j - 4 * T) * 128
                w = 512 - off
                n = self.att_cnt
                self.att_cnt += 1
                b = SB[n % len(SB)]
                pt, ptn = pT[n % NPT], "pT%d" % (n % NPT)
                diag = j >= 4 * T
                fns = [lambda e: e.matmul(bank(b)[:, 0:w], k[0:KD, j * 128:(j + 1) * 128],
                                          q[0:KD, T * 512 + off:(T + 1) * 512], start=True, stop=not diag)]
                if diag:
                    fns.append(lambda e: e.matmul(bank(b)[:, 0:128], self.ident, self.mask, start=False, stop=True))
                s.op("pe", fns, reads=[(qn, T), (qn, "g"), (kn, j // 4), (kn, "g"), ("ident",), ("mask",)],
                     writes=[("ps", b)])
                if fox:
                    s.op("act", lambda e: e.activation(out=pt[:, 0:w], in_=bank(b)[:, 0:w], func=AF.Exp,
                                                       bias=self.Gtok[:, j * 8 + h:j * 8 + h + 1], scale=1.0),
                         reads=[("Gtok",)], writes=[("ps", b), (ptn,)])
                else:
                    s.op("act", lambda e: e.activation(out=pt[:, 0:w], in_=bank(b)[:, 0:w], func=AF.Exp, scale=scale),
                         writes=[("ps", b), (ptn,)])
                info[(hh, j)] = (off, w, pt, ptn)

            def emit_PV(hh, j):
                off, w, pt, ptn = info[(hh, j)]
                acc = accb + hh
                vsl = slice(0, 128) if hh == 0 else slice(64, 192)
                s.op("pe", lambda e: e.matmul(bank(acc)[:, off:512], vaug[:, j, c, vsl], pt[:, 0:w],
                                              start=(j == 0), stop=(j == nj - 1)),
                     reads=[(ptn,), ("vaug", j, 0), ("vaug", j, 1), ("vaug", j, 2)], writes=[("ps", acc)])

            for step in range(nj + LA):
                for hh in (0, 1):
                    if step < nj:
                        emit_S(hh, step)
                    if step - LA >= 0:
                        emit_PV(hh, step - LA)
                nstep += 1
                if bg is not None and nstep % 4 == 0:
                    next(bg, None)
            for hh in (0, 1):
                acc = accb + hh
                r = rec[hh]
                if hh == 0:
                    osl, dsl = slice(0, 64), slice(64, 128)
                else:
                    osl, dsl = slice(64, 128), slice(0, 64)
                s.op("act", lambda e: e.activation(out=r[osl, :], in_=bank(acc)[dsl, :], func=AF.Ln),
                     writes=[("ps", acc), ("rec0", hh)])
                s.op("act", lambda e: e.activation(out=r[osl, :], in_=r[osl, :], func=AF.Exp, scale=-1.0),
                     reads=[("rec0", hh)], writes=[("rec0", hh)])
                s.op("dve", lambda e: e.tensor_tensor(out=oT[osl, c, T * 512:(T + 1) * 512], in0=bank(acc)[osl, :],
                                                      in1=r[osl, :], op=ALU.mult),
                     reads=[("rec0", hh)], writes=[("ps", acc), (on, c, T, hh)])
        if bg is not None:
            for _ in bg:
                pass

    def gate_merge(self, col0, w_proj, first, oT, on):
        s, A = self.s, self.A
        bank = self.bank
        hT, merged = self.hT, self.merged
        mm = lambda out, l, r, st, sp: (lambda e: e.matmul(out, l, r, start=st, stop=sp))
        w_in_v = self.w_in.rearrange("(k p) n -> p k n", p=128)
        wg, wgn, wp, wpn = self.gate_w[0 if first else 1]
        sg = [A.alloc("sg%d" % i, (512,), BF16) for i in range(2)]
        tmp = [A.alloc("gtmp%d" % i, (512,), BF16) for i in range(2)]
        n = 0
        for T in range(NB):
            Tsl = slice(T * 512, (T + 1) * 512)
            for m in range(KC):
                i = n % 2
                n += 1
                bg, bp = 0 + i, 2 + i
                s.op("pe", [mm(bank(bg), wg[:, k, m * 128:(m + 1) * 128], hT[:, k, Tsl], k == 0, k == KC - 1) for k in range(KC)],
                     reads=[("hT", k, T) for k in range(KC)] + [(wgn,)], writes=[("ps", bg)])
                s.op("act", lambda e: e.activation(out=sg[i], in_=bank(bg), func=AF.Sigmoid),
                     writes=[("ps", bg), ("sg%d" % i,)])
                s.op("pe", [mm(bank(bp), wp[:, cc, m * 128:(m + 1) * 128], oT[:, cc, Tsl], cc == 0, cc == 3) for cc in range(4)],
                     reads=[(on, cc, T, hh) for cc in range(4) for hh in range(2)] + [(wpn,)], writes=[("ps", bp)])
                if first:
                    s.op("dve", lambda e: e.tensor_tensor(out=merged[:, m, Tsl], in0=bank(bp), in1=sg[i], op=ALU.mult),
                         reads=[("sg%d" % i,)], writes=[("ps", bp), ("merged", m, T)])
                else:
                    s.op("dve", lambda e: e.tensor_tensor(out=tmp[i], in0=bank(bp), in1=sg[i], op=ALU.mult),
                         reads=[("sg%d" % i,)], writes=[("ps", bp), ("gtmp%d" % i,)])
                    s.op("dve", lambda e: e.tensor_tensor(out=merged[:, m, Tsl], in0=merged[:, m, Tsl], in1=tmp[i], op=ALU.add),
                         reads=[("gtmp%d" % i,), ("merged", m, T)], writes=[("merged", m, T)])
        for nme in (wgn, wpn, "sg0", "sg1", "gtmp0", "gtmp1"):
            A.release(nme)

    def _body3(self, upto):
        nc, s, A = self.nc, self.s, self.A
        bank = self.bank
        hT, CC, SS, qk, vaug = self.hT, self.CC, self.SS, self.qk, self.vaug
        t1, t2 = self.t1, self.t2
        w_in_v = self.w_in.rearrange("(k p) n -> p k n", p=128)
        hkeys = lambda T: [("hT", k, T) for k in range(KC)]
        TS = lambda T: slice(T * 512, (T + 1) * 512)
        mm = lambda out, l, r, st, sp: (lambda e: e.matmul(out, l, r, start=st, stop=sp))


        cqn = A.alloc("cqn", (6, S), BF16)
        ckvn = A.alloc("ckvn", (2, S), BF16)
        sqb = [A.alloc("sqb%d" % i, (512,), BF16) for i in range(2)]
        rstdb = A.alloc("rstdb", (512,), F32)
        wcq = A.alloc("wcq", (KC, 768), BF16)
        wckv = A.alloc("wckv", (KC, 256), BF16)
        s.dma("pool", wcq, w_in_v[:, :, C_CQ:C_CQ + 768], writes=[("wcq",)])
        s.dma("pool", wckv, w_in_v[:, :, C_CKV:C_CKV + 256], writes=[("wckv",)])
        for (wt, wn, nm, dst, dn, gT, gname, nfeat) in ((wcq, "wcq", 6, cqn, "cqn", self.gqT_t, "gqT", 768.0),
                                                        (wckv, "wckv", 2, ckvn, "ckvn", self.gkvT_t, "gkvT", 256.0)):
            for T in range(NB):
                for m in range(nm):
                    b = 6 + m % 2
                    i = m % 2
                    s.op("pe", [mm(bank(b), wt[:, k, m * 128:(m + 1) * 128], hT[:, k, TS(T)], k == 0, k == KC - 1) for k in range(KC)],
                         reads=hkeys(T) + [(wn,)], writes=[("ps", b)])
                    s.op("act", lambda e: e.activation(out=sqb[i], in_=bank(b), func=AF.Square),
                         writes=[("ps", b), ("sqb%d" % i,)])
                    s.op("dve", lambda e: e.tensor_scalar(out=dst[:, m, TS(T)], in0=bank(b), scalar1=gT[:, m:m + 1],
                                                          scalar2=None, op0=ALU.mult),
                         reads=[(gname,)], writes=[("ps", b), (dn, m, T)])
                    s.op("pe", mm(bank(5), self.ones_b, sqb[i], m == 0, m == nm - 1),
                         reads=[("sqb%d" % i,), ("ones_b",)], writes=[("ps", 5)])
                s.op("act", lambda e: e.activation(out=rstdb, in_=bank(5), func=AF.Ln, bias=self.eps_c, scale=1.0 / nfeat),
                     reads=[("eps_c",)], writes=[("ps", 5), ("rstdb",)])
                s.op("act", lambda e: e.activation(out=rstdb, in_=rstdb, func=AF.Exp, scale=-0.5),
                     reads=[("rstdb",)], writes=[("rstdb",)])
                for m in range(nm):
                    s.op("dve", lambda e: e.tensor_tensor(out=dst[:, m, TS(T)], in0=dst[:, m, TS(T)], in1=rstdb, op=ALU.mult),
                         reads=[(dn, m, T), ("rstdb",)], writes=[(dn, m, T)])
        self.dump("cqn0", cqn[:, 0, :], 128, S, BF16, [("cqn", 0, T) for T in range(NB)])
        self.dump("ckvn1", ckvn[:, 1, :], 128, S, BF16, [("ckvn", 1, T) for T in range(NB)])
        A.release("wcq"); A.release("wckv"); A.release("sqb0"); A.release("sqb1"); A.release("rstdb")

        oT2 = A.alloc("oT2", (4, S), BF16)
        self.oT2 = oT2
        wuq = A.alloc("wuq", (6, 8, 128), BF16)
        w_uq_v = self.w_uq.rearrange("(k p) (h e) -> p k h e", p=128, e=96)
        for kc in range(6):
            s.dma("pool", wuq[:, kc, :, 0:96], w_uq_v[:, kc, :, :], writes=[("wuq", 0, kc)])
            s.dma("pool", wuq[:, kc, :, 96:112], w_uq_v[:, kc, :, 80:96], writes=[("wuq", 1, kc)])
            s.dma("pool", wuq[:, kc, :, 112:128], w_uq_v[:, kc, :, 64:80], writes=[("wuq", 2, kc)])
        wkn = A.alloc("wkn", (2, 512), BF16)
        wvm = A.alloc("wvm", (2, 512), BF16)
        w_ukv_v = self.w_ukv.rearrange("(k p) (h e) -> p k h e", p=128, e=128)
        for kc in range(2):
            s.dma("pool", wkn[:, kc, :].rearrange("p (h e) -> p h e", e=64), w_ukv_v[:, kc, :, 0:64], writes=[("wkn", kc)])
            s.dma("pool", wvm[:, kc, :].rearrange("p (h e) -> p h e", e=64), w_ukv_v[:, kc, :, 64:128], writes=[("wvm", kc)])

        for it in range(NT):
            b = 6 + it % 2
            T = it // 4
            s.op("pe", [mm(bank(b), ckvn[:, k, it * 128:(it + 1) * 128], wvm[:, k, :], k == 0, k == 1) for k in range(2)],
                 reads=[("ckvn", k, T) for k in range(2)] + [("wvm", 0), ("wvm", 1)], writes=[("ps", b)])
            pv = bank(b).rearrange("p (c e d) -> p c e d", c=4, e=2, d=64)
            s.op("act", lambda e: e.activation(out=vaug[:, it, :, 0:64], in_=pv[:, :, 0, :], func=AF.Copy),
                 writes=[("ps", b), ("vaug", it, 0)])
            s.op("dve", lambda e: e.tensor_copy(out=vaug[:, it, :, 128:192], in_=pv[:, :, 1, :]),
                 writes=[("ps", b), ("vaug", it, 1)])

        wuq_keys = [("wuq", i, kc) for i in range(3) for kc in range(6)]

        def mla_proj(c):
            st = c % 2
            Q = qk[st]
            for kn in ("kA", "kB"):
                s.op("dve", lambda e: e.tensor_copy(out=Q[kn][64:96, :], in_=self.kpe_t),
                     reads=[("kpe", T) for T in range(NB)], writes=[(kn + str(st), "g")])
            yield
            nb = 0
            for T in range(NB):
                b7 = self.sbank()
                s.op("pe", [mm(bank(b7), wkn[:, k, c * 128:(c + 1) * 128], ckvn[:, k, TS(T)], k == 0, k == 1) for k in range(2)],
                     reads=[("ckvn", k, T) for k in range(2)] + [("wkn", 0), ("wkn", 1)], writes=[("ps", b7)])
                s.op("dve", lambda e: e.tensor_copy(out=Q["kA"][0:64, TS(T)], in_=bank(b7)[0:64, :]),
                     writes=[("ps", b7), ("kA%d" % st, T)])
                s.op("dve", lambda e: e.tensor_copy(out=Q["kB"][0:64, TS(T)], in_=bank(b7)[64:128, :]),
                     writes=[("ps", b7), ("kB%d" % st, T)])
                yield
                for hh in (0, 1):
                    h = 2 * c + hh
                    qn = ("qA" if hh == 0 else "qB")
                    q = Q[qn]
                    qn = qn + str(st)
                    b6 = self.sbank()
                    s.op("pe", [mm(bank(b6), wuq[:, k, h, :], cqn[:, k, TS(T)], k == 0, k == 5) for k in range(6)],
                         reads=[("cqn", k, T) for k in range(6)] + wuq_keys, writes=[("ps", b6)])
                    s.op("dve", lambda e: e.tensor_copy(out=q[0:64, TS(T)], in_=bank(b6)[0:64, :]),
                         writes=[("ps", b6), (qn, T)])
                    s.op("dve", lambda e: e.tensor_tensor(out=t1, in0=bank(b6)[64:96, :], in1=CC[:, TS(T)], op=ALU.mult),
                         reads=[("CC", q_) for q_ in range(4)], writes=[("ps", b6), ("t1",)])
                    s.op("dve", lambda e: e.tensor_tensor(out=t2, in0=bank(b6)[96:128, :], in1=SS[:, TS(T)], op=ALU.mult),
                         reads=[("SS", q_) for q_ in range(4)], writes=[("ps", b6), ("t2",)])
                    s.op("dve", lambda e: e.tensor_tensor(out=q[64:96, TS(T)], in0=t1, in1=t2, op=ALU.add),
                         reads=[("t1",), ("t2",)], writes=[(qn, T)])
                    yield

        for _ in mla_proj(0):
            pass
        self.dump("qm0", qk[0]["qA"][0:96, :], 96, S, BF16, [("qA0", T) for T in range(NB)])
        self.dump("km0", qk[0]["kA"][0:96, :], 96, S, BF16, [("kA0", T) for T in range(NB)] + [("kA0", "g")])
        for c in range(4):
            bg = mla_proj(c + 1) if c < 3 else None
            if c == 3:
                for nme in ("wuq", "wkn", "wvm", "cqn", "ckvn", "kpe", "t1", "t2", "CC", "SS"):
                    A.release(nme)
                self.gvec_prefetch()
            self.attention(c, False, c % 2, bg)
        for c in range(4):
            self.dump("obT%d" % c, oT2[:, c, :], 128, S, BF16, [("oT2", c, T, hh) for T in range(NB) for hh in range(2)])
        for nme in ("qA0", "qB0", "kA0", "kB0", "qA1", "qB1", "kA1", "kB1", "vaug", "pT0", "pT1", "pT2", "pT3", "pT4", "pT5",
                    "rec0", "Gtok"):
            A.release(nme)
        if upto <= 4:
            return

        merged = A.alloc("merged", (KC, S), BF16)
        self.merged = merged
        self.gate_w = []
        for i, (col0, wproj) in enumerate(((C_GF, self.w_pf), (C_GM, self.w_pm))):
            wg = A.alloc("wg%d" % i, (KC, D), BF16)
            wp = A.alloc("wp%d" % i, (4, D), BF16)
            s.dma("pool", wg, w_in_v[:, :, col0:col0 + D], writes=[("wg%d" % i,)])
            s.dma("pool", wp, wproj.rearrange("(k p) n -> p k n", p=128), writes=[("wp%d" % i,)])
            self.gate_w.append((wg, "wg%d" % i, wp, "wp%d" % i))
            if i == 0:
                self.gvec_compute()
        self.gate_merge(C_GF, self.w_pf, True, self.oT, "oT")
        self.gate_merge(C_GM, self.w_pm, False, self.oT2, "oT2")
        for m in (0, 7):
            self.dump("mg%d" % m, merged[:, m, :], 128, S, BF16, [("merged", m, T) for T in range(NB)])
        A.release("hT"); A.release("oT"); A.release("oT2")
        if upto <= 5:
            return
        self._body4(upto)

    def gvec_prefetch(self):
        s, A = self.s, self.A
        self.gvec = A.alloc("gvec", (2, D), F32)
        self.silurep = A.alloc("silurep", (KC, 128), BF16)
        self.bada_rep_t = A.alloc("bada_rep", (2 * D,), F32)
        self.gpost_rep_t = A.alloc("gpost_rep", (2 * D,), F32)
        s.dma("pool", self.bada_rep_t, self.bada_rep, writes=[("bada_rep",)])
        s.dma("pool", self.gpost_rep_t, self.gpost_rep, writes=[("gpost_rep",)])
        w_ada_v = self.w_ada.rearrange("(k p) n -> p k n", p=128)
        self.wadag = {}
        for v in (3, 4):
            for hf in range(2):
                nm = "wadag%d%d" % (v, hf)
                t = A.alloc(nm, (KC, 512), BF16)
                s.dma("pool", t, w_ada_v[:, :, v * D + hf * 512:v * D + (hf + 1) * 512], writes=[(nm,)])
                self.wadag[(v, hf)] = (t, nm)

    def gvec_compute(self):
        s, A = self.s, self.A
        bank = self.bank
        gvec, silurep, bada_rep, gpost_rep = self.gvec, self.silurep, self.bada_rep_t, self.gpost_rep_t
        for k in range(KC):
            s.op("dve", lambda e: e.tensor_scalar(out=silurep[:, k, :], in0=self.ones_b, scalar1=self.siluf[:, k:k + 1],
                                                  scalar2=None, op0=ALU.mult),
                 reads=[("ones_b",), ("siluf",)], writes=[("silurep", k)])
        modT, affn = self.modT, self.affn
        MODB = 7
        for v in (3, 4):
            for c in range(KC):
                col = v * 8 + c
                wt, wn = self.wadag[(v, c // 4)]
                cc = c % 4
                s.op("pe", [(lambda e, k=k: e.matmul(bank(MODB)[:, col:col + 1], wt[:, k, cc * 128:(cc + 1) * 128],
                                                    self.silub[:, k:k + 1], start=(k == 0), stop=(k == KC - 1)))
                            for k in range(KC)],
                     reads=[(wn,), ("silub",)], writes=[("ps", MODB)])
        s.op("dve", lambda e: e.tensor_tensor(out=modT[:, 24:40], in0=bank(MODB)[:, 24:40], in1=self.badaT_t[:, 24:40], op=ALU.add),
             reads=[("badaT",)], writes=[("ps", MODB), ("modT2",)])
        s.op("dve", lambda e: e.scalar_tensor_tensor(out=affn, in0=modT[:, 32:40], scalar=1.0, in1=self.gpreT_t[:, 8:16],
                                                     op0=ALU.add, op1=ALU.mult),
             reads=[("modT2",), ("gpreT",)], writes=[("affn",)])
        w_ada_v = self.w_ada.rearrange("(k p) n -> p k n", p=128)
        for (v, vs) in ((2, 3), (5, 4)):
            for hf in range(2):
                t, nm = self.wadag[(vs, hf)]
                s.dma("pool", t, w_ada_v[:, :, v * D + hf * 512:v * D + (hf + 1) * 512], writes=[(nm,)])
                self.wadag[(v, hf)] = (t, nm)
        for g, v in enumerate((2, 5)):
            for half in range(2):
                wt, wn = self.wadag[(v, half)]
                b = 5 + half
                s.op("pe", [(lambda e, k=k: e.matmul(bank(b), silurep[:, k, :], wt[:, k, :],
                                                    start=(k == 0), stop=(k == KC - 1))) for k in range(KC)],
                     reads=[(wn,)] + [("silurep", k) for k in range(KC)], writes=[("ps", b)])
                sl = slice(g * D + half * 512, g * D + (half + 1) * 512)
                hs = slice(half * 512, (half + 1) * 512)
                s.op("dve", lambda e: e.tensor_tensor(out=gvec[:, g, hs], in0=bank(b), in1=bada_rep[:, sl], op=ALU.add),
                     reads=[("bada_rep",)], writes=[("ps", b), ("gvec", g, half)])
                s.op("dve", lambda e: e.tensor_tensor(out=gvec[:, g, hs], in0=gvec[:, g, hs], in1=gpost_rep[:, sl], op=ALU.mult),
                     reads=[("gpost_rep",), ("gvec", g, half)], writes=[("gvec", g, half)])
        self.dump("gvec", gvec.rearrange("p a b -> p (a b)"), 128, 2 * D, F32, [("gvec", g, h) for g in range(2) for h in range(2)])
        for n in ["silurep", "bada_rep", "gpost_rep"] + sorted(set(nm for (_, nm) in self.wadag.values())):
            A.release(n)

    def _body4(self, upto):
        nc, s, A = self.nc, self.s, self.A
        bank = self.bank
        merged, gvec = self.merged, self.gvec
        mm = lambda out, l, r, st, sp: (lambda e: e.matmul(out, l, r, start=st, stop=sp))
        wout = A.alloc("wout", (KC, D), BF16)
        s.dma("pool", wout, self.w_out.rearrange("(k p) n -> p k n", p=128), writes=[("wout",)])
        w2 = A.alloc("w2", (NFC, D), BF16)
        w2v = self.w_f2.rearrange("(j p) n -> p j n", p=128)
        w1v = self.w_f1.rearrange("(k p) n -> p k n", p=128)
        x2 = A.alloc("x2", (4, D), F32)
        h2T = A.alloc("h2T", (KC, 512), BF16)
        actT = A.alloc("actT", (NFC, 512), BF16)
        w1b = [A.alloc("w1b%d" % i, (KC, 512), BF16) for i in range(3)]
        sgl = [A.alloc("sgl%d" % i, (512,), BF16) for i in range(2)]
        xin = [A.alloc("xin%d" % i, (D,), F32) for i in range(2)]
        ot = [A.alloc("ot%d" % i, (D,), F32) for i in range(2)]
        tmpf = [A.alloc("tmpf%d" % i, (512,), F32) for i in range(2)]
        junkp = A.alloc("junkp", (512,), BF16)
        ssq = A.alloc("ssq", (4 * NT,), F32)
        ssy = A.alloc("ssy", (2 * NT,), F32)
        rsy = A.alloc("rsy", (2 * NT,), F32)
        ss, rstd = self.ss, self.rstd
        w2_loaded = [False]

        def postnorm(bks, col, g, resid, rkeys, out_ap, okeys):
            for half, b in enumerate(bks):
                s.op("act", lambda e: e.activation(out=junkp, in_=bank(b), func=AF.Square,
                                                   accum_out=ssq[:, 2 * col + half:2 * col + half + 1]),
                     writes=[("ps", b), ("junkp",), ("ssq", col, half)])
            s.op("dve", lambda e: e.tensor_tensor(out=ssy[:, col:col + 1], in0=ssq[:, 2 * col:2 * col + 1],
                                                  in1=ssq[:, 2 * col + 1:2 * col + 2], op=ALU.add),
                 reads=[("ssq", col, 0), ("ssq", col, 1)], writes=[("ssy", col)])
            s.op("act", lambda e: e.activation(out=rsy[:, col:col + 1], in_=ssy[:, col:col + 1], func=AF.Ln,
                                               bias=self.eps_c, scale=1.0 / D),
                 reads=[("ssy", col), ("eps_c",)], writes=[("rsy", col)])
            s.op("act", lambda e: e.activation(out=rsy[:, col:col + 1], in_=rsy[:, col:col + 1], func=AF.Exp, scale=-0.5),
                 reads=[("rsy", col)], writes=[("rsy", col)])
            for half, b in enumerate(bks):
                hs = slice(half * 512, (half + 1) * 512)
                s.op("dve", lambda e: e.scalar_tensor_tensor(out=tmpf[half], in0=bank(b), scalar=rsy[:, col:col + 1],
                                                             in1=gvec[:, g, hs], op0=ALU.mult, op1=ALU.mult),
                     reads=[("rsy", col), ("gvec", g, half)], writes=[("ps", b), ("tmpf%d" % half,)])
                s.op("dve", lambda e: e.tensor_tensor(out=out_ap[:, hs], in0=tmpf[half], in1=resid[:, hs], op=ALU.add),
                     reads=[("tmpf%d" % half,)] + rkeys, writes=okeys)

        nw1 = 0
        import os
        for T in [int(t_) for t_ in os.environ.get("TLIST", "0,1,2,3").split(",")]:
            for i in range(4):
                it = 4 * T + i
                xi, xn = xin[it % 2], "xin%d" % (it % 2)
                s.dma("sp", xi, self.x[it * 128:(it + 1) * 128, :], writes=[(xn,)])
                bks = (0, 1) if it % 2 == 0 else (2, 3)
                for half, b in enumerate(bks):
                    s.op("pe", [mm(bank(b), merged[:, k, it * 128:(it + 1) * 128], wout[:, k, half * 512:(half + 1) * 512],
                                   k == 0, k == KC - 1) for k in range(KC)],
                         reads=[("merged", k, T) for k in range(KC)] + [("wout",)], writes=[("ps", b)])
                postnorm(bks, it, 0, xi, [(xn,)], x2[:, i, :], [("x2", i)])
            if T == 0:
                self.dump("x2", x2[:, 0, :], 128, D, F32, [("x2", 0)])
            if upto <= 6:
                return
            if not w2_loaded[0]:
                for gi, g0 in enumerate(range(0, NFC, 6)):
                    g1 = min(NFC, g0 + 6)
                    s.dma("pool", w2[:, g0:g1, :], w2v[:, g0:g1, :], writes=[("w2", gi)])
                w2_loaded[0] = True
            self.prenorm(None, self.affn, self.modT[:, 24:32], h2T, "h2T", ss, rstd, "affn", [T], T,
                         src_sbuf=lambda it: (x2[:, it % 4, :], ("x2", it % 4)))
            for j in range(NFC):
                if j % 2 == 0:
                    wb, wbn = w1b[nw1 % 3], "w1b%d" % (nw1 % 3)
                    nw1 += 1
                    s.dma("pool", wb[:, :, 0:256], w1v[:, :, j * 128:(j + 2) * 128], writes=[(wbn, "g")])
                    s.dma("pool", wb[:, :, 256:512], w1v[:, :, DFF + j * 128:DFF + (j + 2) * 128], writes=[(wbn, "u")])
                jo = (j % 2) * 128
                gb, ub = 4 + 2 * (j % 2), 5 + 2 * (j % 2)
                hk = [("h2T", k, 0) for k in range(KC)]
                s.op("pe", [mm(bank(gb), wb[:, k, jo:jo + 128], h2T[:, k, :], k == 0, k == KC - 1) for k in range(KC)],
                     reads=hk + [(wbn, "g")], writes=[("ps", gb)])
                s.op("pe", [mm(bank(ub), wb[:, k, 256 + jo:256 + jo + 128], h2T[:, k, :], k == 0, k == KC - 1) for k in range(KC)],
                     reads=hk + [(wbn, "u")], writes=[("ps", ub)])
                s.op("act", lambda e: e.activation(out=sgl[j % 2], in_=bank(gb), func=AF.Silu),
                     writes=[("ps", gb), ("sgl%d" % (j % 2),)])
                s.op("dve", lambda e: e.tensor_tensor(out=actT[:, j, :], in0=bank(ub), in1=sgl[j % 2], op=ALU.mult),
                     reads=[("sgl%d" % (j % 2),)], writes=[("ps", ub), ("actT", j)])
            if upto <= 7:
                return
            for i in range(4):
                it = 4 * T + i
                bks = (0, 1) if it % 2 == 0 else (2, 3)
                for half, b in enumerate(bks):
                    s.op("pe", [mm(bank(b), actT[:, j, i * 128:(i + 1) * 128], w2[:, j, half * 512:(half + 1) * 512],
                                   j == 0, j == NFC - 1) for j in range(NFC)],
                         reads=[("actT", j) for j in range(NFC)] + [("w2", gi) for gi in range(4)], writes=[("ps", b)])
                o_t, on = ot[it % 2], "ot%d" % (it % 2)
                postnorm(bks, NT + it, 1, x2[:, i, :], [("x2", i)], o_t, [(on,)])
                s.dma("sp", self.out[it * 128:(it + 1) * 128, :], o_t, reads=[(on,)])
            if upto <= 8:
                return


def _consts():
    ident = np.eye(128, dtype=np.float32)
    sk = np.arange(128)[:, None]
    tq = np.arange(128)[None, :]
    mask = np.where(sk > tq, NEG, 0.0).astype(np.float32)
    inv_freq = 1.0 / (10000.0 ** (np.arange(0, 32, 2, dtype=np.float32) / 32.0))
    rope = np.zeros((128, 4), np.float32)
    for p in range(128):
        r = p % 32
        i = r % 16
        rope[p, 0] = inv_freq[i]
        rope[p, 1] = math.pi / 2
        rope[p, 2] = inv_freq[i]
        rope[p, 3] = (math.pi if r < 16 else 0.0)
    return dict(c_ident=ident.astype(ml_dtypes.bfloat16), c_identf=ident, c_mask=mask.astype(ml_dtypes.bfloat16),
                c_rope=rope)


def make_in_maps(x, c, positions, w_ada, b_ada, g_pre_mix, g_post_mix, g_pre_ffn, g_post_ffn,
                 w_in, b_forget, g_q_lora, w_uq, g_kv_lora, w_ukv, w_proj_fox, w_proj_mla,
                 w_out, w_ffn_in, w_ffn_out, cores=range(8)):
    f = lambda a: np.ascontiguousarray(np.asarray(a, dtype=np.float32))
    colT = lambda v, n: f(np.asarray(v, np.float32).reshape(n, 128).T)
    rep = lambda v: f(np.broadcast_to(np.asarray(v, np.float32)[None, :], (128, v.shape[0])))
    b_ada0 = np.asarray(b_ada[0], np.float32)
    shared = dict(
        w_ada=f(w_ada[0]), badaT=colT(b_ada0, 48),
        bada_rep=rep(np.concatenate([b_ada0[2 * D:3 * D], b_ada0[5 * D:6 * D]])),
        gpost_rep=rep(np.concatenate([np.asarray(g_post_mix[0], np.float32), np.asarray(g_post_ffn[0], np.float32)])),
        gpreT=f(np.concatenate([colT(g_pre_mix[0], 8), colT(g_pre_ffn[0], 8)], axis=1)),
        w_in=f(w_in[0]), nbf=f(np.asarray(b_forget[0], np.float32).reshape(8, 1)),
        gqT=colT(g_q_lora[0], 6), gkvT=colT(g_kv_lora[0], 2),
        w_uq=f(w_uq[0]), w_ukv=f(w_ukv[0]), w_pf=f(w_proj_fox[0]), w_pm=f(w_proj_mla[0]),
        w_out=f(w_out[0]), w_f1=f(w_ffn_in[0]), w_f2=f(w_ffn_out[0]),
    )
    shared.update(_consts())
    maps = []
    for b in cores:
        m = dict(shared)
        m["x"] = f(x[b])
        m["cT"] = colT(c[b], 8)
        pq = np.asarray(positions[b], np.int32).reshape(4, 1, 512)
        m["pos"] = np.ascontiguousarray(np.broadcast_to(pq, (4, 32, 512)).reshape(128, 512))
        maps.append(m)
    return maps


_NC_CACHE = {}


def kernel(**inputs):
    if "nc" not in _NC_CACHE:
        _NC_CACHE["nc"] = Builder().build()
    nc = _NC_CACHE["nc"]
    maps = make_in_maps(**inputs)
    res = run_bass_kernel_spmd(nc, maps, core_ids=list(range(8)))
    out = np.stack([np.asarray(r["out"], dtype=np.float32) for r in res.results], axis=0)
    return out
```

```python
import math
import numpy as np
import ml_dtypes
import concourse.bass as bass
import concourse.mybir as mybir
from concourse.bass_utils import run_bass_kernel_spmd

F32 = mybir.dt.float32
BF16 = mybir.dt.bfloat16
I32 = mybir.dt.int32
U8 = mybir.dt.uint8
AF = mybir.ActivationFunctionType
ALU = mybir.AluOpType

S = 2048
D = 1024
NT = 16
NB = 4
KC = 8
DFF = 2816
NFC = 22
D_IN = 4648
EPS = 1e-6
C_Q, C_K, C_V, C_F, C_CQ, C_CKV, C_KR, C_GF, C_GM = 0, 512, 1024, 1536, 1544, 2312, 2568, 2600, 3624
NEG = -30000.0


class Sched:
    def __init__(self, nc, sems, dma_sems):
        self.nc = nc
        self.eng = {"pe": nc.tensor, "act": nc.scalar, "dve": nc.vector, "pool": nc.gpsimd, "sp": nc.sync}
        self.sem = sems
        self.cnt = {e: 0 for e in sems}
        self.seen = {e: {} for e in self.eng}
        self.last_w = {}
        self.readers = {}
        self.dma_sems = dma_sems
        self.dma_n = {q: 0 for q in dma_sems}
        self.inherit = {}
        self.n_wait = 0

    def _collect(self, reads, writes):
        raw, other = [], []
        for k in reads:
            w = self.last_w.get(k)
            if w is not None:
                raw.append(w)
        for k in writes:
            ps = (k[0] == "ps")
            w = self.last_w.get(k)
            if w is not None:
                other.append((w, ps))
            for t in self.readers.get(k, ()):
                other.append((t, ps))
            inh = self.inherit.get(k[0])
            if inh:
                for t in inh:
                    other.append((t, False))
        return raw, other

    def _emit_waits(self, e, raw, other):
        for tok in raw:
            if tok[2] == e and e == "pe":
                continue
            self._wait(e, tok)
        for tok, ps in other:
            if tok[2] == e:
                continue
            self._wait(e, tok)

    def _wait(self, e, tok):
        sem, val, _ = tok
        sid = id(sem)
        if self.seen[e].get(sid, 0) >= val:
            return
        self.eng[e].wait_ge(sem, val)
        self.seen[e][sid] = val
        self.n_wait += 1

    def _register(self, tok, reads, writes):
        for k in reads:
            self.readers.setdefault(k, []).append(tok)
        for k in writes:
            self.last_w[k] = tok
            self.readers[k] = []

    def op(self, e, fns, reads=(), writes=()):
        if callable(fns):
            fns = [fns]
        raw, other = self._collect(reads, writes)
        self._emit_waits(e, raw, other)
        eng = self.eng[e]
        ins = None
        for f in fns:
            ins = f(eng)
        self.cnt[e] += 1
        ins.then_inc(self.sem[e], 1)
        tok = (self.sem[e], self.cnt[e], e)
        self._register(tok, reads, writes)
        return tok

    def dma(self, q, out, in_, reads=(), writes=()):
        raw, other = self._collect(reads, writes)
        self._emit_waits(q, raw, other)
        pool = self.dma_sems[q]
        n = self.dma_n[q]
        self.dma_n[q] += 1
        sem = pool[n % len(pool)]
        rnd = n // len(pool)
        if rnd > 0:
            self._wait(q, (sem, 16 * rnd, "dma"))
        self.eng[q].dma_start(out=out, in_=in_).then_inc(sem, 16)
        tok = (sem, 16 * (rnd + 1), "dma")
        self._register(tok, reads, writes)
        return tok

    def release(self, name):
        toks = []
        for k in list(self.last_w.keys()):
            if k[0] == name:
                toks.append(self.last_w.pop(k))
        for k in list(self.readers.keys()):
            if k[0] == name:
                toks.extend(self.readers.pop(k))
        best = {}
        for t in toks:
            sid = id(t[0])
            if sid not in best or best[sid][1] < t[1]:
                best[sid] = t
        return list(best.values())


class Arena:
    def __init__(self, sched, arena_ap, nbytes):
        self.s = sched
        self.arena = arena_ap
        self.free = [(0, nbytes)]
        self.live = {}
        self.dead = []
        self.peak = 0

    def alloc(self, name, shape, dtype, parts=(0, 128)):
        esz = 4 if dtype in (F32, I32) else 2
        n = 1
        for d in shape:
            n *= d
        size = (n * esz + 63) // 64 * 64
        for i, (off, sz) in enumerate(self.free):
            if sz >= size:
                self.free[i] = (off + size, sz - size)
                break
        else:
            raise RuntimeError(f"arena OOM for {name} ({size} B); free={self.free} live={ {k: v[1] for k, v in self.live.items()} }")
        self.live[name] = (off, size)
        self.peak = max(self.peak, off + size)
        toks = []
        keep = []
        for (o, s_, tk) in self.dead:
            if o < off + size and off < o + s_:
                toks.extend(tk)
            keep.append((o, s_, tk))
        self.s.inherit[name] = toks
        ap = self.arena[parts[0]:parts[1], off // 2:(off + size) // 2]
        if esz == 4:
            ap = ap.bitcast(dtype)
        elif dtype != BF16:
            ap = ap.bitcast(dtype)
        ap = ap[:, 0:n]
        if len(shape) == 2:
            ap = ap.rearrange("p (a b) -> p a b", a=shape[0], b=shape[1])
        elif len(shape) == 3:
            ap = ap.rearrange("p (a b c) -> p a b c", a=shape[0], b=shape[1], c=shape[2])
        return ap

    def release(self, name):
        off, size = self.live.pop(name)
        toks = self.s.release(name)
        self.dead.append((off, size, toks))
        self.free.append((off, size))
        self.free.sort()
        merged = []
        for o, s_ in self.free:
            if merged and merged[-1][0] + merged[-1][1] == o:
                merged[-1] = (merged[-1][0], merged[-1][1] + s_)
            else:
                merged.append((o, s_))
        self.free = merged


class Builder:
    def __init__(self, debug=None):
        self.debug = debug or []
        nc = bass.Bass("TRN2", target_bir_lowering=False)
        self.nc = nc
        self.dbg_out = {}
        d = lambda n, sh, dt, kind="ExternalInput": nc.dram_tensor(n, list(sh), dt, kind=kind).ap()
        self.x = d("x", [S, D], F32)
        self.cT = d("cT", [128, KC], F32)
        self.pos = d("pos", [128, 512], I32)
        self.w_ada = d("w_ada", [D, 6 * D], F32)
        self.badaT = d("badaT", [128, 48], F32)
        self.bada_rep = d("bada_rep", [128, 2 * D], F32)
        self.gpost_rep = d("gpost_rep", [128, 2 * D], F32)
        self.gpreT = d("gpreT", [128, 16], F32)
        self.w_in = d("w_in", [D, D_IN], F32)
        self.nbf = d("nbf", [8, 1], F32)
        self.gqT = d("gqT", [128, 6], F32)
        self.gkvT = d("gkvT", [128, 2], F32)
        self.w_uq = d("w_uq", [768, 768], F32)
        self.w_ukv = d("w_ukv", [256, 1024], F32)
        self.w_pf = d("w_pf", [512, D], F32)
        self.w_pm = d("w_pm", [512, D], F32)
        self.w_out = d("w_out", [D, D], F32)
        self.w_f1 = d("w_f1", [D, 2 * DFF], F32)
        self.w_f2 = d("w_f2", [DFF, D], F32)
        self.c_ident = d("c_ident", [128, 128], BF16)
        self.c_identf = d("c_identf", [128, 128], F32)
        self.c_mask = d("c_mask", [128, 128], BF16)
        self.c_rope = d("c_rope", [128, 4], F32)
        self.out = d("out", [S, D], F32, kind="ExternalOutput")

    def dbg(self, name, ap, shape, dtype):
        if name in self.debug:
            o = self.nc.dram_tensor("dbg_" + name, list(shape), dtype, kind="ExternalOutput").ap()
            self.dbg_out[name] = o
            return o
        return None

    def build(self, upto=99):
        import contextlib
        nc = self.nc
        with contextlib.ExitStack() as es:
            es.enter_context(nc.allow_low_precision("bf16 matmul operands by design; fp32 accumulation"))
            es.enter_context(nc.allow_non_contiguous_dma("small strided weight / constant loads"))
            ARENA = 207 * 1024
            arena_t = es.enter_context(nc.sbuf_tensor("arena", [128, ARENA // 2], BF16))
            self.banks = [es.enter_context(nc.psum_tensor(f"ps{b}", [128, 512], F32)) for b in range(8)]
            sems = {e: es.enter_context(nc.semaphore("s_" + e)) for e in ("pe", "act", "dve", "pool")}
            dma_sems = {q: [es.enter_context(nc.semaphore(f"d{q}{i}")) for i in range(12)] for q in ("sp", "pool")}
            self.s = Sched(nc, sems, dma_sems)
            self.A = Arena(self.s, arena_t, ARENA)
            self._body(upto)
            self._finish()
        return nc

    def bank(self, b):
        return self.banks[b][:, :]

    def bankbf(self, b):
        return self.banks[b][:, :].bitcast(BF16)

    def _finish(self):
        s = self.s
        for q in ("sp", "pool"):
            pool = s.dma_sems[q]
            n = s.dma_n[q]
            for i, sem in enumerate(pool):
                cnt = (n - i + len(pool) - 1) // len(pool) if n > i else 0
                if cnt > 0:
                    s._wait("sp", (sem, 16 * cnt, "dma"))

    def dump(self, name, ap, parts, ncols, dtype, reads):
        o = self.dbg(name, ap, [parts, ncols], dtype)
        if o is not None:
            self.s.dma("sp", o, ap, reads=reads)

    def _body(self, upto):
        nc, s, A = self.nc, self.s, self.A
        bank, bankbf = self.bank, self.bankbf

        def load_const(name, src, shape, dtype, parts=(0, 128), q="pool"):
            t = A.alloc(name, shape, dtype, parts)
            s.dma(q, t, src, writes=[(name,)])
            return t

        ident = load_const("ident", self.c_ident, (128,), BF16)
        identf = load_const("identf", self.c_identf, (128,), F32)
        mask = load_const("mask", self.c_mask, (128,), BF16)
        ropec = load_const("ropec", self.c_rope, (4,), F32)
        cT = load_const("cT", self.cT, (KC,), F32)
        badaT = load_const("badaT", self.badaT, (48,), F32)
        gpreT = load_const("gpreT", self.gpreT, (16,), F32)
        gqT = load_const("gqT", self.gqT, (6,), F32)
        gkvT = load_const("gkvT", self.gkvT, (2,), F32)
        nbf = load_const("nbf", self.nbf, (1,), F32, parts=(0, 8))
        self.ident, self.mask, self.identf, self.nbf_t = ident, mask, identf, nbf
        self.gqT_t, self.gkvT_t = gqT, gkvT
        eps_c = A.alloc("eps_c", (1,), F32)
        s.op("dve", lambda e: e.memset(eps_c, EPS), writes=[("eps_c",)])
        self.eps_c = eps_c

        siluf = A.alloc("siluf", (KC,), F32)
        silub = A.alloc("silub", (KC,), BF16)
        ones_b = A.alloc("ones_b", (128,), BF16)
        modT = A.alloc("modT", (48,), F32)
        amix = A.alloc("amix", (KC,), F32)
        affn = A.alloc("affn", (KC,), F32)
        s.op("act", lambda e: e.activation(out=siluf, in_=cT, func=AF.Silu), reads=[("cT",)], writes=[("siluf",)])
        s.op("dve", lambda e: e.tensor_copy(out=silub, in_=siluf), reads=[("siluf",)], writes=[("silub",)])
        s.op("dve", lambda e: e.memset(ones_b, 1.0), writes=[("ones_b",)])
        wada = [A.alloc("wada0", (KC, D), BF16), A.alloc("wada1", (KC, D), BF16)]
        w_ada_v = self.w_ada.rearrange("(k p) n -> p k n", p=128)
        MODB = 7
        for v in (0, 1):
            s.dma("pool", wada[v], w_ada_v[:, :, v * D:(v + 1) * D], writes=[("wada%d" % v,)])

        def s0_compute():
            for v in (0, 1):
                i = v
                wn = "wada%d" % i
                for c in range(KC):
                    col = v * 8 + c
                    s.op("pe", [(lambda e, k=k: e.matmul(bank(MODB)[:, col:col + 1], wada[i][:, k, c * 128:(c + 1) * 128],
                                                        silub[:, k:k + 1], start=(k == 0), stop=(k == KC - 1)))
                                for k in range(KC)],
                         reads=[(wn,), ("silub",)], writes=[("ps", MODB)])
            s.op("dve", lambda e: e.tensor_tensor(out=modT[:, 0:16], in0=bank(MODB)[:, 0:16], in1=badaT[:, 0:16], op=ALU.add),
                 reads=[("badaT",)], writes=[("ps", MODB), ("modT",)])
            s.op("dve", lambda e: e.scalar_tensor_tensor(out=amix, in0=modT[:, 8:16], scalar=1.0, in1=gpreT[:, 0:8],
                                                         op0=ALU.add, op1=ALU.mult),
                 reads=[("modT",), ("gpreT",)], writes=[("amix",)])
            A.release("wada0"); A.release("wada1")
        self.silub, self.badaT_t, self.gpreT_t = silub, badaT, gpreT
        self.modT, self.amix, self.affn, self.ones_b, self.siluf = modT, amix, affn, ones_b, siluf
        if upto <= 0:
            return

        w_in_v0 = self.w_in.rearrange("(k p) n -> p k n", p=128)
        self.wv_t = A.alloc("wv", (KC, 512), BF16)
        s.dma("pool", self.wv_t, w_in_v0[:, :, C_V:C_V + 512], writes=[("wv",)])
        self.wmisc_t = A.alloc("wmisc", (KC, 128), BF16)
        s.op("dve", lambda e: e.memset(self.wmisc_t.rearrange("p a b -> p (a b)"), 0.0), writes=[("wmisc", i) for i in range(4)])
        s.dma("pool", self.wmisc_t[:, :, 0:8], w_in_v0[:, :, C_F:C_F + 8], writes=[("wmisc", 0)])
        s.dma("pool", self.wmisc_t[:, :, 64:96], w_in_v0[:, :, C_KR:C_KR + 32], writes=[("wmisc", 1)])
        s.dma("pool", self.wmisc_t[:, :, 96:112], w_in_v0[:, :, C_KR + 16:C_KR + 32], writes=[("wmisc", 2)])
        s.dma("pool", self.wmisc_t[:, :, 112:128], w_in_v0[:, :, C_KR:C_KR + 16], writes=[("wmisc", 3)])
        hT = A.alloc("hT", (KC, S), BF16)
        self.hT = hT
        ss = A.alloc("ss", (NT,), F32)
        rstd = A.alloc("rstd", (NT,), F32)
        self.ss, self.rstd = ss, rstd
        self.prenorm(self.x, None, None, hT, "hT", ss, rstd, "amix", list(range(NB)), 0)
        s0_compute()
        for c in range(KC):
            hk = [("hT", c, T) for T in range(NB)]
            if c % 2 == 0:
                s.op("act", lambda e: e.activation(out=hT[:, c, :], in_=hT[:, c, :], func=AF.Identity,
                                                   bias=modT[:, c:c + 1], scale=amix[:, c:c + 1]),
                     reads=hk + [("amix",), ("modT",)], writes=hk)
            else:
                s.op("dve", lambda e: e.tensor_scalar(out=hT[:, c, :], in0=hT[:, c, :], scalar1=amix[:, c:c + 1],
                                                      scalar2=modT[:, c:c + 1], op0=ALU.mult, op1=ALU.add),
                     reads=hk + [("amix",), ("modT",)], writes=hk)
        self.dump("hT0", hT[:, 0, :], 128, S, BF16, [("hT", 0, T) for T in range(NB)])
        self.dump("hT7", hT[:, 7, :], 128, S, BF16, [("hT", 7, T) for T in range(NB)])
        R = (64, 96)
        CC = A.alloc("CC", (S,), BF16, parts=R)
        SS = A.alloc("SS", (S,), BF16, parts=R)
        posi = A.alloc("posi", (512,), I32)
        posf = A.alloc("posf", (512,), F32)
        kf = A.alloc("kf", (512,), F32)
        ki = A.alloc("ki", (512,), I32)
        tq = A.alloc("tabq", (512,), BF16)
        s.dma("pool", posi, self.pos[:, 0:512], writes=[("posi",)])
        s.op("dve", lambda e: e.tensor_copy(out=posf, in_=posi), reads=[("posi",)], writes=[("posf",)])
        ang = posi.bitcast(F32)
        for (tab, name, c0) in ((CC, "CC", 0), (SS, "SS", 2)):
            s.op("dve", lambda e: e.tensor_scalar(out=ang, in0=posf, scalar1=ropec[:, c0:c0 + 1],
                                                  scalar2=ropec[:, c0 + 1:c0 + 2], op0=ALU.mult, op1=ALU.add),
                 reads=[("posf",), ("ropec",)], writes=[("posi",)])
            s.op("dve", lambda e: e.tensor_scalar(out=kf, in0=ang, scalar1=1.0 / (2.0 * math.pi), scalar2=None,
                                                  op0=ALU.mult),
                 reads=[("posi",)], writes=[("kf",)])
            s.op("dve", lambda e: e.tensor_copy(out=ki, in_=kf), reads=[("kf",)], writes=[("ki",)])
            s.op("dve", lambda e: e.tensor_copy(out=kf, in_=ki), reads=[("ki",)], writes=[("kf",)])
            s.op("dve", lambda e: e.scalar_tensor_tensor(out=ang, in0=kf, scalar=-2.0 * math.pi, in1=ang,
                                                         op0=ALU.mult, op1=ALU.add),
                 reads=[("kf",), ("posi",)], writes=[("posi",)])
            s.op("dve", lambda e: e.tensor_scalar(out=kf, in0=ang, scalar1=math.pi, scalar2=-2.0 * math.pi,
                                                  op0=ALU.is_gt, op1=ALU.mult),
                 reads=[("posi",)], writes=[("kf",)])
            s.op("dve", lambda e: e.tensor_tensor(out=ang, in0=ang, in1=kf, op=ALU.add),
                 reads=[("posi",), ("kf",)], writes=[("posi",)])
            s.op("dve", lambda e: e.tensor_scalar(out=ang, in0=ang, scalar1=-math.pi, scalar2=math.pi,
                                                  op0=ALU.max, op1=ALU.min),
                 reads=[("posi",)], writes=[("posi",)])
            s.op("act", lambda e: e.activation(out=tq, in_=ang, func=AF.Sin),
                 reads=[("posi",)], writes=[("tabq",)])
            for q in range(4):
                s.dma("pool", tab[:, q * 512:(q + 1) * 512], tq[q * 32:(q + 1) * 32, :], reads=[("tabq",)], writes=[(name, q)])
        self.CC, self.SS = CC, SS
        self.dump("CC", CC, 32, S, BF16, [("CC", q) for q in range(4)])
        self.dump("SS", SS, 32, S, BF16, [("SS", q) for q in range(4)])
        for nme in ("posi", "posf", "kf", "ki", "tabq"):
            A.release(nme)
        if upto <= 1:
            return
        self._body2(upto)

    def prenorm(self, src_rows, a_sc, shift_sc, dst, dname, ss, rstd, aname, Ts, tok0, src_sbuf=None):
        s, A = self.s, self.A
        nxt = 8 if src_sbuf is None else 4
        xts = [A.alloc("xt%d" % i, (D,), F32) for i in range(nxt)] if src_sbuf is None else None
        tbanks = (0, 1, 2, 3, 6, 7) if src_sbuf is None else (6, 7)
        ntb = 0
        xb = [A.alloc("xb%d" % i, (D,), BF16) for i in range(4)]
        junk = A.alloc("junk", (D,), BF16)
        for T in Ts:
            for i in range(4):
                it = 4 * T + i
                if src_sbuf is None:
                    xt, xk = xts[it % nxt], ("xt%d" % (it % nxt),)
                    s.dma("sp", xt, src_rows[it * 128:(it + 1) * 128, :], writes=[xk])
                else:
                    xt, xk = src_sbuf(it)
                s.op("act", lambda e: e.activation(out=junk, in_=xt, func=AF.Square, accum_out=ss[:, it:it + 1]),
                     reads=[xk], writes=[("junk",), ("ss", it)])
            sl4 = slice(4 * T, 4 * T + 4)
            s.op("act", lambda e: e.activation(out=rstd[:, sl4], in_=ss[:, sl4], func=AF.Ln, bias=self.eps_c, scale=1.0 / D),
                 reads=[("ss", 4 * T + i) for i in range(4)] + [("eps_c",)], writes=[("rstd", T)])
            s.op("act", lambda e: e.activation(out=rstd[:, sl4], in_=rstd[:, sl4], func=AF.Exp, scale=-0.5),
                 reads=[("rstd", T)], writes=[("rstd", T)])
            for i in range(4):
                it = 4 * T + i
                xt, xk = (xts[it % nxt], ("xt%d" % (it % nxt),)) if src_sbuf is None else src_sbuf(it)
                s.op("dve", lambda e: e.tensor_scalar(out=xb[i], in0=xt, scalar1=rstd[:, it:it + 1], scalar2=None,
                                                      op0=ALU.mult),
                     reads=[xk, ("rstd", T)], writes=[("xb%d" % i,)])
            for c in range(KC):
                b = tbanks[ntb % len(tbanks)]
                ntb += 1
                psb = self.bankbf(b)
                s.op("pe", [(lambda e, i=i: e.transpose(out=psb[:, i * 128:(i + 1) * 128],
                                                        in_=xb[i][:, c * 128:(c + 1) * 128], identity=self.ident))
                            for i in range(4)],
                     reads=[("xb%d" % i,) for i in range(4)] + [("ident",)], writes=[("ps", b)])
                dsl = dst[:, c, (T - tok0) * 512:(T - tok0 + 1) * 512]
                if a_sc is None:
                    if c % 2 == 0:
                        s.op("act", lambda e: e.activation(out=dsl, in_=psb[:, 0:512], func=AF.Copy),
                             writes=[("ps", b), (dname, c, T)])
                    else:
                        s.op("dve", lambda e: e.tensor_copy(out=dsl, in_=psb[:, 0:512]),
                             writes=[("ps", b), (dname, c, T)])
                elif c % 2 == 0:
                    s.op("act", lambda e: e.activation(out=dsl, in_=psb[:, 0:512], func=AF.Identity,
                                                       bias=shift_sc[:, c:c + 1], scale=a_sc[:, c:c + 1]),
                         reads=[(aname,), ("modT",), ("modT2",)], writes=[("ps", b), (dname, c, T if dname == "hT" else 0)])
                else:
                    s.op("dve", lambda e: e.tensor_scalar(out=dsl, in0=psb[:, 0:512], scalar1=a_sc[:, c:c + 1],
                                                          scalar2=shift_sc[:, c:c + 1], op0=ALU.mult, op1=ALU.add),
                         reads=[(aname,), ("modT",), ("modT2",)], writes=[("ps", b), (dname, c, T if dname == "hT" else 0)])
        if xts is not None:
            for i in range(nxt):
                A.release("xt%d" % i)
        for i in range(4):
            A.release("xb%d" % i)
        A.release("junk")

    def _body2(self, upto):
        nc, s, A = self.nc, self.s, self.A
        bank, bankbf = self.bank, self.bankbf
        hT, CC, SS = self.hT, self.CC, self.SS
        w_in_v = self.w_in.rearrange("(k p) n -> p k n", p=128)
        hkeys = lambda T: [("hT", k, T) for k in range(KC)]
        TS = lambda T: slice(T * 512, (T + 1) * 512)
        mm = lambda out, l, r, st, sp: (lambda e: e.matmul(out, l, r, start=st, stop=sp))

        oT = A.alloc("oT", (4, S), BF16)
        qk = [{n: A.alloc(n + str(st), (S,), BF16, parts=(0, 96)) for n in ("qA", "qB", "kA", "kB")} for st in range(2)]
        vaug = A.alloc("vaug", (NT, 4, 192), BF16)
        self.pT = [A.alloc("pT%d" % i, (512,), BF16) for i in range(6)]
        rec_t = A.alloc("rec0", (512,), F32)
        self.rec = [rec_t, rec_t]
        self.qk, self.vaug, self.oT = qk, vaug, oT
        self.att_cnt = 0

        s.op("dve", lambda e: e.memset(vaug.rearrange("p a b c -> p (a b c)"), 1.0),
             writes=[("vaug", it, e_) for it in range(NT) for e_ in range(3)])
        wv = self.wv_t

        def v_proj(lhs_of, nk, wt, wname, rkeys):
            for it in range(NT):
                b = 6 + it % 2
                s.op("pe", [mm(bank(b), lhs_of(k, it), wt[:, k, :], k == 0, k == nk - 1) for k in range(nk)],
                     reads=rkeys(it // 4) + [(wname,)], writes=[("ps", b)])
                pv = bank(b).rearrange("p (c e d) -> p c e d", c=4, e=2, d=64)
                s.op("act", lambda e: e.activation(out=vaug[:, it, :, 0:64], in_=pv[:, :, 0, :], func=AF.Copy),
                     writes=[("ps", b), ("vaug", it, 0)])
                s.op("dve", lambda e: e.tensor_copy(out=vaug[:, it, :, 128:192], in_=pv[:, :, 1, :]),
                     writes=[("ps", b), ("vaug", it, 1)])

        v_proj(lambda k, it: hT[:, k, it * 128:(it + 1) * 128], KC, wv, "wv", hkeys)
        A.release("wv")

        wmisc = self.wmisc_t
        P8 = (0, 8)
        nbneg = A.alloc("nbneg", (1,), F32, parts=P8)
        eT = A.alloc("eT", (512,), F32, parts=P8)
        nlf = A.alloc("nlf", (512,), F32, parts=P8)
        onesf = A.alloc("onesf", (512,), F32, parts=P8)
        G = A.alloc("G", (S,), F32, parts=P8)
        negGb = A.alloc("negGb", (S,), BF16, parts=P8)
        Gtok = A.alloc("Gtok", (128,), F32)
        R = (64, 96)
        kpe = A.alloc("kpe", (S,), BF16, parts=R)
        t1 = A.alloc("t1", (512,), BF16, parts=R)
        t2 = A.alloc("t2", (512,), BF16, parts=R)
        self.t1, self.t2, self.Gtok, self.kpe_t = t1, t2, Gtok, kpe
        s.op("dve", lambda e: e.tensor_scalar(out=nbneg, in0=self.nbf_t, scalar1=-1.0, scalar2=None, op0=ALU.mult),
             reads=[("nbf",)], writes=[("nbneg",)])
        s.op("dve", lambda e: e.memset(onesf, 1.0), writes=[("onesf",)])
        for T in range(NB):
            b = 6 + T % 2
            s.op("pe", [mm(bank(b), wmisc[:, k, :], hT[:, k, TS(T)], k == 0, k == KC - 1) for k in range(KC)],
                 reads=hkeys(T) + [("wmisc", i) for i in range(4)], writes=[("ps", b)])
            s.op("act", lambda e: e.activation(out=eT, in_=bank(b)[0:8, :], func=AF.Exp, bias=nbneg, scale=-1.0),
                 reads=[("nbneg",)], writes=[("ps", b), ("eT",)])
            s.op("act", lambda e: e.activation(out=nlf, in_=eT, func=AF.Ln, bias=1.0), reads=[("eT",)], writes=[("nlf",)])
            init = 0.0 if T == 0 else G[:, T * 512 - 1:T * 512]
            s.op("dve", lambda e: e.tensor_tensor_scan(out=G[:, TS(T)], data0=onesf, data1=nlf, initial=init,
                                                       op0=ALU.mult, op1=ALU.add),
                 reads=[("nlf",), ("onesf",)] + ([("G", T - 1)] if T else []), writes=[("G", T)])
            s.op("dve", lambda e: e.tensor_tensor(out=t1, in0=bank(b)[64:96, :], in1=CC[:, TS(T)], op=ALU.mult),
                 reads=[("CC", q_) for q_ in range(4)], writes=[("ps", b), ("t1",)])
            s.op("dve", lambda e: e.tensor_tensor(out=t2, in0=bank(b)[96:128, :], in1=SS[:, TS(T)], op=ALU.mult),
                 reads=[("SS", q_) for q_ in range(4)], writes=[("ps", b), ("t2",)])
            s.op("dve", lambda e: e.tensor_tensor(out=kpe[:, TS(T)], in0=t1, in1=t2, op=ALU.add),
                 reads=[("t1",), ("t2",)], writes=[("kpe", T)])
        s.op("dve", lambda e: e.tensor_scalar(out=negGb, in0=G, scalar1=-1.0, scalar2=None, op0=ALU.mult),
             reads=[("G", T) for T in range(NB)], writes=[("negGb",)])
        GB = 5
        s.op("pe", [(lambda e, it=it: e.transpose(out=bank(GB)[:, it * 8:(it + 1) * 8], in_=G[:, it * 128:(it + 1) * 128],
                                                  identity=self.identf[0:8, 0:8])) for it in range(NT)],
             reads=[("G", T) for T in range(NB)] + [("identf",)], writes=[("ps", GB)])
        s.op("dve", lambda e: e.tensor_copy(out=Gtok, in_=bank(GB)[:, 0:128]), writes=[("ps", GB), ("Gtok",)])
        self.dump("G", G, 8, S, F32, [("G", T) for T in range(NB)])
        self.dump("Gtok", Gtok, 128, 128, F32, [("Gtok",)])
        self.dump("kpe", kpe, 32, S, BF16, [("kpe", T) for T in range(NB)])
        A.release("wmisc"); A.release("eT"); A.release("nlf"); A.release("onesf")
        if upto <= 2:
            return

        wf = [A.alloc("wf0", (KC, 256), BF16), A.alloc("wf1", (KC, 256), BF16)]
        for st in range(2):
            for n in ("kA", "kB"):
                s.op("dve", lambda e: e.memset(qk[st][n][64:65, :], 1.0), writes=[(n + str(st), "g")])

        self.sb_cnt = 0

        def sbank():
            self.sb_cnt += 1
            return 6 + self.sb_cnt % 2

        self.sbank = sbank

        def fox_proj(c):
            st = c % 2
            Q = qk[st]
            w, wn = wf[c % 2], "wf%d" % (c % 2)
            s.dma("pool", w[:, :, 0:128], w_in_v[:, :, C_Q + c * 128:C_Q + (c + 1) * 128], writes=[(wn, "q")])
            s.dma("pool", w[:, :, 128:256], w_in_v[:, :, C_K + c * 128:C_K + (c + 1) * 128], writes=[(wn, "k")])
            s.dma("pool", Q["qA"][64:65, :], negGb[2 * c:2 * c + 1, :], reads=[("negGb",)], writes=[("qA%d" % st, "g")])
            s.dma("pool", Q["qB"][64:65, :], negGb[2 * c + 1:2 * c + 2, :], reads=[("negGb",)], writes=[("qB%d" % st, "g")])
            yield
            for T in range(NB):
                b6 = sbank()
                s.op("pe", [mm(bank(b6), w[:, k, 0:128], hT[:, k, TS(T)], k == 0, k == KC - 1) for k in range(KC)],
                     reads=hkeys(T) + [(wn, "q")], writes=[("ps", b6)])
                s.op("dve", lambda e: e.tensor_scalar(out=Q["qA"][0:64, TS(T)], in0=bank(b6)[0:64, :], scalar1=0.125, scalar2=None, op0=ALU.mult),
                     writes=[("ps", b6), ("qA%d" % st, T)])
                s.op("dve", lambda e: e.tensor_scalar(out=Q["qB"][0:64, TS(T)], in0=bank(b6)[64:128, :], scalar1=0.125, scalar2=None, op0=ALU.mult),
                     writes=[("ps", b6), ("qB%d" % st, T)])
                yield
                b7 = sbank()
                s.op("pe", [mm(bank(b7), w[:, k, 128:256], hT[:, k, TS(T)], k == 0, k == KC - 1) for k in range(KC)],
                     reads=hkeys(T) + [(wn, "k")], writes=[("ps", b7)])
                s.op("dve", lambda e: e.tensor_copy(out=Q["kA"][0:64, TS(T)], in_=bank(b7)[0:64, :]),
                     writes=[("ps", b7), ("kA%d" % st, T)])
                s.op("dve", lambda e: e.tensor_copy(out=Q["kB"][0:64, TS(T)], in_=bank(b7)[64:128, :]),
                     writes=[("ps", b7), ("kB%d" % st, T)])
                yield

        for _ in fox_proj(0):
            pass
        self.dump("qA", qk[0]["qA"][0:65, :], 65, S, BF16, [("qA0", T) for T in range(NB)] + [("qA0", "g")])
        self.dump("kA", qk[0]["kA"][0:65, :], 65, S, BF16, [("kA0", T) for T in range(NB)] + [("kA0", "g")])
        for c in range(4):
            bg = fox_proj(c + 1) if c < 3 else None
            self.attention(c, True, c % 2, bg)
        for c in range(4):
            self.dump("oaT%d" % c, oT[:, c, :], 128, S, BF16, [("oT", c, T, hh) for T in range(NB) for hh in range(2)])
        A.release("wf0"); A.release("wf1"); A.release("G"); A.release("negGb"); A.release("nbneg")
        if upto <= 3:
            return
        self._body3(upto)

    def attention(self, c, fox, st=0, bg=None):
        s = self.s
        bank = self.bank
        qk, vaug, pT, rec = self.qk[st], self.vaug, self.pT, self.rec
        oT, on = (self.oT, "oT") if fox else (self.oT2, "oT2")
        nstep = 0
        KD = 65 if fox else 96
        scale = 1.0 if fox else 1.0 / math.sqrt(96.0)
        SB = (0, 1, 2, 3)
        LA = 2
        NPT = len(pT)
        for T in range(NB):
            nj = 4 * T + 4
            info = {}
            accb = 4

            def emit_S(hh, j):
                h = 2 * c + hh
                qn, kn = ("qA", "kA") if hh == 0 else ("qB", "kB")
                q, k = qk[qn], qk[kn]
                qn, kn = qn + str(st), kn + str(st)
                off = 0 if j < 4 * T else (j - 4 * T) * 128
                w = 512 - off
                n = self.att_cnt
                self.att_cnt += 1
                b = SB[n % len(SB)]
                pt, ptn = pT[n % NPT], "pT%d" % (n % NPT)
                diag = j >= 4 * T
                fns = [lambda e: e.matmul(bank(b)[:, 0:w], k[0:KD, j * 128:(j + 1) * 128],
                                          q[0:KD, T * 512 + off:(T + 1) * 512], start=True, stop=not diag)]
                if diag:
                    fns.append(lambda e: e.matmul(bank(b)[:, 0:128], self.ident, self.mask, start=False, stop=True))
                s.op("pe", fns, reads=[(qn, T), (qn, "g"), (kn, j // 4), (kn, "g"), ("ident",), ("mask",)],
                     writes=[("ps", b)])
                if fox:
                    s.op("act", lambda e: e.activation(out=pt[:, 0:w], in_=bank(b)[:, 0:w], func=AF.Exp,
                                                       bias=self.Gtok[:, j * 8 + h:j * 8 + h + 1], scale=1.0),
                         reads=[("Gtok",)], writes=[("ps", b), (ptn,)])
                else:
                    s.op("act", lambda e: e.activation(out=pt[:, 0:w], in_=bank(b)[:, 0:w], func=AF.Exp, scale=scale),
                         writes=[("ps", b), (ptn,)])
                info[(hh, j)] = (off, w, pt, ptn)

            def emit_PV(hh, j):
                off, w, pt, ptn = info[(hh, j)]
                acc = accb + hh
                vsl = slice(0, 128) if hh == 0 else slice(64, 192)
                s.op("pe", lambda e: e.matmul(bank(acc)[:, off:512], vaug[:, j, c, vsl], pt[:, 0:w],
                                              start=(j == 0), stop=(j == nj - 1)),
                     reads=[(ptn,), ("vaug", j, 0), ("vaug", j, 1), ("vaug", j, 2)], writes=[("ps", acc)])

            for step in range(nj + LA):
                for hh in (0, 1):
                    if step < nj:
                        emit_S(hh, step)
                    if step - LA >= 0:
                        emit_PV(hh, step - LA)
                nstep += 1
                if bg is not None and nstep % 4 == 0:
                    next(bg, None)
            for hh in (0, 1):
                acc = accb + hh
                r = rec[hh]
                if hh == 0:
                    osl, dsl = slice(0, 64), slice(64, 128)
                else:
                    osl, dsl = slice(64, 128), slice(0, 64)
                s.op("act", lambda e: e.activation(out=r[osl, :], in_=bank(acc)[dsl, :], func=AF.Ln),
                     writes=[("ps", acc), ("rec0", hh)])
                s.op("act", lambda e: e.activation(out=r[osl, :], in_=r[osl, :], func=AF.Exp, scale=-1.0),
                     reads=[("rec0", hh)], writes=[("rec0", hh)])
                s.op("dve", lambda e: e.tensor_tensor(out=oT[osl, c, T * 512:(T + 1) * 512], in0=bank(acc)[osl, :],
                                                      in1=r[osl, :], op=ALU.mult),
                     reads=[("rec0", hh)], writes=[("ps", acc), (on, c, T, hh)])
        if bg is not None:
            for _ in bg:
                pass

    def gate_merge(self, col0, w_proj, first, oT, on):
        s, A = self.s, self.A
        bank = self.bank
        hT, merged = self.hT, self.merged
        mm = lambda out, l, r, st, sp: (lambda e: e.matmul(out, l, r, start=st, stop=sp))
        w_in_v = self.w_in.rearrange("(k p) n -> p k n", p=128)
        wg, wgn, wp, wpn = self.gate_w[0 if first else 1]
        sg = [A.alloc("sg%d" % i, (512,), BF16) for i in range(2)]
        tmp = [A.alloc("gtmp%d" % i, (512,), BF16) for i in range(2)]
        n = 0
        for T in range(NB):
            Tsl = slice(T * 512, (T + 1) * 512)
            for m in range(KC):
                i = n % 2
                n += 1
                bg, bp = 0 + i, 2 + i
                s.op("pe", [mm(bank(bg), wg[:, k, m * 128:(m + 1) * 128], hT[:, k, Tsl], k == 0, k == KC - 1) for k in range(KC)],
                     reads=[("hT", k, T) for k in range(KC)] + [(wgn,)], writes=[("ps", bg)])
                s.op("act", lambda e: e.activation(out=sg[i], in_=bank(bg), func=AF.Sigmoid),
                     writes=[("ps", bg), ("sg%d" % i,)])
                s.op("pe", [mm(bank(bp), wp[:, cc, m * 128:(m + 1) * 128], oT[:, cc, Tsl], cc == 0, cc == 3) for cc in range(4)],
                     reads=[(on, cc, T, hh) for cc in range(4) for hh in range(2)] + [(wpn,)], writes=[("ps", bp)])
                if first:
                    s.op("dve", lambda e: e.tensor_tensor(out=merged[:, m, Tsl], in0=bank(bp), in1=sg[i], op=ALU.mult),
                         reads=[("sg%d" % i,)], writes=[("ps", bp), ("merged", m, T)])
                else:
                    s.op("dve", lambda e: e.tensor_tensor(out=tmp[i], in0=bank(bp), in1=sg[i], op=ALU.mult),
                         reads=[("sg%d" % i,)], writes=[("ps", bp), ("gtmp%d" % i,)])
                    s.op("dve", lambda e: e.tensor_tensor(out=merged[:, m, Tsl], in0=merged[:, m, Tsl], in1=tmp[i], op=ALU.add),
                         reads=[("gtmp%d" % i,), ("merged", m, T)], writes=[("merged", m, T)])
        for nme in (wgn, wpn, "sg0", "sg1", "gtmp0", "gtmp1"):
            A.release(nme)

    def _body3(self, upto):
        nc, s, A = self.nc, self.s, self.A
        bank = self.bank
        hT, CC, SS, qk, vaug = self.hT, self.CC, self.SS, self.qk, self.vaug
        t1, t2 = self.t1, self.t2
        w_in_v = self.w_in.rearrange("(k p) n -> p k n", p=128)
        hkeys = lambda T: [("hT", k, T) for k in range(KC)]
        TS = lambda T: slice(T * 512, (T + 1) * 512)
        mm = lambda out, l, r, st, sp: (lambda e: e.matmul(out, l, r, start=st, stop=sp))


        cqn = A.alloc("cqn", (6, S), BF16)
        ckvn = A.alloc("ckvn", (2, S), BF16)
        sqb = [A.alloc("sqb%d" % i, (512,), BF16) for i in range(3)]
        rstdb = A.alloc("rstdb", (512,), F32)
        wcq = A.alloc("wcq", (KC, 768), BF16)
        wckv = A.alloc("wckv", (KC, 256), BF16)
        s.dma("pool", wcq, w_in_v[:, :, C_CQ:C_CQ + 768], writes=[("wcq",)])
        s.dma("pool", wckv, w_in_v[:, :, C_CKV:C_CKV + 256], writes=[("wckv",)])
        for (wt, wn, nm, dst, dn, gT, gname, nfeat) in ((wcq, "wcq", 6, cqn, "cqn", self.gqT_t, "gqT", 768.0),
                                                        (wckv, "wckv", 2, ckvn, "ckvn", self.gkvT_t, "gkvT", 256.0)):
            for T in range(NB):
                for m in range(nm):
                    b = 6 + m % 2
                    i = m % 2
                    s.op("pe", [mm(bank(b), wt[:, k, m * 128:(m + 1) * 128], hT[:, k, TS(T)], k == 0, k == KC - 1) for k in range(KC)],
                         reads=hkeys(T) + [(wn,)], writes=[("ps", b)])
                    s.op("act", lambda e: e.activation(out=sqb[i], in_=bank(b), func=AF.Square),
                         writes=[("ps", b), ("sqb%d" % i,)])
                    s.op("dve", lambda e: e.tensor_scalar(out=dst[:, m, TS(T)], in0=bank(b), scalar1=gT[:, m:m + 1],
                                                          scalar2=None, op0=ALU.mult),
                         reads=[(gname,)], writes=[("ps", b), (dn, m, T)])
                    s.op("pe", mm(bank(5), self.ones_b, sqb[i], m == 0, m == nm - 1),
                         reads=[("sqb%d" % i,), ("ones_b",)], writes=[("ps", 5)])
                s.op("act", lambda e: e.activation(out=rstdb, in_=bank(5), func=AF.Ln, bias=self.eps_c, scale=1.0 / nfeat),
                     reads=[("eps_c",)], writes=[("ps", 5), ("rstdb",)])
                s.op("act", lambda e: e.activation(out=rstdb, in_=rstdb, func=AF.Exp, scale=-0.5),
                     reads=[("rstdb",)], writes=[("rstdb",)])
                for m in range(nm):
                    s.op("dve", lambda e: e.tensor_tensor(out=dst[:, m, TS(T)], in0=dst[:, m, TS(T)], in1=rstdb, op=ALU.mult),
                         reads=[(dn, m, T), ("rstdb",)], writes=[(dn, m, T)])
        self.dump("cqn0", cqn[:, 0, :], 128, S, BF16, [("cqn", 0, T) for T in range(NB)])
        self.dump("ckvn1", ckvn[:, 1, :], 128, S, BF16, [("ckvn", 1, T) for T in range(NB)])
        A.release("wcq"); A.release("wckv"); A.release("sqb0"); A.release("sqb1"); A.release("sqb2"); A.release("rstdb")

        oT2 = A.alloc("oT2", (4, S), BF16)
        self.oT2 = oT2
        wuq = A.alloc("wuq", (6, 8, 128), BF16)
        w_uq_v = self.w_uq.rearrange("(k p) (h e) -> p k h e", p=128, e=96)
        for kc in range(6):
            s.dma("pool", wuq[:, kc, :, 0:96], w_uq_v[:, kc, :, :], writes=[("wuq", 0, kc)])
            s.dma("pool", wuq[:, kc, :, 96:112], w_uq_v[:, kc, :, 80:96], writes=[("wuq", 1, kc)])
            s.dma("pool", wuq[:, kc, :, 112:128], w_uq_v[:, kc, :, 64:80], writes=[("wuq", 2, kc)])
        wkn = A.alloc("wkn", (2, 512), BF16)
        wvm = A.alloc("wvm", (2, 512), BF16)
        w_ukv_v = self.w_ukv.rearrange("(k p) (h e) -> p k h e", p=128, e=128)
        for kc in range(2):
            s.dma("pool", wkn[:, kc, :].rearrange("p (h e) -> p h e", e=64), w_ukv_v[:, kc, :, 0:64], writes=[("wkn", kc)])
            s.dma("pool", wvm[:, kc, :].rearrange("p (h e) -> p h e", e=64), w_ukv_v[:, kc, :, 64:128], writes=[("wvm", kc)])

        for it in range(NT):
            b = 6 + it % 2
            T = it // 4
            s.op("pe", [mm(bank(b), ckvn[:, k, it * 128:(it + 1) * 128], wvm[:, k, :], k == 0, k == 1) for k in range(2)],
                 reads=[("ckvn", k, T) for k in range(2)] + [("wvm", 0), ("wvm", 1)], writes=[("ps", b)])
            pv = bank(b).rearrange("p (c e d) -> p c e d", c=4, e=2, d=64)
            s.op("act", lambda e: e.activation(out=vaug[:, it, :, 0:64], in_=pv[:, :, 0, :], func=AF.Copy),
                 writes=[("ps", b), ("vaug", it, 0)])
            s.op("dve", lambda e: e.tensor_copy(out=vaug[:, it, :, 128:192], in_=pv[:, :, 1, :]),
                 writes=[("ps", b), ("vaug", it, 1)])

        wuq_keys = [("wuq", i, kc) for i in range(3) for kc in range(6)]

        def mla_proj(c):
            st = c % 2
            Q = qk[st]
            for kn in ("kA", "kB"):
                s.op("dve", lambda e: e.tensor_copy(out=Q[kn][64:96, :], in_=self.kpe_t),
                     reads=[("kpe", T) for T in range(NB)], writes=[(kn + str(st), "g")])
            yield
            nb = 0
            for T in range(NB):
                b7 = self.sbank()
                s.op("pe", [mm(bank(b7), wkn[:, k, c * 128:(c + 1) * 128], ckvn[:, k, TS(T)], k == 0, k == 1) for k in range(2)],
                     reads=[("ckvn", k, T) for k in range(2)] + [("wkn", 0), ("wkn", 1)], writes=[("ps", b7)])
                s.op("dve", lambda e: e.tensor_copy(out=Q["kA"][0:64, TS(T)], in_=bank(b7)[0:64, :]),
                     writes=[("ps", b7), ("kA%d" % st, T)])
                s.op("dve", lambda e: e.tensor_copy(out=Q["kB"][0:64, TS(T)], in_=bank(b7)[64:128, :]),
                     writes=[("ps", b7), ("kB%d" % st, T)])
                yield
                for hh in (0, 1):
                    h = 2 * c + hh
                    qn = ("qA" if hh == 0 else "qB")
                    q = Q[qn]
                    qn = qn + str(st)
                    b6 = self.sbank()
                    s.op("pe", [mm(bank(b6), wuq[:, k, h, :], cqn[:, k, TS(T)], k == 0, k == 5) for k in range(6)],
                         reads=[("cqn", k, T) for k in range(6)] + wuq_keys, writes=[("ps", b6)])
                    s.op("dve", lambda e: e.tensor_copy(out=q[0:64, TS(T)], in_=bank(b6)[0:64, :]),
                         writes=[("ps", b6), (qn, T)])
                    s.op("dve", lambda e: e.tensor_tensor(out=t1, in0=bank(b6)[64:96, :], in1=CC[:, TS(T)], op=ALU.mult),
                         reads=[("CC", q_) for q_ in range(4)], writes=[("ps", b6), ("t1",)])
                    s.op("dve", lambda e: e.tensor_tensor(out=t2, in0=bank(b6)[96:128, :], in1=SS[:, TS(T)], op=ALU.mult),
                         reads=[("SS", q_) for q_ in range(4)], writes=[("ps", b6), ("t2",)])
                    s.op("dve", lambda e: e.tensor_tensor(out=q[64:96, TS(T)], in0=t1, in1=t2, op=ALU.add),
                         reads=[("t1",), ("t2",)], writes=[(qn, T)])
                    yield

        for _ in mla_proj(0):
            pass
        self.dump("qm0", qk[0]["qA"][0:96, :], 96, S, BF16, [("qA0", T) for T in range(NB)])
        self.dump("km0", qk[0]["kA"][0:96, :], 96, S, BF16, [("kA0", T) for T in range(NB)] + [("kA0", "g")])
        for c in range(4):
            bg = mla_proj(c + 1) if c < 3 else None
            if c == 3:
                for nme in ("wuq", "wkn", "wvm", "cqn", "ckvn", "kpe", "t1", "t2", "CC", "SS"):
                    A.release(nme)
                self.gvec_prefetch()
            self.attention(c, False, c % 2, bg)
        for c in range(4):
            self.dump("obT%d" % c, oT2[:, c, :], 128, S, BF16, [("oT2", c, T, hh) for T in range(NB) for hh in range(2)])
        for nme in ("qA0", "qB0", "kA0", "kB0", "qA1", "qB1", "kA1", "kB1", "vaug", "pT0", "pT1", "pT2", "pT3", "pT4", "pT5",
                    "rec0", "Gtok"):
            A.release(nme)
        if upto <= 4:
            return

        merged = A.alloc("merged", (KC, S), BF16)
        self.merged = merged
        self.gate_w = []
        for i, (col0, wproj) in enumerate(((C_GF, self.w_pf), (C_GM, self.w_pm))):
            wg = A.alloc("wg%d" % i, (KC, D), BF16)
            wp = A.alloc("wp%d" % i, (4, D), BF16)
            s.dma("pool", wg, w_in_v[:, :, col0:col0 + D], writes=[("wg%d" % i,)])
            s.dma("pool", wp, wproj.rearrange("(k p) n -> p k n", p=128), writes=[("wp%d" % i,)])
            self.gate_w.append((wg, "wg%d" % i, wp, "wp%d" % i))
            if i == 0:
                self.gvec_compute()
        self.gate_merge(C_GF, self.w_pf, True, self.oT, "oT")
        self.gate_merge(C_GM, self.w_pm, False, self.oT2, "oT2")
        for m in (0, 7):
            self.dump("mg%d" % m, merged[:, m, :], 128, S, BF16, [("merged", m, T) for T in range(NB)])
        A.release("hT"); A.release("oT"); A.release("oT2")
        if upto <= 5:
            return
        self._body4(upto)

    def gvec_prefetch(self):
        s, A = self.s, self.A
        self.gvec = A.alloc("gvec", (2, D), F32)
        self.silurep = A.alloc("silurep", (KC, 128), BF16)
        self.bada_rep_t = A.alloc("bada_rep", (2 * D,), F32)
        self.gpost_rep_t = A.alloc("gpost_rep", (2 * D,), F32)
        s.dma("pool", self.bada_rep_t, self.bada_rep, writes=[("bada_rep",)])
        s.dma("pool", self.gpost_rep_t, self.gpost_rep, writes=[("gpost_rep",)])
        w_ada_v = self.w_ada.rearrange("(k p) n -> p k n", p=128)
        self.wadag = {}
        for v, hfs in ((3, (0, 1)), (4, (0, 1)), (2, (0,))):
            for hf in hfs:
                nm = "wadag%d%d" % (v, hf)
                t = A.alloc(nm, (KC, 512), BF16)
                s.dma("pool", t, w_ada_v[:, :, v * D + hf * 512:v * D + (hf + 1) * 512], writes=[(nm,)])
                self.wadag[(v, hf)] = (t, nm)

    def gvec_compute(self):
        s, A = self.s, self.A
        bank = self.bank
        gvec, silurep, bada_rep, gpost_rep = self.gvec, self.silurep, self.bada_rep_t, self.gpost_rep_t
        for k in range(KC):
            s.op("dve", lambda e: e.tensor_scalar(out=silurep[:, k, :], in0=self.ones_b, scalar1=self.siluf[:, k:k + 1],
                                                  scalar2=None, op0=ALU.mult),
                 reads=[("ones_b",), ("siluf",)], writes=[("silurep", k)])
        modT, affn = self.modT, self.affn
        MODB = 7
        for v in (3, 4):
            for c in range(KC):
                col = v * 8 + c
                wt, wn = self.wadag[(v, c // 4)]
                cc = c % 4
                s.op("pe", [(lambda e, k=k: e.matmul(bank(MODB)[:, col:col + 1], wt[:, k, cc * 128:(cc + 1) * 128],
                                                    self.silub[:, k:k + 1], start=(k == 0), stop=(k == KC - 1)))
                            for k in range(KC)],
                     reads=[(wn,), ("silub",)], writes=[("ps", MODB)])
        s.op("dve", lambda e: e.tensor_tensor(out=modT[:, 24:40], in0=bank(MODB)[:, 24:40], in1=self.badaT_t[:, 24:40], op=ALU.add),
             reads=[("badaT",)], writes=[("ps", MODB), ("modT2",)])
        s.op("dve", lambda e: e.scalar_tensor_tensor(out=affn, in0=modT[:, 32:40], scalar=1.0, in1=self.gpreT_t[:, 8:16],
                                                     op0=ALU.add, op1=ALU.mult),
             reads=[("modT2",), ("gpreT",)], writes=[("affn",)])
        w_ada_v = self.w_ada.rearrange("(k p) n -> p k n", p=128)
        for (v, hf, src_key) in ((2, 1, (4, 0)), (5, 0, (3, 0)), (5, 1, (3, 1))):
            if True:
                t, nm = self.wadag[src_key]
                s.dma("pool", t, w_ada_v[:, :, v * D + hf * 512:v * D + (hf + 1) * 512], writes=[(nm,)])
                self.wadag[(v, hf)] = (t, nm)
        for g, v in enumerate((2, 5)):
            for half in range(2):
                wt, wn = self.wadag[(v, half)]
                b = 5 + half
                s.op("pe", [(lambda e, k=k: e.matmul(bank(b), silurep[:, k, :], wt[:, k, :],
                                                    start=(k == 0), stop=(k == KC - 1))) for k in range(KC)],
                     reads=[(wn,)] + [("silurep", k) for k in range(KC)], writes=[("ps", b)])
                sl = slice(g * D + half * 512, g * D + (half + 1) * 512)
                hs = slice(half * 512, (half + 1) * 512)
                s.op("dve", lambda e: e.tensor_tensor(out=gvec[:, g, hs], in0=bank(b), in1=bada_rep[:, sl], op=ALU.add),
                     reads=[("bada_rep",)], writes=[("ps", b), ("gvec", g, half)])
                s.op("dve", lambda e: e.tensor_tensor(out=gvec[:, g, hs], in0=gvec[:, g, hs], in1=gpost_rep[:, sl], op=ALU.mult),
                     reads=[("gpost_rep",), ("gvec", g, half)], writes=[("gvec", g, half)])
        self.dump("gvec", gvec.rearrange("p a b -> p (a b)"), 128, 2 * D, F32, [("gvec", g, h) for g in range(2) for h in range(2)])
        for n in ["silurep", "bada_rep", "gpost_rep"] + sorted(set(nm for (_, nm) in self.wadag.values())):
            A.release(n)

    def _body4(self, upto):
        nc, s, A = self.nc, self.s, self.A
        bank = self.bank
        merged, gvec = self.merged, self.gvec
        mm = lambda out, l, r, st, sp: (lambda e: e.matmul(out, l, r, start=st, stop=sp))
        wout = A.alloc("wout", (KC, D), BF16)
        s.dma("pool", wout, self.w_out.rearrange("(k p) n -> p k n", p=128), writes=[("wout",)])
        w2 = A.alloc("w2", (NFC, D), BF16)
        w2v = self.w_f2.rearrange("(j p) n -> p j n", p=128)
        w1v = self.w_f1.rearrange("(k p) n -> p k n", p=128)
        x2 = A.alloc("x2", (4, D), F32)
        h2T = A.alloc("h2T", (KC, 512), BF16)
        actT = A.alloc("actT", (NFC, 512), BF16)
        w1b = [A.alloc("w1b%d" % i, (KC, 512), BF16) for i in range(3)]
        sgl = [A.alloc("sgl%d" % i, (512,), BF16) for i in range(2)]
        xin = [A.alloc("xin%d" % i, (D,), F32) for i in range(2)]
        ot = [A.alloc("ot%d" % i, (D,), F32) for i in range(2)]
        tmpf = [A.alloc("tmpf%d" % i, (512,), F32) for i in range(2)]
        junkp = A.alloc("junkp", (512,), BF16)
        ssq = A.alloc("ssq", (4 * NT,), F32)
        ssy = A.alloc("ssy", (2 * NT,), F32)
        rsy = A.alloc("rsy", (2 * NT,), F32)
        ss, rstd = self.ss, self.rstd
        w2_loaded = [False]

        def postnorm(bks, col, g, resid, rkeys, out_ap, okeys):
            for half, b in enumerate(bks):
                s.op("act", lambda e: e.activation(out=junkp, in_=bank(b), func=AF.Square,
                                                   accum_out=ssq[:, 2 * col + half:2 * col + half + 1]),
                     writes=[("ps", b), ("junkp",), ("ssq", col, half)])
            s.op("dve", lambda e: e.tensor_tensor(out=ssy[:, col:col + 1], in0=ssq[:, 2 * col:2 * col + 1],
                                                  in1=ssq[:, 2 * col + 1:2 * col + 2], op=ALU.add),
                 reads=[("ssq", col, 0), ("ssq", col, 1)], writes=[("ssy", col)])
            s.op("act", lambda e: e.activation(out=rsy[:, col:col + 1], in_=ssy[:, col:col + 1], func=AF.Ln,
                                               bias=self.eps_c, scale=1.0 / D),
                 reads=[("ssy", col), ("eps_c",)], writes=[("rsy", col)])
            s.op("act", lambda e: e.activation(out=rsy[:, col:col + 1], in_=rsy[:, col:col + 1], func=AF.Exp, scale=-0.5),
                 reads=[("rsy", col)], writes=[("rsy", col)])
            for half, b in enumerate(bks):
                hs = slice(half * 512, (half + 1) * 512)
                s.op("dve", lambda e: e.scalar_tensor_tensor(out=tmpf[half], in0=bank(b), scalar=rsy[:, col:col + 1],
                                                             in1=gvec[:, g, hs], op0=ALU.mult, op1=ALU.mult),
                     reads=[("rsy", col), ("gvec", g, half)], writes=[("ps", b), ("tmpf%d" % half,)])
                s.op("dve", lambda e: e.tensor_tensor(out=out_ap[:, hs], in0=tmpf[half], in1=resid[:, hs], op=ALU.add),
                     reads=[("tmpf%d" % half,)] + rkeys, writes=okeys)

        nw1 = 0
        import os
        for T in [int(t_) for t_ in os.environ.get("TLIST", "0,1,2,3").split(",")]:
            for i in range(4):
                it = 4 * T + i
                xi, xn = xin[it % 2], "xin%d" % (it % 2)
                s.dma("sp", xi, self.x[it * 128:(it + 1) * 128, :], writes=[(xn,)])
                bks = (0, 1) if it % 2 == 0 else (2, 3)
                for half, b in enumerate(bks):
                    s.op("pe", [mm(bank(b), merged[:, k, it * 128:(it + 1) * 128], wout[:, k, half * 512:(half + 1) * 512],
                                   k == 0, k == KC - 1) for k in range(KC)],
                         reads=[("merged", k, T) for k in range(KC)] + [("wout",)], writes=[("ps", b)])
                postnorm(bks, it, 0, xi, [(xn,)], x2[:, i, :], [("x2", i)])
            if T == 0:
                self.dump("x2", x2[:, 0, :], 128, D, F32, [("x2", 0)])
            if upto <= 6:
                return
            if not w2_loaded[0]:
                for gi, g0 in enumerate(range(0, NFC, 6)):
                    g1 = min(NFC, g0 + 6)
                    s.dma("pool", w2[:, g0:g1, :], w2v[:, g0:g1, :], writes=[("w2", gi)])
                w2_loaded[0] = True
            self.prenorm(None, self.affn, self.modT[:, 24:32], h2T, "h2T", ss, rstd, "affn", [T], T,
                         src_sbuf=lambda it: (x2[:, it % 4, :], ("x2", it % 4)))
            for j in range(NFC):
                if j % 2 == 0:
                    wb, wbn = w1b[nw1 % 3], "w1b%d" % (nw1 % 3)
                    nw1 += 1
                    s.dma("pool", wb[:, :, 0:256], w1v[:, :, j * 128:(j + 2) * 128], writes=[(wbn, "g")])
                    s.dma("pool", wb[:, :, 256:512], w1v[:, :, DFF + j * 128:DFF + (j + 2) * 128], writes=[(wbn, "u")])
                jo = (j % 2) * 128
                gb, ub = 4 + 2 * (j % 2), 5 + 2 * (j % 2)
                hk = [("h2T", k, 0) for k in range(KC)]
                s.op("pe", [mm(bank(gb), wb[:, k, jo:jo + 128], h2T[:, k, :], k == 0, k == KC - 1) for k in range(KC)],
                     reads=hk + [(wbn, "g")], writes=[("ps", gb)])
                s.op("pe", [mm(bank(ub), wb[:, k, 256 + jo:256 + jo + 128], h2T[:, k, :], k == 0, k == KC - 1) for k in range(KC)],
                     reads=hk + [(wbn, "u")], writes=[("ps", ub)])
                s.op("act", lambda e: e.activation(out=sgl[j % 2], in_=bank(gb), func=AF.Silu),
                     writes=[("ps", gb), ("sgl%d" % (j % 2),)])
                s.op("dve", lambda e: e.tensor_tensor(out=actT[:, j, :], in0=bank(ub), in1=sgl[j % 2], op=ALU.mult),
                     reads=[("sgl%d" % (j % 2),)], writes=[("ps", ub), ("actT", j)])
            if upto <= 7:
                return
            for i in range(4):
                it = 4 * T + i
                bks = (0, 1) if it % 2 == 0 else (2, 3)
                for half, b in enumerate(bks):
                    s.op("pe", [mm(bank(b), actT[:, j, i * 128:(i + 1) * 128], w2[:, j, half * 512:(half + 1) * 512],
                                   j == 0, j == NFC - 1) for j in range(NFC)],
                         reads=[("actT", j) for j in range(NFC)] + [("w2", gi) for gi in range(4)], writes=[("ps", b)])
                o_t, on = ot[it % 2], "ot%d" % (it % 2)
                postnorm(bks, NT + it, 1, x2[:, i, :], [("x2", i)], o_t, [(on,)])
                s.dma("sp", self.out[it * 128:(it + 1) * 128, :], o_t, reads=[(on,)])
            if upto <= 8:
                return


def _consts():
    ident = np.eye(128, dtype=np.float32)
    sk = np.arange(128)[:, None]
    tq = np.arange(128)[None, :]
    mask = np.where(sk > tq, NEG, 0.0).astype(np.float32)
    inv_freq = 1.0 / (10000.0 ** (np.arange(0, 32, 2, dtype=np.float32) / 32.0))
    rope = np.zeros((128, 4), np.float32)
    for p in range(128):
        r = p % 32
        i = r % 16
        rope[p, 0] = inv_freq[i]
        rope[p, 1] = math.pi / 2
        rope[p, 2] = inv_freq[i]
        rope[p, 3] = (math.pi if r < 16 else 0.0)
    return dict(c_ident=ident.astype(ml_dtypes.bfloat16), c_identf=ident, c_mask=mask.astype(ml_dtypes.bfloat16),
                c_rope=rope)


def make_in_maps(x, c, positions, w_ada, b_ada, g_pre_mix, g_post_mix, g_pre_ffn, g_post_ffn,
                 w_in, b_forget, g_q_lora, w_uq, g_kv_lora, w_ukv, w_proj_fox, w_proj_mla,
                 w_out, w_ffn_in, w_ffn_out, cores=range(8)):
    f = lambda a: np.ascontiguousarray(np.asarray(a, dtype=np.float32))
    colT = lambda v, n: f(np.asarray(v, np.float32).reshape(n, 128).T)
    rep = lambda v: f(np.broadcast_to(np.asarray(v, np.float32)[None, :], (128, v.shape[0])))
    b_ada0 = np.asarray(b_ada[0], np.float32)
    shared = dict(
        w_ada=f(w_ada[0]), badaT=colT(b_ada0, 48),
        bada_rep=rep(np.concatenate([b_ada0[2 * D:3 * D], b_ada0[5 * D:6 * D]])),
        gpost_rep=rep(np.concatenate([np.asarray(g_post_mix[0], np.float32), np.asarray(g_post_ffn[0], np.float32)])),
        gpreT=f(np.concatenate([colT(g_pre_mix[0], 8), colT(g_pre_ffn[0], 8)], axis=1)),
        w_in=f(w_in[0]), nbf=f(np.asarray(b_forget[0], np.float32).reshape(8, 1)),
        gqT=colT(g_q_lora[0], 6), gkvT=colT(g_kv_lora[0], 2),
        w_uq=f(w_uq[0]), w_ukv=f(w_ukv[0]), w_pf=f(w_proj_fox[0]), w_pm=f(w_proj_mla[0]),
        w_out=f(w_out[0]), w_f1=f(w_ffn_in[0]), w_f2=f(w_ffn_out[0]),
    )
    shared.update(_consts())
    maps = []
    for b in cores:
        m = dict(shared)
        m["x"] = f(x[b])
        m["cT"] = colT(c[b], 8)
        pq = np.asarray(positions[b], np.int32).reshape(4, 1, 512)
        m["pos"] = np.ascontiguousarray(np.broadcast_to(pq, (4, 32, 512)).reshape(128, 512))
        maps.append(m)
    return maps


_NC_CACHE = {}


def kernel(**inputs):
    if "nc" not in _NC_CACHE:
        _NC_CACHE["nc"] = Builder().build()
    nc = _NC_CACHE["nc"]
    maps = make_in_maps(**inputs)
    res = run_bass_kernel_spmd(nc, maps, core_ids=list(range(8)))
    out = np.stack([np.asarray(r["out"], dtype=np.float32) for r in res.results], axis=0)
    return out
```

```python
import math
import numpy as np
import ml_dtypes
import concourse.bass as bass
import concourse.mybir as mybir
from concourse.bass_utils import run_bass_kernel_spmd

F32 = mybir.dt.float32
BF16 = mybir.dt.bfloat16
I32 = mybir.dt.int32
U8 = mybir.dt.uint8
AF = mybir.ActivationFunctionType
ALU = mybir.AluOpType

S = 2048
D = 1024
NT = 16
NB = 4
KC = 8
DFF = 2816
NFC = 22
D_IN = 4648
EPS = 1e-6
C_Q, C_K, C_V, C_F, C_CQ, C_CKV, C_KR, C_GF, C_GM = 0, 512, 1024, 1536, 1544, 2312, 2568, 2600, 3624
NEG = -30000.0


class Sched:
    def __init__(self, nc, sems, dma_sems):
        self.nc = nc
        self.eng = {"pe": nc.tensor, "act": nc.scalar, "dve": nc.vector, "pool": nc.gpsimd, "sp": nc.sync}
        self.sem = sems
        self.cnt = {e: 0 for e in sems}
        self.seen = {e: {} for e in self.eng}
        self.last_w = {}
        self.readers = {}
        self.dma_sems = dma_sems
        self.dma_n = {q: 0 for q in dma_sems}
        self.inherit = {}
        self.n_wait = 0

    def _collect(self, reads, writes):
        raw, other = [], []
        for k in reads:
            w = self.last_w.get(k)
            if w is not None:
                raw.append(w)
        for k in writes:
            ps = (k[0] == "ps")
            w = self.last_w.get(k)
            if w is not None:
                other.append((w, ps))
            for t in self.readers.get(k, ()):
                other.append((t, ps))
            inh = self.inherit.get(k[0])
            if inh:
                for t in inh:
                    other.append((t, False))
        return raw, other

    def _emit_waits(self, e, raw, other):
        for tok in raw:
            if tok[2] == e and e == "pe":
                continue
            self._wait(e, tok)
        for tok, ps in other:
            if tok[2] == e:
                continue
            self._wait(e, tok)

    def _wait(self, e, tok):
        sem, val, _ = tok
        sid = id(sem)
        if self.seen[e].get(sid, 0) >= val:
            return
        self.eng[e].wait_ge(sem, val)
        self.seen[e][sid] = val
        self.n_wait += 1

    def _register(self, tok, reads, writes):
        for k in reads:
            self.readers.setdefault(k, []).append(tok)
        for k in writes:
            self.last_w[k] = tok
            self.readers[k] = []

    def op(self, e, fns, reads=(), writes=()):
        if callable(fns):
            fns = [fns]
        raw, other = self._collect(reads, writes)
        self._emit_waits(e, raw, other)
        eng = self.eng[e]
        ins = None
        for f in fns:
            ins = f(eng)
        self.cnt[e] += 1
        ins.then_inc(self.sem[e], 1)
        tok = (self.sem[e], self.cnt[e], e)
        self._register(tok, reads, writes)
        return tok

    def dma(self, q, out, in_, reads=(), writes=()):
        raw, other = self._collect(reads, writes)
        self._emit_waits(q, raw, other)
        pool = self.dma_sems[q]
        n = self.dma_n[q]
        self.dma_n[q] += 1
        sem = pool[n % len(pool)]
        rnd = n // len(pool)
        if rnd > 0:
            self._wait(q, (sem, 16 * rnd, "dma"))
        self.eng[q].dma_start(out=out, in_=in_).then_inc(sem, 16)
        tok = (sem, 16 * (rnd + 1), "dma")
        self._register(tok, reads, writes)
        return tok

    def release(self, name):
        toks = []
        for k in list(self.last_w.keys()):
            if k[0] == name:
                toks.append(self.last_w.pop(k))
        for k in list(self.readers.keys()):
            if k[0] == name:
                toks.extend(self.readers.pop(k))
        best = {}
        for t in toks:
            sid = id(t[0])
            if sid not in best or best[sid][1] < t[1]:
                best[sid] = t
        return list(best.values())


class Arena:
    def __init__(self, sched, arena_ap, nbytes):
        self.s = sched
        self.arena = arena_ap
        self.free = [(0, nbytes)]
        self.live = {}
        self.dead = []
        self.peak = 0

    def alloc(self, name, shape, dtype, parts=(0, 128)):
        esz = 4 if dtype in (F32, I32) else 2
        n = 1
        for d in shape:
            n *= d
        size = (n * esz + 63) // 64 * 64
        for i, (off, sz) in enumerate(self.free):
            if sz >= size:
                self.free[i] = (off + size, sz - size)
                break
        else:
            raise RuntimeError(f"arena OOM for {name} ({size} B); free={self.free} live={ {k: v[1] for k, v in self.live.items()} }")
        self.live[name] = (off, size)
        self.peak = max(self.peak, off + size)
        toks = []
        keep = []
        for (o, s_, tk) in self.dead:
            if o < off + size and off < o + s_:
                toks.extend(tk)
            keep.append((o, s_, tk))
        self.s.inherit[name] = toks
        ap = self.arena[parts[0]:parts[1], off // 2:(off + size) // 2]
        if esz == 4:
            ap = ap.bitcast(dtype)
        elif dtype != BF16:
            ap = ap.bitcast(dtype)
        ap = ap[:, 0:n]
        if len(shape) == 2:
            ap = ap.rearrange("p (a b) -> p a b", a=shape[0], b=shape[1])
        elif len(shape) == 3:
            ap = ap.rearrange("p (a b c) -> p a b c", a=shape[0], b=shape[1], c=shape[2])
        return ap

    def release(self, name):
        off, size = self.live.pop(name)
        toks = self.s.release(name)
        self.dead.append((off, size, toks))
        self.free.append((off, size))
        self.free.sort()
        merged = []
        for o, s_ in self.free:
            if merged and merged[-1][0] + merged[-1][1] == o:
                merged[-1] = (merged[-1][0], merged[-1][1] + s_)
            else:
                merged.append((o, s_))
        self.free = merged


class Builder:
    def __init__(self, debug=None):
        self.debug = debug or []
        nc = bass.Bass("TRN2", target_bir_lowering=False)
        self.nc = nc
        self.dbg_out = {}
        d = lambda n, sh, dt, kind="ExternalInput": nc.dram_tensor(n, list(sh), dt, kind=kind).ap()
        self.x = d("x", [S, D], F32)
        self.cT = d("cT", [128, KC], F32)
        self.pos = d("pos", [128, 512], I32)
        self.w_ada = d("w_ada", [D, 6 * D], F32)
        self.badaT = d("badaT", [128, 48], F32)
        self.bada_rep = d("bada_rep", [128, 2 * D], F32)
        self.gpost_rep = d("gpost_rep", [128, 2 * D], F32)
        self.gpreT = d("gpreT", [128, 16], F32)
        self.w_in = d("w_in", [D, D_IN], F32)
        self.nbf = d("nbf", [8, 1], F32)
        self.gqT = d("gqT", [128, 6], F32)
        self.gkvT = d("gkvT", [128, 2], F32)
        self.w_uq = d("w_uq", [768, 768], F32)
        self.w_ukv = d("w_ukv", [256, 1024], F32)
        self.w_pf = d("w_pf", [512, D], F32)
        self.w_pm = d("w_pm", [512, D], F32)
        self.w_out = d("w_out", [D, D], F32)
        self.w_f1 = d("w_f1", [D, 2 * DFF], F32)
        self.w_f2 = d("w_f2", [DFF, D], F32)
        self.c_ident = d("c_ident", [128, 128], BF16)
        self.c_identf = d("c_identf", [128, 128], F32)
        self.c_mask = d("c_mask", [128, 128], BF16)
        self.c_rope = d("c_rope", [128, 4], F32)
        self.out = d("out", [S, D], F32, kind="ExternalOutput")

    def dbg(self, name, ap, shape, dtype):
        if name in self.debug:
            o = self.nc.dram_tensor("dbg_" + name, list(shape), dtype, kind="ExternalOutput").ap()
            self.dbg_out[name] = o
            return o
        return None

    def build(self, upto=99):
        import contextlib
        nc = self.nc
        with contextlib.ExitStack() as es:
            es.enter_context(nc.allow_low_precision("bf16 matmul operands by design; fp32 accumulation"))
            es.enter_context(nc.allow_non_contiguous_dma("small strided weight / constant loads"))
            ARENA = 207 * 1024
            arena_t = es.enter_context(nc.sbuf_tensor("arena", [128, ARENA // 2], BF16))
            self.banks = [es.enter_context(nc.psum_tensor(f"ps{b}", [128, 512], F32)) for b in range(8)]
            sems = {e: es.enter_context(nc.semaphore("s_" + e)) for e in ("pe", "act", "dve", "pool")}
            dma_sems = {q: [es.enter_context(nc.semaphore(f"d{q}{i}")) for i in range(12)] for q in ("sp", "pool")}
            self.s = Sched(nc, sems, dma_sems)
            self.A = Arena(self.s, arena_t, ARENA)
            self._body(upto)
            self._finish()
        return nc

    def bank(self, b):
        return self.banks[b][:, :]

    def bankbf(self, b):
        return self.banks[b][:, :].bitcast(BF16)

    def _finish(self):
        s = self.s
        for q in ("sp", "pool"):
            pool = s.dma_sems[q]
            n = s.dma_n[q]
            for i, sem in enumerate(pool):
                cnt = (n - i + len(pool) - 1) // len(pool) if n > i else 0
                if cnt > 0:
                    s._wait("sp", (sem, 16 * cnt, "dma"))

    def dump(self, name, ap, parts, ncols, dtype, reads):
        o = self.dbg(name, ap, [parts, ncols], dtype)
        if o is not None:
            self.s.dma("sp", o, ap, reads=reads)

    def _body(self, upto):
        nc, s, A = self.nc, self.s, self.A
        bank, bankbf = self.bank, self.bankbf

        def load_const(name, src, shape, dtype, parts=(0, 128), q="pool"):
            t = A.alloc(name, shape, dtype, parts)
            s.dma(q, t, src, writes=[(name,)])
            return t

        ident = load_const("ident", self.c_ident, (128,), BF16)
        identf = load_const("identf", self.c_identf, (128,), F32)
        mask = load_const("mask", self.c_mask, (128,), BF16)
        ropec = load_const("ropec", self.c_rope, (4,), F32)
        cT = load_const("cT", self.cT, (KC,), F32)
        badaT = load_const("badaT", self.badaT, (48,), F32)
        gpreT = load_const("gpreT", self.gpreT, (16,), F32)
        gqT = load_const("gqT", self.gqT, (6,), F32)
        gkvT = load_const("gkvT", self.gkvT, (2,), F32)
        nbf = load_const("nbf", self.nbf, (1,), F32, parts=(0, 8))
        self.ident, self.mask, self.identf, self.nbf_t = ident, mask, identf, nbf
        self.gqT_t, self.gkvT_t = gqT, gkvT
        eps_c = A.alloc("eps_c", (1,), F32)
        s.op("dve", lambda e: e.memset(eps_c, EPS), writes=[("eps_c",)])
        self.eps_c = eps_c

        siluf = A.alloc("siluf", (KC,), F32)
        silub = A.alloc("silub", (KC,), BF16)
        ones_b = A.alloc("ones_b", (128,), BF16)
        modT = A.alloc("modT", (48,), F32)
        amix = A.alloc("amix", (KC,), F32)
        affn = A.alloc("affn", (KC,), F32)
        s.op("act", lambda e: e.activation(out=siluf, in_=cT, func=AF.Silu), reads=[("cT",)], writes=[("siluf",)])
        s.op("dve", lambda e: e.tensor_copy(out=silub, in_=siluf), reads=[("siluf",)], writes=[("silub",)])
        s.op("dve", lambda e: e.memset(ones_b, 1.0), writes=[("ones_b",)])
        wada = [A.alloc("wada0", (KC, D), BF16), A.alloc("wada1", (KC, D), BF16)]
        w_ada_v = self.w_ada.rearrange("(k p) n -> p k n", p=128)
        MODB = 7
        for v in (0, 1):
            s.dma("pool", wada[v], w_ada_v[:, :, v * D:(v + 1) * D], writes=[("wada%d" % v,)])

        def s0_compute():
            for v in (0, 1):
                i = v
                wn = "wada%d" % i
                for c in range(KC):
                    col = v * 8 + c
                    s.op("pe", [(lambda e, k=k: e.matmul(bank(MODB)[:, col:col + 1], wada[i][:, k, c * 128:(c + 1) * 128],
                                                        silub[:, k:k + 1], start=(k == 0), stop=(k == KC - 1)))
                                for k in range(KC)],
                         reads=[(wn,), ("silub",)], writes=[("ps", MODB)])
            s.op("dve", lambda e: e.tensor_tensor(out=modT[:, 0:16], in0=bank(MODB)[:, 0:16], in1=badaT[:, 0:16], op=ALU.add),
                 reads=[("badaT",)], writes=[("ps", MODB), ("modT",)])
            s.op("dve", lambda e: e.scalar_tensor_tensor(out=amix, in0=modT[:, 8:16], scalar=1.0, in1=gpreT[:, 0:8],
                                                         op0=ALU.add, op1=ALU.mult),
                 reads=[("modT",), ("gpreT",)], writes=[("amix",)])
            A.release("wada0"); A.release("wada1")
        self.silub, self.badaT_t, self.gpreT_t = silub, badaT, gpreT
        self.modT, self.amix, self.affn, self.ones_b, self.siluf = modT, amix, affn, ones_b, siluf
        if upto <= 0:
            return

        w_in_v0 = self.w_in.rearrange("(k p) n -> p k n", p=128)
        self.wv_t = A.alloc("wv", (KC, 512), BF16)
        s.dma("pool", self.wv_t, w_in_v0[:, :, C_V:C_V + 512], writes=[("wv",)])
        self.wmisc_t = A.alloc("wmisc", (KC, 128), BF16)
        s.op("dve", lambda e: e.memset(self.wmisc_t.rearrange("p a b -> p (a b)"), 0.0), writes=[("wmisc", i) for i in range(4)])
        s.dma("pool", self.wmisc_t[:, :, 0:8], w_in_v0[:, :, C_F:C_F + 8], writes=[("wmisc", 0)])
        s.dma("pool", self.wmisc_t[:, :, 64:96], w_in_v0[:, :, C_KR:C_KR + 32], writes=[("wmisc", 1)])
        s.dma("pool", self.wmisc_t[:, :, 96:112], w_in_v0[:, :, C_KR + 16:C_KR + 32], writes=[("wmisc", 2)])
        s.dma("pool", self.wmisc_t[:, :, 112:128], w_in_v0[:, :, C_KR:C_KR + 16], writes=[("wmisc", 3)])
        hT = A.alloc("hT", (KC, S), BF16)
        self.hT = hT
        ss = A.alloc("ss", (NT,), F32)
        rstd = A.alloc("rstd", (NT,), F32)
        self.ss, self.rstd = ss, rstd
        self.prenorm(self.x, None, None, hT, "hT", ss, rstd, "amix", list(range(NB)), 0)
        s0_compute()
        for c in range(KC):
            hk = [("hT", c, T) for T in range(NB)]
            if c % 2 == 0:
                s.op("act", lambda e: e.activation(out=hT[:, c, :], in_=hT[:, c, :], func=AF.Identity,
                                                   bias=modT[:, c:c + 1], scale=amix[:, c:c + 1]),
                     reads=hk + [("amix",), ("modT",)], writes=hk)
            else:
                s.op("dve", lambda e: e.tensor_scalar(out=hT[:, c, :], in0=hT[:, c, :], scalar1=amix[:, c:c + 1],
                                                      scalar2=modT[:, c:c + 1], op0=ALU.mult, op1=ALU.add),
                     reads=hk + [("amix",), ("modT",)], writes=hk)
        self.dump("hT0", hT[:, 0, :], 128, S, BF16, [("hT", 0, T) for T in range(NB)])
        self.dump("hT7", hT[:, 7, :], 128, S, BF16, [("hT", 7, T) for T in range(NB)])
        R = (64, 96)
        CC = A.alloc("CC", (S,), BF16, parts=R)
        SS = A.alloc("SS", (S,), BF16, parts=R)
        posi = A.alloc("posi", (512,), I32)
        posf = A.alloc("posf", (512,), F32)
        kf = A.alloc("kf", (512,), F32)
        ki = A.alloc("ki", (512,), I32)
        tq = A.alloc("tabq", (512,), BF16)
        s.dma("pool", posi, self.pos[:, 0:512], writes=[("posi",)])
        s.op("dve", lambda e: e.tensor_copy(out=posf, in_=posi), reads=[("posi",)], writes=[("posf",)])
        ang = posi.bitcast(F32)
        for (tab, name, c0) in ((CC, "CC", 0), (SS, "SS", 2)):
            s.op("dve", lambda e: e.tensor_scalar(out=ang, in0=posf, scalar1=ropec[:, c0:c0 + 1],
                                                  scalar2=ropec[:, c0 + 1:c0 + 2], op0=ALU.mult, op1=ALU.add),
                 reads=[("posf",), ("ropec",)], writes=[("posi",)])
            s.op("dve", lambda e: e.tensor_scalar(out=kf, in0=ang, scalar1=1.0 / (2.0 * math.pi), scalar2=None,
                                                  op0=ALU.mult),
                 reads=[("posi",)], writes=[("kf",)])
            s.op("dve", lambda e: e.tensor_copy(out=ki, in_=kf), reads=[("kf",)], writes=[("ki",)])
            s.op("dve", lambda e: e.tensor_copy(out=kf, in_=ki), reads=[("ki",)], writes=[("kf",)])
            s.op("dve", lambda e: e.scalar_tensor_tensor(out=ang, in0=kf, scalar=-2.0 * math.pi, in1=ang,
                                                         op0=ALU.mult, op1=ALU.add),
                 reads=[("kf",), ("posi",)], writes=[("posi",)])
            s.op("dve", lambda e: e.tensor_scalar(out=kf, in0=ang, scalar1=math.pi, scalar2=-2.0 * math.pi,
                                                  op0=ALU.is_gt, op1=ALU.mult),
                 reads=[("posi",)], writes=[("kf",)])
            s.op("dve", lambda e: e.tensor_tensor(out=ang, in0=ang, in1=kf, op=ALU.add),
                 reads=[("posi",), ("kf",)], writes=[("posi",)])
            s.op("dve", lambda e: e.tensor_scalar(out=ang, in0=ang, scalar1=-math.pi, scalar2=math.pi,
                                                  op0=ALU.max, op1=ALU.min),
                 reads=[("posi",)], writes=[("posi",)])
            s.op("act", lambda e: e.activation(out=tq, in_=ang, func=AF.Sin),
                 reads=[("posi",)], writes=[("tabq",)])
            for q in range(4):
                s.dma("pool", tab[:, q * 512:(q + 1) * 512], tq[q * 32:(q + 1) * 32, :], reads=[("tabq",)], writes=[(name, q)])
        self.CC, self.SS = CC, SS
        self.dump("CC", CC, 32, S, BF16, [("CC", q) for q in range(4)])
        self.dump("SS", SS, 32, S, BF16, [("SS", q) for q in range(4)])
        for nme in ("posi", "posf", "kf", "ki", "tabq"):
            A.release(nme)
        if upto <= 1:
            return
        self._body2(upto)

    def prenorm(self, src_rows, a_sc, shift_sc, dst, dname, ss, rstd, aname, Ts, tok0, src_sbuf=None):
        s, A = self.s, self.A
        nxt = 8 if src_sbuf is None else 4
        xts = [A.alloc("xt%d" % i, (D,), F32) for i in range(nxt)] if src_sbuf is None else None
        tbanks = (0, 1, 2, 3, 6, 7) if src_sbuf is None else (6, 7)
        ntb = 0
        xb = [A.alloc("xb%d" % i, (D,), BF16) for i in range(4)]
        junk = A.alloc("junk", (D,), BF16)
        for T in Ts:
            for i in range(4):
                it = 4 * T + i
                if src_sbuf is None:
                    xt, xk = xts[it % nxt], ("xt%d" % (it % nxt),)
                    s.dma("sp", xt, src_rows[it * 128:(it + 1) * 128, :], writes=[xk])
                else:
                    xt, xk = src_sbuf(it)
                s.op("act", lambda e: e.activation(out=junk, in_=xt, func=AF.Square, accum_out=ss[:, it:it + 1]),
                     reads=[xk], writes=[("junk",), ("ss", it)])
            sl4 = slice(4 * T, 4 * T + 4)
            s.op("act", lambda e: e.activation(out=rstd[:, sl4], in_=ss[:, sl4], func=AF.Ln, bias=self.eps_c, scale=1.0 / D),
                 reads=[("ss", 4 * T + i) for i in range(4)] + [("eps_c",)], writes=[("rstd", T)])
            s.op("act", lambda e: e.activation(out=rstd[:, sl4], in_=rstd[:, sl4], func=AF.Exp, scale=-0.5),
                 reads=[("rstd", T)], writes=[("rstd", T)])
            for i in range(4):
                it = 4 * T + i
                xt, xk = (xts[it % nxt], ("xt%d" % (it % nxt),)) if src_sbuf is None else src_sbuf(it)
                s.op("dve", lambda e: e.tensor_scalar(out=xb[i], in0=xt, scalar1=rstd[:, it:it + 1], scalar2=None,
                                                      op0=ALU.mult),
                     reads=[xk, ("rstd", T)], writes=[("xb%d" % i,)])
            for c in range(KC):
                b = tbanks[ntb % len(tbanks)]
                ntb += 1
                psb = self.bankbf(b)
                s.op("pe", [(lambda e, i=i: e.transpose(out=psb[:, i * 128:(i + 1) * 128],
                                                        in_=xb[i][:, c * 128:(c + 1) * 128], identity=self.ident))
                            for i in range(4)],
                     reads=[("xb%d" % i,) for i in range(4)] + [("ident",)], writes=[("ps", b)])
                dsl = dst[:, c, (T - tok0) * 512:(T - tok0 + 1) * 512]
                if a_sc is None:
                    if c % 2 == 0:
                        s.op("act", lambda e: e.activation(out=dsl, in_=psb[:, 0:512], func=AF.Copy),
                             writes=[("ps", b), (dname, c, T)])
                    else:
                        s.op("dve", lambda e: e.tensor_copy(out=dsl, in_=psb[:, 0:512]),
                             writes=[("ps", b), (dname, c, T)])
                elif c % 2 == 0:
                    s.op("act", lambda e: e.activation(out=dsl, in_=psb[:, 0:512], func=AF.Identity,
                                                       bias=shift_sc[:, c:c + 1], scale=a_sc[:, c:c + 1]),
                         reads=[(aname,), ("modT",), ("modT2",)], writes=[("ps", b), (dname, c, T if dname == "hT" else 0)])
                else:
                    s.op("dve", lambda e: e.tensor_scalar(out=dsl, in0=psb[:, 0:512], scalar1=a_sc[:, c:c + 1],
                                                          scalar2=shift_sc[:, c:c + 1], op0=ALU.mult, op1=ALU.add),
                         reads=[(aname,), ("modT",), ("modT2",)], writes=[("ps", b), (dname, c, T if dname == "hT" else 0)])
        if xts is not None:
            for i in range(nxt):
                A.release("xt%d" % i)
        for i in range(4):
            A.release("xb%d" % i)
        A.release("junk")

    def _body2(self, upto):
        nc, s, A = self.nc, self.s, self.A
        bank, bankbf = self.bank, self.bankbf
        hT, CC, SS = self.hT, self.CC, self.SS
        w_in_v = self.w_in.rearrange("(k p) n -> p k n", p=128)
        hkeys = lambda T: [("hT", k, T) for k in range(KC)]
        TS = lambda T: slice(T * 512, (T + 1) * 512)
        mm = lambda out, l, r, st, sp: (lambda e: e.matmul(out, l, r, start=st, stop=sp))

        oT = A.alloc("oT", (4, S), BF16)
        qk = [{n: A.alloc(n + str(st), (S,), BF16, parts=(0, 96)) for n in ("qA", "qB", "kA", "kB")} for st in range(2)]
        vaug = A.alloc("vaug", (NT, 4, 192), BF16)
        self.pT = [A.alloc("pT%d" % i, (512,), BF16) for i in range(6)]
        rec_t = A.alloc("rec0", (512,), F32)
        self.rec = [rec_t, rec_t]
        self.qk, self.vaug, self.oT = qk, vaug, oT
        self.att_cnt = 0

        s.op("dve", lambda e: e.memset(vaug.rearrange("p a b c -> p (a b c)"), 1.0),
             writes=[("vaug", it, e_) for it in range(NT) for e_ in range(3)])
        wv = self.wv_t

        def v_proj(lhs_of, nk, wt, wname, rkeys):
            for it in range(NT):
                b = 6 + it % 2
                s.op("pe", [mm(bank(b), lhs_of(k, it), wt[:, k, :], k == 0, k == nk - 1) for k in range(nk)],
                     reads=rkeys(it // 4) + [(wname,)], writes=[("ps", b)])
                pv = bank(b).rearrange("p (c e d) -> p c e d", c=4, e=2, d=64)
                s.op("act", lambda e: e.activation(out=vaug[:, it, :, 0:64], in_=pv[:, :, 0, :], func=AF.Copy),
                     writes=[("ps", b), ("vaug", it, 0)])
                s.op("dve", lambda e: e.tensor_copy(out=vaug[:, it, :, 128:192], in_=pv[:, :, 1, :]),
                     writes=[("ps", b), ("vaug", it, 1)])

        v_proj(lambda k, it: hT[:, k, it * 128:(it + 1) * 128], KC, wv, "wv", hkeys)
        A.release("wv")

        wmisc = self.wmisc_t
        P8 = (0, 8)
        nbneg = A.alloc("nbneg", (1,), F32, parts=P8)
        eT = A.alloc("eT", (512,), F32, parts=P8)
        nlf = A.alloc("nlf", (512,), F32, parts=P8)
        onesf = A.alloc("onesf", (512,), F32, parts=P8)
        G = A.alloc("G", (S,), F32, parts=P8)
        negGb = A.alloc("negGb", (S,), BF16, parts=P8)
        Gtok = A.alloc("Gtok", (128,), F32)
        R = (64, 96)
        kpe = A.alloc("kpe", (S,), BF16, parts=R)
        t1 = A.alloc("t1", (512,), BF16, parts=R)
        t2 = A.alloc("t2", (512,), BF16, parts=R)
        self.t1, self.t2, self.Gtok, self.kpe_t = t1, t2, Gtok, kpe
        s.op("dve", lambda e: e.tensor_scalar(out=nbneg, in0=self.nbf_t, scalar1=-1.0, scalar2=None, op0=ALU.mult),
             reads=[("nbf",)], writes=[("nbneg",)])
        s.op("dve", lambda e: e.memset(onesf, 1.0), writes=[("onesf",)])
        for T in range(NB):
            b = 6 + T % 2
            s.op("pe", [mm(bank(b), wmisc[:, k, :], hT[:, k, TS(T)], k == 0, k == KC - 1) for k in range(KC)],
                 reads=hkeys(T) + [("wmisc", i) for i in range(4)], writes=[("ps", b)])
            s.op("act", lambda e: e.activation(out=eT, in_=bank(b)[0:8, :], func=AF.Exp, bias=nbneg, scale=-1.0),
                 reads=[("nbneg",)], writes=[("ps", b), ("eT",)])
            s.op("act", lambda e: e.activation(out=nlf, in_=eT, func=AF.Ln, bias=1.0), reads=[("eT",)], writes=[("nlf",)])
            init = 0.0 if T == 0 else G[:, T * 512 - 1:T * 512]
            s.op("dve", lambda e: e.tensor_tensor_scan(out=G[:, TS(T)], data0=onesf, data1=nlf, initial=init,
                                                       op0=ALU.mult, op1=ALU.add),
                 reads=[("nlf",), ("onesf",)] + ([("G", T - 1)] if T else []), writes=[("G", T)])
            s.op("dve", lambda e: e.tensor_tensor(out=t1, in0=bank(b)[64:96, :], in1=CC[:, TS(T)], op=ALU.mult),
                 reads=[("CC", q_) for q_ in range(4)], writes=[("ps", b), ("t1",)])
            s.op("dve", lambda e: e.tensor_tensor(out=t2, in0=bank(b)[96:128, :], in1=SS[:, TS(T)], op=ALU.mult),
                 reads=[("SS", q_) for q_ in range(4)], writes=[("ps", b), ("t2",)])
            s.op("dve", lambda e: e.tensor_tensor(out=kpe[:, TS(T)], in0=t1, in1=t2, op=ALU.add),
                 reads=[("t1",), ("t2",)], writes=[("kpe", T)])
        s.op("dve", lambda e: e.tensor_scalar(out=negGb, in0=G, scalar1=-1.0, scalar2=None, op0=ALU.mult),
             reads=[("G", T) for T in range(NB)], writes=[("negGb",)])
        GB = 5
        s.op("pe", [(lambda e, it=it: e.transpose(out=bank(GB)[:, it * 8:(it + 1) * 8], in_=G[:, it * 128:(it + 1) * 128],
                                                  identity=self.identf[0:8, 0:8])) for it in range(NT)],
             reads=[("G", T) for T in range(NB)] + [("identf",)], writes=[("ps", GB)])
        s.op("dve", lambda e: e.tensor_copy(out=Gtok, in_=bank(GB)[:, 0:128]), writes=[("ps", GB), ("Gtok",)])
        self.dump("G", G, 8, S, F32, [("G", T) for T in range(NB)])
        self.dump("Gtok", Gtok, 128, 128, F32, [("Gtok",)])
        self.dump("kpe", kpe, 32, S, BF16, [("kpe", T) for T in range(NB)])
        A.release("wmisc"); A.release("eT"); A.release("nlf"); A.release("onesf")
        if upto <= 2:
            return

        wf = [A.alloc("wf0", (KC, 256), BF16), A.alloc("wf1", (KC, 256), BF16)]
        for st in range(2):
            for n in ("kA", "kB"):
                s.op("dve", lambda e: e.memset(qk[st][n][64:65, :], 1.0), writes=[(n + str(st), "g")])

        self.sb_cnt = 0

        def sbank():
            self.sb_cnt += 1
            return 6 + self.sb_cnt % 2

        self.sbank = sbank

        def fox_proj(c):
            st = c % 2
            Q = qk[st]
            w, wn = wf[c % 2], "wf%d" % (c % 2)
            s.dma("pool", w[:, :, 0:128], w_in_v[:, :, C_Q + c * 128:C_Q + (c + 1) * 128], writes=[(wn, "q")])
            s.dma("pool", w[:, :, 128:256], w_in_v[:, :, C_K + c * 128:C_K + (c + 1) * 128], writes=[(wn, "k")])
            s.dma("pool", Q["qA"][64:65, :], negGb[2 * c:2 * c + 1, :], reads=[("negGb",)], writes=[("qA%d" % st, "g")])
            s.dma("pool", Q["qB"][64:65, :], negGb[2 * c + 1:2 * c + 2, :], reads=[("negGb",)], writes=[("qB%d" % st, "g")])
            yield
            for T in range(NB):
                b6 = sbank()
                s.op("pe", [mm(bank(b6), w[:, k, 0:128], hT[:, k, TS(T)], k == 0, k == KC - 1) for k in range(KC)],
                     reads=hkeys(T) + [(wn, "q")], writes=[("ps", b6)])
                s.op("dve", lambda e: e.tensor_scalar(out=Q["qA"][0:64, TS(T)], in0=bank(b6)[0:64, :], scalar1=0.125, scalar2=None, op0=ALU.mult),
                     writes=[("ps", b6), ("qA%d" % st, T)])
                s.op("dve", lambda e: e.tensor_scalar(out=Q["qB"][0:64, TS(T)], in0=bank(b6)[64:128, :], scalar1=0.125, scalar2=None, op0=ALU.mult),
                     writes=[("ps", b6), ("qB%d" % st, T)])
                yield
                b7 = sbank()
                s.op("pe", [mm(bank(b7), w[:, k, 128:256], hT[:, k, TS(T)], k == 0, k == KC - 1) for k in range(KC)],
                     reads=hkeys(T) + [(wn, "k")], writes=[("ps", b7)])
                s.op("dve", lambda e: e.tensor_copy(out=Q["kA"][0:64, TS(T)], in_=bank(b7)[0:64, :]),
                     writes=[("ps", b7), ("kA%d" % st, T)])
                s.op("dve", lambda e: e.tensor_copy(out=Q["kB"][0:64, TS(T)], in_=bank(b7)[64:128, :]),
                     writes=[("ps", b7), ("kB%d" % st, T)])
                yield

        for _ in fox_proj(0):
            pass
        self.dump("qA", qk[0]["qA"][0:65, :], 65, S, BF16, [("qA0", T) for T in range(NB)] + [("qA0", "g")])
        self.dump("kA", qk[0]["kA"][0:65, :], 65, S, BF16, [("kA0", T) for T in range(NB)] + [("kA0", "g")])
        for c in range(4):
            bg = fox_proj(c + 1) if c < 3 else None
            self.attention(c, True, c % 2, bg)
        for c in range(4):
            self.dump("oaT%d" % c, oT[:, c, :], 128, S, BF16, [("oT", c, T, hh) for T in range(NB) for hh in range(2)])
        A.release("wf0"); A.release("wf1"); A.release("G"); A.release("negGb"); A.release("nbneg")
        if upto <= 3:
            return
        self._body3(upto)

    def attention(self, c, fox, st=0, bg=None):
        s = self.s
        bank = self.bank
        qk, vaug, pT, rec = self.qk[st], self.vaug, self.pT, self.rec
        oT, on = (self.oT, "oT") if fox else (self.oT2, "oT2")
        nstep = 0
        KD = 65 if fox else 96
        scale = 1.0 if fox else 1.0 / math.sqrt(96.0)
        SB = (0, 1, 2, 3)
        LA = 2
        NPT = len(pT)
        for T in range(NB):
            nj = 4 * T + 4
            info = {}
            accb = 4

            def emit_S(hh, j):
                h = 2 * c + hh
                qn, kn = ("qA", "kA") if hh == 0 else ("qB", "kB")
                q, k = qk[qn], qk[kn]
                qn, kn = qn + str(st), kn + str(st)
                off = 0 if j < 4 * T else (j - 4 * T) * 128
                w = 512 - off
                n = self.att_cnt
                self.att_cnt += 1
                b = SB[n % len(SB)]
                pt, ptn = pT[n % NPT], "pT%d" % (n % NPT)
                diag = j >= 4 * T
                fns = [lambda e: e.matmul(bank(b)[:, 0:w], k[0:KD, j * 128:(j + 1) * 128],
                                          q[0:KD, T * 512 + off:(T + 1) * 512], start=True, stop=not diag)]
                if diag:
                    fns.append(lambda e: e.matmul(bank(b)[:, 0:128], self.ident, self.mask, start=False, stop=True))
                s.op("pe", fns, reads=[(qn, T), (qn, "g"), (kn, j // 4), (kn, "g"), ("ident",), ("mask",)],
                     writes=[("ps", b)])
                if fox:
                    s.op("act", lambda e: e.activation(out=pt[:, 0:w], in_=bank(b)[:, 0:w], func=AF.Exp,
                                                       bias=self.Gtok[:, j * 8 + h:j * 8 + h + 1], scale=1.0),
                         reads=[("Gtok",)], writes=[("ps", b), (ptn,)])
                else:
                    s.op("act", lambda e: e.activation(out=pt[:, 0:w], in_=bank(b)[:, 0:w], func=AF.Exp, scale=scale),
                         writes=[("ps", b), (ptn,)])
                info[(hh, j)] = (off, w, pt, ptn)

            def emit_PV(hh, j):
                off, w, pt, ptn = info[(hh, j)]
                acc = accb + hh
                vsl = slice(0, 128) if hh == 0 else slice(64, 192)
                s.op("pe", lambda e: e.matmul(bank(acc)[:, off:512], vaug[:, j, c, vsl], pt[:, 0:w],
                                              start=(j == 0), stop=(j == nj - 1)),
                     reads=[(ptn,), ("vaug", j, 0), ("vaug", j, 1), ("vaug", j, 2)], writes=[("ps", acc)])

            for step in range(nj + LA):
                for hh in (0, 1):
                    if step < nj:
                        emit_S(hh, step)
                    if step - LA >= 0:
                        emit_PV(hh, step - LA)
                nstep += 1
                if bg is not None and nstep % 4 == 0:
                    next(bg, None)
            for hh in (0, 1):
                acc = accb + hh
                r = rec[hh]
                if hh == 0:
                    osl, dsl = slice(0, 64), slice(64, 128)
                else:
                    osl, dsl = slice(64, 128), slice(0, 64)
                s.op("act", lambda e: e.activation(out=r[osl, :], in_=bank(acc)[dsl, :], func=AF.Ln),
                     writes=[("ps", acc), ("rec0", hh)])
                s.op("act", lambda e: e.activation(out=r[osl, :], in_=r[osl, :], func=AF.Exp, scale=-1.0),
                     reads=[("rec0", hh)], writes=[("rec0", hh)])
                s.op("dve", lambda e: e.tensor_tensor(out=oT[osl, c, T * 512:(T + 1) * 512], in0=bank(acc)[osl, :],
                                                      in1=r[osl, :], op=ALU.mult),
                     reads=[("rec0", hh)], writes=[("ps", acc), (on, c, T, hh)])
        if bg is not None:
            for _ in bg:
                pass

    def gate_merge(self, col0, w_proj, first, oT, on):
        s, A = self.s, self.A
        bank = self.bank
        hT, merged = self.hT, self.merged
        mm = lambda out, l, r, st, sp: (lambda e: e.matmul(out, l, r, start=st, stop=sp))
        w_in_v = self.w_in.rearrange("(k p) n -> p k n", p=128)
        wg, wgn, wp, wpn = self.gate_w[0 if first else 1]
        sg = [A.alloc("sg%d" % i, (512,), BF16) for i in range(2)]
        tmp = [A.alloc("gtmp%d" % i, (512,), BF16) for i in range(2)]
        n = 0
        for T in range(NB):
            Tsl = slice(T * 512, (T + 1) * 512)
            for m in range(KC):
                i = n % 2
                n += 1
                bg, bp = 0 + i, 2 + i
                s.op("pe", [mm(bank(bg), wg[:, k, m * 128:(m + 1) * 128], hT[:, k, Tsl], k == 0, k == KC - 1) for k in range(KC)],
                     reads=[("hT", k, T) for k in range(KC)] + [(wgn,)], writes=[("ps", bg)])
                s.op("act", lambda e: e.activation(out=sg[i], in_=bank(bg), func=AF.Sigmoid),
                     writes=[("ps", bg), ("sg%d" % i,)])
                s.op("pe", [mm(bank(bp), wp[:, cc, m * 128:(m + 1) * 128], oT[:, cc, Tsl], cc == 0, cc == 3) for cc in range(4)],
                     reads=[(on, cc, T, hh) for cc in range(4) for hh in range(2)] + [(wpn,)], writes=[("ps", bp)])
                if first:
                    s.op("dve", lambda e: e.tensor_tensor(out=merged[:, m, Tsl], in0=bank(bp), in1=sg[i], op=ALU.mult),
                         reads=[("sg%d" % i,)], writes=[("ps", bp), ("merged", m, T)])
                else:
                    s.op("dve", lambda e: e.tensor_tensor(out=tmp[i], in0=bank(bp), in1=sg[i], op=ALU.mult),
                         reads=[("sg%d" % i,)], writes=[("ps", bp), ("gtmp%d" % i,)])
                    s.op("dve", lambda e: e.tensor_tensor(out=merged[:, m, Tsl], in0=merged[:, m, Tsl], in1=tmp[i], op=ALU.add),
                         reads=[("gtmp%d" % i,), ("merged", m, T)], writes=[("merged", m, T)])
        for nme in (wgn, wpn, "sg0", "sg1", "gtmp0", "gtmp1"):
            A.release(nme)

    def _body3(self, upto):
        nc, s, A = self.nc, self.s, self.A
        bank = self.bank
        hT, CC, SS, qk, vaug = self.hT, self.CC, self.SS, self.qk, self.vaug
        t1, t2 = self.t1, self.t2
        w_in_v = self.w_in.rearrange("(k p) n -> p k n", p=128)
        hkeys = lambda T: [("hT", k, T) for k in range(KC)]
        TS = lambda T: slice(T * 512, (T + 1) * 512)
        mm = lambda out, l, r, st, sp: (lambda e: e.matmul(out, l, r, start=st, stop=sp))


        cqn = A.alloc("cqn", (6, S), BF16)
        ckvn = A.alloc("ckvn", (2, S), BF16)
        sqb = [A.alloc("sqb%d" % i, (512,), BF16) for i in range(3)]
        rstdb = A.alloc("rstdb", (512,), F32)
        wcq = A.alloc("wcq", (KC, 768), BF16)
        wckv = A.alloc("wckv", (KC, 256), BF16)
        s.dma("pool", wcq, w_in_v[:, :, C_CQ:C_CQ + 768], writes=[("wcq",)])
        s.dma("pool", wckv, w_in_v[:, :, C_CKV:C_CKV + 256], writes=[("wckv",)])
        for (wt, wn, nm, dst, dn, gT, gname, nfeat) in ((wcq, "wcq", 6, cqn, "cqn", self.gqT_t, "gqT", 768.0),
                                                        (wckv, "wckv", 2, ckvn, "ckvn", self.gkvT_t, "gkvT", 256.0)):
            for T in range(NB):
                for m in range(nm):
                    b = 6 + m % 2
                    i = m % 2
                    s.op("pe", [mm(bank(b), wt[:, k, m * 128:(m + 1) * 128], hT[:, k, TS(T)], k == 0, k == KC - 1) for k in range(KC)],
                         reads=hkeys(T) + [(wn,)], writes=[("ps", b)])
                    s.op("act", lambda e: e.activation(out=sqb[i], in_=bank(b), func=AF.Square),
                         writes=[("ps", b), ("sqb%d" % i,)])
                    s.op("dve", lambda e: e.tensor_scalar(out=dst[:, m, TS(T)], in0=bank(b), scalar1=gT[:, m:m + 1],
                                                          scalar2=None, op0=ALU.mult),
                         reads=[(gname,)], writes=[("ps", b), (dn, m, T)])
                    s.op("pe", mm(bank(5), self.ones_b, sqb[i], m == 0, m == nm - 1),
                         reads=[("sqb%d" % i,), ("ones_b",)], writes=[("ps", 5)])
                s.op("act", lambda e: e.activation(out=rstdb, in_=bank(5), func=AF.Ln, bias=self.eps_c, scale=1.0 / nfeat),
                     reads=[("eps_c",)], writes=[("ps", 5), ("rstdb",)])
                s.op("act", lambda e: e.activation(out=rstdb, in_=rstdb, func=AF.Exp, scale=-0.5),
                     reads=[("rstdb",)], writes=[("rstdb",)])
                for m in range(nm):
                    s.op("dve", lambda e: e.tensor_tensor(out=dst[:, m, TS(T)], in0=dst[:, m, TS(T)], in1=rstdb, op=ALU.mult),
                         reads=[(dn, m, T), ("rstdb",)], writes=[(dn, m, T)])
        self.dump("cqn0", cqn[:, 0, :], 128, S, BF16, [("cqn", 0, T) for T in range(NB)])
        self.dump("ckvn1", ckvn[:, 1, :], 128, S, BF16, [("ckvn", 1, T) for T in range(NB)])
        A.release("wcq"); A.release("wckv"); A.release("sqb0"); A.release("sqb1"); A.release("sqb2"); A.release("rstdb")

        oT2 = A.alloc("oT2", (4, S), BF16)
        self.oT2 = oT2
        wuq = A.alloc("wuq", (6, 8, 128), BF16)
        w_uq_v = self.w_uq.rearrange("(k p) (h e) -> p k h e", p=128, e=96)
        for kc in range(6):
            s.dma("pool", wuq[:, kc, :, 0:96], w_uq_v[:, kc, :, :], writes=[("wuq", 0, kc)])
            s.dma("pool", wuq[:, kc, :, 96:112], w_uq_v[:, kc, :, 80:96], writes=[("wuq", 1, kc)])
            s.dma("pool", wuq[:, kc, :, 112:128], w_uq_v[:, kc, :, 64:80], writes=[("wuq", 2, kc)])
        wkn = A.alloc("wkn", (2, 512), BF16)
        wvm = A.alloc("wvm", (2, 512), BF16)
        w_ukv_v = self.w_ukv.rearrange("(k p) (h e) -> p k h e", p=128, e=128)
        for kc in range(2):
            s.dma("pool", wkn[:, kc, :].rearrange("p (h e) -> p h e", e=64), w_ukv_v[:, kc, :, 0:64], writes=[("wkn", kc)])
            s.dma("pool", wvm[:, kc, :].rearrange("p (h e) -> p h e", e=64), w_ukv_v[:, kc, :, 64:128], writes=[("wvm", kc)])

        for it in range(NT):
            b = 6 + it % 2
            T = it // 4
            s.op("pe", [mm(bank(b), ckvn[:, k, it * 128:(it + 1) * 128], wvm[:, k, :], k == 0, k == 1) for k in range(2)],
                 reads=[("ckvn", k, T) for k in range(2)] + [("wvm", 0), ("wvm", 1)], writes=[("ps", b)])
            pv = bank(b).rearrange("p (c e d) -> p c e d", c=4, e=2, d=64)
            s.op("act", lambda e: e.activation(out=vaug[:, it, :, 0:64], in_=pv[:, :, 0, :], func=AF.Copy),
                 writes=[("ps", b), ("vaug", it, 0)])
            s.op("dve", lambda e: e.tensor_copy(out=vaug[:, it, :, 128:192], in_=pv[:, :, 1, :]),
                 writes=[("ps", b), ("vaug", it, 1)])

        wuq_keys = [("wuq", i, kc) for i in range(3) for kc in range(6)]

        def mla_proj(c):
            st = c % 2
            Q = qk[st]
            for kn in ("kA", "kB"):
                s.op("dve", lambda e: e.tensor_copy(out=Q[kn][64:96, :], in_=self.kpe_t),
                     reads=[("kpe", T) for T in range(NB)], writes=[(kn + str(st), "g")])
            yield
            nb = 0
            for T in range(NB):
                b7 = self.sbank()
                s.op("pe", [mm(bank(b7), wkn[:, k, c * 128:(c + 1) * 128], ckvn[:, k, TS(T)], k == 0, k == 1) for k in range(2)],
                     reads=[("ckvn", k, T) for k in range(2)] + [("wkn", 0), ("wkn", 1)], writes=[("ps", b7)])
                s.op("dve", lambda e: e.tensor_copy(out=Q["kA"][0:64, TS(T)], in_=bank(b7)[0:64, :]),
                     writes=[("ps", b7), ("kA%d" % st, T)])
                s.op("dve", lambda e: e.tensor_copy(out=Q["kB"][0:64, TS(T)], in_=bank(b7)[64:128, :]),
                     writes=[("ps", b7), ("kB%d" % st, T)])
                yield
                for hh in (0, 1):
                    h = 2 * c + hh
                    qn = ("qA" if hh == 0 else "qB")
                    q = Q[qn]
                    qn = qn + str(st)
                    b6 = self.sbank()
                    s.op("pe", [mm(bank(b6), wuq[:, k, h, :], cqn[:, k, TS(T)], k == 0, k == 5) for k in range(6)],
                         reads=[("cqn", k, T) for k in range(6)] + wuq_keys, writes=[("ps", b6)])
                    s.op("dve", lambda e: e.tensor_copy(out=q[0:64, TS(T)], in_=bank(b6)[0:64, :]),
                         writes=[("ps", b6), (qn, T)])
                    s.op("dve", lambda e: e.tensor_tensor(out=t1, in0=bank(b6)[64:96, :], in1=CC[:, TS(T)], op=ALU.mult),
                         reads=[("CC", q_) for q_ in range(4)], writes=[("ps", b6), ("t1",)])
                    s.op("dve", lambda e: e.tensor_tensor(out=t2, in0=bank(b6)[96:128, :], in1=SS[:, TS(T)], op=ALU.mult),
                         reads=[("SS", q_) for q_ in range(4)], writes=[("ps", b6), ("t2",)])
                    s.op("dve", lambda e: e.tensor_tensor(out=q[64:96, TS(T)], in0=t1, in1=t2, op=ALU.add),
                         reads=[("t1",), ("t2",)], writes=[(qn, T)])
                    yield

        for _ in mla_proj(0):
            pass
        self.dump("qm0", qk[0]["qA"][0:96, :], 96, S, BF16, [("qA0", T) for T in range(NB)])
        self.dump("km0", qk[0]["kA"][0:96, :], 96, S, BF16, [("kA0", T) for T in range(NB)] + [("kA0", "g")])
        for c in range(4):
            bg = mla_proj(c + 1) if c < 3 else None
            if c == 3:
                for nme in ("wuq", "wkn", "wvm", "cqn", "ckvn", "kpe", "t1", "t2", "CC", "SS"):
                    A.release(nme)
                self.gvec_prefetch()
            self.attention(c, False, c % 2, bg)
        for c in range(4):
            self.dump("obT%d" % c, oT2[:, c, :], 128, S, BF16, [("oT2", c, T, hh) for T in range(NB) for hh in range(2)])
        for nme in ("qA0", "qB0", "kA0", "kB0", "qA1", "qB1", "kA1", "kB1", "vaug", "pT0", "pT1", "pT2", "pT3", "pT4", "pT5",
                    "rec0", "Gtok"):
            A.release(nme)
        if upto <= 4:
            return

        merged = A.alloc("merged", (KC, S), BF16)
        self.merged = merged
        self.gate_w = []
        for i, (col0, wproj) in enumerate(((C_GF, self.w_pf), (C_GM, self.w_pm))):
            wg = A.alloc("wg%d" % i, (KC, D), BF16)
            wp = A.alloc("wp%d" % i, (4, D), BF16)
            s.dma("pool", wg, w_in_v[:, :, col0:col0 + D], writes=[("wg%d" % i,)])
            s.dma("pool", wp, wproj.rearrange("(k p) n -> p k n", p=128), writes=[("wp%d" % i,)])
            self.gate_w.append((wg, "wg%d" % i, wp, "wp%d" % i))
            if i == 0:
                self.gvec_compute()
        self.gate_merge(C_GF, self.w_pf, True, self.oT, "oT")
        self.gate_merge(C_GM, self.w_pm, False, self.oT2, "oT2")
        for m in (0, 7):
            self.dump("mg%d" % m, merged[:, m, :], 128, S, BF16, [("merged", m, T) for T in range(NB)])
        A.release("hT"); A.release("oT"); A.release("oT2")
        if upto <= 5:
            return
        self._body4(upto)

    def gvec_prefetch(self):
        s, A = self.s, self.A
        self.gvec = A.alloc("gvec", (2, D), F32)
        self.silurep = A.alloc("silurep", (KC, 128), BF16)
        self.bada_rep_t = A.alloc("bada_rep", (2 * D,), F32)
        self.gpost_rep_t = A.alloc("gpost_rep", (2 * D,), F32)
        s.dma("pool", self.bada_rep_t, self.bada_rep, writes=[("bada_rep",)])
        s.dma("pool", self.gpost_rep_t, self.gpost_rep, writes=[("gpost_rep",)])
        w_ada_v = self.w_ada.rearrange("(k p) n -> p k n", p=128)
        self.wadag = {}
        for v, hfs in ((3, (0, 1)), (4, (0, 1)), (2, (0,))):
            for hf in hfs:
                nm = "wadag%d%d" % (v, hf)
                t = A.alloc(nm, (KC, 512), BF16)
                s.dma("pool", t, w_ada_v[:, :, v * D + hf * 512:v * D + (hf + 1) * 512], writes=[(nm,)])
                self.wadag[(v, hf)] = (t, nm)

    def gvec_compute(self):
        s, A = self.s, self.A
        bank = self.bank
        gvec, silurep, bada_rep, gpost_rep = self.gvec, self.silurep, self.bada_rep_t, self.gpost_rep_t
        for k in range(KC):
            s.op("dve", lambda e: e.tensor_scalar(out=silurep[:, k, :], in0=self.ones_b, scalar1=self.siluf[:, k:k + 1],
                                                  scalar2=None, op0=ALU.mult),
                 reads=[("ones_b",), ("siluf",)], writes=[("silurep", k)])
        modT, affn = self.modT, self.affn
        MODB = 7
        for v in (3, 4):
            for c in range(KC):
                col = v * 8 + c
                wt, wn = self.wadag[(v, c // 4)]
                cc = c % 4
                s.op("pe", [(lambda e, k=k: e.matmul(bank(MODB)[:, col:col + 1], wt[:, k, cc * 128:(cc + 1) * 128],
                                                    self.silub[:, k:k + 1], start=(k == 0), stop=(k == KC - 1)))
                            for k in range(KC)],
                     reads=[(wn,), ("silub",)], writes=[("ps", MODB)])
        s.op("dve", lambda e: e.tensor_tensor(out=modT[:, 24:40], in0=bank(MODB)[:, 24:40], in1=self.badaT_t[:, 24:40], op=ALU.add),
             reads=[("badaT",)], writes=[("ps", MODB), ("modT2",)])
        s.op("dve", lambda e: e.scalar_tensor_tensor(out=affn, in0=modT[:, 32:40], scalar=1.0, in1=self.gpreT_t[:, 8:16],
                                                     op0=ALU.add, op1=ALU.mult),
             reads=[("modT2",), ("gpreT",)], writes=[("affn",)])
        w_ada_v = self.w_ada.rearrange("(k p) n -> p k n", p=128)
        for (v, hf, src_key) in ((2, 1, (4, 0)), (5, 0, (3, 0)), (5, 1, (3, 1))):
            if True:
                t, nm = self.wadag[src_key]
                s.dma("pool", t, w_ada_v[:, :, v * D + hf * 512:v * D + (hf + 1) * 512], writes=[(nm,)])
                self.wadag[(v, hf)] = (t, nm)
        for g, v in enumerate((2, 5)):
            for half in range(2):
                wt, wn = self.wadag[(v, half)]
                b = 5 + half
                s.op("pe", [(lambda e, k=k: e.matmul(bank(b), silurep[:, k, :], wt[:, k, :],
                                                    start=(k == 0), stop=(k == KC - 1))) for k in range(KC)],
                     reads=[(wn,)] + [("silurep", k) for k in range(KC)], writes=[("ps", b)])
                sl = slice(g * D + half * 512, g * D + (half + 1) * 512)
                hs = slice(half * 512, (half + 1) * 512)
                s.op("dve", lambda e: e.tensor_tensor(out=gvec[:, g, hs], in0=bank(b), in1=bada_rep[:, sl], op=ALU.add),
                     reads=[("bada_rep",)], writes=[("ps", b), ("gvec", g, half)])
                s.op("dve", lambda e: e.tensor_tensor(out=gvec[:, g, hs], in0=gvec[:, g, hs], in1=gpost_rep[:, sl], op=ALU.mult),
                     reads=[("gpost_rep",), ("gvec", g, half)], writes=[("gvec", g, half)])
        self.dump("gvec", gvec.rearrange("p a b -> p (a b)"), 128, 2 * D, F32, [("gvec", g, h) for g in range(2) for h in range(2)])
        for n in ["silurep", "bada_rep", "gpost_rep"] + sorted(set(nm for (_, nm) in self.wadag.values())):
            A.release(n)

    def _body4(self, upto):
        nc, s, A = self.nc, self.s, self.A
        bank = self.bank
        merged, gvec = self.merged, self.gvec
        mm = lambda out, l, r, st, sp: (lambda e: e.matmul(out, l, r, start=st, stop=sp))
        wout = A.alloc("wout", (KC, D), BF16)
        s.dma("pool", wout, self.w_out.rearrange("(k p) n -> p k n", p=128), writes=[("wout",)])
        w2v = self.w_f2.rearrange("(j p) n -> p j n", p=128)
        w1v = self.w_f1.rearrange("(k p) n -> p k n", p=128)
        x2 = A.alloc("x2", (NT, D), F32)
        xin = [A.alloc("xin%d" % i, (D,), F32) for i in range(2)]
        tmpf = [A.alloc("tmpf%d" % i, (512,), F32) for i in range(2)]
        junkp = A.alloc("junkp", (512,), BF16)
        ssq = A.alloc("ssq", (4 * NT,), F32)
        ssy = A.alloc("ssy", (2 * NT,), F32)
        rsy = A.alloc("rsy", (2 * NT,), F32)
        ss, rstd = self.ss, self.rstd
        w2h = [A.alloc("w2a", (11, D), BF16), A.alloc("w2b", (11, D), BF16)]
        for gi, (hf, a, b_) in enumerate(((0, 0, 6), (0, 6, 11), (1, 0, 6), (1, 6, 11))):
            s.dma("pool", w2h[hf][:, a:b_, :], w2v[:, hf * 11 + a:hf * 11 + b_, :], writes=[("w2ab"[0:2] + "ab"[hf], gi)])
        w2keys = [("w2a", 0), ("w2a", 1), ("w2b", 2), ("w2b", 3)]

        def postnorm(bks, col, g, resid, rkeys, out_ap, okeys):
            for half, b in enumerate(bks):
                s.op("act", lambda e: e.activation(out=junkp, in_=bank(b), func=AF.Square,
                                                   accum_out=ssq[:, 2 * col + half:2 * col + half + 1]),
                     writes=[("ps", b), ("junkp",), ("ssq", col, half)])
            s.op("dve", lambda e: e.tensor_tensor(out=ssy[:, col:col + 1], in0=ssq[:, 2 * col:2 * col + 1],
                                                  in1=ssq[:, 2 * col + 1:2 * col + 2], op=ALU.add),
                 reads=[("ssq", col, 0), ("ssq", col, 1)], writes=[("ssy", col)])
            s.op("act", lambda e: e.activation(out=rsy[:, col:col + 1], in_=ssy[:, col:col + 1], func=AF.Ln,
                                               bias=self.eps_c, scale=1.0 / D),
                 reads=[("ssy", col), ("eps_c",)], writes=[("rsy", col)])
            s.op("act", lambda e: e.activation(out=rsy[:, col:col + 1], in_=rsy[:, col:col + 1], func=AF.Exp, scale=-0.5),
                 reads=[("rsy", col)], writes=[("rsy", col)])
            for half, b in enumerate(bks):
                hs = slice(half * 512, (half + 1) * 512)
                s.op("dve", lambda e: e.scalar_tensor_tensor(out=tmpf[half], in0=bank(b), scalar=rsy[:, col:col + 1],
                                                             in1=gvec[:, g, hs], op0=ALU.mult, op1=ALU.mult),
                     reads=[("rsy", col), ("gvec", g, half)], writes=[("ps", b), ("tmpf%d" % half,)])
                s.op("dve", lambda e: e.tensor_tensor(out=out_ap[:, hs], in0=tmpf[half], in1=resid[:, hs], op=ALU.add),
                     reads=[("tmpf%d" % half,)] + rkeys, writes=okeys)

        for it in range(NT):
            T = it // 4
            xi, xn = xin[it % 2], "xin%d" % (it % 2)
            s.dma("sp", xi, self.x[it * 128:(it + 1) * 128, :], writes=[(xn,)])
            bks = (0, 1) if it % 2 == 0 else (2, 3)
            for half, b in enumerate(bks):
                s.op("pe", [mm(bank(b), merged[:, k, it * 128:(it + 1) * 128], wout[:, k, half * 512:(half + 1) * 512],
                               k == 0, k == KC - 1) for k in range(KC)],
                     reads=[("merged", k, T) for k in range(KC)] + [("wout",)], writes=[("ps", b)])
            postnorm(bks, it, 0, xi, [(xn,)], x2[:, it, :], [("x2", it)])
        self.dump("x2", x2[:, 0, :], 128, D, F32, [("x2", 0)])
        for nme in ("merged", "wout", "xin0", "xin1"):
            A.release(nme)
        if upto <= 6:
            return

        h2T = A.alloc("h2T", (KC, 512), BF16)
        actT = A.alloc("actT", (NFC, 512), BF16)
        w1b = [A.alloc("w1b%d" % i, (KC, 512), BF16) for i in range(3)]
        sgl = [A.alloc("sgl%d" % i, (512,), BF16) for i in range(2)]
        ot = [A.alloc("ot%d" % i, (D,), F32) for i in range(2)]
        nw1 = 0

        def ffn_prenorm(T):
            self.prenorm(None, self.affn, self.modT[:, 24:32], h2T, "h2T", ss, rstd, "affn", [T], T,
                         src_sbuf=lambda it: (x2[:, it, :], ("x2", it)))

        ffn_prenorm(0)
        for T in range(NB):
            for j in range(NFC):
                if j % 2 == 0:
                    wb, wbn = w1b[nw1 % 3], "w1b%d" % (nw1 % 3)
                    nw1 += 1
                    s.dma("pool", wb[:, :, 0:256], w1v[:, :, j * 128:(j + 2) * 128], writes=[(wbn, "g")])
                    s.dma("pool", wb[:, :, 256:512], w1v[:, :, DFF + j * 128:DFF + (j + 2) * 128], writes=[(wbn, "u")])
                jo = (j % 2) * 128
                gb, ub = 4 + 2 * (j % 2), 5 + 2 * (j % 2)
                hk = [("h2T", k, 0) for k in range(KC)]
                s.op("pe", [mm(bank(gb), wb[:, k, jo:jo + 128], h2T[:, k, :], k == 0, k == KC - 1) for k in range(KC)],
                     reads=hk + [(wbn, "g")], writes=[("ps", gb)])
                s.op("pe", [mm(bank(ub), wb[:, k, 256 + jo:256 + jo + 128], h2T[:, k, :], k == 0, k == KC - 1) for k in range(KC)],
                     reads=hk + [(wbn, "u")], writes=[("ps", ub)])
                s.op("act", lambda e: e.activation(out=sgl[j % 2], in_=bank(gb), func=AF.Silu),
                     writes=[("ps", gb), ("sgl%d" % (j % 2),)])
                s.op("dve", lambda e: e.tensor_tensor(out=actT[:, j, :], in0=bank(ub), in1=sgl[j % 2], op=ALU.mult),
                     reads=[("sgl%d" % (j % 2),)], writes=[("ps", ub), ("actT", j)])
            if T + 1 < NB:
                ffn_prenorm(T + 1)
            for i in range(4):
                it = 4 * T + i
                bks = (0, 1) if it % 2 == 0 else (2, 3)
                for half, b in enumerate(bks):
                    s.op("pe", [mm(bank(b), actT[:, j, i * 128:(i + 1) * 128], w2h[j // 11][:, j % 11, half * 512:(half + 1) * 512],
                                   j == 0, j == NFC - 1) for j in range(NFC)],
                         reads=[("actT", j) for j in range(NFC)] + w2keys, writes=[("ps", b)])
                o_t, on = ot[it % 2], "ot%d" % (it % 2)
                postnorm(bks, NT + it, 1, x2[:, it, :], [("x2", it)], o_t, [(on,)])
                s.dma("sp", self.out[it * 128:(it + 1) * 128, :], o_t, reads=[(on,)])


def _consts():
    ident = np.eye(128, dtype=np.float32)
    sk = np.arange(128)[:, None]
    tq = np.arange(128)[None, :]
    mask = np.where(sk > tq, NEG, 0.0).astype(np.float32)
    inv_freq = 1.0 / (10000.0 ** (np.arange(0, 32, 2, dtype=np.float32) / 32.0))
    rope = np.zeros((128, 4), np.float32)
    for p in range(128):
        r = p % 32
        i = r % 16
        rope[p, 0] = inv_freq[i]
        rope[p, 1] = math.pi / 2
        rope[p, 2] = inv_freq[i]
        rope[p, 3] = (math.pi if r < 16 else 0.0)
    return dict(c_ident=ident.astype(ml_dtypes.bfloat16), c_identf=ident, c_mask=mask.astype(ml_dtypes.bfloat16),
                c_rope=rope)


def make_in_maps(x, c, positions, w_ada, b_ada, g_pre_mix, g_post_mix, g_pre_ffn, g_post_ffn,
                 w_in, b_forget, g_q_lora, w_uq, g_kv_lora, w_ukv, w_proj_fox, w_proj_mla,
                 w_out, w_ffn_in, w_ffn_out, cores=range(8)):
    f = lambda a: np.ascontiguousarray(np.asarray(a, dtype=np.float32))
    colT = lambda v, n: f(np.asarray(v, np.float32).reshape(n, 128).T)
    rep = lambda v: f(np.broadcast_to(np.asarray(v, np.float32)[None, :], (128, v.shape[0])))
    b_ada0 = np.asarray(b_ada[0], np.float32)
    shared = dict(
        w_ada=f(w_ada[0]), badaT=colT(b_ada0, 48),
        bada_rep=rep(np.concatenate([b_ada0[2 * D:3 * D], b_ada0[5 * D:6 * D]])),
        gpost_rep=rep(np.concatenate([np.asarray(g_post_mix[0], np.float32), np.asarray(g_post_ffn[0], np.float32)])),
        gpreT=f(np.concatenate([colT(g_pre_mix[0], 8), colT(g_pre_ffn[0], 8)], axis=1)),
        w_in=f(w_in[0]), nbf=f(np.asarray(b_forget[0], np.float32).reshape(8, 1)),
        gqT=colT(g_q_lora[0], 6), gkvT=colT(g_kv_lora[0], 2),
        w_uq=f(w_uq[0]), w_ukv=f(w_ukv[0]), w_pf=f(w_proj_fox[0]), w_pm=f(w_proj_mla[0]),
        w_out=f(w_out[0]), w_f1=f(w_ffn_in[0]), w_f2=f(w_ffn_out[0]),
    )
    shared.update(_consts())
    maps = []
    for b in cores:
        m = dict(shared)
        m["x"] = f(x[b])
        m["cT"] = colT(c[b], 8)
        pq = np.asarray(positions[b], np.int32).reshape(4, 1, 512)
        m["pos"] = np.ascontiguousarray(np.broadcast_to(pq, (4, 32, 512)).reshape(128, 512))
        maps.append(m)
    return maps


_NC_CACHE = {}


def kernel(**inputs):
    if "nc" not in _NC_CACHE:
        _NC_CACHE["nc"] = Builder().build()
    nc = _NC_CACHE["nc"]
    maps = make_in_maps(**inputs)
    res = run_bass_kernel_spmd(nc, maps, core_ids=list(range(8)))
    out = np.stack([np.asarray(r["out"], dtype=np.float32) for r in res.results], axis=0)
    return out
```

```python
import math
import numpy as np
import ml_dtypes
import concourse.bass as bass
import concourse.mybir as mybir
from concourse.bass_utils import run_bass_kernel_spmd

F32 = mybir.dt.float32
BF16 = mybir.dt.bfloat16
I32 = mybir.dt.int32
U8 = mybir.dt.uint8
AF = mybir.ActivationFunctionType
ALU = mybir.AluOpType

S = 2048
D = 1024
NT = 16
NB = 4
KC = 8
DFF = 2816
NFC = 22
D_IN = 4648
EPS = 1e-6
C_Q, C_K, C_V, C_F, C_CQ, C_CKV, C_KR, C_GF, C_GM = 0, 512, 1024, 1536, 1544, 2312, 2568, 2600, 3624
NEG = -30000.0


class Sched:
    def __init__(self, nc, sems, dma_sems):
        self.nc = nc
        self.eng = {"pe": nc.tensor, "act": nc.scalar, "dve": nc.vector, "pool": nc.gpsimd, "sp": nc.sync}
        self.sem = sems
        self.cnt = {e: 0 for e in sems}
        self.seen = {e: {} for e in self.eng}
        self.last_w = {}
        self.readers = {}
        self.dma_sems = dma_sems
        self.dma_n = {q: 0 for q in dma_sems}
        self.inherit = {}
        self.n_wait = 0

    def _collect(self, reads, writes):
        raw, other = [], []
        for k in reads:
            w = self.last_w.get(k)
            if w is not None:
                raw.append(w)
        for k in writes:
            ps = (k[0] == "ps")
            w = self.last_w.get(k)
            if w is not None:
                other.append((w, ps))
            for t in self.readers.get(k, ()):
                other.append((t, ps))
            inh = self.inherit.get(k[0])
            if inh:
                for t in inh:
                    other.append((t, False))
        return raw, other

    def _emit_waits(self, e, raw, other):
        for tok in raw:
            if tok[2] == e and e == "pe":
                continue
            self._wait(e, tok)
        for tok, ps in other:
            if tok[2] == e:
                continue
            self._wait(e, tok)

    def _wait(self, e, tok):
        sem, val, _ = tok
        sid = id(sem)
        if self.seen[e].get(sid, 0) >= val:
            return
        self.eng[e].wait_ge(sem, val)
        self.seen[e][sid] = val
        self.n_wait += 1

    def _register(self, tok, reads, writes):
        for k in reads:
            self.readers.setdefault(k, []).append(tok)
        for k in writes:
            self.last_w[k] = tok
            self.readers[k] = []

    def op(self, e, fns, reads=(), writes=()):
        if callable(fns):
            fns = [fns]
        raw, other = self._collect(reads, writes)
        self._emit_waits(e, raw, other)
        eng = self.eng[e]
        ins = None
        for f in fns:
            ins = f(eng)
        self.cnt[e] += 1
        ins.then_inc(self.sem[e], 1)
        tok = (self.sem[e], self.cnt[e], e)
        self._register(tok, reads, writes)
        return tok

    def dma(self, q, out, in_, reads=(), writes=()):
        raw, other = self._collect(reads, writes)
        self._emit_waits(q, raw, other)
        pool = self.dma_sems[q]
        n = self.dma_n[q]
        self.dma_n[q] += 1
        sem = pool[n % len(pool)]
        rnd = n // len(pool)
        if rnd > 0:
            self._wait(q, (sem, 16 * rnd, "dma"))
        self.eng[q].dma_start(out=out, in_=in_).then_inc(sem, 16)
        tok = (sem, 16 * (rnd + 1), "dma")
        self._register(tok, reads, writes)
        return tok

    def release(self, name):
        toks = []
        for k in list(self.last_w.keys()):
            if k[0] == name:
                toks.append(self.last_w.pop(k))
        for k in list(self.readers.keys()):
            if k[0] == name:
                toks.extend(self.readers.pop(k))
        best = {}
        for t in toks:
            sid = id(t[0])
            if sid not in best or best[sid][1] < t[1]:
                best[sid] = t
        return list(best.values())


class Arena:
    def __init__(self, sched, arena_ap, nbytes):
        self.s = sched
        self.arena = arena_ap
        self.free = [(0, nbytes)]
        self.live = {}
        self.dead = []
        self.peak = 0

    def alloc(self, name, shape, dtype, parts=(0, 128)):
        esz = 4 if dtype in (F32, I32) else 2
        n = 1
        for d in shape:
            n *= d
        size = (n * esz + 63) // 64 * 64
        for i, (off, sz) in enumerate(self.free):
            if sz >= size:
                self.free[i] = (off + size, sz - size)
                break
        else:
            raise RuntimeError(f"arena OOM for {name} ({size} B); free={self.free} live={ {k: v[1] for k, v in self.live.items()} }")
        self.live[name] = (off, size)
        self.peak = max(self.peak, off + size)
        toks = []
        keep = []
        for (o, s_, tk) in self.dead:
            if o < off + size and off < o + s_:
                toks.extend(tk)
            keep.append((o, s_, tk))
        self.s.inherit[name] = toks
        ap = self.arena[parts[0]:parts[1], off // 2:(off + size) // 2]
        if esz == 4:
            ap = ap.bitcast(dtype)
        elif dtype != BF16:
            ap = ap.bitcast(dtype)
        ap = ap[:, 0:n]
        if len(shape) == 2:
            ap = ap.rearrange("p (a b) -> p a b", a=shape[0], b=shape[1])
        elif len(shape) == 3:
            ap = ap.rearrange("p (a b c) -> p a b c", a=shape[0], b=shape[1], c=shape[2])
        return ap

    def release(self, name):
        off, size = self.live.pop(name)
        toks = self.s.release(name)
        self.dead.append((off, size, toks))
        self.free.append((off, size))
        self.free.sort()
        merged = []
        for o, s_ in self.free:
            if merged and merged[-1][0] + merged[-1][1] == o:
                merged[-1] = (merged[-1][0], merged[-1][1] + s_)
            else:
                merged.append((o, s_))
        self.free = merged


class Builder:
    def __init__(self, debug=None):
        self.debug = debug or []
        nc = bass.Bass("TRN2", target_bir_lowering=False)
        self.nc = nc
        self.dbg_out = {}
        d = lambda n, sh, dt, kind="ExternalInput": nc.dram_tensor(n, list(sh), dt, kind=kind).ap()
        self.x = d("x", [S, D], F32)
        self.cT = d("cT", [128, KC], F32)
        self.pos = d("pos", [128, 512], I32)
        self.w_ada = d("w_ada", [D, 6 * D], F32)
        self.badaT = d("badaT", [128, 48], F32)
        self.bada_rep = d("bada_rep", [128, 2 * D], F32)
        self.gpost_rep = d("gpost_rep", [128, 2 * D], F32)
        self.gpreT = d("gpreT", [128, 16], F32)
        self.w_in = d("w_in", [D, D_IN], F32)
        self.nbf = d("nbf", [8, 1], F32)
        self.gqT = d("gqT", [128, 6], F32)
        self.gkvT = d("gkvT", [128, 2], F32)
        self.w_uq = d("w_uq", [768, 768], F32)
        self.w_ukv = d("w_ukv", [256, 1024], F32)
        self.w_pf = d("w_pf", [512, D], F32)
        self.w_pm = d("w_pm", [512, D], F32)
        self.w_out = d("w_out", [D, D], F32)
        self.w_f1 = d("w_f1", [D, 2 * DFF], F32)
        self.w_f2 = d("w_f2", [DFF, D], F32)
        self.c_ident = d("c_ident", [128, 128], BF16)
        self.c_identf = d("c_identf", [128, 128], F32)
        self.c_mask = d("c_mask", [128, 128], BF16)
        self.c_rope = d("c_rope", [128, 4], F32)
        self.out = d("out", [S, D], F32, kind="ExternalOutput")

    def dbg(self, name, ap, shape, dtype):
        if name in self.debug:
            o = self.nc.dram_tensor("dbg_" + name, list(shape), dtype, kind="ExternalOutput").ap()
            self.dbg_out[name] = o
            return o
        return None

    def build(self, upto=99):
        import contextlib
        nc = self.nc
        with contextlib.ExitStack() as es:
            es.enter_context(nc.allow_low_precision("bf16 matmul operands by design; fp32 accumulation"))
            es.enter_context(nc.allow_non_contiguous_dma("small strided weight / constant loads"))
            ARENA = 207 * 1024
            arena_t = es.enter_context(nc.sbuf_tensor("arena", [128, ARENA // 2], BF16))
            self.banks = [es.enter_context(nc.psum_tensor(f"ps{b}", [128, 512], F32)) for b in range(8)]
            sems = {e: es.enter_context(nc.semaphore("s_" + e)) for e in ("pe", "act", "dve", "pool")}
            dma_sems = {q: [es.enter_context(nc.semaphore(f"d{q}{i}")) for i in range(12)] for q in ("sp", "pool")}
            self.s = Sched(nc, sems, dma_sems)
            self.A = Arena(self.s, arena_t, ARENA)
            self._body(upto)
            self._finish()
        return nc

    def bank(self, b):
        return self.banks[b][:, :]

    def bankbf(self, b):
        return self.banks[b][:, :].bitcast(BF16)

    def _finish(self):
        s = self.s
        for q in ("sp", "pool"):
            pool = s.dma_sems[q]
            n = s.dma_n[q]
            for i, sem in enumerate(pool):
                cnt = (n - i + len(pool) - 1) // len(pool) if n > i else 0
                if cnt > 0:
                    s._wait("sp", (sem, 16 * cnt, "dma"))

    def dump(self, name, ap, parts, ncols, dtype, reads):
        o = self.dbg(name, ap, [parts, ncols], dtype)
        if o is not None:
            self.s.dma("sp", o, ap, reads=reads)

    def _body(self, upto):
        nc, s, A = self.nc, self.s, self.A
        bank, bankbf = self.bank, self.bankbf

        def load_const(name, src, shape, dtype, parts=(0, 128), q="pool"):
            t = A.alloc(name, shape, dtype, parts)
            s.dma(q, t, src, writes=[(name,)])
            return t

        ident = load_const("ident", self.c_ident, (128,), BF16)
        identf = load_const("identf", self.c_identf, (128,), F32)
        mask = load_const("mask", self.c_mask, (128,), BF16)
        ropec = load_const("ropec", self.c_rope, (4,), F32)
        cT = load_const("cT", self.cT, (KC,), F32)
        badaT = load_const("badaT", self.badaT, (48,), F32)
        gpreT = load_const("gpreT", self.gpreT, (16,), F32)
        gqT = load_const("gqT", self.gqT, (6,), F32)
        gkvT = load_const("gkvT", self.gkvT, (2,), F32)
        nbf = load_const("nbf", self.nbf, (1,), F32, parts=(0, 8))
        self.ident, self.mask, self.identf, self.nbf_t = ident, mask, identf, nbf
        self.gqT_t, self.gkvT_t = gqT, gkvT
        eps_c = A.alloc("eps_c", (1,), F32)
        s.op("dve", lambda e: e.memset(eps_c, EPS), writes=[("eps_c",)])
        self.eps_c = eps_c

        siluf = A.alloc("siluf", (KC,), F32)
        silub = A.alloc("silub", (KC,), BF16)
        ones_b = A.alloc("ones_b", (128,), BF16)
        modT = A.alloc("modT", (48,), F32)
        amix = A.alloc("amix", (KC,), F32)
        affn = A.alloc("affn", (KC,), F32)
        s.op("act", lambda e: e.activation(out=siluf, in_=cT, func=AF.Silu), reads=[("cT",)], writes=[("siluf",)])
        s.op("dve", lambda e: e.tensor_copy(out=silub, in_=siluf), reads=[("siluf",)], writes=[("silub",)])
        s.op("dve", lambda e: e.memset(ones_b, 1.0), writes=[("ones_b",)])
        wada = [A.alloc("wada0", (KC, D), BF16), A.alloc("wada1", (KC, D), BF16)]
        w_ada_v = self.w_ada.rearrange("(k p) n -> p k n", p=128)
        MODB = 7
        for v in (0, 1):
            s.dma("pool", wada[v], w_ada_v[:, :, v * D:(v + 1) * D], writes=[("wada%d" % v,)])

        def s0_compute():
            for v in (0, 1):
                i = v
                wn = "wada%d" % i
                for c in range(KC):
                    col = v * 8 + c
                    s.op("pe", [(lambda e, k=k: e.matmul(bank(MODB)[:, col:col + 1], wada[i][:, k, c * 128:(c + 1) * 128],
                                                        silub[:, k:k + 1], start=(k == 0), stop=(k == KC - 1)))
                                for k in range(KC)],
                         reads=[(wn,), ("silub",)], writes=[("ps", MODB)])
            s.op("dve", lambda e: e.tensor_tensor(out=modT[:, 0:16], in0=bank(MODB)[:, 0:16], in1=badaT[:, 0:16], op=ALU.add),
                 reads=[("badaT",)], writes=[("ps", MODB), ("modT",)])
            s.op("dve", lambda e: e.scalar_tensor_tensor(out=amix, in0=modT[:, 8:16], scalar=1.0, in1=gpreT[:, 0:8],
                                                         op0=ALU.add, op1=ALU.mult),
                 reads=[("modT",), ("gpreT",)], writes=[("amix",)])
            A.release("wada0"); A.release("wada1")
        self.silub, self.badaT_t, self.gpreT_t = silub, badaT, gpreT
        self.modT, self.amix, self.affn, self.ones_b, self.siluf = modT, amix, affn, ones_b, siluf
        if upto <= 0:
            return

        w_in_v0 = self.w_in.rearrange("(k p) n -> p k n", p=128)
        self.wv_t = A.alloc("wv", (KC, 512), BF16)
        s.dma("pool", self.wv_t, w_in_v0[:, :, C_V:C_V + 512], writes=[("wv",)])
        self.wmisc_t = A.alloc("wmisc", (KC, 128), BF16)
        s.op("dve", lambda e: e.memset(self.wmisc_t.rearrange("p a b -> p (a b)"), 0.0), writes=[("wmisc", i) for i in range(4)])
        s.dma("pool", self.wmisc_t[:, :, 0:8], w_in_v0[:, :, C_F:C_F + 8], writes=[("wmisc", 0)])
        s.dma("pool", self.wmisc_t[:, :, 64:96], w_in_v0[:, :, C_KR:C_KR + 32], writes=[("wmisc", 1)])
        s.dma("pool", self.wmisc_t[:, :, 96:112], w_in_v0[:, :, C_KR + 16:C_KR + 32], writes=[("wmisc", 2)])
        s.dma("pool", self.wmisc_t[:, :, 112:128], w_in_v0[:, :, C_KR:C_KR + 16], writes=[("wmisc", 3)])
        hT = A.alloc("hT", (KC, S), BF16)
        self.hT = hT
        ss = A.alloc("ss", (NT,), F32)
        rstd = A.alloc("rstd", (NT,), F32)
        self.ss, self.rstd = ss, rstd
        self.prenorm(self.x, None, None, hT, "hT", ss, rstd, "amix", list(range(NB)), 0)
        s0_compute()
        for c in range(KC):
            hk = [("hT", c, T) for T in range(NB)]
            if c % 2 == 0:
                s.op("act", lambda e: e.activation(out=hT[:, c, :], in_=hT[:, c, :], func=AF.Identity,
                                                   bias=modT[:, c:c + 1], scale=amix[:, c:c + 1]),
                     reads=hk + [("amix",), ("modT",)], writes=hk)
            else:
                s.op("dve", lambda e: e.tensor_scalar(out=hT[:, c, :], in0=hT[:, c, :], scalar1=amix[:, c:c + 1],
                                                      scalar2=modT[:, c:c + 1], op0=ALU.mult, op1=ALU.add),
                     reads=hk + [("amix",), ("modT",)], writes=hk)
        self.dump("hT0", hT[:, 0, :], 128, S, BF16, [("hT", 0, T) for T in range(NB)])
        self.dump("hT7", hT[:, 7, :], 128, S, BF16, [("hT", 7, T) for T in range(NB)])
        R = (64, 96)
        CC = A.alloc("CC", (S,), BF16, parts=R)
        SS = A.alloc("SS", (S,), BF16, parts=R)
        posi = A.alloc("posi", (512,), I32)
        posf = A.alloc("posf", (512,), F32)
        kf = A.alloc("kf", (512,), F32)
        ki = A.alloc("ki", (512,), I32)
        tq = A.alloc("tabq", (512,), BF16)
        s.dma("pool", posi, self.pos[:, 0:512], writes=[("posi",)])
        s.op("dve", lambda e: e.tensor_copy(out=posf, in_=posi), reads=[("posi",)], writes=[("posf",)])
        ang = posi.bitcast(F32)
        for (tab, name, c0) in ((CC, "CC", 0), (SS, "SS", 2)):
            s.op("dve", lambda e: e.tensor_scalar(out=ang, in0=posf, scalar1=ropec[:, c0:c0 + 1],
                                                  scalar2=ropec[:, c0 + 1:c0 + 2], op0=ALU.mult, op1=ALU.add),
                 reads=[("posf",), ("ropec",)], writes=[("posi",)])
            s.op("dve", lambda e: e.tensor_scalar(out=kf, in0=ang, scalar1=1.0 / (2.0 * math.pi), scalar2=None,
                                                  op0=ALU.mult),
                 reads=[("posi",)], writes=[("kf",)])
            s.op("dve", lambda e: e.tensor_copy(out=ki, in_=kf), reads=[("kf",)], writes=[("ki",)])
            s.op("dve", lambda e: e.tensor_copy(out=kf, in_=ki), reads=[("ki",)], writes=[("kf",)])
            s.op("dve", lambda e: e.scalar_tensor_tensor(out=ang, in0=kf, scalar=-2.0 * math.pi, in1=ang,
                                                         op0=ALU.mult, op1=ALU.add),
                 reads=[("kf",), ("posi",)], writes=[("posi",)])
            s.op("dve", lambda e: e.tensor_scalar(out=kf, in0=ang, scalar1=math.pi, scalar2=-2.0 * math.pi,
                                                  op0=ALU.is_gt, op1=ALU.mult),
                 reads=[("posi",)], writes=[("kf",)])
            s.op("dve", lambda e: e.tensor_tensor(out=ang, in0=ang, in1=kf, op=ALU.add),
                 reads=[("posi",), ("kf",)], writes=[("posi",)])
            s.op("dve", lambda e: e.tensor_scalar(out=ang, in0=ang, scalar1=-math.pi, scalar2=math.pi,
                                                  op0=ALU.max, op1=ALU.min),
                 reads=[("posi",)], writes=[("posi",)])
            s.op("act", lambda e: e.activation(out=tq, in_=ang, func=AF.Sin),
                 reads=[("posi",)], writes=[("tabq",)])
            for q in range(4):
                s.dma("pool", tab[:, q * 512:(q + 1) * 512], tq[q * 32:(q + 1) * 32, :], reads=[("tabq",)], writes=[(name, q)])
        self.CC, self.SS = CC, SS
        self.dump("CC", CC, 32, S, BF16, [("CC", q) for q in range(4)])
        self.dump("SS", SS, 32, S, BF16, [("SS", q) for q in range(4)])
        for nme in ("posi", "posf", "kf", "ki", "tabq"):
            A.release(nme)
        if upto <= 1:
            return
        self._body2(upto)

    def prenorm(self, src_rows, a_sc, shift_sc, dst, dname, ss, rstd, aname, Ts, tok0, src_sbuf=None):
        s, A = self.s, self.A
        nxt = 8 if src_sbuf is None else 4
        xts = [A.alloc("xt%d" % i, (D,), F32) for i in range(nxt)] if src_sbuf is None else None
        tbanks = (0, 1, 2, 3, 6, 7) if src_sbuf is None else (6, 7)
        ntb = 0
        xb = [A.alloc("xb%d" % i, (D,), BF16) for i in range(4)]
        junk = A.alloc("junk", (D,), BF16)
        for T in Ts:
            for i in range(4):
                it = 4 * T + i
                if src_sbuf is None:
                    xt, xk = xts[it % nxt], ("xt%d" % (it % nxt),)
                    s.dma("sp", xt, src_rows[it * 128:(it + 1) * 128, :], writes=[xk])
                else:
                    xt, xk = src_sbuf(it)
                s.op("act", lambda e: e.activation(out=junk, in_=xt, func=AF.Square, accum_out=ss[:, it:it + 1]),
                     reads=[xk], writes=[("junk",), ("ss", it)])
            sl4 = slice(4 * T, 4 * T + 4)
            s.op("act", lambda e: e.activation(out=rstd[:, sl4], in_=ss[:, sl4], func=AF.Ln, bias=self.eps_c, scale=1.0 / D),
                 reads=[("ss", 4 * T + i) for i in range(4)] + [("eps_c",)], writes=[("rstd", T)])
            s.op("act", lambda e: e.activation(out=rstd[:, sl4], in_=rstd[:, sl4], func=AF.Exp, scale=-0.5),
                 reads=[("rstd", T)], writes=[("rstd", T)])
            for i in range(4):
                it = 4 * T + i
                xt, xk = (xts[it % nxt], ("xt%d" % (it % nxt),)) if src_sbuf is None else src_sbuf(it)
                s.op("dve", lambda e: e.tensor_scalar(out=xb[i], in0=xt, scalar1=rstd[:, it:it + 1], scalar2=None,
                                                      op0=ALU.mult),
                     reads=[xk, ("rstd", T)], writes=[("xb%d" % i,)])
            for c in range(KC):
                b = tbanks[ntb % len(tbanks)]
                ntb += 1
                psb = self.bankbf(b)
                s.op("pe", [(lambda e, i=i: e.transpose(out=psb[:, i * 128:(i + 1) * 128],
                                                        in_=xb[i][:, c * 128:(c + 1) * 128], identity=self.ident))
                            for i in range(4)],
                     reads=[("xb%d" % i,) for i in range(4)] + [("ident",)], writes=[("ps", b)])
                dsl = dst[:, c, (T - tok0) * 512:(T - tok0 + 1) * 512]
                if a_sc is None:
                    if c % 2 == 0:
                        s.op("act", lambda e: e.activation(out=dsl, in_=psb[:, 0:512], func=AF.Copy),
                             writes=[("ps", b), (dname, c, T)])
                    else:
                        s.op("dve", lambda e: e.tensor_copy(out=dsl, in_=psb[:, 0:512]),
                             writes=[("ps", b), (dname, c, T)])
                elif c % 2 == 0:
                    s.op("act", lambda e: e.activation(out=dsl, in_=psb[:, 0:512], func=AF.Identity,
                                                       bias=shift_sc[:, c:c + 1], scale=a_sc[:, c:c + 1]),
                         reads=[(aname,), ("modT",), ("modT2",)], writes=[("ps", b), (dname, c, T if dname == "hT" else 0)])
                else:
                    s.op("dve", lambda e: e.tensor_scalar(out=dsl, in0=psb[:, 0:512], scalar1=a_sc[:, c:c + 1],
                                                          scalar2=shift_sc[:, c:c + 1], op0=ALU.mult, op1=ALU.add),
                         reads=[(aname,), ("modT",), ("modT2",)], writes=[("ps", b), (dname, c, T if dname == "hT" else 0)])
        if xts is not None:
            for i in range(nxt):
                A.release("xt%d" % i)
        for i in range(4):
            A.release("xb%d" % i)
        A.release("junk")

    def _body2(self, upto):
        nc, s, A = self.nc, self.s, self.A
        bank, bankbf = self.bank, self.bankbf
        hT, CC, SS = self.hT, self.CC, self.SS
        w_in_v = self.w_in.rearrange("(k p) n -> p k n", p=128)
        hkeys = lambda T: [("hT", k, T) for k in range(KC)]
        TS = lambda T: slice(T * 512, (T + 1) * 512)
        mm = lambda out, l, r, st, sp: (lambda e: e.matmul(out, l, r, start=st, stop=sp))

        oT = A.alloc("oT", (4, S), BF16)
        qk = [{n: A.alloc(n + str(st), (S,), BF16, parts=(0, 96)) for n in ("qA", "qB", "kA", "kB")} for st in range(2)]
        vaug = A.alloc("vaug", (NT, 4, 192), BF16)
        self.pT = [A.alloc("pT%d" % i, (512,), BF16) for i in range(6)]
        rec_t = A.alloc("rec0", (512,), F32)
        self.rec = [rec_t, rec_t]
        self.qk, self.vaug, self.oT = qk, vaug, oT
        self.att_cnt = 0

        s.op("dve", lambda e: e.memset(vaug.rearrange("p a b c -> p (a b c)"), 1.0),
             writes=[("vaug", it, e_) for it in range(NT) for e_ in range(3)])
        wv = self.wv_t

        def v_proj(lhs_of, nk, wt, wname, rkeys):
            for it in range(NT):
                b = 6 + it % 2
                s.op("pe", [mm(bank(b), lhs_of(k, it), wt[:, k, :], k == 0, k == nk - 1) for k in range(nk)],
                     reads=rkeys(it // 4) + [(wname,)], writes=[("ps", b)])
                pv = bank(b).rearrange("p (c e d) -> p c e d", c=4, e=2, d=64)
                s.op("act", lambda e: e.activation(out=vaug[:, it, :, 0:64], in_=pv[:, :, 0, :], func=AF.Copy),
                     writes=[("ps", b), ("vaug", it, 0)])
                s.op("dve", lambda e: e.tensor_copy(out=vaug[:, it, :, 128:192], in_=pv[:, :, 1, :]),
                     writes=[("ps", b), ("vaug", it, 1)])

        v_proj(lambda k, it: hT[:, k, it * 128:(it + 1) * 128], KC, wv, "wv", hkeys)
        A.release("wv")

        wmisc = self.wmisc_t
        P8 = (0, 8)
        nbneg = A.alloc("nbneg", (1,), F32, parts=P8)
        eT = A.alloc("eT", (512,), F32, parts=P8)
        nlf = A.alloc("nlf", (512,), F32, parts=P8)
        onesf = A.alloc("onesf", (512,), F32, parts=P8)
        G = A.alloc("G", (S,), F32, parts=P8)
        negGb = A.alloc("negGb", (S,), BF16, parts=P8)
        Gtok = A.alloc("Gtok", (128,), F32)
        R = (64, 96)
        kpe = A.alloc("kpe", (S,), BF16, parts=R)
        t1 = A.alloc("t1", (512,), BF16, parts=R)
        t2 = A.alloc("t2", (512,), BF16, parts=R)
        self.t1, self.t2, self.Gtok, self.kpe_t = t1, t2, Gtok, kpe
        s.op("dve", lambda e: e.tensor_scalar(out=nbneg, in0=self.nbf_t, scalar1=-1.0, scalar2=None, op0=ALU.mult),
             reads=[("nbf",)], writes=[("nbneg",)])
        s.op("dve", lambda e: e.memset(onesf, 1.0), writes=[("onesf",)])
        for T in range(NB):
            b = 6 + T % 2
            s.op("pe", [mm(bank(b), wmisc[:, k, :], hT[:, k, TS(T)], k == 0, k == KC - 1) for k in range(KC)],
                 reads=hkeys(T) + [("wmisc", i) for i in range(4)], writes=[("ps", b)])
            s.op("act", lambda e: e.activation(out=eT, in_=bank(b)[0:8, :], func=AF.Exp, bias=nbneg, scale=-1.0),
                 reads=[("nbneg",)], writes=[("ps", b), ("eT",)])
            s.op("act", lambda e: e.activation(out=nlf, in_=eT, func=AF.Ln, bias=1.0), reads=[("eT",)], writes=[("nlf",)])
            init = 0.0 if T == 0 else G[:, T * 512 - 1:T * 512]
            s.op("dve", lambda e: e.tensor_tensor_scan(out=G[:, TS(T)], data0=onesf, data1=nlf, initial=init,
                                                       op0=ALU.mult, op1=ALU.add),
                 reads=[("nlf",), ("onesf",)] + ([("G", T - 1)] if T else []), writes=[("G", T)])
            s.op("dve", lambda e: e.tensor_tensor(out=t1, in0=bank(b)[64:96, :], in1=CC[:, TS(T)], op=ALU.mult),
                 reads=[("CC", q_) for q_ in range(4)], writes=[("ps", b), ("t1",)])
            s.op("dve", lambda e: e.tensor_tensor(out=t2, in0=bank(b)[96:128, :], in1=SS[:, TS(T)], op=ALU.mult),
                 reads=[("SS", q_) for q_ in range(4)], writes=[("ps", b), ("t2",)])
            s.op("dve", lambda e: e.tensor_tensor(out=kpe[:, TS(T)], in0=t1, in1=t2, op=ALU.add),
                 reads=[("t1",), ("t2",)], writes=[("kpe", T)])
        s.op("dve", lambda e: e.tensor_scalar(out=negGb, in0=G, scalar1=-1.0, scalar2=None, op0=ALU.mult),
             reads=[("G", T) for T in range(NB)], writes=[("negGb",)])
        GB = 5
        s.op("pe", [(lambda e, it=it: e.transpose(out=bank(GB)[:, it * 8:(it + 1) * 8], in_=G[:, it * 128:(it + 1) * 128],
                                                  identity=self.identf[0:8, 0:8])) for it in range(NT)],
             reads=[("G", T) for T in range(NB)] + [("identf",)], writes=[("ps", GB)])
        s.op("dve", lambda e: e.tensor_copy(out=Gtok, in_=bank(GB)[:, 0:128]), writes=[("ps", GB), ("Gtok",)])
        self.dump("G", G, 8, S, F32, [("G", T) for T in range(NB)])
        self.dump("Gtok", Gtok, 128, 128, F32, [("Gtok",)])
        self.dump("kpe", kpe, 32, S, BF16, [("kpe", T) for T in range(NB)])
        A.release("wmisc"); A.release("eT"); A.release("nlf"); A.release("onesf")
        if upto <= 2:
            return

        wf = [A.alloc("wf0", (KC, 256), BF16), A.alloc("wf1", (KC, 256), BF16)]
        for st in range(2):
            for n in ("kA", "kB"):
                s.op("dve", lambda e: e.memset(qk[st][n][64:65, :], 1.0), writes=[(n + str(st), "g")])

        self.sb_cnt = 0

        def sbank():
            self.sb_cnt += 1
            return 6 + self.sb_cnt % 2

        self.sbank = sbank

        def fox_proj(c, fg=False):
            st = c % 2
            Q = qk[st]
            w, wn = wf[c % 2], "wf%d" % (c % 2)
            s.dma("pool", w[:, :, 0:128], w_in_v[:, :, C_Q + c * 128:C_Q + (c + 1) * 128], writes=[(wn, "q")])
            s.dma("pool", w[:, :, 128:256], w_in_v[:, :, C_K + c * 128:C_K + (c + 1) * 128], writes=[(wn, "k")])
            s.dma("pool", Q["qA"][64:65, :], negGb[2 * c:2 * c + 1, :], reads=[("negGb",)], writes=[("qA%d" % st, "g")])
            s.dma("pool", Q["qB"][64:65, :], negGb[2 * c + 1:2 * c + 2, :], reads=[("negGb",)], writes=[("qB%d" % st, "g")])
            yield
            for T in range(NB):
                b6 = sbank()
                s.op("pe", [mm(bank(b6), w[:, k, 0:128], hT[:, k, TS(T)], k == 0, k == KC - 1) for k in range(KC)],
                     reads=hkeys(T) + [(wn, "q")], writes=[("ps", b6)])
                s.op("dve", lambda e: e.tensor_scalar(out=Q["qA"][0:64, TS(T)], in0=bank(b6)[0:64, :], scalar1=0.125, scalar2=None, op0=ALU.mult),
                     writes=[("ps", b6), ("qA%d" % st, T)])
                if fg:
                    s.op("act", lambda e: e.activation(out=Q["qB"][0:64, TS(T)], in_=bank(b6)[64:128, :], func=AF.Copy, scale=0.125),
                         writes=[("ps", b6), ("qB%d" % st, T)])
                else:
                    s.op("dve", lambda e: e.tensor_scalar(out=Q["qB"][0:64, TS(T)], in0=bank(b6)[64:128, :], scalar1=0.125, scalar2=None, op0=ALU.mult),
                         writes=[("ps", b6), ("qB%d" % st, T)])
                yield
                b7 = sbank()
                s.op("pe", [mm(bank(b7), w[:, k, 128:256], hT[:, k, TS(T)], k == 0, k == KC - 1) for k in range(KC)],
                     reads=hkeys(T) + [(wn, "k")], writes=[("ps", b7)])
                s.op("dve", lambda e: e.tensor_copy(out=Q["kA"][0:64, TS(T)], in_=bank(b7)[0:64, :]),
                     writes=[("ps", b7), ("kA%d" % st, T)])
                if fg:
                    s.op("act", lambda e: e.activation(out=Q["kB"][0:64, TS(T)], in_=bank(b7)[64:128, :], func=AF.Copy),
                         writes=[("ps", b7), ("kB%d" % st, T)])
                else:
                    s.op("dve", lambda e: e.tensor_copy(out=Q["kB"][0:64, TS(T)], in_=bank(b7)[64:128, :]),
                         writes=[("ps", b7), ("kB%d" % st, T)])
                yield

        for _ in fox_proj(0, True):
            pass
        self.dump("qA", qk[0]["qA"][0:65, :], 65, S, BF16, [("qA0", T) for T in range(NB)] + [("qA0", "g")])
        self.dump("kA", qk[0]["kA"][0:65, :], 65, S, BF16, [("kA0", T) for T in range(NB)] + [("kA0", "g")])
        for c in range(4):
            bg = fox_proj(c + 1) if c < 3 else None
            self.attention(c, True, c % 2, bg)
        for c in range(4):
            self.dump("oaT%d" % c, oT[:, c, :], 128, S, BF16, [("oT", c, T, hh) for T in range(NB) for hh in range(2)])
        A.release("wf0"); A.release("wf1"); A.release("G"); A.release("negGb"); A.release("nbneg")
        if upto <= 3:
            return
        self._body3(upto)

    def attention(self, c, fox, st=0, bg=None):
        s = self.s
        bank = self.bank
        qk, vaug, pT, rec = self.qk[st], self.vaug, self.pT, self.rec
        oT, on = (self.oT, "oT") if fox else (self.oT2, "oT2")
        nstep = 0
        KD = 65 if fox else 96
        scale = 1.0 if fox else 1.0 / math.sqrt(96.0)
        SB = (0, 1, 2, 3)
        LA = 2
        NPT = len(pT)
        for T in range(NB):
            nj = 4 * T + 4
            info = {}
            accb = 4

            def emit_S(hh, j):
                h = 2 * c + hh
                qn, kn = ("qA", "kA") if hh == 0 else ("qB", "kB")
                q, k = qk[qn], qk[kn]
                qn, kn = qn + str(st), kn + str(st)
                off = 0 if j < 4 * T else (j - 4 * T) * 128
                w = 512 - off
                n = self.att_cnt
                self.att_cnt += 1
                b = SB[n % len(SB)]
                pt, ptn = pT[n % NPT], "pT%d" % (n % NPT)
                diag = j >= 4 * T
                fns = [lambda e: e.matmul(bank(b)[:, 0:w], k[0:KD, j * 128:(j + 1) * 128],
                                          q[0:KD, T * 512 + off:(T + 1) * 512], start=True, stop=not diag)]
                if diag:
                    fns.append(lambda e: e.matmul(bank(b)[:, 0:128], self.ident, self.mask, start=False, stop=True))
                s.op("pe", fns, reads=[(qn, T), (qn, "g"), (kn, j // 4), (kn, "g"), ("ident",), ("mask",)],
                     writes=[("ps", b)])
                if fox:
                    s.op("act", lambda e: e.activation(out=pt[:, 0:w], in_=bank(b)[:, 0:w], func=AF.Exp,
                                                       bias=self.Gtok[:, j * 8 + h:j * 8 + h + 1], scale=1.0),
                         reads=[("Gtok",)], writes=[("ps", b), (ptn,)])
                else:
                    s.op("act", lambda e: e.activation(out=pt[:, 0:w], in_=bank(b)[:, 0:w], func=AF.Exp, scale=scale),
                         writes=[("ps", b), (ptn,)])
                info[(hh, j)] = (off, w, pt, ptn)

            def emit_PV(hh, j):
                off, w, pt, ptn = info[(hh, j)]
                acc = accb + hh
                vsl = slice(0, 128) if hh == 0 else slice(64, 192)
                s.op("pe", lambda e: e.matmul(bank(acc)[:, off:512], vaug[:, j, c, vsl], pt[:, 0:w],
                                              start=(j == 0), stop=(j == nj - 1)),
                     reads=[(ptn,), ("vaug", j, 0), ("vaug", j, 1), ("vaug", j, 2)], writes=[("ps", acc)])

            for step in range(nj + LA):
                for hh in (0, 1):
                    if step < nj:
                        emit_S(hh, step)
                    if step - LA >= 0:
                        emit_PV(hh, step - LA)
                nstep += 1
                if bg is not None and nstep % 4 == 0:
                    next(bg, None)
            for hh in (0, 1):
                acc = accb + hh
                r = rec[hh]
                if hh == 0:
                    osl, dsl = slice(0, 64), slice(64, 128)
                else:
                    osl, dsl = slice(64, 128), slice(0, 64)
                s.op("act", lambda e: e.activation(out=r[osl, :], in_=bank(acc)[dsl, :], func=AF.Ln),
                     writes=[("ps", acc), ("rec0", hh)])
                s.op("act", lambda e: e.activation(out=r[osl, :], in_=r[osl, :], func=AF.Exp, scale=-1.0),
                     reads=[("rec0", hh)], writes=[("rec0", hh)])
                s.op("dve", lambda e: e.tensor_tensor(out=oT[osl, c, T * 512:(T + 1) * 512], in0=bank(acc)[osl, :],
                                                      in1=r[osl, :], op=ALU.mult),
                     reads=[("rec0", hh)], writes=[("ps", acc), (on, c, T, hh)])
        if bg is not None:
            for _ in bg:
                pass

    def gate_merge(self, col0, w_proj, first, oT, on):
        s, A = self.s, self.A
        bank = self.bank
        hT, merged = self.hT, self.merged
        mm = lambda out, l, r, st, sp: (lambda e: e.matmul(out, l, r, start=st, stop=sp))
        w_in_v = self.w_in.rearrange("(k p) n -> p k n", p=128)
        wg, wgn, wp, wpn = self.gate_w[0 if first else 1]
        sg = [A.alloc("sg%d" % i, (512,), BF16) for i in range(2)]
        tmp = [A.alloc("gtmp%d" % i, (512,), BF16) for i in range(2)]
        n = 0
        for T in range(NB):
            Tsl = slice(T * 512, (T + 1) * 512)
            for m in range(KC):
                i = n % 2
                n += 1
                bg, bp = 0 + i, 2 + i
                s.op("pe", [mm(bank(bg), wg[:, k, m * 128:(m + 1) * 128], hT[:, k, Tsl], k == 0, k == KC - 1) for k in range(KC)],
                     reads=[("hT", k, T) for k in range(KC)] + [(wgn,)], writes=[("ps", bg)])
                s.op("act", lambda e: e.activation(out=sg[i], in_=bank(bg), func=AF.Sigmoid),
                     writes=[("ps", bg), ("sg%d" % i,)])
                s.op("pe", [mm(bank(bp), wp[:, cc, m * 128:(m + 1) * 128], oT[:, cc, Tsl], cc == 0, cc == 3) for cc in range(4)],
                     reads=[(on, cc, T, hh) for cc in range(4) for hh in range(2)] + [(wpn,)], writes=[("ps", bp)])
                if first:
                    s.op("dve", lambda e: e.tensor_tensor(out=merged[:, m, Tsl], in0=bank(bp), in1=sg[i], op=ALU.mult),
                         reads=[("sg%d" % i,)], writes=[("ps", bp), ("merged", m, T)])
                else:
                    s.op("dve", lambda e: e.tensor_tensor(out=tmp[i], in0=bank(bp), in1=sg[i], op=ALU.mult),
                         reads=[("sg%d" % i,)], writes=[("ps", bp), ("gtmp%d" % i,)])
                    s.op("dve", lambda e: e.tensor_tensor(out=merged[:, m, Tsl], in0=merged[:, m, Tsl], in1=tmp[i], op=ALU.add),
                         reads=[("gtmp%d" % i,), ("merged", m, T)], writes=[("merged", m, T)])
        for nme in (wgn, wpn, "sg0", "sg1", "gtmp0", "gtmp1"):
            A.release(nme)

    def _body3(self, upto):
        nc, s, A = self.nc, self.s, self.A
        bank = self.bank
        hT, CC, SS, qk, vaug = self.hT, self.CC, self.SS, self.qk, self.vaug
        t1, t2 = self.t1, self.t2
        w_in_v = self.w_in.rearrange("(k p) n -> p k n", p=128)
        hkeys = lambda T: [("hT", k, T) for k in range(KC)]
        TS = lambda T: slice(T * 512, (T + 1) * 512)
        mm = lambda out, l, r, st, sp: (lambda e: e.matmul(out, l, r, start=st, stop=sp))


        cqn = A.alloc("cqn", (6, S), BF16)
        ckvn = A.alloc("ckvn", (2, S), BF16)
        sqb = [A.alloc("sqb%d" % i, (512,), BF16) for i in range(3)]
        rstdb = A.alloc("rstdb", (512,), F32)
        wcq = A.alloc("wcq", (KC, 768), BF16)
        wckv = A.alloc("wckv", (KC, 256), BF16)
        s.dma("pool", wcq, w_in_v[:, :, C_CQ:C_CQ + 768], writes=[("wcq",)])
        s.dma("pool", wckv, w_in_v[:, :, C_CKV:C_CKV + 256], writes=[("wckv",)])
        for (wt, wn, nm, dst, dn, gT, gname, nfeat) in ((wcq, "wcq", 6, cqn, "cqn", self.gqT_t, "gqT", 768.0),
                                                        (wckv, "wckv", 2, ckvn, "ckvn", self.gkvT_t, "gkvT", 256.0)):
            for T in range(NB):
                for m in range(nm):
                    b = 6 + m % 2
                    i = m % 2
                    s.op("pe", [mm(bank(b), wt[:, k, m * 128:(m + 1) * 128], hT[:, k, TS(T)], k == 0, k == KC - 1) for k in range(KC)],
                         reads=hkeys(T) + [(wn,)], writes=[("ps", b)])
                    s.op("act", lambda e: e.activation(out=sqb[i], in_=bank(b), func=AF.Square),
                         writes=[("ps", b), ("sqb%d" % i,)])
                    s.op("dve", lambda e: e.tensor_scalar(out=dst[:, m, TS(T)], in0=bank(b), scalar1=gT[:, m:m + 1],
                                                          scalar2=None, op0=ALU.mult),
                         reads=[(gname,)], writes=[("ps", b), (dn, m, T)])
                    s.op("pe", mm(bank(5), self.ones_b, sqb[i], m == 0, m == nm - 1),
                         reads=[("sqb%d" % i,), ("ones_b",)], writes=[("ps", 5)])
                s.op("act", lambda e: e.activation(out=rstdb, in_=bank(5), func=AF.Ln, bias=self.eps_c, scale=1.0 / nfeat),
                     reads=[("eps_c",)], writes=[("ps", 5), ("rstdb",)])
                s.op("act", lambda e: e.activation(out=rstdb, in_=rstdb, func=AF.Exp, scale=-0.5),
                     reads=[("rstdb",)], writes=[("rstdb",)])
                for m in range(nm):
                    s.op("dve", lambda e: e.tensor_tensor(out=dst[:, m, TS(T)], in0=dst[:, m, TS(T)], in1=rstdb, op=ALU.mult),
                         reads=[(dn, m, T), ("rstdb",)], writes=[(dn, m, T)])
        self.dump("cqn0", cqn[:, 0, :], 128, S, BF16, [("cqn", 0, T) for T in range(NB)])
        self.dump("ckvn1", ckvn[:, 1, :], 128, S, BF16, [("ckvn", 1, T) for T in range(NB)])
        A.release("wcq"); A.release("wckv"); A.release("sqb0"); A.release("sqb1"); A.release("sqb2"); A.release("rstdb")

        oT2 = A.alloc("oT2", (4, S), BF16)
        self.oT2 = oT2
        wuq = A.alloc("wuq", (6, 8, 128), BF16)
        w_uq_v = self.w_uq.rearrange("(k p) (h e) -> p k h e", p=128, e=96)
        for kc in range(6):
            s.dma("pool", wuq[:, kc, :, 0:96], w_uq_v[:, kc, :, :], writes=[("wuq", 0, kc)])
            s.dma("pool", wuq[:, kc, :, 96:112], w_uq_v[:, kc, :, 80:96], writes=[("wuq", 1, kc)])
            s.dma("pool", wuq[:, kc, :, 112:128], w_uq_v[:, kc, :, 64:80], writes=[("wuq", 2, kc)])
        wkn = A.alloc("wkn", (2, 512), BF16)
        wvm = A.alloc("wvm", (2, 512), BF16)
        w_ukv_v = self.w_ukv.rearrange("(k p) (h e) -> p k h e", p=128, e=128)
        for kc in range(2):
            s.dma("pool", wkn[:, kc, :].rearrange("p (h e) -> p h e", e=64), w_ukv_v[:, kc, :, 0:64], writes=[("wkn", kc)])
            s.dma("pool", wvm[:, kc, :].rearrange("p (h e) -> p h e", e=64), w_ukv_v[:, kc, :, 64:128], writes=[("wvm", kc)])

        for it in range(NT):
            b = 6 + it % 2
            T = it // 4
            s.op("pe", [mm(bank(b), ckvn[:, k, it * 128:(it + 1) * 128], wvm[:, k, :], k == 0, k == 1) for k in range(2)],
                 reads=[("ckvn", k, T) for k in range(2)] + [("wvm", 0), ("wvm", 1)], writes=[("ps", b)])
            pv = bank(b).rearrange("p (c e d) -> p c e d", c=4, e=2, d=64)
            s.op("act", lambda e: e.activation(out=vaug[:, it, :, 0:64], in_=pv[:, :, 0, :], func=AF.Copy),
                 writes=[("ps", b), ("vaug", it, 0)])
            s.op("dve", lambda e: e.tensor_copy(out=vaug[:, it, :, 128:192], in_=pv[:, :, 1, :]),
                 writes=[("ps", b), ("vaug", it, 1)])

        wuq_keys = [("wuq", i, kc) for i in range(3) for kc in range(6)]

        def mla_proj(c, fg=False):
            st = c % 2
            Q = qk[st]
            for kn in ("kA", "kB"):
                s.op("dve", lambda e: e.tensor_copy(out=Q[kn][64:96, :], in_=self.kpe_t),
                     reads=[("kpe", T) for T in range(NB)], writes=[(kn + str(st), "g")])
            yield
            nb = 0
            for T in range(NB):
                b7 = self.sbank()
                s.op("pe", [mm(bank(b7), wkn[:, k, c * 128:(c + 1) * 128], ckvn[:, k, TS(T)], k == 0, k == 1) for k in range(2)],
                     reads=[("ckvn", k, T) for k in range(2)] + [("wkn", 0), ("wkn", 1)], writes=[("ps", b7)])
                s.op("dve", lambda e: e.tensor_copy(out=Q["kA"][0:64, TS(T)], in_=bank(b7)[0:64, :]),
                     writes=[("ps", b7), ("kA%d" % st, T)])
                if fg:
                    s.op("act", lambda e: e.activation(out=Q["kB"][0:64, TS(T)], in_=bank(b7)[64:128, :], func=AF.Copy),
                         writes=[("ps", b7), ("kB%d" % st, T)])
                else:
                    s.op("dve", lambda e: e.tensor_copy(out=Q["kB"][0:64, TS(T)], in_=bank(b7)[64:128, :]),
                         writes=[("ps", b7), ("kB%d" % st, T)])
                yield
                for hh in (0, 1):
                    h = 2 * c + hh
                    qn = ("qA" if hh == 0 else "qB")
                    q = Q[qn]
                    qn = qn + str(st)
                    b6 = self.sbank()
                    s.op("pe", [mm(bank(b6), wuq[:, k, h, :], cqn[:, k, TS(T)], k == 0, k == 5) for k in range(6)],
                         reads=[("cqn", k, T) for k in range(6)] + wuq_keys, writes=[("ps", b6)])
                    if fg:
                        s.op("act", lambda e: e.activation(out=q[0:64, TS(T)], in_=bank(b6)[0:64, :], func=AF.Copy),
                             writes=[("ps", b6), (qn, T)])
                    else:
                        s.op("dve", lambda e: e.tensor_copy(out=q[0:64, TS(T)], in_=bank(b6)[0:64, :]),
                             writes=[("ps", b6), (qn, T)])
                    s.op("dve", lambda e: e.tensor_tensor(out=t1, in0=bank(b6)[64:96, :], in1=CC[:, TS(T)], op=ALU.mult),
                         reads=[("CC", q_) for q_ in range(4)], writes=[("ps", b6), ("t1",)])
                    s.op("dve", lambda e: e.tensor_tensor(out=t2, in0=bank(b6)[96:128, :], in1=SS[:, TS(T)], op=ALU.mult),
                         reads=[("SS", q_) for q_ in range(4)], writes=[("ps", b6), ("t2",)])
                    s.op("dve", lambda e: e.tensor_tensor(out=q[64:96, TS(T)], in0=t1, in1=t2, op=ALU.add),
                         reads=[("t1",), ("t2",)], writes=[(qn, T)])
                    yield

        for _ in mla_proj(0, True):
            pass
        self.dump("qm0", qk[0]["qA"][0:96, :], 96, S, BF16, [("qA0", T) for T in range(NB)])
        self.dump("km0", qk[0]["kA"][0:96, :], 96, S, BF16, [("kA0", T) for T in range(NB)] + [("kA0", "g")])
        for c in range(4):
            bg = mla_proj(c + 1) if c < 3 else None
            if c == 3:
                for nme in ("wuq", "wkn", "wvm", "cqn", "ckvn", "kpe", "t1", "t2", "CC", "SS"):
                    A.release(nme)
                self.gvec_prefetch()
            self.attention(c, False, c % 2, bg)
        for c in range(4):
            self.dump("obT%d" % c, oT2[:, c, :], 128, S, BF16, [("oT2", c, T, hh) for T in range(NB) for hh in range(2)])
        for nme in ("qA0", "qB0", "kA0", "kB0", "qA1", "qB1", "kA1", "kB1", "vaug", "pT0", "pT1", "pT2", "pT3", "pT4", "pT5",
                    "rec0", "Gtok"):
            A.release(nme)
        if upto <= 4:
            return

        merged = A.alloc("merged", (KC, S), BF16)
        self.merged = merged
        self.gate_w = []
        for i, (col0, wproj) in enumerate(((C_GF, self.w_pf), (C_GM, self.w_pm))):
            wg = A.alloc("wg%d" % i, (KC, D), BF16)
            wp = A.alloc("wp%d" % i, (4, D), BF16)
            s.dma("pool", wg, w_in_v[:, :, col0:col0 + D], writes=[("wg%d" % i,)])
            s.dma("pool", wp, wproj.rearrange("(k p) n -> p k n", p=128), writes=[("wp%d" % i,)])
            self.gate_w.append((wg, "wg%d" % i, wp, "wp%d" % i))
            if i == 0:
                self.gvec_compute()
        self.gate_merge(C_GF, self.w_pf, True, self.oT, "oT")
        self.gate_merge(C_GM, self.w_pm, False, self.oT2, "oT2")
        for m in (0, 7):
            self.dump("mg%d" % m, merged[:, m, :], 128, S, BF16, [("merged", m, T) for T in range(NB)])
        A.release("hT"); A.release("oT"); A.release("oT2")
        if upto <= 5:
            return
        self._body4(upto)

    def gvec_prefetch(self):
        s, A = self.s, self.A
        self.gvec = A.alloc("gvec", (2, D), F32)
        self.silurep = A.alloc("silurep", (KC, 128), BF16)
        self.bada_rep_t = A.alloc("bada_rep", (2 * D,), F32)
        self.gpost_rep_t = A.alloc("gpost_rep", (2 * D,), F32)
        s.dma("pool", self.bada_rep_t, self.bada_rep, writes=[("bada_rep",)])
        s.dma("pool", self.gpost_rep_t, self.gpost_rep, writes=[("gpost_rep",)])
        w_ada_v = self.w_ada.rearrange("(k p) n -> p k n", p=128)
        self.wadag = {}
        for v, hfs in ((3, (0, 1)), (4, (0, 1)), (2, (0,))):
            for hf in hfs:
                nm = "wadag%d%d" % (v, hf)
                t = A.alloc(nm, (KC, 512), BF16)
                s.dma("pool", t, w_ada_v[:, :, v * D + hf * 512:v * D + (hf + 1) * 512], writes=[(nm,)])
                self.wadag[(v, hf)] = (t, nm)

    def gvec_compute(self):
        s, A = self.s, self.A
        bank = self.bank
        gvec, silurep, bada_rep, gpost_rep = self.gvec, self.silurep, self.bada_rep_t, self.gpost_rep_t
        for k in range(KC):
            s.op("dve", lambda e: e.tensor_scalar(out=silurep[:, k, :], in0=self.ones_b, scalar1=self.siluf[:, k:k + 1],
                                                  scalar2=None, op0=ALU.mult),
                 reads=[("ones_b",), ("siluf",)], writes=[("silurep", k)])
        modT, affn = self.modT, self.affn
        MODB = 7
        for v in (3, 4):
            for c in range(KC):
                col = v * 8 + c
                wt, wn = self.wadag[(v, c // 4)]
                cc = c % 4
                s.op("pe", [(lambda e, k=k: e.matmul(bank(MODB)[:, col:col + 1], wt[:, k, cc * 128:(cc + 1) * 128],
                                                    self.silub[:, k:k + 1], start=(k == 0), stop=(k == KC - 1)))
                            for k in range(KC)],
                     reads=[(wn,), ("silub",)], writes=[("ps", MODB)])
        s.op("dve", lambda e: e.tensor_tensor(out=modT[:, 24:40], in0=bank(MODB)[:, 24:40], in1=self.badaT_t[:, 24:40], op=ALU.add),
             reads=[("badaT",)], writes=[("ps", MODB), ("modT2",)])
        s.op("dve", lambda e: e.scalar_tensor_tensor(out=affn, in0=modT[:, 32:40], scalar=1.0, in1=self.gpreT_t[:, 8:16],
                                                     op0=ALU.add, op1=ALU.mult),
             reads=[("modT2",), ("gpreT",)], writes=[("affn",)])
        w_ada_v = self.w_ada.rearrange("(k p) n -> p k n", p=128)
        for (v, hf, src_key) in ((2, 1, (4, 0)), (5, 0, (3, 0)), (5, 1, (3, 1))):
            if True:
                t, nm = self.wadag[src_key]
                s.dma("pool", t, w_ada_v[:, :, v * D + hf * 512:v * D + (hf + 1) * 512], writes=[(nm,)])
                self.wadag[(v, hf)] = (t, nm)
        for g, v in enumerate((2, 5)):
            for half in range(2):
                wt, wn = self.wadag[(v, half)]
                b = 5 + half
                s.op("pe", [(lambda e, k=k: e.matmul(bank(b), silurep[:, k, :], wt[:, k, :],
                                                    start=(k == 0), stop=(k == KC - 1))) for k in range(KC)],
                     reads=[(wn,)] + [("silurep", k) for k in range(KC)], writes=[("ps", b)])
                sl = slice(g * D + half * 512, g * D + (half + 1) * 512)
                hs = slice(half * 512, (half + 1) * 512)
                s.op("dve", lambda e: e.tensor_tensor(out=gvec[:, g, hs], in0=bank(b), in1=bada_rep[:, sl], op=ALU.add),
                     reads=[("bada_rep",)], writes=[("ps", b), ("gvec", g, half)])
                s.op("dve", lambda e: e.tensor_tensor(out=gvec[:, g, hs], in0=gvec[:, g, hs], in1=gpost_rep[:, sl], op=ALU.mult),
                     reads=[("gpost_rep",), ("gvec", g, half)], writes=[("gvec", g, half)])
        self.dump("gvec", gvec.rearrange("p a b -> p (a b)"), 128, 2 * D, F32, [("gvec", g, h) for g in range(2) for h in range(2)])
        for n in ["silurep", "bada_rep", "gpost_rep"] + sorted(set(nm for (_, nm) in self.wadag.values())):
            A.release(n)

    def _body4(self, upto):
        nc, s, A = self.nc, self.s, self.A
        bank = self.bank
        merged, gvec = self.merged, self.gvec
        mm = lambda out, l, r, st, sp: (lambda e: e.matmul(out, l, r, start=st, stop=sp))
        wout = A.alloc("wout", (KC, D), BF16)
        s.dma("pool", wout, self.w_out.rearrange("(k p) n -> p k n", p=128), writes=[("wout",)])
        w2v = self.w_f2.rearrange("(j p) n -> p j n", p=128)
        w1v = self.w_f1.rearrange("(k p) n -> p k n", p=128)
        x2 = A.alloc("x2", (NT, D), F32)
        xin = [A.alloc("xin%d" % i, (D,), F32) for i in range(2)]
        tmpf = [A.alloc("tmpf%d" % i, (512,), F32) for i in range(2)]
        junkp = A.alloc("junkp", (512,), BF16)
        ssq = A.alloc("ssq", (4 * NT,), F32)
        ssy = A.alloc("ssy", (2 * NT,), F32)
        rsy = A.alloc("rsy", (2 * NT,), F32)
        ss, rstd = self.ss, self.rstd
        w2h = [A.alloc("w2a", (11, D), BF16), A.alloc("w2b", (11, D), BF16)]
        for gi, (hf, a, b_) in enumerate(((0, 0, 6), (0, 6, 11), (1, 0, 6), (1, 6, 11))):
            s.dma("pool", w2h[hf][:, a:b_, :], w2v[:, hf * 11 + a:hf * 11 + b_, :], writes=[("w2ab"[0:2] + "ab"[hf], gi)])
        w2keys = [("w2a", 0), ("w2a", 1), ("w2b", 2), ("w2b", 3)]

        def postnorm(bks, col, g, resid, rkeys, out_ap, okeys):
            for half, b in enumerate(bks):
                s.op("act", lambda e: e.activation(out=junkp, in_=bank(b), func=AF.Square,
                                                   accum_out=ssq[:, 2 * col + half:2 * col + half + 1]),
                     writes=[("ps", b), ("junkp",), ("ssq", col, half)])
            s.op("dve", lambda e: e.tensor_tensor(out=ssy[:, col:col + 1], in0=ssq[:, 2 * col:2 * col + 1],
                                                  in1=ssq[:, 2 * col + 1:2 * col + 2], op=ALU.add),
                 reads=[("ssq", col, 0), ("ssq", col, 1)], writes=[("ssy", col)])
            s.op("act", lambda e: e.activation(out=rsy[:, col:col + 1], in_=ssy[:, col:col + 1], func=AF.Ln,
                                               bias=self.eps_c, scale=1.0 / D),
                 reads=[("ssy", col), ("eps_c",)], writes=[("rsy", col)])
            s.op("act", lambda e: e.activation(out=rsy[:, col:col + 1], in_=rsy[:, col:col + 1], func=AF.Exp, scale=-0.5),
                 reads=[("rsy", col)], writes=[("rsy", col)])
            for half, b in enumerate(bks):
                hs = slice(half * 512, (half + 1) * 512)
                s.op("dve", lambda e: e.scalar_tensor_tensor(out=tmpf[half], in0=bank(b), scalar=rsy[:, col:col + 1],
                                                             in1=gvec[:, g, hs], op0=ALU.mult, op1=ALU.mult),
                     reads=[("rsy", col), ("gvec", g, half)], writes=[("ps", b), ("tmpf%d" % half,)])
                s.op("dve", lambda e: e.tensor_tensor(out=out_ap[:, hs], in0=tmpf[half], in1=resid[:, hs], op=ALU.add),
                     reads=[("tmpf%d" % half,)] + rkeys, writes=okeys)

        for it in range(NT):
            T = it // 4
            xi, xn = xin[it % 2], "xin%d" % (it % 2)
            s.dma("sp", xi, self.x[it * 128:(it + 1) * 128, :], writes=[(xn,)])
            bks = (0, 1) if it % 2 == 0 else (2, 3)
            for half, b in enumerate(bks):
                s.op("pe", [mm(bank(b), merged[:, k, it * 128:(it + 1) * 128], wout[:, k, half * 512:(half + 1) * 512],
                               k == 0, k == KC - 1) for k in range(KC)],
                     reads=[("merged", k, T) for k in range(KC)] + [("wout",)], writes=[("ps", b)])
            postnorm(bks, it, 0, xi, [(xn,)], x2[:, it, :], [("x2", it)])
        self.dump("x2", x2[:, 0, :], 128, D, F32, [("x2", 0)])
        for nme in ("merged", "wout", "xin0", "xin1"):
            A.release(nme)
        if upto <= 6:
            return

        h2T = A.alloc("h2T", (KC, 512), BF16)
        actT = A.alloc("actT", (NFC, 512), BF16)
        w1b = [A.alloc("w1b%d" % i, (KC, 512), BF16) for i in range(3)]
        sgl = [A.alloc("sgl%d" % i, (512,), BF16) for i in range(2)]
        ot = [A.alloc("ot%d" % i, (D,), F32) for i in range(2)]
        nw1 = 0

        def ffn_prenorm(T):
            self.prenorm(None, self.affn, self.modT[:, 24:32], h2T, "h2T", ss, rstd, "affn", [T], T,
                         src_sbuf=lambda it: (x2[:, it, :], ("x2", it)))

        ffn_prenorm(0)
        for T in range(NB):
            for j in range(NFC):
                if j % 2 == 0:
                    wb, wbn = w1b[nw1 % 3], "w1b%d" % (nw1 % 3)
                    nw1 += 1
                    s.dma("pool", wb[:, :, 0:256], w1v[:, :, j * 128:(j + 2) * 128], writes=[(wbn, "g")])
                    s.dma("pool", wb[:, :, 256:512], w1v[:, :, DFF + j * 128:DFF + (j + 2) * 128], writes=[(wbn, "u")])
                jo = (j % 2) * 128
                gb, ub = 4 + 2 * (j % 2), 5 + 2 * (j % 2)
                hk = [("h2T", k, 0) for k in range(KC)]
                s.op("pe", [mm(bank(gb), wb[:, k, jo:jo + 128], h2T[:, k, :], k == 0, k == KC - 1) for k in range(KC)],
                     reads=hk + [(wbn, "g")], writes=[("ps", gb)])
                s.op("pe", [mm(bank(ub), wb[:, k, 256 + jo:256 + jo + 128], h2T[:, k, :], k == 0, k == KC - 1) for k in range(KC)],
                     reads=hk + [(wbn, "u")], writes=[("ps", ub)])
                s.op("act", lambda e: e.activation(out=sgl[j % 2], in_=bank(gb), func=AF.Silu),
                     writes=[("ps", gb), ("sgl%d" % (j % 2),)])
                s.op("dve", lambda e: e.tensor_tensor(out=actT[:, j, :], in0=bank(ub), in1=sgl[j % 2], op=ALU.mult),
                     reads=[("sgl%d" % (j % 2),)], writes=[("ps", ub), ("actT", j)])
            if T + 1 < NB:
                ffn_prenorm(T + 1)
            for i in range(4):
                it = 4 * T + i
                bks = (0, 1) if it % 2 == 0 else (2, 3)
                for half, b in enumerate(bks):
                    s.op("pe", [mm(bank(b), actT[:, j, i * 128:(i + 1) * 128], w2h[j // 11][:, j % 11, half * 512:(half + 1) * 512],
                                   j == 0, j == NFC - 1) for j in range(NFC)],
                         reads=[("actT", j) for j in range(NFC)] + w2keys, writes=[("ps", b)])
                o_t, on = ot[it % 2], "ot%d" % (it % 2)
                postnorm(bks, NT + it, 1, x2[:, it, :], [("x2", it)], o_t, [(on,)])
                s.dma("sp", self.out[it * 128:(it + 1) * 128, :], o_t, reads=[(on,)])


def _consts():
    ident = np.eye(128, dtype=np.float32)
    sk = np.arange(128)[:, None]
    tq = np.arange(128)[None, :]
    mask = np.where(sk > tq, NEG, 0.0).astype(np.float32)
    inv_freq = 1.0 / (10000.0 ** (np.arange(0, 32, 2, dtype=np.float32) / 32.0))
    rope = np.zeros((128, 4), np.float32)
    for p in range(128):
        r = p % 32
        i = r % 16
        rope[p, 0] = inv_freq[i]
        rope[p, 1] = math.pi / 2
        rope[p, 2] = inv_freq[i]
        rope[p, 3] = (math.pi if r < 16 else 0.0)
    return dict(c_ident=ident.astype(ml_dtypes.bfloat16), c_identf=ident, c_mask=mask.astype(ml_dtypes.bfloat16),
                c_rope=rope)


def make_in_maps(x, c, positions, w_ada, b_ada, g_pre_mix, g_post_mix, g_pre_ffn, g_post_ffn,
                 w_in, b_forget, g_q_lora, w_uq, g_kv_lora, w_ukv, w_proj_fox, w_proj_mla,
                 w_out, w_ffn_in, w_ffn_out, cores=range(8)):
    f = lambda a: np.ascontiguousarray(np.asarray(a, dtype=np.float32))
    colT = lambda v, n: f(np.asarray(v, np.float32).reshape(n, 128).T)
    rep = lambda v: f(np.broadcast_to(np.asarray(v, np.float32)[None, :], (128, v.shape[0])))
    b_ada0 = np.asarray(b_ada[0], np.float32)
    shared = dict(
        w_ada=f(w_ada[0]), badaT=colT(b_ada0, 48),
        bada_rep=rep(np.concatenate([b_ada0[2 * D:3 * D], b_ada0[5 * D:6 * D]])),
        gpost_rep=rep(np.concatenate([np.asarray(g_post_mix[0], np.float32), np.asarray(g_post_ffn[0], np.float32)])),
        gpreT=f(np.concatenate([colT(g_pre_mix[0], 8), colT(g_pre_ffn[0], 8)], axis=1)),
        w_in=f(w_in[0]), nbf=f(np.asarray(b_forget[0], np.float32).reshape(8, 1)),
        gqT=colT(g_q_lora[0], 6), gkvT=colT(g_kv_lora[0], 2),
        w_uq=f(w_uq[0]), w_ukv=f(w_ukv[0]), w_pf=f(w_proj_fox[0]), w_pm=f(w_proj_mla[0]),
        w_out=f(w_out[0]), w_f1=f(w_ffn_in[0]), w_f2=f(w_ffn_out[0]),
    )
    shared.update(_consts())
    maps = []
    for b in cores:
        m = dict(shared)
        m["x"] = f(x[b])
        m["cT"] = colT(c[b], 8)
        pq = np.asarray(positions[b], np.int32).reshape(4, 1, 512)
        m["pos"] = np.ascontiguousarray(np.broadcast_to(pq, (4, 32, 512)).reshape(128, 512))
        maps.append(m)
    return maps


_NC_CACHE = {}


def kernel(**inputs):
    if "nc" not in _NC_CACHE:
        _NC_CACHE["nc"] = Builder().build()
    nc = _NC_CACHE["nc"]
    maps = make_in_maps(**inputs)
    res = run_bass_kernel_spmd(nc, maps, core_ids=list(range(8)))
    out = np.stack([np.asarray(r["out"], dtype=np.float32) for r in res.results], axis=0)
    return out
```
